# Optimizing a Trainium2 kernel written in Bass

```python
import math
import jax
import jax.numpy as jnp
from jax import lax
import numpy as np

D_MODEL = 1024
BATCH = 2
SEQ = 16384
DEPTH = 4

GRID_W = 64
CTX_LEN = 256
EPS = 1e-6
F32 = jnp.float32
POOL_WINDOWS = (2, 4, 8, 16)
N_POOL_GROUPS = len(POOL_WINDOWS)
D_POOL = D_MODEL // 4
POOL_GROUP = D_POOL // N_POOL_GROUPS
D_DIFF = D_MODEL - D_POOL
DIFF_HEAD = 64
DIFF_V = 2 * DIFF_HEAD
DIFF_HEADS = D_DIFF // DIFF_V
DIFF_SCALE = DIFF_HEAD ** -0.5
Q_BLOCK = 128
ROPE_BASE = 10000.0
ROPE_FREQS = DIFF_HEAD // 4
D_CONV = D_MODEL // 2
CONV_WIDTH = 31
D_HGRN = D_MODEL - D_CONV
HGRN_HEAD = 128
HGRN_HEADS = D_HGRN // HGRN_HEAD
CHUNK = 64
N_EVEN = (DEPTH + 1) // 2
N_ODD = DEPTH // 2
EVEN_SPLITS = (D_POOL, D_POOL, D_DIFF, D_DIFF, D_DIFF, D_DIFF)
ODD_SPLITS = (D_CONV, D_CONV, D_CONV, D_HGRN, D_HGRN, D_HGRN, D_HGRN, D_HGRN)
D_IN_EVEN = sum(EVEN_SPLITS)
D_IN_ODD = sum(ODD_SPLITS)
D_MIX = D_POOL + D_DIFF

kernel_name = 'hybrid_pool_diffattn_conformer_hgrn2_prefix_dit'


def rmsnorm(x, g):
    xf = x.astype(F32)
    y = xf * lax.rsqrt(jnp.mean(xf * xf, axis=-1, keepdims=True) + EPS)
    return (y * g.astype(F32)).astype(x.dtype)


def layernorm(x, g, b):
    xf = x.astype(F32)
    xc = xf - jnp.mean(xf, axis=-1, keepdims=True)
    y = xc * lax.rsqrt(jnp.mean(xc * xc, axis=-1, keepdims=True) + EPS)
    return (y * g.astype(F32) + b.astype(F32)).astype(x.dtype)


def split_cols(p, sizes):
    return jnp.split(p, [int(s) for s in np.cumsum(sizes)[:-1]], axis=-1)


def pool_mix(u, w_pool, scale):
    B, L, _ = u.shape
    uf = u.astype(F32)
    cs = jnp.concatenate([jnp.zeros((B, 1, D_POOL), F32), jnp.cumsum(uf, axis=1)], axis=1)
    t = np.arange(L)
    outs = []
    for g, w in enumerate(POOL_WINDOWS):
        lo = np.maximum(t - w // 2, 0)
        hi = np.minimum(t + w // 2 - 1, L - 1) + 1
        cnt = (hi - lo).astype(np.float32)
        sl = slice(g * POOL_GROUP, (g + 1) * POOL_GROUP)
        csg = cs[..., sl]
        outs.append((csg[:, hi] - csg[:, lo]) / cnt[None, :, None] - uf[..., sl])
    d = jnp.stack(outs, axis=2)
    y = jnp.einsum('blgc,gcd->blgd', d, w_pool.astype(F32)).reshape(B, L, D_POOL)
    return (y * scale.astype(F32)).astype(u.dtype)


def axial_rope(n_lat):
    rows = n_lat // GRID_W
    row = jnp.repeat(jnp.arange(rows), GRID_W)
    col = jnp.arange(rows * GRID_W) % GRID_W
    inv = ROPE_BASE ** (-jnp.arange(ROPE_FREQS, dtype=F32) / ROPE_FREQS)
    ang = jnp.stack([row, col], axis=-1).astype(F32)[:, :, None] * inv
    return jnp.cos(ang), jnp.sin(ang)


def apply_rope(t, cos, sin):
    tf = t.astype(F32).reshape(t.shape[:-1] + (2, 2, ROPE_FREQS))
    t1, t2 = tf[..., 0, :], tf[..., 1, :]
    out = jnp.stack([t1 * cos - t2 * sin, t2 * cos + t1 * sin], axis=-2)
    return out.reshape(t.shape).astype(t.dtype)


def diff_heads(q, k, v):
    B, L, _ = q.shape
    qh = q.reshape(B, L, DIFF_HEADS, 2, DIFF_HEAD).transpose(0, 2, 3, 1, 4)
    kh = k.reshape(B, L, DIFF_HEADS, 2, DIFF_HEAD).transpose(0, 2, 3, 1, 4)
    vh = v.reshape(B, L, DIFF_HEADS, DIFF_V).transpose(0, 2, 1, 3)
    return qh, kh, vh


def diff_softmax_mix(qb, k_all, v_all, lam):
    s = jnp.einsum('bhmqd,bhmkd->bhmqk', qb, k_all).astype(F32) * DIFF_SCALE
    p = jax.nn.softmax(s, axis=-1)
    a = p[:, :, 0] - lam * p[:, :, 1]
    return jnp.einsum('bhqk,bhkv->bhqv', a, v_all.astype(F32))


def diff_attention(q, k, v, qc, kc, vc, lam, with_ctx):
    B, L, _ = q.shape
    qh, kh, vh = diff_heads(q, k, v)
    qch, kch, vch = diff_heads(qc, kc, vc)
    cos, sin = axial_rope(L)
    qh = apply_rope(qh, cos, sin)
    kh = apply_rope(kh, cos, sin)
    k_all = jnp.concatenate([kch, kh], axis=3)
    v_all = jnp.concatenate([vch, vh], axis=2)
    nb = L // Q_BLOCK
    qb = jnp.moveaxis(qh.reshape(B, DIFF_HEADS, 2, nb, Q_BLOCK, DIFF_HEAD), 3, 0)
    ob = lax.map(lambda blk: diff_softmax_mix(blk, k_all, v_all, lam), qb)
    o = jnp.moveaxis(ob, 0, 2).reshape(B, DIFF_HEADS, L, DIFF_V)
    oc = diff_softmax_mix(qch, kch, vch, lam) if with_ctx else None
    return o, oc


def diff_post(o, g, lam_init):
    B, H, L, _ = o.shape
    return (rmsnorm(o, g) * (1.0 - lam_init)).transpose(0, 2, 1, 3).reshape(B, L, H * DIFF_V)


def even_mixer(h, hc, w_in, w_out, pool_w, pool_scale, lam_p, subln_g, lam_init, with_ctx):
    u, ga, q, k, v, gb = split_cols(h @ w_in, EVEN_SPLITS)
    uc, gac, qc, kc, vc, gbc = split_cols(hc @ w_in, EVEN_SPLITS)
    lp = lam_p.astype(F32)
    lam = jnp.exp(jnp.sum(lp[0] * lp[1])) - jnp.exp(jnp.sum(lp[2] * lp[3])) + lam_init
    o, oc = diff_attention(q, k, v, qc, kc, vc, lam, with_ctx)
    y_a = pool_mix(u, pool_w, pool_scale) * jax.nn.silu(ga)
    y_b = diff_post(o, subln_g, lam_init).astype(h.dtype) * jax.nn.silu(gb)
    y = jnp.concatenate([y_a, y_b], axis=-1) @ w_out
    if not with_ctx:
        return y, None
    yc_a = pool_mix(uc, pool_w, pool_scale) * jax.nn.silu(gac)
    yc_b = diff_post(oc, subln_g, lam_init).astype(hc.dtype) * jax.nn.silu(gbc)
    yc = jnp.concatenate([yc_a, yc_b], axis=-1) @ w_out
    return y, yc


def conv_module(a, b, gate, conv_w, conv_b, ln_g, ln_b):
    glu = a * jax.nn.sigmoid(b)
    z = lax.conv_general_dilated(glu, conv_w[:, None, :], window_strides=(1,),
                                 padding=[(CONV_WIDTH // 2, CONV_WIDTH // 2)],
                                 dimension_numbers=('NWC', 'WIO', 'NWC'),
                                 feature_group_count=D_CONV) + conv_b
    z = layernorm(z, ln_g, ln_b)
    return jax.nn.silu(z) * jax.nn.silu(gate)


def to_chunks(t):
    B, L, H, d = t.shape
    return t.reshape(B, L // CHUNK, CHUNK, H, d).transpose(1, 0, 3, 2, 4)


def from_chunks(t):
    nc, B, H, C, d = t.shape
    return t.transpose(1, 0, 3, 2, 4).reshape(B, nc * C, H, d)


def hgrn_scan(q, lf, k, v, s0):
    tri = jnp.tril(jnp.ones((CHUNK, CHUNK), dtype=bool))

    def step(S, inp):
        qt, lft, kt, vt = inp
        b = jnp.cumsum(lft, axis=2)
        inter = jnp.einsum('bhtc,bhcv->bhtv', qt * jnp.exp(b), S)
        rel = jnp.where(tri[:, :, None], b[:, :, :, None, :] - b[:, :, None, :, :], -jnp.inf)
        att = jnp.einsum('bhtsc,bhsc->bhts', qt[:, :, :, None, :] * jnp.exp(rel), kt)
        intra = jnp.einsum('bhts,bhsv->bhtv', att, vt)
        bl = b[:, :, -1:, :]
        S_new = jnp.exp(bl[:, :, 0, :])[..., None] * S + jnp.einsum('bhsc,bhsv->bhcv', kt * jnp.exp(bl - b), vt)
        return S_new, inter + intra

    s_fin, o = lax.scan(step, s0, (to_chunks(q), to_chunks(lf), to_chunks(k), to_chunks(v)))
    return s_fin, from_chunks(o)


def hgrn_gates(logit, lb):
    f = lb + (1.0 - lb) * jax.nn.sigmoid(logit)
    return jnp.log(f), 1.0 - f


def hgrn_bidir(q, ff, fb, v, qc, ffc, fbc, vc, lb, with_ctx):
    B = q.shape[0]
    s0 = jnp.zeros((B, HGRN_HEADS, HGRN_HEAD, HGRN_HEAD), F32)
    flip = lambda t: jnp.flip(t, axis=1)
    lf_f, k_f = hgrn_gates(ff, lb[0])
    lf_b, k_b = hgrn_gates(fb, lb[1])
    lfc_f, kc_f = hgrn_gates(ffc, lb[0])
    lfc_b, kc_b = hgrn_gates(fbc, lb[1])
    sc_f, oc_f = hgrn_scan(qc, lfc_f, kc_f, vc, s0)
    sc_b, oc_b = hgrn_scan(flip(qc), flip(lfc_b), flip(kc_b), flip(vc), s0)
    _, o_f = hgrn_scan(q, lf_f, k_f, v, sc_f)
    _, o_b = hgrn_scan(flip(q), flip(lf_b), flip(k_b), flip(v), sc_b)
    o = o_f + flip(o_b)
    oc = oc_f + flip(oc_b) if with_ctx else None
    return o, oc


def odd_mixer(h, hc, w_in, w_out, conv_w, conv_b, ln_g, ln_b, o_norm, lb, with_ctx):
    B, L, _ = h.shape
    Lc = hc.shape[1]
    ca, cb, cg, q, ff, fb, iv, og = split_cols(h @ w_in, ODD_SPLITS)
    cca, ccb, ccg, qc, ffc, fbc, ivc, ogc = split_cols(hc @ w_in, ODD_SPLITS)
    heads = lambda t: t.reshape(t.shape[0], t.shape[1], HGRN_HEADS, HGRN_HEAD).astype(F32)
    o, oc = hgrn_bidir(heads(jax.nn.silu(q)), heads(ff), heads(fb), heads(iv),
                       heads(jax.nn.silu(qc)), heads(ffc), heads(fbc), heads(ivc), lb, with_ctx)
    g_heads = o_norm.reshape(HGRN_HEADS, HGRN_HEAD)
    y_d = rmsnorm(o, g_heads).reshape(B, L, D_HGRN).astype(h.dtype) * jax.nn.silu(og)
    y_c = conv_module(ca, cb, cg, conv_w, conv_b, ln_g, ln_b)
    y = jnp.concatenate([y_c, y_d], axis=-1) @ w_out
    if not with_ctx:
        return y, None
    yc_d = rmsnorm(oc, g_heads).reshape(B, Lc, D_HGRN).astype(hc.dtype) * jax.nn.silu(ogc)
    yc_c = conv_module(cca, ccb, ccg, conv_w, conv_b, ln_g, ln_b)
    yc = jnp.concatenate([yc_c, yc_d], axis=-1) @ w_out
    return y, yc


def setup_inputs(seed: int = 0) -> dict:
    key = jax.random.key(seed)
    ks = jax.random.split(key, 22)

    def nrm(k, shape, s):
        return jax.random.normal(k, shape, F32) * s

    d = D_MODEL
    return {
        'x': nrm(ks[0], (BATCH, SEQ, d), 1.0),
        'c': nrm(ks[1], (BATCH, d), 1.0),
        'ctx': nrm(ks[2], (BATCH, CTX_LEN, d), 1.0),
        'c_ctx': nrm(ks[3], (d,), 1.0),
        'ada_w': nrm(ks[4], (DEPTH, d, 3 * d), 0.5 * d ** -0.5),
        'ada_b': nrm(ks[5], (DEPTH, 3 * d), 0.02),
        'norm_pre': 1.0 + nrm(ks[6], (DEPTH, d), 0.05),
        'norm_post': 1.0 + nrm(ks[7], (DEPTH, d), 0.05),
        'w_in_even': nrm(ks[8], (N_EVEN, d, D_IN_EVEN), d ** -0.5),
        'w_out_even': nrm(ks[9], (N_EVEN, D_MIX, d), D_MIX ** -0.5),
        'pool_w': nrm(ks[10], (N_EVEN, N_POOL_GROUPS, POOL_GROUP, POOL_GROUP), POOL_GROUP ** -0.5),
        'pool_scale': 1.0 + nrm(ks[11], (N_EVEN, D_POOL), 0.05),
        'diff_lambda': nrm(ks[12], (N_EVEN, 4, DIFF_HEAD), 0.1),
        'diff_subln': 1.0 + nrm(ks[13], (N_EVEN, DIFF_V), 0.05),
        'w_in_odd': nrm(ks[14], (N_ODD, d, D_IN_ODD), d ** -0.5),
        'w_out_odd': nrm(ks[15], (N_ODD, D_MIX, d), D_MIX ** -0.5),
        'conv_w': nrm(ks[16], (N_ODD, CONV_WIDTH, D_CONV), CONV_WIDTH ** -0.5),
        'conv_b': nrm(ks[17], (N_ODD, D_CONV), 0.02),
        'conv_ln_g': 1.0 + nrm(ks[18], (N_ODD, D_CONV), 0.05),
        'conv_ln_b': nrm(ks[19], (N_ODD, D_CONV), 0.02),
        'hgrn_norm': 1.0 + nrm(ks[20], (N_ODD, D_HGRN), 0.05),
        'hgrn_lb': 1.0 + nrm(ks[21], (2, DEPTH, D_HGRN), 0.5),
    }


def reference(x, c, ctx, c_ctx, ada_w, ada_b, norm_pre, norm_post, w_in_even, w_out_even,
              pool_w, pool_scale, diff_lambda, diff_subln, w_in_odd, w_out_odd,
              conv_w, conv_b, conv_ln_g, conv_ln_b, hgrn_norm, hgrn_lb):
    c_act = jax.nn.silu(c)
    cc_act = jax.nn.silu(c_ctx)
    lbs = jnp.cumsum(jax.nn.softmax(hgrn_lb.astype(F32), axis=1), axis=1)
    lbs = lbs - lbs[:, :1]
    for l in range(DEPTH):
        with_ctx = l < DEPTH - 1
        shift, scale, gate = jnp.split(c_act @ ada_w[l] + ada_b[l], 3, axis=-1)
        shift_c, scale_c, gate_c = jnp.split(cc_act @ ada_w[l] + ada_b[l], 3, axis=-1)
        h = rmsnorm(x, norm_pre[l]) * (1.0 + scale[:, None]) + shift[:, None]
        hc = rmsnorm(ctx, norm_pre[l]) * (1.0 + scale_c) + shift_c
        j = l // 2
        if l % 2 == 0:
            lam_init = 0.8 - 0.6 * math.exp(-0.3 * l)
            y, yc = even_mixer(h, hc, w_in_even[j], w_out_even[j], pool_w[j], pool_scale[j],
                               diff_lambda[j], diff_subln[j], lam_init, with_ctx)
        else:
            lb = lbs[:, l].reshape(2, HGRN_HEADS, HGRN_HEAD)
            y, yc = odd_mixer(h, hc, w_in_odd[j], w_out_odd[j], conv_w[j], conv_b[j],
                              conv_ln_g[j], conv_ln_b[j], hgrn_norm[j], lb, with_ctx)
        x = x + gate[:, None] * rmsnorm(y, norm_post[l])
        if with_ctx:
            ctx = ctx + gate_c * rmsnorm(yc, norm_post[l])
    return x
```

```python
import contextlib
import numpy as np
import concourse.bass as bass
import concourse.mybir as mybir
from concourse.bass_utils import run_bass_kernel_spmd

F32 = mybir.dt.float32
BF16 = mybir.dt.bfloat16
ALU = mybir.AluOpType
AF = mybir.ActivationFunctionType
AX = mybir.AxisListType


class Buf:
    __slots__ = ("name", "last_w", "readers", "excl")

    def __init__(self, name):
        self.name = name
        self.last_w = None
        self.readers = []
        self.excl = False


class V:
    __slots__ = ("ap", "bufs")

    def __init__(self, ap, bufs):
        self.ap = ap
        self.bufs = bufs


class T:
    def __init__(self, h, name, nslot=1, sdim=1):
        self.h = h
        self.name = name
        self.nslot = nslot
        self.sdim = sdim
        self.bufs = [Buf(f"{name}.{i}") for i in range(nslot)]
        self.shape = list(h.shape)

    def __getitem__(self, idx):
        ap = self.h[idx]
        if self.nslot == 1:
            return V(ap, self.bufs)
        if not isinstance(idx, tuple):
            idx = (idx,)
        bufs = self.bufs
        if len(idx) > self.sdim:
            s = idx[self.sdim]
            n = self.shape[self.sdim]
            per = n // self.nslot
            if isinstance(s, int):
                bufs = [self.bufs[s // per]]
            elif isinstance(s, slice):
                a = 0 if s.start is None else s.start
                b = n if s.stop is None else s.stop
                bufs = self.bufs[a // per:(b - 1) // per + 1]
        return V(ap, bufs)

    def v(self, ap):
        return V(ap, self.bufs)


class Op:
    __slots__ = ("eng", "fn", "waits", "sem", "val", "is_dma", "inc")


ENGS = ("pe", "act", "dve", "pool", "sp")


class Prog:
    N_DMA_SEM = 8

    def __init__(self, nc):
        self.nc = nc
        self.es = contextlib.ExitStack()
        self.ops = {e: [] for e in ENGS}
        self.cnt = {e: 0 for e in ENGS}
        self.ndma = {e: 0 for e in ENGS}
        self.sem = {}
        self.dsem = {}
        self.waited = {e: {} for e in ENGS}
        self.semvals = {}
        for e in ENGS:
            self.sem[e] = self.es.enter_context(nc.semaphore(f"s_{e}"))
        for e in ("sp", "pool", "act"):
            self.dsem[e] = [self.es.enter_context(nc.semaphore(f"d_{e}{i}")) for i in range(self.N_DMA_SEM)]
        self.n_ops = 0

    def sbuf(self, name, shape, dt, nslot=1, sdim=1):
        h = self.es.enter_context(self.nc.sbuf_tensor("sb_" + name, list(shape), dt))
        return T(h, name, nslot, sdim)

    def psum(self, name, shape, dt, nslot=1, sdim=1):
        h = self.es.enter_context(self.nc.psum_tensor("ps_" + name, list(shape), dt))
        t = T(h, name, nslot, sdim)
        for b in t.bufs:
            b.excl = True
        return t

    def dram(self, name, shape, dt, kind="Internal", nslot=1, sdim=0):
        h = self.nc.dram_tensor(name, list(shape), dt, kind=kind)
        return T(h, name, nslot, sdim)

    def _deps(self, eng, reads, writes):
        deps = []
        for b in reads:
            if b.last_w is not None:
                deps.append((b.last_w, "raw"))
            if b.excl:
                for r in b.readers:
                    if r.eng != eng:
                        deps.append((r, "rar"))
        for b in writes:
            if b.last_w is not None:
                deps.append((b.last_w, "waw"))
            for r in b.readers:
                deps.append((r, "war"))
        need = {}
        for d, kind in deps:
            if not d.is_dma and d.eng == eng:
                if eng in ("pe", "sp"):
                    continue
                if kind in ("war", "rar"):
                    continue
            k = id(d.sem)
            if k not in need or need[k][1] < d.val:
                need[k] = (d.sem, d.val)
        out = []
        w = self.waited[eng]
        for k, (s, v) in need.items():
            if w.get(k, 0) >= v:
                continue
            w[k] = v
            out.append((s, v))
        return out

    def _record(self, eng, fn, reads, writes, is_dma=False):
        op = Op()
        op.eng = eng
        op.fn = fn
        op.is_dma = is_dma
        op.waits = self._deps(eng, reads, writes)
        if is_dma:
            i = self.ndma[eng]
            self.ndma[eng] += 1
            R = self.N_DMA_SEM
            op.sem = self.dsem[eng][i % R]
            op.val = 16 * (i // R + 1)
            op.inc = 16
            if i >= R:
                k = id(op.sem)
                pv = 16 * (i // R)
                if self.waited[eng].get(k, 0) < pv:
                    self.waited[eng][k] = pv
                    op.waits.append((op.sem, pv))
        else:
            self.cnt[eng] += 1
            op.sem = self.sem[eng]
            op.val = self.cnt[eng]
            op.inc = 1
        self.semvals[id(op.sem)] = (op.sem, op.val)
        for b in reads:
            b.readers.append(op)
        for b in writes:
            b.last_w = op
            b.readers = []
        self.ops[eng].append(op)
        self.n_ops += 1
        return op

    def I(self, eng, meth, *, extra_reads=(), extra_writes=(), dma_like=False, **kw):
        reads, writes = [], []
        kws = {}
        for k, a in kw.items():
            if isinstance(a, V):
                if k.startswith("out") or k in ("accum_out", "ap"):
                    writes += a.bufs
                else:
                    reads += a.bufs
                kws[k] = a.ap
            else:
                kws[k] = a
        for a in extra_reads:
            reads += a.bufs
        for a in extra_writes:
            writes += a.bufs
        is_dma = meth == "dma_start" or dma_like

        def fn(e, meth=meth, kws=kws):
            return getattr(e, meth)(**kws)
        return self._record(eng, fn, reads, writes, is_dma)

    def dma(self, eng, out, in_, **kw):
        o = out if isinstance(out, V) else V(out, [])
        i = in_ if isinstance(in_, V) else V(in_, [])
        return self.I(eng, "dma_start", out=o, in_=i, **kw)

    def mm(self, out, lhsT, rhs, start=True, stop=True, acc_read=False, **kw):
        return self.I("pe", "matmul", out=out, lhsT=lhsT, rhs=rhs, start=start, stop=stop, **kw)

    def tr(self, out, in_, ident):
        return self.I("pe", "transpose", out=out, in_=in_, identity=ident)

    def act(self, out, in_, func, eng="act", **kw):
        return self.I(eng, "activation", out=out, in_=in_, func=func, **kw)

    def finish(self):
        nc = self.nc
        fin = []
        for k, (s, v) in self.semvals.items():
            if self.waited["sp"].get(k, 0) < v:
                fin.append((s, v))
        with nc.Block() as block:
            def emit(eng_obj, name):
                for op in self.ops[name]:
                    for (s, v) in op.waits:
                        eng_obj.wait_ge(s, v)
                    ins = op.fn(eng_obj)
                    ins.then_inc(op.sem, op.inc)
                if name == "sp":
                    for (s, v) in fin:
                        eng_obj.wait_ge(s, v)

            if self.ops["sp"] or True:
                @block.sync
                def _(e):
                    emit(e, "sp")
            if self.ops["pe"]:
                @block.tensor
                def _(e):
                    emit(e, "pe")
            if self.ops["act"]:
                @block.scalar
                def _(e):
                    emit(e, "act")
            if self.ops["dve"]:
                @block.vector
                def _(e):
                    emit(e, "dve")
            if self.ops["pool"]:
                @block.gpsimd
                def _(e):
                    emit(e, "pool")
        self.es.close()


D = 1024
NOWN = 4096
NCTX = 256
NT = NOWN + NCTX
EPS = 1e-6
import ml_dtypes
NPBF = ml_dtypes.bfloat16


def mk_consts(P, inp):
    C = {}
    id32 = P.sbuf("id32", [128, 128], F32)
    P.dma("sp", id32[:, :], inp["ident"])
    idb = P.sbuf("idb", [128, 128], BF16)
    P.I("dve", "tensor_copy", out=idb[:, :], in_=id32[:, :])
    ones = P.sbuf("ones", [1, 128], F32)
    P.I("dve", "memset", ap=ones[:, :], constant=1.0)
    C["id32"], C["idb"], C["ones"] = id32, idb, ones
    return C


def load_w_bf16(P, w_dram, ncols, name, stage, engs=("dve", "pool")):
    wb = P.sbuf(name, [128, 8, ncols], BF16)
    wv = w_dram.rearrange("(j p) c -> p j c", p=128)
    nb = (ncols + 511) // 512
    for b in range(nb):
        c0, c1 = b * 512, min(ncols, (b + 1) * 512)
        for jh in range(2):
            st = stage[(2 * b + jh) % len(stage)]
            P.dma("sp", st[:, :, 0:c1 - c0], wv[:, jh * 4:jh * 4 + 4, c0:c1])
            P.I(engs[(2 * b + jh) % len(engs)], "tensor_copy", out=wb[:, jh * 4:jh * 4 + 4, c0:c1], in_=st[:, :, 0:c1 - c0])
    return wb


def ada_bcast(P, C, inp, nblk, modes, dests, stage, psA, gkey):
    ccol = P.sbuf("ccol", [128, 16], F32)
    P.dma("sp", ccol[:, :], inp["ccols"])
    csig = P.sbuf("csig", [128, 16], F32)
    P.act(csig[:, :], ccol[:, :], AF.Sigmoid)
    cact = P.sbuf("cact", [128, 16], BF16)
    P.I("dve", "tensor_tensor", out=cact[:, :], in0=ccol[:, :], in1=csig[:, :], op=ALU.mult)
    wv = inp["ada_w"].rearrange("(j p) c -> p j c", p=128)
    wbb = P.sbuf("adawb", [128, 8, 512], BF16)
    brow = [P.sbuf(f"brow{i}", [1, 512], F32) for i in range(2)]
    grow = [P.sbuf(f"grow{i}", [1, 512], F32) for i in range(2)]
    row = [P.sbuf(f"arow{i}", [1, 512], F32) for i in range(2)]
    row2 = [P.sbuf(f"arow2{i}", [1, 512], F32) for i in range(2)]
    k = 0
    for b in range(nblk):
        for jh in range(2):
            st = stage[(2 * b + jh) % len(stage)]
            P.dma("sp", st[:, :, :], wv[:, jh * 4:jh * 4 + 4, b * 512:(b + 1) * 512])
            P.I("dve", "tensor_copy", out=wbb[:, jh * 4:jh * 4 + 4, :], in_=st[:, :, :])
        P.dma("sp", brow[b % 2][:, :], inp["ada_b"][:, b * 512:(b + 1) * 512])
        mode = modes[b]
        if mode != "plain":
            gc = (b * 512) % D
            P.dma("sp", grow[b % 2][:, :], inp[gkey][:, gc:gc + 512])
        for v in range(2):
            for j in range(8):
                P.mm(psA[0:1, :], lhsT=cact[:, v * 8 + j:v * 8 + j + 1], rhs=wbb[:, j, :], start=(j == 0), stop=(j == 7))
            r = row[k % 2]
            P.I("dve", "tensor_tensor", out=r[:, :], in0=psA[0:1, :], in1=brow[b % 2][:, :], op=ALU.add)
            if mode == "onep_g":
                r2 = row2[k % 2]
                P.I("dve", "scalar_tensor_tensor", out=r2[:, :], in0=r[:, :], scalar=1.0, in1=grow[b % 2][:, :], op0=ALU.add, op1=ALU.mult)
                r = r2
            elif mode == "g":
                r2 = row2[k % 2]
                P.I("dve", "tensor_tensor", out=r2[:, :], in0=r[:, :], in1=grow[b % 2][:, :], op=ALU.mult)
                r = r2
            k += 1
            dt_, dc = dests[b][v]
            P.mm(psA[:, :], lhsT=C["ones"][:, :], rhs=r[0:1, :])
            P.I("dve", "tensor_copy", out=dt_[:, dc:dc + 512], in_=psA[:, :])


def pre_norm_rows(P, C, inp, _unused, stage, psA):
    A = [P.sbuf(f"Abc{v}", [128, 1024], F32) for v in range(2)]
    B = [P.sbuf(f"Bbc{v}", [128, 1024], F32) for v in range(2)]
    dests = [[(B[0], 0), (B[1], 0)], [(B[0], 512), (B[1], 512)], [(A[0], 0), (A[1], 0)], [(A[0], 512), (A[1], 512)]]
    ada_bcast(P, C, inp, 4, ["plain", "plain", "onep_g", "onep_g"], dests, stage, psA, "norm_pre")
    return [A[0], B[0], A[1], B[1]]


class PreCtx:
    pass


def emit_pre_group(P, C, K, x_view_fn, ntile, AB, gi):
    hT = K.hT[gi % 2]
    for i in range(ntile):
        k = K.cnt
        K.cnt += 1
        xt = K.xt[k % 2]
        P.dma("sp", xt[:, :], x_view_fn(i))
        sq = K.sq
        P.act(sq[:, :], xt[:, :], AF.Square)
        ssq = K.ssq[k % 2]
        P.I("dve", "tensor_reduce", out=ssq[:, 0:1], in_=sq[:, :], axis=AX.X, op=ALU.add)
        P.I("dve", "tensor_scalar", out=ssq[:, 1:2], in0=ssq[:, 0:1], scalar1=1.0 / D, scalar2=EPS, op0=ALU.mult, op1=ALU.add)
        P.act(ssq[:, 3:4], ssq[:, 1:2], AF.Sqrt)
        P.I("dve", "reciprocal", out=ssq[:, 2:3], in_=ssq[:, 3:4])
        t32 = K.t32
        P.I("dve", "scalar_tensor_tensor", out=t32[:, :], in0=xt[:, :], scalar=ssq[:, 2:3], in1=AB[0][:, :], op0=ALU.mult, op1=ALU.mult)
        hb = K.hb[k % 2]
        P.I("pool", "tensor_tensor", out=hb[:, :], in0=t32[:, :], in1=AB[1][:, :], op=ALU.add)
        ptr = K.ptr[k % 2]
        for j in range(8):
            P.tr(ptr[:, j * 128:(j + 1) * 128], hb[:, j * 128:(j + 1) * 128], C["idb"][:, :])
        P.act(hT.v(hT.h[:, :, i * 128:(i + 1) * 128]), ptr.v(ptr.h[:, :].rearrange("p (j t) -> p j t", j=8)), AF.Copy)
    return hT


def alloc_pre(P):
    K = PreCtx()
    K.cnt = 0
    K.xt = [P.sbuf(f"xt{i}", [128, D], F32) for i in range(2)]
    K.sq = P.sbuf("sq", [128, D], BF16)
    K.ssq = [P.sbuf(f"ssq{i}", [128, 4], F32) for i in range(2)]
    K.t32 = P.sbuf("t32", [128, D], F32)
    K.hb = [P.sbuf(f"hb{i}", [128, D], BF16) for i in range(2)]
    K.hT = [P.sbuf(f"hT{i}", [128, 8, 512], BF16) for i in range(2)]
    K.ptr = [P.psum(f"ptr{i}", [128, D], BF16) for i in range(2)]
    return K


def dram_in(nc, name, shape, dt=F32):
    return nc.dram_tensor(name, list(shape), dt, kind="ExternalInput").ap()


def dram_out(nc, name, shape, dt):
    return nc.dram_tensor(name, list(shape), dt, kind="ExternalOutput").ap()


STOP = 99


def build_A_even(n_own=NOWN, n_ctx=NCTX):
    nt = n_own + n_ctx
    nc = bass.Bass("TRN2", target_bir_lowering=False)
    P = Prog(nc)
    inp = dict(
        x=dram_in(nc, "x", [n_own, D]), ctx=dram_in(nc, "ctx", [n_ctx, D]), ccols=dram_in(nc, "ccols", [128, 16]),
        ada_w=dram_in(nc, "ada_w", [D, 2048]), ada_b=dram_in(nc, "ada_b", [1, 2048]), norm_pre=dram_in(nc, "norm_pre", [1, D]),
        w_in=dram_in(nc, "w_in", [D, 3584]), ident=dram_in(nc, "ident", [128, 128]), rmat=dram_in(nc, "rmat", [128, 128]),
        cosT=dram_in(nc, "cosT", [128, n_own]), sinT=dram_in(nc, "sinT", [128, n_own]))
    tm_d = dram_out(nc, "tm", [nt, 2048], BF16)
    fm_d = dram_out(nc, "fm", [12, 128, nt], BF16)
    C = mk_consts(P, inp)
    stage = [P.sbuf(f"stage{i}", [128, 4, 512], F32) for i in range(2)]
    psA = P.psum("psA", [128, 512], F32)
    if STOP == 0:
        P.finish()
        return nc, P
    AB = pre_norm_rows(P, C, inp, None, stage, psA)
    if STOP == 1:
        P.finish()
        return nc, P
    wb = load_w_bf16(P, inp["w_in"], 3584, "wb", stage)
    rm32 = P.sbuf("rm32", [128, 128], F32)
    P.dma("sp", rm32[:, :], inp["rmat"])
    rmb = P.sbuf("rmb", [128, 128], BF16)
    P.I("dve", "tensor_copy", out=rmb[:, :], in_=rm32[:, :])
    if STOP == 2:
        P.finish()
        return nc, P
    K = alloc_pre(P)
    pst = [P.psum(f"pst{i}", [128, 512], F32) for i in range(2)]
    psf = [P.psum(f"psf{i}", [128, 512], F32) for i in range(2)]
    psr = P.psum("psr", [128, 512], F32)
    tmt = [P.sbuf(f"tmt{i}", [128, 2048], BF16) for i in range(2)]
    cs = [P.sbuf(f"cs{i}", [128, 512], F32) for i in range(2)]
    sn = [P.sbuf(f"sn{i}", [128, 512], F32) for i in range(2)]
    qraw = [P.sbuf(f"qraw{i}", [128, 512], BF16) for i in range(2)]
    t1 = [P.sbuf(f"t1{i}", [128, 512], F32) for i in range(2)]
    t2 = [P.sbuf(f"t2{i}", [128, 512], F32) for i in range(2)]
    fmo = [P.sbuf(f"fmo{i}", [128, 512], BF16) for i in range(2)]
    TMB = [(0, 512, [(0, 0, 256, False), (256, 256, 256, True)]),
           (2048, 512, [(0, 512, 512, False)]),
           (2560, 512, [(0, 1024, 256, False), (256, 1280, 256, True)]),
           (3072, 512, [(0, 1536, 512, True)])]
    groups = [(g * 512, 4, False) for g in range(n_own // 512)] + [(n_own, n_ctx // 128, True)]
    kt = 0
    kf = 0
    for gi, (tok0, ntile, is_ctx) in enumerate(groups):
        n = ntile * 128
        if is_ctx:
            xfn = lambda i: inp["ctx"][i * 128:(i + 1) * 128, :]
            ab = AB[2:4]
        else:
            xfn = lambda i, tok0=tok0: inp["x"][tok0 + i * 128: tok0 + (i + 1) * 128, :]
            ab = AB[0:2]
            P.dma("sp", cs[gi % 2][:, 0:n], inp["cosT"][:, tok0:tok0 + n])
            P.dma("sp", sn[gi % 2][:, 0:n], inp["sinT"][:, tok0:tok0 + n])
        hT = emit_pre_group(P, C, K, xfn, ntile, ab, gi)
        if STOP == 3:
            break
        for i in range(ntile):
            tt = tmt[(gi * 4 + i) % 2]
            for (wc0, wn, eps_) in TMB:
                ps = pst[kt % 2]
                kt += 1
                for j in range(8):
                    P.mm(ps[:, 0:wn], lhsT=hT[:, j, i * 128:(i + 1) * 128], rhs=wb[:, j, wc0:wc0 + wn], start=(j == 0), stop=(j == 7))
                for (pc, dc, nn, silu) in eps_:
                    if silu:
                        P.act(tt[:, dc:dc + nn], ps[:, pc:pc + nn], AF.Silu)
                    else:
                        P.I("dve", "tensor_copy", out=tt[:, dc:dc + nn], in_=ps[:, pc:pc + nn])
            P.dma("sp", tm_d[tok0 + i * 128: tok0 + (i + 1) * 128, :], tt[:, :])
        if STOP == 4:
            break
        for blk in range(12):
            wc0 = 512 + blk * 128
            ps = psf[kf % 2]
            for j in range(8):
                P.mm(ps[:, 0:n], lhsT=wb[:, j, wc0:wc0 + 128], rhs=hT[:, j, 0:n], start=(j == 0), stop=(j == 7))
            fo = fmo[kf % 2]
            if is_ctx or STOP == 7:
                P.act(fo[:, 0:n], ps[:, 0:n], AF.Copy)
            else:
                qr = qraw[kf % 2]
                P.act(qr[:, 0:n], ps[:, 0:n], AF.Copy)
                if STOP not in (8, 9):
                    P.mm(psr[:, 0:n], lhsT=rmb[:, :], rhs=qr[:, 0:n])
                P.I("dve", "tensor_tensor", out=t1[kf % 2][:, 0:n], in0=(qr if STOP == 9 else ps)[:, 0:n], in1=cs[gi % 2][:, 0:n], op=ALU.mult, extra_reads=([qr[:, 0:n]] if STOP == 10 else []))
                P.I("dve", "tensor_tensor", out=t2[kf % 2][:, 0:n], in0=(psr if STOP not in (8, 9) else (ps if STOP == 8 else qr))[:, 0:n], in1=sn[gi % 2][:, 0:n], op=ALU.mult)
                P.I("dve" if STOP == 5 else "pool", "tensor_tensor", out=fo[:, 0:n], in0=t1[kf % 2][:, 0:n], in1=t2[kf % 2][:, 0:n], op=ALU.add)
            if STOP != 6:
                P.dma("sp", fm_d[blk, :, tok0:tok0 + n], fo[:, 0:n])
            kf += 1
    P.finish()
    return nc, P


def post_setup(P, C, inp, stage, psM):
    G = [P.sbuf(f"Gbc{v}", [128, 1024], F32) for v in range(2)]
    dests = [[(G[0], 0), (G[1], 0)], [(G[0], 512), (G[1], 512)]]
    ada_bcast(P, C, inp, 2, ["g", "g"], dests, stage, psM, "norm_post")
    wo = load_w_bf16(P, inp["w_out"], 1024, "wo", stage)
    K = PreCtx()
    K.G = G
    K.wo = wo
    K.mixt = [P.sbuf(f"mixt{i}", [128, D], BF16) for i in range(2)]
    K.mixT = [P.sbuf(f"mixT{i}", [128, 8, 128], BF16) for i in range(2)]
    K.xt = [P.sbuf(f"pxt{i}", [128, D], F32) for i in range(2)]
    K.sq = P.sbuf("psq", [128, D], BF16)
    K.ssq = [P.sbuf(f"pssq{i}", [128, 4], F32) for i in range(2)]
    K.t32 = P.sbuf("pt32", [128, D], F32)
    K.xo = [P.sbuf(f"pxo{i}", [128, D], F32) for i in range(2)]
    K.cnt = 0
    return K


def post_tile(P, C, K, mix_src, x_src, x_dst, v, psT, psY, mix_in_sbuf=None):
    k = K.cnt
    K.cnt += 1
    if mix_in_sbuf is None:
        mt = K.mixt[k % 2]
        P.dma("sp", mt[:, :], mix_src)
    else:
        mt = mix_in_sbuf
    for j in range(8):
        P.tr(psT[:, j * 128:(j + 1) * 128], mt[:, j * 128:(j + 1) * 128], C["idb"][:, :])
    mT = K.mixT[k % 2]
    P.act(mT.v(mT.h[:, :, :]), psT.v(psT.h[:, :].rearrange("p (j t) -> p j t", j=8)), AF.Copy)
    for cb in range(2):
        for j in range(8):
            P.mm(psY[:, cb * 512:(cb + 1) * 512], lhsT=mT[:, j, :], rhs=K.wo[:, j, cb * 512:(cb + 1) * 512], start=(j == 0), stop=(j == 7))
    xt = K.xt[k % 2]
    P.dma("sp", xt[:, :], x_src)
    P.act(K.sq[:, :], psY[:, 0:1024], AF.Square)
    ssq = K.ssq[k % 2]
    P.I("dve", "tensor_reduce", out=ssq[:, 0:1], in_=K.sq[:, :], axis=AX.X, op=ALU.add)
    P.I("dve", "tensor_scalar", out=ssq[:, 1:2], in0=ssq[:, 0:1], scalar1=1.0 / D, scalar2=EPS, op0=ALU.mult, op1=ALU.add)
    P.act(ssq[:, 3:4], ssq[:, 1:2], AF.Sqrt)
    P.I("dve", "reciprocal", out=ssq[:, 2:3], in_=ssq[:, 3:4])
    P.I("dve", "scalar_tensor_tensor", out=K.t32[:, :], in0=psY[:, 0:1024], scalar=ssq[:, 2:3], in1=K.G[v][:, :], op0=ALU.mult, op1=ALU.mult)
    xo = K.xo[k % 2]
    P.I("pool", "tensor_tensor", out=xo[:, :], in0=K.t32[:, :], in1=xt[:, :], op=ALU.add)
    P.dma("sp", x_dst, xo[:, :])


def build_B_even(n_own=NOWN, n_lat=4 * NOWN, n_ctx=NCTX):
    nt = n_own + n_ctx
    nk = n_lat + n_ctx
    nkb = nk // 128
    nc = bass.Bass("TRN2", target_bir_lowering=False)
    P = Prog(nc)
    inp = dict(
        x=dram_in(nc, "x", [n_own, D]), ctx=dram_in(nc, "ctx", [n_ctx, D]), ccols=dram_in(nc, "ccols", [128, 16]),
        ada_w=dram_in(nc, "ada_w", [D, 1024]), ada_b=dram_in(nc, "ada_b", [1, 1024]), norm_post=dram_in(nc, "norm_post", [1, D]),
        w_out=dram_in(nc, "w_out", [D, D]), ident=dram_in(nc, "ident", [128, 128]),
        qT=dram_in(nc, "qT", [6, 128, nt], BF16), kT=dram_in(nc, "kT", [6, 128, nk], BF16), vall=dram_in(nc, "vall", [nk, 768], BF16),
        gates=dram_in(nc, "gates", [nt, 1024], BF16), uh=dram_in(nc, "uh", [n_own + 16, 256], BF16), uch=dram_in(nc, "uch", [n_ctx + 16, 256], BF16),
        bandA=dram_in(nc, "bandA", [128, 5 * 4 * 128]), bandB=dram_in(nc, "bandB", [16, 5 * 4 * 128]),
        pool_w=dram_in(nc, "pool_w", [64, 4 * 64]), pool_scale=dram_in(nc, "pool_scale", [1, 256]),
        lamp=dram_in(nc, "lamp", [1, 256]), subln=dram_in(nc, "subln", [1, 128]), lamc=dram_in(nc, "lamc", [1, 2]))
    xo_d = dram_out(nc, "xo", [n_own, D], F32)
    co_d = dram_out(nc, "co", [n_ctx, D], F32)
    mix_d = P.dram("mixd", [nt, D], BF16, nslot=nt // 128, sdim=0)
    C = mk_consts(P, inp)
    stage = [P.sbuf(f"stage{i}", [128, 4, 512], F32) for i in range(2)]
    PSA = P.psum("PSA", [128, 3 * 512], F32, nslot=3)
    PSO = P.psum("PSO", [128, 3 * 512], F32, nslot=3)
    PST = P.psum("PST", [128, D], BF16)
    PSM = P.psum("PSM", [128, 512], F32)
    KP = post_setup(P, C, inp, stage, PSM)

    lamp = P.sbuf("lamp", [1, 256], F32)
    P.dma("sp", lamp[:, :], inp["lamp"])
    lamc = P.sbuf("lamc", [1, 2], F32)
    P.dma("sp", lamc[:, :], inp["lamc"])
    lw = P.sbuf("lw", [1, 16], F32)
    lpp = P.sbuf("lpp", [1, 128], F32)
    P.I("dve", "tensor_tensor", out=lpp[:, 0:64], in0=lamp[:, 0:64], in1=lamp[:, 64:128], op=ALU.mult)
    P.I("dve", "tensor_tensor", out=lpp[:, 64:128], in0=lamp[:, 128:192], in1=lamp[:, 192:256], op=ALU.mult)
    P.I("dve", "tensor_reduce", out=lw[:, 0:1], in_=lpp[:, 0:64], axis=AX.X, op=ALU.add)
    P.I("dve", "tensor_reduce", out=lw[:, 1:2], in_=lpp[:, 64:128], axis=AX.X, op=ALU.add)
    P.act(lw[:, 2:4], lw[:, 0:2], AF.Exp)
    P.I("dve", "tensor_tensor", out=lw[:, 4:5], in0=lw[:, 2:3], in1=lw[:, 3:4], op=ALU.subtract)
    P.I("dve", "tensor_tensor", out=lw[:, 5:6], in0=lw[:, 4:5], in1=lamc[:, 0:1], op=ALU.add)
    P.I("dve", "tensor_scalar", out=lw[:, 6:7], in0=lw[:, 5:6], scalar1=-1.0, scalar2=None, op0=ALU.mult)
    neglam = P.sbuf("neglam", [128, 1], F32)
    P.mm(PSM[:, 0:1], lhsT=C["ones"][:, :], rhs=lw[0:1, 6:7])
    P.I("dve", "tensor_copy", out=neglam[:, :], in_=PSM[:, 0:1])
    sub = P.sbuf("sub", [1, 128], F32)
    P.dma("sp", sub[:, :], inp["subln"])
    sub2 = P.sbuf("sub2", [1, 128], F32)
    P.I("dve", "tensor_scalar", out=sub2[:, :], in0=sub[:, :], scalar1=lamc[:, 1:2], scalar2=None, op0=ALU.mult)
    subg = P.sbuf("subg", [128, 128], F32)
    P.mm(PSM[:, 0:128], lhsT=C["ones"][:, :], rhs=sub2[0:1, :])
    P.I("dve", "tensor_copy", out=subg[:, :], in_=PSM[:, 0:128])
    psc = P.sbuf("psc", [1, 256], F32)
    P.dma("sp", psc[:, :], inp["pool_scale"])
    pscb = P.sbuf("pscb", [128, 256], F32)
    P.mm(PSM[:, 0:256], lhsT=C["ones"][:, :], rhs=psc[0:1, :])
    P.I("dve", "tensor_copy", out=pscb[:, :], in_=PSM[:, 0:256])

    bA = P.sbuf("bA", [128, 2560], BF16)
    bB = P.sbuf("bB", [16, 2560], BF16)
    sflat = [st.h[:, :, :].rearrange("p a b -> p (a b)") for st in stage]
    P.dma("sp", stage[0].v(sflat[0][:, 0:2048]), inp["bandA"][:, 0:2048])
    P.I("dve", "tensor_copy", out=bA[:, 0:2048], in_=stage[0].v(sflat[0][:, 0:2048]))
    P.dma("sp", stage[1].v(sflat[1][:, 0:512]), inp["bandA"][:, 2048:2560])
    P.I("dve", "tensor_copy", out=bA[:, 2048:2560], in_=stage[1].v(sflat[1][:, 0:512]))
    P.dma("sp", stage[0].v(sflat[0][0:16, 0:2048]), inp["bandB"][:, 0:2048])
    P.I("dve", "tensor_copy", out=bB[:, 0:2048], in_=stage[0].v(sflat[0][0:16, 0:2048]))
    P.dma("sp", stage[1].v(sflat[1][0:16, 0:512]), inp["bandB"][:, 2048:2560])
    P.I("dve", "tensor_copy", out=bB[:, 2048:2560], in_=stage[1].v(sflat[1][0:16, 0:512]))
    pw32 = P.sbuf("pw32", [64, 256], F32)
    P.dma("sp", pw32[:, :], inp["pool_w"])
    pw = P.sbuf("pw", [64, 256], BF16)
    P.I("dve", "tensor_copy", out=pw[:, :], in_=pw32[:, :])
    uA = [P.sbuf(f"uA{i}", [128, 256], BF16) for i in range(2)]
    uB = [P.sbuf(f"uB{i}", [16, 256], BF16) for i in range(2)]
    dT = [P.sbuf(f"dT{i}", [64, 512], BF16) for i in range(2)]
    gat = [P.sbuf(f"gat{i}", [128, 256], BF16) for i in range(2)]
    ytmp = [P.sbuf(f"ytmp{i}", [128, 256], F32) for i in range(2)]
    ya = [P.sbuf(f"ya{i}", [128, 256], BF16) for i in range(2)]
    ntile_own = n_own // 128
    ntile_ctx = n_ctx // 128
    tiles = [("x", i) for i in range(ntile_own)] + [("c", i) for i in range(ntile_ctx)]
    for k, (kind, i) in enumerate(tiles):
        if kind == "x":
            src = inp["uh"]
            typ = 0 if i == 0 else (2 if i == ntile_own - 1 else 1)
            row0 = i * 128
            tok0 = i * 128
        else:
            src = inp["uch"]
            typ = 3 if i == 0 else 4
            row0 = i * 128
            tok0 = n_own + i * 128
        P.dma("sp", uA[k % 2][:, :], src[row0:row0 + 128, :])
        P.dma("sp", uB[k % 2][:, :], src[row0 + 128:row0 + 144, :])
        P.dma("sp", gat[k % 2][:, :], inp["gates"][tok0:tok0 + 128, 0:256])
        for g in range(4):
            bo = (typ * 4 + g) * 128
            P.mm(PSM[0:64, g * 128:(g + 1) * 128], lhsT=uA[k % 2][:, g * 64:(g + 1) * 64], rhs=bA[:, bo:bo + 128], start=True, stop=False)
            P.mm(PSM[0:64, g * 128:(g + 1) * 128], lhsT=uB[k % 2][:, g * 64:(g + 1) * 64], rhs=bB[:, bo:bo + 128], start=False, stop=True)
        P.act(dT[k % 2][:, :], PSM[0:64, :], AF.Copy)
        yps = PSA[:, 1024:1024 + 256]
        for g in range(4):
            P.mm(PSA[:, 1024 + g * 64:1024 + (g + 1) * 64], lhsT=dT[k % 2][:, g * 128:(g + 1) * 128], rhs=pw[:, g * 64:(g + 1) * 64])
        P.I("dve", "tensor_tensor", out=ytmp[k % 2][:, :], in0=yps, in1=pscb[:, :], op=ALU.mult)
        P.I("pool", "tensor_tensor", out=ya[k % 2][:, :], in0=ytmp[k % 2][:, :], in1=gat[k % 2][:, :], op=ALU.mult)
        P.dma("sp", mix_d[tok0:tok0 + 128, 0:256], ya[k % 2][:, :])

    kTs = P.sbuf("kTs", [128, nk], BF16)
    vau = P.sbuf("vau", [128, nkb, 128], BF16)
    qTs = [P.sbuf(f"qTs{i}", [128, 512], BF16) for i in range(2)]
    pT = P.sbuf("pT", [128, 3 * 512], BF16, nslot=3)
    dacc = [KP.xt[m] for m in range(2)]
    onec = P.sbuf("onec", [128, 1], F32)
    P.I("dve", "memset", ap=onec[:, :], constant=1.0)
    rrow = [P.sbuf(f"rrow{m}", [1, 512], F32) for m in range(2)]
    rbc = [KP.xo[m] for m in range(2)]
    tt1 = KP.t32
    tt2 = T(KP.t32.h, "t32b")
    tt2.bufs = KP.t32.bufs
    oTb = P.sbuf("oTb", [128, 512], BF16)
    gb = [P.sbuf(f"gb{i}", [128, 128], BF16) for i in range(2)]
    osq = P.sbuf("osq", [128, 128], BF16)
    oss = [P.sbuf(f"oss{i}", [128, 4], F32) for i in range(2)]
    o3 = [P.sbuf(f"o3{i}", [128, 128], F32) for i in range(2)]
    ob = [P.sbuf(f"ob{i}", [128, 128], BF16) for i in range(2)]
    vsrc = inp["vall"].rearrange("(kb p) c -> p kb c", p=128)
    qblocks = [(qb * 512, 512, list(range(nkb))) for qb in range(n_own // 512)] + \
              [(n_own + c0, min(512, n_ctx - c0), list(range(n_lat // 128, nkb))) for c0 in range(0, n_ctx, 512)]
    kq = 0
    kf = 0
    for h in range(6):
        P.dma("sp", kTs[:, :], inp["kT"][h, :, :])
        for c0 in range(0, nkb, 32):
            c1 = min(nkb, c0 + 32)
            P.dma("sp", vau.v(vau.h[:, c0:c1, :]), vsrc[:, c0:c1, h * 128:(h + 1) * 128])
        for (q0, qn, kbs) in qblocks:
            nqs = qn // 128
            qt = qTs[kq % 2]
            kq += 1
            P.dma("sp", qt[:, 0:qn], inp["qT"][h, :, q0:q0 + qn])
            its = [(kb, m) for kb in kbs for m in range(2)]

            def QK(it):
                kb, m = its[it]
                s = it % 3
                P.mm(PSA[:, s * 512:s * 512 + qn], lhsT=kTs[m * 64:(m + 1) * 64, kb * 128:(kb + 1) * 128], rhs=qt[m * 64:(m + 1) * 64, 0:qn])

            QK(0)
            if len(its) > 1:
                QK(1)
            for it, (kb, m) in enumerate(its):
                s = it % 3
                if it + 2 < len(its):
                    QK(it + 2)
                P.act(pT[:, s * 512:s * 512 + qn], PSA[:, s * 512:s * 512 + qn], AF.Exp, scale=0.125)
                P.mm(PSO[:, m * 512:m * 512 + qn], lhsT=vau[:, kb, :], rhs=pT[:, s * 512:s * 512 + qn], start=(kb == kbs[0]), stop=(kb == kbs[-1]))
                eng = "dve" if m == 0 else "pool"
                if kb == kbs[0]:
                    P.I(eng, "tensor_copy", out=dacc[m][:, 0:qn], in_=pT[:, s * 512:s * 512 + qn])
                else:
                    P.I(eng, "tensor_tensor", out=dacc[m][:, 0:qn], in0=dacc[m][:, 0:qn], in1=pT[:, s * 512:s * 512 + qn], op=ALU.add)
            for m in range(2):
                P.mm(PSM[0:1, 0:qn], lhsT=onec[:, :], rhs=dacc[m][:, 0:qn])
                P.I("dve", "reciprocal", out=rrow[m][:, 0:qn], in_=PSM[0:1, 0:qn])
                if m == 1:
                    P.I("dve", "tensor_scalar", out=rrow[m][:, 0:qn], in0=rrow[m][:, 0:qn], scalar1=neglam[0:1, 0:1], scalar2=None, op0=ALU.mult)
                P.mm(PSO[:, 1024:1024 + qn], lhsT=C["ones"][:, :], rhs=rrow[m][0:1, 0:qn])
                P.act(rbc[m][:, 0:qn], PSO[:, 1024:1024 + qn], AF.Copy)
            P.I("dve", "tensor_tensor", out=tt1[:, 0:qn], in0=PSO[:, 0:qn], in1=rbc[0][:, 0:qn], op=ALU.mult)
            P.I("dve", "tensor_tensor", out=tt2[:, 512:512 + qn], in0=PSO[:, 512:512 + qn], in1=rbc[1][:, 0:qn], op=ALU.mult)
            P.I("pool", "tensor_tensor", out=oTb[:, 0:qn], in0=tt1[:, 0:qn], in1=tt2[:, 512:512 + qn], op=ALU.add)
            for qs in range(nqs):
                P.tr(PST[:, qs * 128:(qs + 1) * 128], oTb[:, qs * 128:(qs + 1) * 128], C["idb"][:, :])
            for qs in range(nqs):
                tok0 = q0 + qs * 128
                o2v = PST[:, qs * 128:(qs + 1) * 128]
                P.dma("sp", gb[kf % 2][:, :], inp["gates"][tok0:tok0 + 128, 256 + h * 128:256 + (h + 1) * 128])
                P.act(osq[:, :], o2v, AF.Square)
                ss = oss[kf % 2]
                P.I("dve", "tensor_reduce", out=ss[:, 0:1], in_=osq[:, :], axis=AX.X, op=ALU.add)
                P.I("dve", "tensor_scalar", out=ss[:, 1:2], in0=ss[:, 0:1], scalar1=1.0 / 128, scalar2=EPS, op0=ALU.mult, op1=ALU.add)
                P.act(ss[:, 3:4], ss[:, 1:2], AF.Sqrt)
                P.I("dve", "reciprocal", out=ss[:, 2:3], in_=ss[:, 3:4])
                P.I("dve", "scalar_tensor_tensor", out=o3[kf % 2][:, :], in0=o2v, scalar=ss[:, 2:3], in1=subg[:, :], op0=ALU.mult, op1=ALU.mult)
                P.I("pool", "tensor_tensor", out=ob[kf % 2][:, :], in0=o3[kf % 2][:, :], in1=gb[kf % 2][:, :], op=ALU.mult)
                P.dma("sp", mix_d[tok0:tok0 + 128, 256 + h * 128:256 + (h + 1) * 128], ob[kf % 2][:, :])
                kf += 1

    for k, (kind, i) in enumerate(tiles):
        if kind == "x":
            post_tile(P, C, KP, mix_d[i * 128:(i + 1) * 128, :], inp["x"][i * 128:(i + 1) * 128, :], xo_d[i * 128:(i + 1) * 128, :], 0, PST, PSA)
        else:
            t0 = n_own + i * 128
            post_tile(P, C, KP, mix_d[t0:t0 + 128, :], inp["ctx"][i * 128:(i + 1) * 128, :], co_d[i * 128:(i + 1) * 128, :], 1, PST, PSA)
    P.finish()
    return nc, P


def build_A_odd(layer, n_own=NOWN, n_ctx=NCTX):
    nt = n_own + n_ctx
    nc = bass.Bass("TRN2", target_bir_lowering=False)
    P = Prog(nc)
    inp = dict(
        x=dram_in(nc, "x", [n_own, D]), ctx=dram_in(nc, "ctx", [n_ctx, D]), ccols=dram_in(nc, "ccols", [128, 16]),
        ada_w=dram_in(nc, "ada_w", [D, 2048]), ada_b=dram_in(nc, "ada_b", [1, 2048]), norm_pre=dram_in(nc, "norm_pre", [1, D]),
        w_in=dram_in(nc, "w_in", [D, 4096]), ident=dram_in(nc, "ident", [128, 128]), lbraw=dram_in(nc, "lbraw", [128, 32]))
    tm_d = dram_out(nc, "tm", [nt, 1536], BF16)
    fmb_d = dram_out(nc, "fmb", [4, 128, nt], BF16)
    fmf_d = dram_out(nc, "fmf", [12, 128, nt], F32)
    C = mk_consts(P, inp)
    stage = [P.sbuf(f"stage{i}", [128, 4, 512], F32) for i in range(2)]
    psA = P.psum("psA", [128, 512], F32)
    AB = pre_norm_rows(P, C, inp, None, stage, psA)
    wb = load_w_bf16(P, inp["w_in"], 4096, "wb", stage)
    lbr = P.sbuf("lbr", [128, 32], F32)
    P.dma("sp", lbr[:, :], inp["lbraw"])
    lbe = P.sbuf("lbe", [128, 32], F32)
    P.act(lbe[:, :], lbr[:, :], AF.Exp)
    ev = lambda li: lbe.v(lbe.h[:, :].rearrange("p (d l h) -> p d l h", d=2, l=4)[:, :, li, :])
    den = P.sbuf("lbden", [128, 8], F32)
    num = P.sbuf("lbnum", [128, 8], F32)
    v3 = lambda t: t.v(t.h[:, :].rearrange("p (d h) -> p d h", d=2))
    P.I("dve", "tensor_tensor", out=v3(den), in0=ev(0), in1=ev(1), op=ALU.add)
    P.I("dve", "tensor_tensor", out=v3(den), in0=v3(den), in1=ev(2), op=ALU.add)
    P.I("dve", "tensor_tensor", out=v3(den), in0=v3(den), in1=ev(3), op=ALU.add)
    P.I("dve", "tensor_copy", out=v3(num), in_=ev(1))
    for li in range(2, layer + 1):
        P.I("dve", "tensor_tensor", out=v3(num), in0=v3(num), in1=ev(li), op=ALU.add)
    rden = P.sbuf("lbrden", [128, 8], F32)
    P.I("dve", "reciprocal", out=rden[:, :], in_=den[:, :])
    lb = P.sbuf("lb", [128, 8], F32)
    P.I("dve", "tensor_tensor", out=lb[:, :], in0=num[:, :], in1=rden[:, :], op=ALU.mult)
    oml = P.sbuf("oml", [128, 8], F32)
    P.I("dve", "tensor_scalar", out=oml[:, :], in0=lb[:, :], scalar1=-1.0, scalar2=1.0, op0=ALU.mult, op1=ALU.add)

    K = alloc_pre(P)
    pst = [P.psum(f"pst{i}", [128, 512], F32) for i in range(2)]
    psf = [P.psum(f"psf{i}", [128, 512], F32) for i in range(3)]
    tmt = [P.sbuf(f"tmt{i}", [128, 1536], BF16) for i in range(2)]
    sg = [P.sbuf(f"sg{i}", [128, 512], F32) for i in range(2)]
    fob = [P.sbuf(f"fob{i}", [128, 512], BF16) for i in range(2)]
    fof = [P.sbuf(f"fof{i}", [128, 512], F32) for i in range(2)]
    TMB = [(1024, 512, 0, True), (3584, 512, 512, True), (3072, 512, 1024, False)]
    groups = [(g * 512, 4, False) for g in range(n_own // 512)] + [(n_own, n_ctx // 128, True)]
    kt = 0
    kf = 0
    kb_ = 0
    kff = 0
    for gi, (tok0, ntile, is_ctx) in enumerate(groups):
        n = ntile * 128
        if is_ctx:
            xfn = lambda i: inp["ctx"][i * 128:(i + 1) * 128, :]
            ab = AB[2:4]
        else:
            xfn = lambda i, tok0=tok0: inp["x"][tok0 + i * 128: tok0 + (i + 1) * 128, :]
            ab = AB[0:2]
        hT = emit_pre_group(P, C, K, xfn, ntile, ab, gi)
        for i in range(ntile):
            tt = tmt[(gi * 4 + i) % 2]
            for (wc0, wn, dc, silu) in TMB:
                ps = pst[kt % 2]
                kt += 1
                for j in range(8):
                    P.mm(ps[:, 0:wn], lhsT=hT[:, j, i * 128:(i + 1) * 128], rhs=wb[:, j, wc0:wc0 + wn], start=(j == 0), stop=(j == 7))
                if silu:
                    P.act(tt[:, dc:dc + wn], ps[:, 0:wn], AF.Silu)
                else:
                    P.I("dve", "tensor_copy", out=tt[:, dc:dc + wn], in_=ps[:, 0:wn])
            P.dma("sp", tm_d[tok0 + i * 128: tok0 + (i + 1) * 128, :], tt[:, :])

        def fproj(wc0):
            nonlocal kf
            ps = psf[kf % 3]
            kf += 1
            for j in range(8):
                P.mm(ps[:, 0:n], lhsT=wb[:, j, wc0:wc0 + 128], rhs=hT[:, j, 0:n], start=(j == 0), stop=(j == 7))
            return ps
        for c in range(4):
            pa = fproj(c * 128)
            pb = fproj(512 + c * 128)
            s_ = sg[kb_ % 2]
            P.act(s_[:, 0:n], pb[:, 0:n], AF.Sigmoid)
            fo = fob[kb_ % 2]
            kb_ += 1
            P.I("dve", "tensor_tensor", out=fo[:, 0:n], in0=pa[:, 0:n], in1=s_[:, 0:n], op=ALU.mult)
            P.dma("sp", fmb_d[c, :, tok0:tok0 + n], fo[:, 0:n])
        for hh in range(4):
            pq = fproj(1536 + hh * 128)
            fo = fof[kff % 2]
            kff += 1
            P.act(fo[:, 0:n], pq[:, 0:n], AF.Silu)
            P.dma("sp", fmf_d[hh, :, tok0:tok0 + n], fo[:, 0:n])
            for dr in range(2):
                pf = fproj(2048 + dr * 512 + hh * 128)
                s_ = sg[kb_ % 2]
                kb_ += 1
                P.act(s_[:, 0:n], pf[:, 0:n], AF.Sigmoid)
                fo = fof[kff % 2]
                kff += 1
                ci = dr * 4 + hh
                P.I("dve", "tensor_scalar", out=fo[:, 0:n], in0=s_[:, 0:n], scalar1=oml[:, ci:ci + 1], scalar2=lb[:, ci:ci + 1], op0=ALU.mult, op1=ALU.add)
                P.dma("sp", fmf_d[4 + dr * 4 + hh, :, tok0:tok0 + n], fo[:, 0:n])
    P.finish()
    return nc, P


def build_H(n_lat=4 * NOWN, n_ctx=NCTX):
    N = n_lat + n_ctx
    SEG = min(2048, n_lat)
    nc = bass.Bass("TRN2", target_bir_lowering=False)
    P = Prog(nc)
    inp = dict(qT=dram_in(nc, "qT", [128, N]), fTf=dram_in(nc, "fTf", [128, N]), fTb=dram_in(nc, "fTb", [128, N]),
               v=dram_in(nc, "v", [N, 128], BF16), ogs=dram_in(nc, "ogs", [N, 128], BF16), gn=dram_in(nc, "gn", [1, 128]),
               ident=dram_in(nc, "ident", [128, 128]), masks=dram_in(nc, "masks", [64, 128]), cmask=dram_in(nc, "cmask", [128, SEG]))
    yd_d = dram_out(nc, "yd", [N, 128], BF16)
    of_d = P.dram("ofd", [N, 128], F32, nslot=N // 64, sdim=0)
    C = mk_consts(P, inp)
    msk = P.sbuf("msk", [64, 128], F32)
    P.dma("sp", msk[:, :], inp["masks"])
    cm = P.sbuf("cm", [128, SEG], F32)
    P.dma("sp", cm[:, :], inp["cmask"])
    gnr = P.sbuf("gnr", [1, 128], F32)
    P.dma("sp", gnr[:, :], inp["gn"])
    psg = P.psum("psg", [128, 128], F32)
    gbc = P.sbuf("gbc", [128, 128], F32)
    P.mm(psg[:, :], lhsT=C["ones"][:, :], rhs=gnr[0:1, :])
    P.I("dve", "tensor_copy", out=gbc[:, :], in_=psg[:, :])
    qs = P.sbuf("qs", [128, SEG], F32)
    fs = P.sbuf("fs", [128, SEG], F32)
    lfs = P.sbuf("lfs", [128, SEG], F32)
    ks = P.sbuf("ks", [128, SEG], F32)
    Pc = P.sbuf("Pc", [128, SEG], F32)
    Pe = P.sbuf("Pe", [128, SEG], F32)
    vs = P.sbuf("vs", [64, SEG // 64, 128], BF16)
    gs = P.sbuf("gs", [64, SEG // 64, 128], BF16)
    S = [P.sbuf(f"S{i}", [128, 128], F32) for i in range(2)]
    cols = [P.sbuf(f"cols{i}", [128, 8], F32) for i in range(2)]
    eq = [P.sbuf(f"eq{i}", [128, 64], F32) for i in range(2)]
    ek = [P.sbuf(f"ek{i}", [128, 64], F32) for i in range(2)]
    ekl = [P.sbuf(f"ekl{i}", [128, 64], F32) for i in range(2)]
    qt = [P.sbuf(f"qt{i}", [128, 64], BF16) for i in range(2)]
    kt = [P.sbuf(f"kt{i}", [128, 64], BF16) for i in range(2)]
    kh = [P.sbuf(f"kh{i}", [128, 64], BF16) for i in range(2)]
    khs = [P.sbuf(f"khs{i}", [64, 128], BF16) for i in range(2)]
    S0m = [P.sbuf(f"S0m{i}", [128, 128], BF16) for i in range(2)]
    attm = [P.sbuf(f"attm{i}", [64, 64], BF16) for i in range(2)]
    osb = [P.sbuf(f"osb{i}", [64, 128], F32) for i in range(2)]
    ofs = [P.sbuf(f"ofs{i}", [64, 128], F32) for i in range(2)]
    osq = P.sbuf("hosq", [64, 128], BF16)
    oss = [P.sbuf(f"hoss{i}", [64, 4], F32) for i in range(2)]
    on = [P.sbuf(f"on{i}", [64, 128], F32) for i in range(2)]
    yb = [P.sbuf(f"yb{i}", [64, 128], BF16) for i in range(2)]
    p_kh = P.psum("p_kh", [64, 128], BF16)
    p_att = [P.psum(f"p_att{i}", [64, 64], F32) for i in range(2)]
    p_o = [P.psum(f"p_o{i}", [64, 128], F32) for i in range(2)]
    p_ds = [P.psum(f"p_ds{i}", [128, 128], F32) for i in range(2)]
    segs = [(0, n_ctx)] + [(n_ctx + i * SEG, SEG) for i in range(n_lat // SEG)]
    kc = 0
    for dirn in range(2):
        bwd = dirn == 1
        P.I("dve", "memset", ap=S[0][:, :], constant=0.0)
        cur = 0
        order = segs if not bwd else [segs[0]] + segs[1:][::-1]
        E = Pe if bwd else Pc
        for (a0, sl) in order:
            P.dma("sp", qs[:, 0:sl], inp["qT"][:, a0:a0 + sl])
            P.dma("sp", fs[:, 0:sl], inp["fTb" if bwd else "fTf"][:, a0:a0 + sl])
            P.dma("sp", vs.v(vs.h[:, 0:sl // 64, :]), inp["v"][a0:a0 + sl, :].rearrange("(n p) c -> p n c", p=64))
            if bwd:
                P.dma("sp", gs.v(gs.h[:, 0:sl // 64, :]), inp["ogs"][a0:a0 + sl, :].rearrange("(n p) c -> p n c", p=64))
            P.act(lfs[:, 0:sl], fs[:, 0:sl], AF.Ln)
            P.I("pool", "tensor_scalar", out=ks[:, 0:sl], in0=fs[:, 0:sl], scalar1=-1.0, scalar2=1.0, op0=ALU.mult, op1=ALU.add)
            P.I("dve", "tensor_tensor_scan", out=Pc[:, 0:sl], data0=cm[:, 0:sl], data1=lfs[:, 0:sl], initial=0.0, op0=ALU.mult, op1=ALU.add)
            if bwd:
                P.I("dve", "tensor_tensor", out=Pe[:, 0:sl], in0=Pc[:, 0:sl], in1=lfs[:, 0:sl], op=ALU.subtract)
            chunks = list(range(sl // 64))
            if bwd:
                chunks = chunks[::-1]
            for n_ in chunks:
                a = n_ * 64
                k2 = kc % 2
                kc += 1
                cl = cols[k2]
                tot = Pc[:, a + 63:a + 64]
                P.I("dve", "tensor_copy", out=cl[:, 0:1], in_=E[:, a + 31:a + 32])
                P.I("dve", "tensor_scalar", out=cl[:, 1:2], in0=E[:, a + 31:a + 32], scalar1=-1.0, scalar2=None, op0=ALU.mult)
                if bwd:
                    P.I("dve", "tensor_tensor", out=cl[:, 3:4], in0=tot, in1=cl[:, 0:1], op=ALU.subtract)
                else:
                    P.I("dve", "tensor_copy", out=cl[:, 3:4], in_=cl[:, 0:1])
                P.I("dve", "tensor_copy", out=cl[:, 4:5], in_=tot)
                P.act(cl[:, 5:7], cl[:, 3:5], AF.Exp)
                Ec = E[:, a:a + 64]
                if bwd:
                    P.act(eq[k2][:, :], Ec, AF.Exp, scale=-1.0, bias=cl[:, 0:1])
                    P.act(ek[k2][:, :], Ec, AF.Exp, scale=1.0, bias=cl[:, 1:2])
                    P.act(ekl[k2][:, :], Ec, AF.Exp)
                else:
                    P.act(eq[k2][:, :], Ec, AF.Exp, scale=1.0, bias=cl[:, 1:2])
                    P.act(ek[k2][:, :], Ec, AF.Exp, scale=-1.0, bias=cl[:, 0:1])
                    P.act(ekl[k2][:, :], Ec, AF.Exp, scale=-1.0, bias=cl[:, 4:5])
                P.I("dve", "tensor_tensor", out=qt[k2][:, :], in0=qs[:, a:a + 64], in1=eq[k2][:, :], op=ALU.mult)
                P.I("pool", "tensor_tensor", out=kt[k2][:, :], in0=ks[:, a:a + 64], in1=ek[k2][:, :], op=ALU.mult)
                P.I("pool", "tensor_tensor", out=kh[k2][:, :], in0=ks[:, a:a + 64], in1=ekl[k2][:, :], op=ALU.mult)
                P.tr(p_kh[:, :], kh[k2][:, :], C["idb"][:, :])
                P.act(khs[k2][:, :], p_kh[:, :], AF.Copy)
                P.I("dve", "tensor_scalar", out=S0m[k2][:, :], in0=S[cur][:, :], scalar1=cl[:, 5:6], scalar2=None, op0=ALU.mult)
                P.mm(p_att[k2][:, :], lhsT=kt[k2][:, :], rhs=qt[k2][:, :])
                mo = 64 if bwd else 0
                P.I("dve", "tensor_tensor", out=attm[k2][:, :], in0=p_att[k2][:, :], in1=msk[:, mo:mo + 64], op=ALU.mult)
                P.mm(p_o[k2][:, :], lhsT=qt[k2][:, :], rhs=S0m[k2][:, :], start=True, stop=False)
                P.mm(p_o[k2][:, :], lhsT=attm[k2][:, :], rhs=vs[:, n_, :], start=False, stop=True)
                P.mm(p_ds[k2][:, :], lhsT=khs[k2][:, :], rhs=vs[:, n_, :])
                P.I("dve", "scalar_tensor_tensor", out=S[1 - cur][:, :], in0=S[cur][:, :], scalar=cl[:, 6:7], in1=p_ds[k2][:, :], op0=ALU.mult, op1=ALU.add)
                cur = 1 - cur
                r0 = a0 + a
                if not bwd:
                    P.act(osb[k2][:, :], p_o[k2][:, :], AF.Copy)
                    P.dma("sp", of_d[r0:r0 + 64, :], osb[k2][:, :])
                else:
                    P.dma("sp", ofs[k2][:, :], of_d[r0:r0 + 64, :])
                    P.I("dve", "tensor_tensor", out=osb[k2][:, :], in0=p_o[k2][:, :], in1=ofs[k2][:, :], op=ALU.add)
                    P.act(osq[:, :], osb[k2][:, :], AF.Square)
                    ss = oss[k2]
                    P.I("dve", "tensor_reduce", out=ss[:, 0:1], in_=osq[:, :], axis=AX.X, op=ALU.add)
                    P.I("dve", "tensor_scalar", out=ss[:, 1:2], in0=ss[:, 0:1], scalar1=1.0 / 128, scalar2=EPS, op0=ALU.mult, op1=ALU.add)
                    P.act(ss[:, 3:4], ss[:, 1:2], AF.Sqrt)
                    P.I("dve", "reciprocal", out=ss[:, 2:3], in_=ss[:, 3:4])
                    P.I("dve", "scalar_tensor_tensor", out=on[k2][:, :], in0=osb[k2][:, :], scalar=ss[:, 2:3], in1=gbc[0:64, :], op0=ALU.mult, op1=ALU.mult)
                    P.I("pool", "tensor_tensor", out=yb[k2][:, :], in0=on[k2][:, :], in1=gs[:, n_, :], op=ALU.mult)
                    P.dma("sp", yd_d[r0:r0 + 64, :], yb[k2][:, :])
    P.finish()
    return nc, P


CSTOP = 0


def build_C_odd(n_own=NOWN, n_ctx=NCTX, with_ctx=True):
    nt = n_own + n_ctx
    nc = bass.Bass("TRN2", target_bir_lowering=False)
    P = Prog(nc)
    inp = dict(
        x=dram_in(nc, "x", [n_own, D]), ctx=dram_in(nc, "ctx", [n_ctx, D]), ccols=dram_in(nc, "ccols", [128, 16]),
        ada_w=dram_in(nc, "ada_w", [D, 1024]), ada_b=dram_in(nc, "ada_b", [1, 1024]), norm_post=dram_in(nc, "norm_post", [1, D]),
        w_out=dram_in(nc, "w_out", [D, D]), ident=dram_in(nc, "ident", [128, 128]),
        gluh=dram_in(nc, "gluh", [4, 128, n_own + 30], BF16), gluch=dram_in(nc, "gluch", [4, 128, n_ctx + 30], BF16),
        cgs=dram_in(nc, "cgs", [nt, 512], BF16), yd=dram_in(nc, "yd", [nt, 512], BF16),
        convw=dram_in(nc, "convw", [128, 124]), convb=dram_in(nc, "convb", [128, 4]),
        lng=dram_in(nc, "lng", [1, 512]), lnb=dram_in(nc, "lnb", [1, 512]))
    xo_d = dram_out(nc, "xo", [n_own, D], F32)
    co_d = dram_out(nc, "co", [n_ctx, D], F32)
    C = mk_consts(P, inp)
    stage = [P.sbuf(f"stage{i}", [128, 4, 512], F32) for i in range(2)]
    PSY = P.psum("PSY", [128, 1024], F32)
    PST = P.psum("PST", [128, D], BF16)
    PSM = P.psum("PSM", [128, 512], F32)
    PSZ = [P.psum(f"PSZ{i}", [128, 512], BF16) for i in range(2)]
    KP = post_setup(P, C, inp, stage, PSM)
    cw = P.sbuf("cw", [128, 124], F32)
    P.dma("sp", cw[:, :], inp["convw"])
    cb = P.sbuf("cb", [128, 4], F32)
    P.dma("sp", cb[:, :], inp["convb"])
    lrow = P.sbuf("lrow", [1, 1024], F32)
    P.dma("sp", lrow[:, 0:512], inp["lng"])
    P.dma("sp", lrow[:, 512:1024], inp["lnb"])
    lgb = P.sbuf("lgb", [128, 1024], F32)
    for hh in range(2):
        P.mm(PSM[:, :], lhsT=C["ones"][:, :], rhs=lrow[0:1, hh * 512:(hh + 1) * 512])
        P.I("dve", "tensor_copy", out=lgb[:, hh * 512:(hh + 1) * 512], in_=PSM[:, :])
    glu = [P.sbuf(f"glu{i}", [128, 4, 542], BF16) for i in range(2)]
    zT = [P.sbuf(f"zT{i}", [128, 4, 512], F32) for i in range(2)]
    zs = [P.sbuf(f"zs{i}", [128, 512], F32) for i in range(2)]
    zb = [P.sbuf(f"zb{i}", [128, 4, 512], BF16) for i in range(2)]
    zsq = P.sbuf("zsq", [128, 512], F32)
    st_ = [P.sbuf(f"lst{i}", [128, 8], F32) for i in range(2)]
    zn = [P.sbuf(f"zn{i}", [128, 512], F32) for i in range(2)]
    zl = [P.sbuf(f"zl{i}", [128, 512], F32) for i in range(2)]
    zsl = [P.sbuf(f"zsl{i}", [128, 512], F32) for i in range(2)]
    cg = [P.sbuf(f"cg{i}", [128, 512], BF16) for i in range(2)]
    mixs = [P.sbuf(f"mixs{i}", [128, D], BF16) for i in range(2)]
    blocks = [("x", b0, 512) for b0 in range(0, n_own, 512)]
    if with_ctx:
        blocks += [("c", 0, n_ctx)]
    kt = 0
    for bi, (kind, b0, nb) in enumerate(blocks):
        g_ = glu[bi % 2]
        src = inp["gluh"] if kind == "x" else inp["gluch"]
        P.dma("sp", g_.v(g_.h[:, :, 0:nb + 30]), src[:, :, b0:b0 + nb + 30].rearrange("c p t -> p c t"))
        z_ = zT[bi % 2]
        for c in range(4):
            acc = z_[:, c, 0:nb]
            P.I("dve", "tensor_scalar", out=acc, in0=g_[:, c, 0:nb], scalar1=cw[:, c * 31:c * 31 + 1], scalar2=cb[:, c:c + 1], op0=ALU.mult, op1=ALU.add)
            for w in range(1, 31):
                P.I("dve", "scalar_tensor_tensor", out=acc, in0=g_[:, c, w:w + nb], scalar=cw[:, c * 31 + w:c * 31 + w + 1], in1=acc, op0=ALU.mult, op1=ALU.add)
            P.act(zb[bi % 2][:, c, 0:nb], acc, AF.Copy)
        if CSTOP == 1:
            continue
        for i in range(nb // 128):
            k2 = kt % 2
            kt += 1
            tok0 = (b0 if kind == "x" else n_own) + i * 128
            pz = PSZ[k2]
            for c in range(4):
                P.tr(pz[:, c * 128:(c + 1) * 128], zb[bi % 2][:, c, i * 128:(i + 1) * 128], C["idb"][:, :])
            P.act(zs[k2][:, :], pz[:, :], AF.Copy)
            if CSTOP == 2:
                continue
            s = st_[k2]
            P.I("dve", "tensor_reduce", out=s[:, 0:1], in_=zs[k2][:, :], axis=AX.X, op=ALU.add)
            P.act(zsq[:, :], zs[k2][:, :], AF.Square)
            P.I("dve", "tensor_reduce", out=s[:, 1:2], in_=zsq[:, :], axis=AX.X, op=ALU.add)
            P.I("dve", "tensor_scalar", out=s[:, 2:3], in0=s[:, 0:1], scalar1=1.0 / 512, scalar2=None, op0=ALU.mult)
            P.I("dve", "tensor_tensor", out=s[:, 3:4], in0=s[:, 2:3], in1=s[:, 2:3], op=ALU.mult)
            P.I("dve", "scalar_tensor_tensor", out=s[:, 4:5], in0=s[:, 1:2], scalar=1.0 / 512, in1=s[:, 3:4], op0=ALU.mult, op1=ALU.subtract)
            P.I("dve", "tensor_scalar", out=s[:, 5:6], in0=s[:, 4:5], scalar1=EPS, scalar2=None, op0=ALU.add)
            P.act(s[:, 6:7], s[:, 5:6], AF.Sqrt)
            P.I("dve", "reciprocal", out=s[:, 7:8], in_=s[:, 6:7])
            if CSTOP == 3:
                continue
            P.I("dve", "tensor_scalar", out=zn[k2][:, :], in0=zs[k2][:, :], scalar1=s[:, 2:3], scalar2=s[:, 7:8], op0=ALU.subtract, op1=ALU.mult)
            P.I("pool", "tensor_tensor", out=zl[k2][:, :], in0=zn[k2][:, :], in1=lgb[:, 0:512], op=ALU.mult)
            P.I("pool", "tensor_tensor", out=zl[k2][:, :], in0=zl[k2][:, :], in1=lgb[:, 512:1024], op=ALU.add)
            P.act(zsl[k2][:, :], zl[k2][:, :], AF.Silu)
            P.dma("sp", cg[k2][:, :], inp["cgs"][tok0:tok0 + 128, :])
            mt = mixs[k2]
            P.dma("sp", mt[:, 512:1024], inp["yd"][tok0:tok0 + 128, :])
            P.I("pool", "tensor_tensor", out=mt[:, 0:512], in0=zsl[k2][:, :], in1=cg[k2][:, :], op=ALU.mult)
            if CSTOP == 4:
                continue
            if kind == "x":
                post_tile(P, C, KP, None, inp["x"][tok0:tok0 + 128, :], xo_d[tok0:tok0 + 128, :], 0, PST, PSY, mix_in_sbuf=mt)
            else:
                r0 = i * 128
                post_tile(P, C, KP, None, inp["ctx"][r0:r0 + 128, :], co_d[r0:r0 + 128, :], 1, PST, PSY, mix_in_sbuf=mt)
    if not with_ctx:
        for i in range(n_ctx // 128):
            xt = KP.xt[i % 2]
            P.dma("sp", xt[:, :], inp["ctx"][i * 128:(i + 1) * 128, :])
            P.dma("sp", co_d[i * 128:(i + 1) * 128, :], xt[:, :])
    P.finish()
    return nc, P


GRID_W = 64
import math
def rope_tables(tok0, n):
    t = np.arange(tok0, tok0 + n)
    row = (t // GRID_W).astype(np.float32); col = (t % GRID_W).astype(np.float32)
    inv = (10000.0 ** (-np.arange(16, dtype=np.float32) / 16)).astype(np.float32)
    cosT = np.zeros((128, n), np.float32); sinT = np.zeros((128, n), np.float32)
    for r in range(128):
        d = r % 64
        pos = row if d < 32 else col
        ang = (pos * inv[d % 16]).astype(np.float32)
        cosT[r] = np.cos(ang); sinT[r] = np.sin(ang)
    return cosT, sinT
def rope_rmat():
    R = np.zeros((128, 128), np.float32)
    for i in range(128):
        d = i % 32
        if d < 16:
            R[i + 16, i] = -1.0
        else:
            R[i - 16, i] = 1.0
    return R
def ccols(c_b, c_ctx):
    return np.concatenate([c_b.reshape(8, 128).T, c_ctx.reshape(8, 128).T], axis=1).astype(np.float32).copy()

POOL_WINDOWS = (2, 4, 8, 16)
def pool_bands(L, tile0, is_first, is_last):
    out = np.zeros((4, 144, 128), np.float32)
    for g, w in enumerate(POOL_WINDOWS):
        for c in range(128):
            t = tile0 + c
            lo = max(t - w // 2, 0); hi = min(t + w // 2 - 1, L - 1) + 1
            cnt = hi - lo
            for s in range(lo, hi):
                r = s - (tile0 - 8)
                out[g, r, c] += 1.0 / cnt
            out[g, c + 8, c] -= 1.0
    return out
def band_pack(L_lat, own0, n_own, L_ctx):
    types = [pool_bands(L_lat, own0, True, False), pool_bands(L_lat, own0 + 128 if n_own > 256 else own0 + 128, False, False),
             pool_bands(L_lat, own0 + n_own - 128, False, True), pool_bands(L_ctx, 0, True, False), pool_bands(L_ctx, L_ctx - 128, False, True)]
    A = np.zeros((128, 2560), np.float32); B = np.zeros((16, 2560), np.float32)
    for t, bm in enumerate(types):
        for g in range(4):
            A[:, (t * 4 + g) * 128:(t * 4 + g + 1) * 128] = bm[g, :128]
            B[:, (t * 4 + g) * 128:(t * 4 + g + 1) * 128] = bm[g, 128:]
    return A, B
def halo(u_seq, a, n):
    L = u_seq.shape[0]
    out = np.zeros((n + 16, u_seq.shape[1]), u_seq.dtype)
    lo = max(a - 8, 0); hi = min(a + n + 8, L)
    out[lo - (a - 8): hi - (a - 8)] = u_seq[lo:hi]
    return out


def halo15_T(seqT, a, n):
    L = seqT.shape[2]
    out = np.zeros((4, 128, n + 30), seqT.dtype)
    lo = max(a - 15, 0)
    hi = min(a + n + 15, L)
    out[:, :, lo - (a - 15):hi - (a - 15)] = seqT[:, :, lo:hi]
    return out


_PROGS = {}
_DEBUG_HOOK = None


def _prog(key, fn):
    if key not in _PROGS:
        _PROGS[key] = fn()[0]
    return _PROGS[key]


def _run(nc, in_maps):
    res = run_bass_kernel_spmd(nc, in_maps, core_ids=list(range(8)))
    return res.results


def kernel(x, c, ctx, c_ctx, ada_w, ada_b, norm_pre, norm_post, w_in_even, w_out_even,
           pool_w, pool_scale, diff_lambda, diff_subln, w_in_odd, w_out_odd,
           conv_w, conv_b, conv_ln_g, conv_ln_b, hgrn_norm, hgrn_lb):
    f32 = lambda a: np.ascontiguousarray(np.asarray(a, dtype=np.float32))
    x, c, ctx, c_ctx, ada_w, ada_b = f32(x), f32(c), f32(ctx), f32(c_ctx), f32(ada_w), f32(ada_b)
    norm_pre, norm_post = f32(norm_pre), f32(norm_post)
    NB, L = x.shape[0], x.shape[1]
    S4 = 4
    ident = np.eye(128, dtype=np.float32)
    rmat = rope_rmat()
    xs = [np.ascontiguousarray(x[r // 4, (r % 4) * NOWN:(r % 4 + 1) * NOWN]) for r in range(8)]
    cs = [np.ascontiguousarray(ctx[b]) for b in range(NB)]
    cc = [ccols(c[b], c_ctx) for b in range(NB)]
    ropes = [rope_tables((r % 4) * NOWN, NOWN) for r in range(4)]
    bands = [band_pack(L, s * NOWN, NOWN, NCTX) for s in range(4)]
    masks = np.zeros((64, 128), np.float32)
    s_, t_ = np.meshgrid(np.arange(64), np.arange(64), indexing="ij")
    masks[:, :64] = (t_ >= s_)
    masks[:, 64:] = (s_ >= t_)
    cmask = np.ones((128, 2048), np.float32)
    cmask[:, ::64] = 0
    lbraw = np.ascontiguousarray(f32(hgrn_lb).reshape(2, 4, 4, 128).transpose(3, 0, 1, 2).reshape(128, 32))
    for l in range(4):
        j = l // 2
        aw_pre = np.ascontiguousarray(ada_w[l][:, :2048])
        ab_pre = np.ascontiguousarray(ada_b[l][None, :2048])
        aw_post = np.ascontiguousarray(ada_w[l][:, 2048:])
        ab_post = np.ascontiguousarray(ada_b[l][None, 2048:])
        npre = np.ascontiguousarray(norm_pre[l][None])
        npost = np.ascontiguousarray(norm_post[l][None])
        if l % 2 == 0:
            w_in = f32(w_in_even[j])
            ncA = _prog("A_even", build_A_even)
            ims = [dict(x=xs[r], ctx=cs[r // 4], ccols=cc[r // 4], ada_w=aw_pre, ada_b=ab_pre, norm_pre=npre, w_in=w_in,
                        ident=ident, rmat=rmat, cosT=ropes[r % 4][0], sinT=ropes[r % 4][1]) for r in range(8)]
            ra = _run(ncA, ims)
            ncB = _prog("B_even", build_B_even)
            lam_init = 0.8 - 0.6 * math.exp(-0.3 * l)
            pw = np.ascontiguousarray(f32(pool_w[j]).transpose(1, 0, 2).reshape(64, 256))
            ims = []
            for r in range(8):
                b, s = r // 4, r % 4
                grp = [ra[b * 4 + q] for q in range(4)]
                tm = ra[r]["tm"]
                kT = np.concatenate([g["fm"][6:12, :, :NOWN] for g in grp] + [ra[r]["fm"][6:12, :, NOWN:]], axis=2)
                vall = np.concatenate([g["tm"][:NOWN, 512:1280] for g in grp] + [tm[NOWN:, 512:1280]], axis=0)
                useq = np.concatenate([g["tm"][:NOWN, 0:256] for g in grp], axis=0)
                ims.append(dict(
                    x=xs[r], ctx=cs[b], ccols=cc[b], ada_w=aw_post, ada_b=ab_post, norm_post=npost, w_out=f32(w_out_even[j]), ident=ident,
                    qT=np.ascontiguousarray(ra[r]["fm"][0:6]), kT=np.ascontiguousarray(kT), vall=np.ascontiguousarray(vall),
                    gates=np.ascontiguousarray(np.concatenate([tm[:, 256:512], tm[:, 1280:2048]], axis=1)),
                    uh=halo(useq, s * NOWN, NOWN), uch=halo(np.ascontiguousarray(tm[NOWN:, 0:256]), 0, NCTX),
                    bandA=bands[s][0], bandB=bands[s][1], pool_w=pw, pool_scale=f32(pool_scale[j])[None].copy(),
                    lamp=f32(diff_lambda[j]).reshape(1, 256).copy(), subln=f32(diff_subln[j])[None].copy(),
                    lamc=np.array([[lam_init, 1.0 - lam_init]], np.float32)))
            rb = _run(ncB, ims)
            del ra
        else:
            ncA = _prog(("A_odd", l), lambda: build_A_odd(l))
            ims = [dict(x=xs[r], ctx=cs[r // 4], ccols=cc[r // 4], ada_w=aw_pre, ada_b=ab_pre, norm_pre=npre, w_in=f32(w_in_odd[j]),
                        ident=ident, lbraw=lbraw) for r in range(8)]
            ra = _run(ncA, ims)
            ncH = _prog("H", build_H)
            ims = []
            for r in range(8):
                b, hh = r // 4, r % 4
                grp = [ra[b * 4 + q] for q in range(4)]
                cat_f = lambda blk: np.ascontiguousarray(np.concatenate([grp[0]["fmf"][blk][:, NOWN:]] + [g["fmf"][blk][:, :NOWN] for g in grp], axis=1))
                cat_t = lambda c0: np.ascontiguousarray(np.concatenate([grp[0]["tm"][NOWN:, c0:c0 + 128]] + [g["tm"][:NOWN, c0:c0 + 128] for g in grp], axis=0))
                ims.append(dict(qT=cat_f(hh), fTf=cat_f(4 + hh), fTb=cat_f(8 + hh), v=cat_t(1024 + hh * 128), ogs=cat_t(512 + hh * 128),
                                gn=f32(hgrn_norm[j])[None, hh * 128:(hh + 1) * 128].copy(), ident=ident, masks=masks, cmask=cmask))
            rh = _run(ncH, ims)
            with_ctx = l < 3
            ncC = _prog(("C_odd", with_ctx), lambda: build_C_odd(NOWN, NCTX, with_ctx))
            cw = f32(conv_w[j])
            convw = np.ascontiguousarray(cw.T.reshape(4, 128, 31).transpose(1, 0, 2).reshape(128, 124))
            convb = np.ascontiguousarray(f32(conv_b[j]).reshape(4, 128).T)
            ims = []
            for r in range(8):
                b, s = r // 4, r % 4
                grp = [ra[b * 4 + q] for q in range(4)]
                gseq = np.concatenate([g["fmb"][:, :, :NOWN] for g in grp], axis=2)
                gctx = np.ascontiguousarray(ra[r]["fmb"][:, :, NOWN:])
                yd = np.concatenate([np.concatenate([rh[b * 4 + hh]["yd"][NCTX + s * NOWN: NCTX + (s + 1) * NOWN] for hh in range(4)], axis=1),
                                     np.concatenate([rh[b * 4 + hh]["yd"][0:NCTX] for hh in range(4)], axis=1)], axis=0)
                ims.append(dict(
                    x=xs[r], ctx=cs[b], ccols=cc[b], ada_w=aw_post, ada_b=ab_post, norm_post=npost, w_out=f32(w_out_odd[j]), ident=ident,
                    gluh=halo15_T(gseq, s * NOWN, NOWN), gluch=halo15_T(gctx, 0, NCTX),
                    cgs=np.ascontiguousarray(ra[r]["tm"][:, 0:512]), yd=np.ascontiguousarray(yd), convw=convw, convb=convb,
                    lng=f32(conv_ln_g[j])[None].copy(), lnb=f32(conv_ln_b[j])[None].copy()))
            rb = _run(ncC, ims)
            del ra, rh
        xs = [np.ascontiguousarray(rb[r]["xo"]) for r in range(8)]
        cs = [np.ascontiguousarray(rb[b * 4]["co"]) for b in range(NB)]
        if _DEBUG_HOOK is not None:
            _DEBUG_HOOK(l, xs, cs)
    out = np.stack([np.concatenate(xs[b * 4:(b + 1) * 4], axis=0) for b in range(NB)], axis=0)
    return out.astype(np.float32)
```

```python
import contextlib
import numpy as np
import concourse.bass as bass
import concourse.mybir as mybir
from concourse.bass_utils import run_bass_kernel_spmd

F32 = mybir.dt.float32
BF16 = mybir.dt.bfloat16
ALU = mybir.AluOpType
AF = mybir.ActivationFunctionType
AX = mybir.AxisListType


class Buf:
    __slots__ = ("name", "last_w", "readers", "excl")

    def __init__(self, name):
        self.name = name
        self.last_w = None
        self.readers = []
        self.excl = False


class V:
    __slots__ = ("ap", "bufs")

    def __init__(self, ap, bufs):
        self.ap = ap
        self.bufs = bufs


class T:
    def __init__(self, h, name, nslot=1, sdim=1):
        self.h = h
        self.name = name
        self.nslot = nslot
        self.sdim = sdim
        self.bufs = [Buf(f"{name}.{i}") for i in range(nslot)]
        self.shape = list(h.shape)

    def __getitem__(self, idx):
        ap = self.h[idx]
        if self.nslot == 1:
            return V(ap, self.bufs)
        if not isinstance(idx, tuple):
            idx = (idx,)
        bufs = self.bufs
        if len(idx) > self.sdim:
            s = idx[self.sdim]
            n = self.shape[self.sdim]
            per = n // self.nslot
            if isinstance(s, int):
                bufs = [self.bufs[s // per]]
            elif isinstance(s, slice):
                a = 0 if s.start is None else s.start
                b = n if s.stop is None else s.stop
                bufs = self.bufs[a // per:(b - 1) // per + 1]
        return V(ap, bufs)

    def v(self, ap):
        return V(ap, self.bufs)


class Op:
    __slots__ = ("eng", "fn", "waits", "sem", "val", "is_dma", "inc")


ENGS = ("pe", "act", "dve", "pool", "sp")


class Prog:
    N_DMA_SEM = 8

    def __init__(self, nc):
        self.nc = nc
        self.es = contextlib.ExitStack()
        self.ops = {e: [] for e in ENGS}
        self.cnt = {e: 0 for e in ENGS}
        self.ndma = {e: 0 for e in ENGS}
        self.sem = {}
        self.dsem = {}
        self.waited = {e: {} for e in ENGS}
        self.semvals = {}
        for e in ENGS:
            self.sem[e] = self.es.enter_context(nc.semaphore(f"s_{e}"))
        for e in ("sp", "pool", "act"):
            self.dsem[e] = [self.es.enter_context(nc.semaphore(f"d_{e}{i}")) for i in range(self.N_DMA_SEM)]
        self.n_ops = 0

    def sbuf(self, name, shape, dt, nslot=1, sdim=1):
        h = self.es.enter_context(self.nc.sbuf_tensor("sb_" + name, list(shape), dt))
        return T(h, name, nslot, sdim)

    def psum(self, name, shape, dt, nslot=1, sdim=1):
        h = self.es.enter_context(self.nc.psum_tensor("ps_" + name, list(shape), dt))
        t = T(h, name, nslot, sdim)
        for b in t.bufs:
            b.excl = True
        return t

    def dram(self, name, shape, dt, kind="Internal", nslot=1, sdim=0):
        h = self.nc.dram_tensor(name, list(shape), dt, kind=kind)
        return T(h, name, nslot, sdim)

    def _deps(self, eng, reads, writes):
        deps = []
        for b in reads:
            if b.last_w is not None:
                deps.append((b.last_w, "raw"))
            if b.excl:
                for r in b.readers:
                    if r.eng != eng:
                        deps.append((r, "rar"))
        for b in writes:
            if b.last_w is not None:
                deps.append((b.last_w, "waw"))
            for r in b.readers:
                deps.append((r, "war"))
        need = {}
        for d, kind in deps:
            if not d.is_dma and d.eng == eng:
                if eng in ("pe", "sp"):
                    continue
                if kind in ("war", "rar"):
                    continue
            k = id(d.sem)
            if k not in need or need[k][1] < d.val:
                need[k] = (d.sem, d.val)
        out = []
        w = self.waited[eng]
        for k, (s, v) in need.items():
            if w.get(k, 0) >= v:
                continue
            w[k] = v
            out.append((s, v))
        return out

    def _record(self, eng, fn, reads, writes, is_dma=False):
        op = Op()
        op.eng = eng
        op.fn = fn
        op.is_dma = is_dma
        op.waits = self._deps(eng, reads, writes)
        if is_dma:
            i = self.ndma[eng]
            self.ndma[eng] += 1
            R = self.N_DMA_SEM
            op.sem = self.dsem[eng][i % R]
            op.val = 16 * (i // R + 1)
            op.inc = 16
            if i >= R:
                k = id(op.sem)
                pv = 16 * (i // R)
                if self.waited[eng].get(k, 0) < pv:
                    self.waited[eng][k] = pv
                    op.waits.append((op.sem, pv))
        else:
            self.cnt[eng] += 1
            op.sem = self.sem[eng]
            op.val = self.cnt[eng]
            op.inc = 1
        self.semvals[id(op.sem)] = (op.sem, op.val)
        for b in reads:
            b.readers.append(op)
        for b in writes:
            b.last_w = op
            b.readers = []
        self.ops[eng].append(op)
        self.n_ops += 1
        return op

    def I(self, eng, meth, *, extra_reads=(), extra_writes=(), dma_like=False, **kw):
        reads, writes = [], []
        kws = {}
        for k, a in kw.items():
            if isinstance(a, V):
                if k.startswith("out") or k in ("accum_out", "ap"):
                    writes += a.bufs
                else:
                    reads += a.bufs
                kws[k] = a.ap
            else:
                kws[k] = a
        for a in extra_reads:
            reads += a.bufs
        for a in extra_writes:
            writes += a.bufs
        is_dma = meth == "dma_start" or dma_like

        def fn(e, meth=meth, kws=kws):
            return getattr(e, meth)(**kws)
        return self._record(eng, fn, reads, writes, is_dma)

    def dma(self, eng, out, in_, **kw):
        o = out if isinstance(out, V) else V(out, [])
        i = in_ if isinstance(in_, V) else V(in_, [])
        return self.I(eng, "dma_start", out=o, in_=i, **kw)

    def mm(self, out, lhsT, rhs, start=True, stop=True, acc_read=False, **kw):
        return self.I("pe", "matmul", out=out, lhsT=lhsT, rhs=rhs, start=start, stop=stop, **kw)

    def tr(self, out, in_, ident):
        return self.I("pe", "transpose", out=out, in_=in_, identity=ident)

    def act(self, out, in_, func, eng="act", **kw):
        return self.I(eng, "activation", out=out, in_=in_, func=func, **kw)

    def finish(self):
        nc = self.nc
        fin = []
        for k, (s, v) in self.semvals.items():
            if self.waited["sp"].get(k, 0) < v:
                fin.append((s, v))
        with nc.Block() as block:
            def emit(eng_obj, name):
                for op in self.ops[name]:
                    for (s, v) in op.waits:
                        eng_obj.wait_ge(s, v)
                    ins = op.fn(eng_obj)
                    ins.then_inc(op.sem, op.inc)
                if name == "sp":
                    for (s, v) in fin:
                        eng_obj.wait_ge(s, v)

            if self.ops["sp"] or True:
                @block.sync
                def _(e):
                    emit(e, "sp")
            if self.ops["pe"]:
                @block.tensor
                def _(e):
                    emit(e, "pe")
            if self.ops["act"]:
                @block.scalar
                def _(e):
                    emit(e, "act")
            if self.ops["dve"]:
                @block.vector
                def _(e):
                    emit(e, "dve")
            if self.ops["pool"]:
                @block.gpsimd
                def _(e):
                    emit(e, "pool")
        self.es.close()


D = 1024
NOWN = 4096
NCTX = 256
NT = NOWN + NCTX
EPS = 1e-6
import ml_dtypes
NPBF = ml_dtypes.bfloat16


def mk_consts(P, inp):
    C = {}
    id32 = P.sbuf("id32", [128, 128], F32)
    P.dma("sp", id32[:, :], inp["ident"])
    idb = P.sbuf("idb", [128, 128], BF16)
    P.I("dve", "tensor_copy", out=idb[:, :], in_=id32[:, :])
    ones = P.sbuf("ones", [1, 128], F32)
    P.I("dve", "memset", ap=ones[:, :], constant=1.0)
    C["id32"], C["idb"], C["ones"] = id32, idb, ones
    return C


def load_w_bf16(P, w_dram, ncols, name, stage, engs=("dve", "pool")):
    wb = P.sbuf(name, [128, 8, ncols], BF16)
    wv = w_dram.rearrange("(j p) c -> p j c", p=128)
    nb = (ncols + 511) // 512
    for b in range(nb):
        c0, c1 = b * 512, min(ncols, (b + 1) * 512)
        for jh in range(2):
            st = stage[(2 * b + jh) % len(stage)]
            P.dma("sp", st[:, :, 0:c1 - c0], wv[:, jh * 4:jh * 4 + 4, c0:c1])
            P.I(engs[(2 * b + jh) % len(engs)], "tensor_copy", out=wb[:, jh * 4:jh * 4 + 4, c0:c1], in_=st[:, :, 0:c1 - c0])
    return wb


def ada_bcast(P, C, inp, nblk, modes, dests, stage, psA, gkey):
    ccol = P.sbuf("ccol", [128, 16], F32)
    P.dma("sp", ccol[:, :], inp["ccols"])
    csig = P.sbuf("csig", [128, 16], F32)
    P.act(csig[:, :], ccol[:, :], AF.Sigmoid)
    cact = P.sbuf("cact", [128, 16], BF16)
    P.I("dve", "tensor_tensor", out=cact[:, :], in0=ccol[:, :], in1=csig[:, :], op=ALU.mult)
    wv = inp["ada_w"].rearrange("(j p) c -> p j c", p=128)
    wbb = P.sbuf("adawb", [128, 8, 512], BF16)
    brow = [P.sbuf(f"brow{i}", [1, 512], F32) for i in range(2)]
    grow = [P.sbuf(f"grow{i}", [1, 512], F32) for i in range(2)]
    row = [P.sbuf(f"arow{i}", [1, 512], F32) for i in range(2)]
    row2 = [P.sbuf(f"arow2{i}", [1, 512], F32) for i in range(2)]
    k = 0
    for b in range(nblk):
        for jh in range(2):
            st = stage[(2 * b + jh) % len(stage)]
            P.dma("sp", st[:, :, :], wv[:, jh * 4:jh * 4 + 4, b * 512:(b + 1) * 512])
            P.I("dve", "tensor_copy", out=wbb[:, jh * 4:jh * 4 + 4, :], in_=st[:, :, :])
        P.dma("sp", brow[b % 2][:, :], inp["ada_b"][:, b * 512:(b + 1) * 512])
        mode = modes[b]
        if mode != "plain":
            gc = (b * 512) % D
            P.dma("sp", grow[b % 2][:, :], inp[gkey][:, gc:gc + 512])
        for v in range(2):
            for j in range(8):
                P.mm(psA[0:1, :], lhsT=cact[:, v * 8 + j:v * 8 + j + 1], rhs=wbb[:, j, :], start=(j == 0), stop=(j == 7))
            r = row[k % 2]
            P.I("dve", "tensor_tensor", out=r[:, :], in0=psA[0:1, :], in1=brow[b % 2][:, :], op=ALU.add)
            if mode == "onep_g":
                r2 = row2[k % 2]
                P.I("dve", "scalar_tensor_tensor", out=r2[:, :], in0=r[:, :], scalar=1.0, in1=grow[b % 2][:, :], op0=ALU.add, op1=ALU.mult)
                r = r2
            elif mode == "g":
                r2 = row2[k % 2]
                P.I("dve", "tensor_tensor", out=r2[:, :], in0=r[:, :], in1=grow[b % 2][:, :], op=ALU.mult)
                r = r2
            k += 1
            dt_, dc = dests[b][v]
            P.mm(psA[:, :], lhsT=C["ones"][:, :], rhs=r[0:1, :])
            P.I("dve", "tensor_copy", out=dt_[:, dc:dc + 512], in_=psA[:, :])


def pre_norm_rows(P, C, inp, _unused, stage, psA):
    A = [P.sbuf(f"Abc{v}", [128, 1024], F32) for v in range(2)]
    B = [P.sbuf(f"Bbc{v}", [128, 1024], F32) for v in range(2)]
    dests = [[(B[0], 0), (B[1], 0)], [(B[0], 512), (B[1], 512)], [(A[0], 0), (A[1], 0)], [(A[0], 512), (A[1], 512)]]
    ada_bcast(P, C, inp, 4, ["plain", "plain", "onep_g", "onep_g"], dests, stage, psA, "norm_pre")
    return [A[0], B[0], A[1], B[1]]


class PreCtx:
    pass


def emit_pre_group(P, C, K, x_view_fn, ntile, AB, gi):
    hT = K.hT[gi % 2]
    for i in range(ntile):
        k = K.cnt
        K.cnt += 1
        xt = K.xt[k % 2]
        P.dma("sp", xt[:, :], x_view_fn(i))
        sq = K.sq
        P.act(sq[:, :], xt[:, :], AF.Square)
        ssq = K.ssq[k % 2]
        P.I("dve", "tensor_reduce", out=ssq[:, 0:1], in_=sq[:, :], axis=AX.X, op=ALU.add)
        P.I("dve", "tensor_scalar", out=ssq[:, 1:2], in0=ssq[:, 0:1], scalar1=1.0 / D, scalar2=EPS, op0=ALU.mult, op1=ALU.add)
        P.act(ssq[:, 3:4], ssq[:, 1:2], AF.Sqrt)
        P.I("dve", "reciprocal", out=ssq[:, 2:3], in_=ssq[:, 3:4])
        t32 = K.t32
        P.I("dve", "scalar_tensor_tensor", out=t32[:, :], in0=xt[:, :], scalar=ssq[:, 2:3], in1=AB[0][:, :], op0=ALU.mult, op1=ALU.mult)
        hb = K.hb[k % 2]
        P.I("pool", "tensor_tensor", out=hb[:, :], in0=t32[:, :], in1=AB[1][:, :], op=ALU.add)
        ptr = K.ptr[k % 2]
        for j in range(8):
            P.tr(ptr[:, j * 128:(j + 1) * 128], hb[:, j * 128:(j + 1) * 128], C["idb"][:, :])
        P.act(hT.v(hT.h[:, :, i * 128:(i + 1) * 128]), ptr.v(ptr.h[:, :].rearrange("p (j t) -> p j t", j=8)), AF.Copy)
    return hT


def alloc_pre(P):
    K = PreCtx()
    K.cnt = 0
    K.xt = [P.sbuf(f"xt{i}", [128, D], F32) for i in range(2)]
    K.sq = P.sbuf("sq", [128, D], BF16)
    K.ssq = [P.sbuf(f"ssq{i}", [128, 4], F32) for i in range(2)]
    K.t32 = P.sbuf("t32", [128, D], F32)
    K.hb = [P.sbuf(f"hb{i}", [128, D], BF16) for i in range(2)]
    K.hT = [P.sbuf(f"hT{i}", [128, 8, 512], BF16) for i in range(2)]
    K.ptr = [P.psum(f"ptr{i}", [128, D], BF16) for i in range(2)]
    return K


def dram_in(nc, name, shape, dt=F32):
    return nc.dram_tensor(name, list(shape), dt, kind="ExternalInput").ap()


def dram_out(nc, name, shape, dt):
    return nc.dram_tensor(name, list(shape), dt, kind="ExternalOutput").ap()


STOP = 99


def build_A_even(n_own=NOWN, n_ctx=NCTX):
    nt = n_own + n_ctx
    nc = bass.Bass("TRN2", target_bir_lowering=False)
    P = Prog(nc)
    inp = dict(
        x=dram_in(nc, "x", [n_own, D]), ctx=dram_in(nc, "ctx", [n_ctx, D]), ccols=dram_in(nc, "ccols", [128, 16]),
        ada_w=dram_in(nc, "ada_w", [D, 2048]), ada_b=dram_in(nc, "ada_b", [1, 2048]), norm_pre=dram_in(nc, "norm_pre", [1, D]),
        w_in=dram_in(nc, "w_in", [D, 3584]), ident=dram_in(nc, "ident", [128, 128]), rmat=dram_in(nc, "rmat", [128, 128]),
        cosT=dram_in(nc, "cosT", [128, n_own]), sinT=dram_in(nc, "sinT", [128, n_own]))
    tm_d = dram_out(nc, "tm", [nt, 2048], BF16)
    fm_d = dram_out(nc, "fm", [12, 128, nt], BF16)
    C = mk_consts(P, inp)
    stage = [P.sbuf(f"stage{i}", [128, 4, 512], F32) for i in range(2)]
    psA = P.psum("psA", [128, 512], F32)
    if STOP == 0:
        P.finish()
        return nc, P
    AB = pre_norm_rows(P, C, inp, None, stage, psA)
    if STOP == 1:
        P.finish()
        return nc, P
    wb = load_w_bf16(P, inp["w_in"], 3584, "wb", stage)
    rm32 = P.sbuf("rm32", [128, 128], F32)
    P.dma("sp", rm32[:, :], inp["rmat"])
    rmb = P.sbuf("rmb", [128, 128], BF16)
    P.I("dve", "tensor_copy", out=rmb[:, :], in_=rm32[:, :])
    if STOP == 2:
        P.finish()
        return nc, P
    K = alloc_pre(P)
    pst = [P.psum(f"pst{i}", [128, 512], F32) for i in range(2)]
    psf = [P.psum(f"psf{i}", [128, 512], F32) for i in range(2)]
    psr = P.psum("psr", [128, 512], F32)
    tmt = [P.sbuf(f"tmt{i}", [128, 2048], BF16) for i in range(2)]
    cs = [P.sbuf(f"cs{i}", [128, 512], F32) for i in range(2)]
    sn = [P.sbuf(f"sn{i}", [128, 512], F32) for i in range(2)]
    qraw = [P.sbuf(f"qraw{i}", [128, 512], BF16) for i in range(2)]
    t1 = [P.sbuf(f"t1{i}", [128, 512], F32) for i in range(2)]
    t2 = [P.sbuf(f"t2{i}", [128, 512], F32) for i in range(2)]
    fmo = [P.sbuf(f"fmo{i}", [128, 512], BF16) for i in range(2)]
    TMB = [(0, 512, [(0, 0, 256, False), (256, 256, 256, True)]),
           (2048, 512, [(0, 512, 512, False)]),
           (2560, 512, [(0, 1024, 256, False), (256, 1280, 256, True)]),
           (3072, 512, [(0, 1536, 512, True)])]
    groups = [(g * 512, 4, False) for g in range(n_own // 512)] + [(n_own, n_ctx // 128, True)]
    kt = 0
    kf = 0
    for gi, (tok0, ntile, is_ctx) in enumerate(groups):
        n = ntile * 128
        if is_ctx:
            xfn = lambda i: inp["ctx"][i * 128:(i + 1) * 128, :]
            ab = AB[2:4]
        else:
            xfn = lambda i, tok0=tok0: inp["x"][tok0 + i * 128: tok0 + (i + 1) * 128, :]
            ab = AB[0:2]
            P.dma("sp", cs[gi % 2][:, 0:n], inp["cosT"][:, tok0:tok0 + n])
            P.dma("sp", sn[gi % 2][:, 0:n], inp["sinT"][:, tok0:tok0 + n])
        hT = emit_pre_group(P, C, K, xfn, ntile, ab, gi)
        if STOP == 3:
            break
        for i in range(ntile):
            tt = tmt[(gi * 4 + i) % 2]
            for (wc0, wn, eps_) in TMB:
                ps = pst[kt % 2]
                kt += 1
                for j in range(8):
                    P.mm(ps[:, 0:wn], lhsT=hT[:, j, i * 128:(i + 1) * 128], rhs=wb[:, j, wc0:wc0 + wn], start=(j == 0), stop=(j == 7))
                for (pc, dc, nn, silu) in eps_:
                    if silu:
                        P.act(tt[:, dc:dc + nn], ps[:, pc:pc + nn], AF.Silu)
                    else:
                        P.I("dve", "tensor_copy", out=tt[:, dc:dc + nn], in_=ps[:, pc:pc + nn])
            P.dma("sp", tm_d[tok0 + i * 128: tok0 + (i + 1) * 128, :], tt[:, :])
        if STOP == 4:
            break
        for blk in range(12):
            wc0 = 512 + blk * 128
            ps = psf[kf % 2]
            for j in range(8):
                P.mm(ps[:, 0:n], lhsT=wb[:, j, wc0:wc0 + 128], rhs=hT[:, j, 0:n], start=(j == 0), stop=(j == 7))
            fo = fmo[kf % 2]
            if is_ctx or STOP == 7:
                P.act(fo[:, 0:n], ps[:, 0:n], AF.Copy)
            else:
                qr = qraw[kf % 2]
                P.act(qr[:, 0:n], ps[:, 0:n], AF.Copy)
                if STOP not in (8, 9):
                    P.mm(psr[:, 0:n], lhsT=rmb[:, :], rhs=qr[:, 0:n])
                P.I("dve", "tensor_tensor", out=t1[kf % 2][:, 0:n], in0=(qr if STOP == 9 else ps)[:, 0:n], in1=cs[gi % 2][:, 0:n], op=ALU.mult, extra_reads=([qr[:, 0:n]] if STOP == 10 else []))
                P.I("dve", "tensor_tensor", out=t2[kf % 2][:, 0:n], in0=(psr if STOP not in (8, 9) else (ps if STOP == 8 else qr))[:, 0:n], in1=sn[gi % 2][:, 0:n], op=ALU.mult)
                P.I("dve" if STOP == 5 else "pool", "tensor_tensor", out=fo[:, 0:n], in0=t1[kf % 2][:, 0:n], in1=t2[kf % 2][:, 0:n], op=ALU.add)
            if STOP != 6:
                P.dma("sp", fm_d[blk, :, tok0:tok0 + n], fo[:, 0:n])
            kf += 1
    P.finish()
    return nc, P


def post_setup(P, C, inp, stage, psM):
    G = [P.sbuf(f"Gbc{v}", [128, 1024], F32) for v in range(2)]
    dests = [[(G[0], 0), (G[1], 0)], [(G[0], 512), (G[1], 512)]]
    ada_bcast(P, C, inp, 2, ["g", "g"], dests, stage, psM, "norm_post")
    wo = load_w_bf16(P, inp["w_out"], 1024, "wo", stage)
    K = PreCtx()
    K.G = G
    K.wo = wo
    K.mixt = [P.sbuf(f"mixt{i}", [128, D], BF16) for i in range(2)]
    K.mixT = [P.sbuf(f"mixT{i}", [128, 8, 128], BF16) for i in range(2)]
    K.xt = [P.sbuf(f"pxt{i}", [128, D], F32) for i in range(2)]
    K.sq = P.sbuf("psq", [128, D], BF16)
    K.ssq = [P.sbuf(f"pssq{i}", [128, 4], F32) for i in range(2)]
    K.t32 = P.sbuf("pt32", [128, D], F32)
    K.xo = [P.sbuf(f"pxo{i}", [128, D], F32) for i in range(2)]
    K.cnt = 0
    return K


def post_tile(P, C, K, mix_src, x_src, x_dst, v, psT, psY, mix_in_sbuf=None):
    k = K.cnt
    K.cnt += 1
    if mix_in_sbuf is None:
        mt = K.mixt[k % 2]
        P.dma("sp", mt[:, :], mix_src)
    else:
        mt = mix_in_sbuf
    for j in range(8):
        P.tr(psT[:, j * 128:(j + 1) * 128], mt[:, j * 128:(j + 1) * 128], C["idb"][:, :])
    mT = K.mixT[k % 2]
    P.act(mT.v(mT.h[:, :, :]), psT.v(psT.h[:, :].rearrange("p (j t) -> p j t", j=8)), AF.Copy)
    for cb in range(2):
        for j in range(8):
            P.mm(psY[:, cb * 512:(cb + 1) * 512], lhsT=mT[:, j, :], rhs=K.wo[:, j, cb * 512:(cb + 1) * 512], start=(j == 0), stop=(j == 7))
    xt = K.xt[k % 2]
    P.dma("sp", xt[:, :], x_src)
    P.act(K.sq[:, :], psY[:, 0:1024], AF.Square)
    ssq = K.ssq[k % 2]
    P.I("dve", "tensor_reduce", out=ssq[:, 0:1], in_=K.sq[:, :], axis=AX.X, op=ALU.add)
    P.I("dve", "tensor_scalar", out=ssq[:, 1:2], in0=ssq[:, 0:1], scalar1=1.0 / D, scalar2=EPS, op0=ALU.mult, op1=ALU.add)
    P.act(ssq[:, 3:4], ssq[:, 1:2], AF.Sqrt)
    P.I("dve", "reciprocal", out=ssq[:, 2:3], in_=ssq[:, 3:4])
    P.I("dve", "scalar_tensor_tensor", out=K.t32[:, :], in0=psY[:, 0:1024], scalar=ssq[:, 2:3], in1=K.G[v][:, :], op0=ALU.mult, op1=ALU.mult)
    xo = K.xo[k % 2]
    P.I("pool", "tensor_tensor", out=xo[:, :], in0=K.t32[:, :], in1=xt[:, :], op=ALU.add)
    P.dma("sp", x_dst, xo[:, :])


def build_B_even(n_own=NOWN, n_lat=4 * NOWN, n_ctx=NCTX):
    nt = n_own + n_ctx
    nk = n_lat + n_ctx
    nkb = nk // 128
    nc = bass.Bass("TRN2", target_bir_lowering=False)
    P = Prog(nc)
    inp = dict(
        x=dram_in(nc, "x", [n_own, D]), ctx=dram_in(nc, "ctx", [n_ctx, D]), ccols=dram_in(nc, "ccols", [128, 16]),
        ada_w=dram_in(nc, "ada_w", [D, 1024]), ada_b=dram_in(nc, "ada_b", [1, 1024]), norm_post=dram_in(nc, "norm_post", [1, D]),
        w_out=dram_in(nc, "w_out", [D, D]), ident=dram_in(nc, "ident", [128, 128]),
        qT=dram_in(nc, "qT", [6, 128, nt], BF16), kT=dram_in(nc, "kT", [6, 128, nk], BF16), vall=dram_in(nc, "vall", [nk, 768], BF16),
        gates=dram_in(nc, "gates", [nt, 1024], BF16), uh=dram_in(nc, "uh", [n_own + 16, 256], BF16), uch=dram_in(nc, "uch", [n_ctx + 16, 256], BF16),
        bandA=dram_in(nc, "bandA", [128, 5 * 4 * 128]), bandB=dram_in(nc, "bandB", [16, 5 * 4 * 128]),
        pool_w=dram_in(nc, "pool_w", [64, 4 * 64]), pool_scale=dram_in(nc, "pool_scale", [1, 256]),
        lamp=dram_in(nc, "lamp", [1, 256]), subln=dram_in(nc, "subln", [1, 128]), lamc=dram_in(nc, "lamc", [1, 2]))
    xo_d = dram_out(nc, "xo", [n_own, D], F32)
    co_d = dram_out(nc, "co", [n_ctx, D], F32)
    mix_d = P.dram("mixd", [nt, D], BF16, nslot=nt // 128, sdim=0)
    C = mk_consts(P, inp)
    stage = [P.sbuf(f"stage{i}", [128, 4, 512], F32) for i in range(2)]
    PSA = P.psum("PSA", [128, 3 * 512], F32, nslot=3)
    PSO = P.psum("PSO", [128, 3 * 512], F32, nslot=3)
    PST = P.psum("PST", [128, D], BF16)
    PSM = P.psum("PSM", [128, 512], F32)
    KP = post_setup(P, C, inp, stage, PSM)

    lamp = P.sbuf("lamp", [1, 256], F32)
    P.dma("sp", lamp[:, :], inp["lamp"])
    lamc = P.sbuf("lamc", [1, 2], F32)
    P.dma("sp", lamc[:, :], inp["lamc"])
    lw = P.sbuf("lw", [1, 16], F32)
    lpp = P.sbuf("lpp", [1, 128], F32)
    P.I("dve", "tensor_tensor", out=lpp[:, 0:64], in0=lamp[:, 0:64], in1=lamp[:, 64:128], op=ALU.mult)
    P.I("dve", "tensor_tensor", out=lpp[:, 64:128], in0=lamp[:, 128:192], in1=lamp[:, 192:256], op=ALU.mult)
    P.I("dve", "tensor_reduce", out=lw[:, 0:1], in_=lpp[:, 0:64], axis=AX.X, op=ALU.add)
    P.I("dve", "tensor_reduce", out=lw[:, 1:2], in_=lpp[:, 64:128], axis=AX.X, op=ALU.add)
    P.act(lw[:, 2:4], lw[:, 0:2], AF.Exp)
    P.I("dve", "tensor_tensor", out=lw[:, 4:5], in0=lw[:, 2:3], in1=lw[:, 3:4], op=ALU.subtract)
    P.I("dve", "tensor_tensor", out=lw[:, 5:6], in0=lw[:, 4:5], in1=lamc[:, 0:1], op=ALU.add)
    P.I("dve", "tensor_scalar", out=lw[:, 6:7], in0=lw[:, 5:6], scalar1=-1.0, scalar2=None, op0=ALU.mult)
    neglam = P.sbuf("neglam", [128, 1], F32)
    P.mm(PSM[:, 0:1], lhsT=C["ones"][:, :], rhs=lw[0:1, 6:7])
    P.I("dve", "tensor_copy", out=neglam[:, :], in_=PSM[:, 0:1])
    sub = P.sbuf("sub", [1, 128], F32)
    P.dma("sp", sub[:, :], inp["subln"])
    sub2 = P.sbuf("sub2", [1, 128], F32)
    P.I("dve", "tensor_scalar", out=sub2[:, :], in0=sub[:, :], scalar1=lamc[:, 1:2], scalar2=None, op0=ALU.mult)
    subg = P.sbuf("subg", [128, 128], F32)
    P.mm(PSM[:, 0:128], lhsT=C["ones"][:, :], rhs=sub2[0:1, :])
    P.I("dve", "tensor_copy", out=subg[:, :], in_=PSM[:, 0:128])
    psc = P.sbuf("psc", [1, 256], F32)
    P.dma("sp", psc[:, :], inp["pool_scale"])
    pscb = P.sbuf("pscb", [128, 256], F32)
    P.mm(PSM[:, 0:256], lhsT=C["ones"][:, :], rhs=psc[0:1, :])
    P.I("dve", "tensor_copy", out=pscb[:, :], in_=PSM[:, 0:256])

    bA = P.sbuf("bA", [128, 2560], BF16)
    bB = P.sbuf("bB", [16, 2560], BF16)
    sflat = [st.h[:, :, :].rearrange("p a b -> p (a b)") for st in stage]
    P.dma("sp", stage[0].v(sflat[0][:, 0:2048]), inp["bandA"][:, 0:2048])
    P.I("dve", "tensor_copy", out=bA[:, 0:2048], in_=stage[0].v(sflat[0][:, 0:2048]))
    P.dma("sp", stage[1].v(sflat[1][:, 0:512]), inp["bandA"][:, 2048:2560])
    P.I("dve", "tensor_copy", out=bA[:, 2048:2560], in_=stage[1].v(sflat[1][:, 0:512]))
    P.dma("sp", stage[0].v(sflat[0][0:16, 0:2048]), inp["bandB"][:, 0:2048])
    P.I("dve", "tensor_copy", out=bB[:, 0:2048], in_=stage[0].v(sflat[0][0:16, 0:2048]))
    P.dma("sp", stage[1].v(sflat[1][0:16, 0:512]), inp["bandB"][:, 2048:2560])
    P.I("dve", "tensor_copy", out=bB[:, 2048:2560], in_=stage[1].v(sflat[1][0:16, 0:512]))
    pw32 = P.sbuf("pw32", [64, 256], F32)
    P.dma("sp", pw32[:, :], inp["pool_w"])
    pw = P.sbuf("pw", [64, 256], BF16)
    P.I("dve", "tensor_copy", out=pw[:, :], in_=pw32[:, :])
    uA = [P.sbuf(f"uA{i}", [128, 256], BF16) for i in range(2)]
    uB = [P.sbuf(f"uB{i}", [16, 256], BF16) for i in range(2)]
    dT = [P.sbuf(f"dT{i}", [64, 512], BF16) for i in range(2)]
    gat = [P.sbuf(f"gat{i}", [128, 256], BF16) for i in range(2)]
    ytmp = [P.sbuf(f"ytmp{i}", [128, 256], F32) for i in range(2)]
    ya = [P.sbuf(f"ya{i}", [128, 256], BF16) for i in range(2)]
    ntile_own = n_own // 128
    ntile_ctx = n_ctx // 128
    tiles = [("x", i) for i in range(ntile_own)] + [("c", i) for i in range(ntile_ctx)]
    for k, (kind, i) in enumerate(tiles):
        if kind == "x":
            src = inp["uh"]
            typ = 0 if i == 0 else (2 if i == ntile_own - 1 else 1)
            row0 = i * 128
            tok0 = i * 128
        else:
            src = inp["uch"]
            typ = 3 if i == 0 else 4
            row0 = i * 128
            tok0 = n_own + i * 128
        P.dma("sp", uA[k % 2][:, :], src[row0:row0 + 128, :])
        P.dma("sp", uB[k % 2][:, :], src[row0 + 128:row0 + 144, :])
        P.dma("sp", gat[k % 2][:, :], inp["gates"][tok0:tok0 + 128, 0:256])
        for g in range(4):
            bo = (typ * 4 + g) * 128
            P.mm(PSM[0:64, g * 128:(g + 1) * 128], lhsT=uA[k % 2][:, g * 64:(g + 1) * 64], rhs=bA[:, bo:bo + 128], start=True, stop=False)
            P.mm(PSM[0:64, g * 128:(g + 1) * 128], lhsT=uB[k % 2][:, g * 64:(g + 1) * 64], rhs=bB[:, bo:bo + 128], start=False, stop=True)
        P.act(dT[k % 2][:, :], PSM[0:64, :], AF.Copy)
        yps = PSA[:, 1024:1024 + 256]
        for g in range(4):
            P.mm(PSA[:, 1024 + g * 64:1024 + (g + 1) * 64], lhsT=dT[k % 2][:, g * 128:(g + 1) * 128], rhs=pw[:, g * 64:(g + 1) * 64])
        P.I("dve", "tensor_tensor", out=ytmp[k % 2][:, :], in0=yps, in1=pscb[:, :], op=ALU.mult)
        P.I("pool", "tensor_tensor", out=ya[k % 2][:, :], in0=ytmp[k % 2][:, :], in1=gat[k % 2][:, :], op=ALU.mult)
        P.dma("sp", mix_d[tok0:tok0 + 128, 0:256], ya[k % 2][:, :])

    kTs = P.sbuf("kTs", [128, nk], BF16)
    vau = P.sbuf("vau", [128, nkb, 128], BF16)
    qz = [[P.sbuf(f"qz{i}_{m}", [128, 512], BF16) for m in range(2)] for i in range(2)]
    for i in range(2):
        for m in range(2):
            P.I("pool", "memset", ap=qz[i][m][:, :], constant=0.0)
    pT = P.sbuf("pT", [128, 3 * 512], BF16, nslot=3)
    dacc = [KP.xt[m] for m in range(2)]
    onec = P.sbuf("onec", [128, 1], F32)
    P.I("dve", "memset", ap=onec[:, :], constant=1.0)
    rrow = [P.sbuf(f"rrow{m}", [1, 512], F32) for m in range(2)]
    rbc = [KP.xo[m] for m in range(2)]
    tt1 = KP.t32
    tt2 = T(KP.t32.h, "t32b")
    tt2.bufs = KP.t32.bufs
    oTb = P.sbuf("oTb", [128, 512], BF16)
    gb = [P.sbuf(f"gb{i}", [128, 128], BF16) for i in range(2)]
    osq = P.sbuf("osq", [128, 128], BF16)
    oss = [P.sbuf(f"oss{i}", [128, 4], F32) for i in range(2)]
    o3 = [P.sbuf(f"o3{i}", [128, 128], F32) for i in range(2)]
    ob = [P.sbuf(f"ob{i}", [128, 128], BF16) for i in range(2)]
    vsrc = inp["vall"].rearrange("(kb p) c -> p kb c", p=128)
    qblocks = [(qb * 512, 512, list(range(nkb))) for qb in range(n_own // 512)] + \
              [(n_own + c0, min(512, n_ctx - c0), list(range(n_lat // 128, nkb))) for c0 in range(0, n_ctx, 512)]
    kq = 0
    kf = 0
    for h in range(6):
        P.dma("sp", kTs[:, :], inp["kT"][h, :, :])
        for c0 in range(0, nkb, 32):
            c1 = min(nkb, c0 + 32)
            P.dma("sp", vau.v(vau.h[:, c0:c1, :]), vsrc[:, c0:c1, h * 128:(h + 1) * 128])
        for (q0, qn, kbs) in qblocks:
            nqs = qn // 128
            qt = qz[kq % 2]
            kq += 1
            for m in range(2):
                P.dma("sp", qt[m][m * 64:(m + 1) * 64, 0:qn], inp["qT"][h, m * 64:(m + 1) * 64, q0:q0 + qn])
            its = [(kb, m) for kb in kbs for m in range(2)]

            def QK(it):
                kb, m = its[it]
                s = it % 3
                P.mm(PSA[:, s * 512:s * 512 + qn], lhsT=kTs[:, kb * 128:(kb + 1) * 128], rhs=qt[m][:, 0:qn])

            QK(0)
            if len(its) > 1:
                QK(1)
            for it, (kb, m) in enumerate(its):
                s = it % 3
                if it + 2 < len(its):
                    QK(it + 2)
                P.act(pT[:, s * 512:s * 512 + qn], PSA[:, s * 512:s * 512 + qn], AF.Exp, scale=0.125)
                P.mm(PSO[:, m * 512:m * 512 + qn], lhsT=vau[:, kb, :], rhs=pT[:, s * 512:s * 512 + qn], start=(kb == kbs[0]), stop=(kb == kbs[-1]))
                eng = "dve"
                if kb == kbs[0]:
                    P.I(eng, "tensor_copy", out=dacc[m][:, 0:qn], in_=pT[:, s * 512:s * 512 + qn])
                else:
                    P.I(eng, "tensor_tensor", out=dacc[m][:, 0:qn], in0=dacc[m][:, 0:qn], in1=pT[:, s * 512:s * 512 + qn], op=ALU.add)
            for m in range(2):
                P.mm(PSM[0:1, 0:qn], lhsT=onec[:, :], rhs=dacc[m][:, 0:qn])
                P.I("dve", "reciprocal", out=rrow[m][:, 0:qn], in_=PSM[0:1, 0:qn])
                if m == 1:
                    P.I("dve", "tensor_scalar", out=rrow[m][:, 0:qn], in0=rrow[m][:, 0:qn], scalar1=neglam[0:1, 0:1], scalar2=None, op0=ALU.mult)
                P.mm(PSO[:, 1024:1024 + qn], lhsT=C["ones"][:, :], rhs=rrow[m][0:1, 0:qn])
                P.act(rbc[m][:, 0:qn], PSO[:, 1024:1024 + qn], AF.Copy)
            P.I("dve", "tensor_tensor", out=tt1[:, 0:qn], in0=PSO[:, 0:qn], in1=rbc[0][:, 0:qn], op=ALU.mult)
            P.I("dve", "tensor_tensor", out=tt2[:, 512:512 + qn], in0=PSO[:, 512:512 + qn], in1=rbc[1][:, 0:qn], op=ALU.mult)
            P.I("pool", "tensor_tensor", out=oTb[:, 0:qn], in0=tt1[:, 0:qn], in1=tt2[:, 512:512 + qn], op=ALU.add)
            for qs in range(nqs):
                P.tr(PST[:, qs * 128:(qs + 1) * 128], oTb[:, qs * 128:(qs + 1) * 128], C["idb"][:, :])
            for qs in range(nqs):
                tok0 = q0 + qs * 128
                o2v = PST[:, qs * 128:(qs + 1) * 128]
                P.dma("sp", gb[kf % 2][:, :], inp["gates"][tok0:tok0 + 128, 256 + h * 128:256 + (h + 1) * 128])
                P.act(osq[:, :], o2v, AF.Square)
                ss = oss[kf % 2]
                P.I("dve", "tensor_reduce", out=ss[:, 0:1], in_=osq[:, :], axis=AX.X, op=ALU.add)
                P.I("dve", "tensor_scalar", out=ss[:, 1:2], in0=ss[:, 0:1], scalar1=1.0 / 128, scalar2=EPS, op0=ALU.mult, op1=ALU.add)
                P.act(ss[:, 3:4], ss[:, 1:2], AF.Sqrt)
                P.I("dve", "reciprocal", out=ss[:, 2:3], in_=ss[:, 3:4])
                P.I("dve", "scalar_tensor_tensor", out=o3[kf % 2][:, :], in0=o2v, scalar=ss[:, 2:3], in1=subg[:, :], op0=ALU.mult, op1=ALU.mult)
                P.I("pool", "tensor_tensor", out=ob[kf % 2][:, :], in0=o3[kf % 2][:, :], in1=gb[kf % 2][:, :], op=ALU.mult)
                P.dma("sp", mix_d[tok0:tok0 + 128, 256 + h * 128:256 + (h + 1) * 128], ob[kf % 2][:, :])
                kf += 1

    for k, (kind, i) in enumerate(tiles):
        if kind == "x":
            post_tile(P, C, KP, mix_d[i * 128:(i + 1) * 128, :], inp["x"][i * 128:(i + 1) * 128, :], xo_d[i * 128:(i + 1) * 128, :], 0, PST, PSA)
        else:
            t0 = n_own + i * 128
            post_tile(P, C, KP, mix_d[t0:t0 + 128, :], inp["ctx"][i * 128:(i + 1) * 128, :], co_d[i * 128:(i + 1) * 128, :], 1, PST, PSA)
    P.finish()
    return nc, P


def build_A_odd(layer, n_own=NOWN, n_ctx=NCTX):
    nt = n_own + n_ctx
    nc = bass.Bass("TRN2", target_bir_lowering=False)
    P = Prog(nc)
    inp = dict(
        x=dram_in(nc, "x", [n_own, D]), ctx=dram_in(nc, "ctx", [n_ctx, D]), ccols=dram_in(nc, "ccols", [128, 16]),
        ada_w=dram_in(nc, "ada_w", [D, 2048]), ada_b=dram_in(nc, "ada_b", [1, 2048]), norm_pre=dram_in(nc, "norm_pre", [1, D]),
        w_in=dram_in(nc, "w_in", [D, 4096]), ident=dram_in(nc, "ident", [128, 128]), lbraw=dram_in(nc, "lbraw", [128, 32]))
    tm_d = dram_out(nc, "tm", [nt, 1536], BF16)
    fmb_d = dram_out(nc, "fmb", [4, 128, nt], BF16)
    fmf_d = dram_out(nc, "fmf", [12, 128, nt], F32)
    C = mk_consts(P, inp)
    stage = [P.sbuf(f"stage{i}", [128, 4, 512], F32) for i in range(2)]
    psA = P.psum("psA", [128, 512], F32)
    AB = pre_norm_rows(P, C, inp, None, stage, psA)
    wb = load_w_bf16(P, inp["w_in"], 4096, "wb", stage)
    lbr = P.sbuf("lbr", [128, 32], F32)
    P.dma("sp", lbr[:, :], inp["lbraw"])
    lbe = P.sbuf("lbe", [128, 32], F32)
    P.act(lbe[:, :], lbr[:, :], AF.Exp)
    ev = lambda li: lbe.v(lbe.h[:, :].rearrange("p (d l h) -> p d l h", d=2, l=4)[:, :, li, :])
    den = P.sbuf("lbden", [128, 8], F32)
    num = P.sbuf("lbnum", [128, 8], F32)
    v3 = lambda t: t.v(t.h[:, :].rearrange("p (d h) -> p d h", d=2))
    P.I("dve", "tensor_tensor", out=v3(den), in0=ev(0), in1=ev(1), op=ALU.add)
    P.I("dve", "tensor_tensor", out=v3(den), in0=v3(den), in1=ev(2), op=ALU.add)
    P.I("dve", "tensor_tensor", out=v3(den), in0=v3(den), in1=ev(3), op=ALU.add)
    P.I("dve", "tensor_copy", out=v3(num), in_=ev(1))
    for li in range(2, layer + 1):
        P.I("dve", "tensor_tensor", out=v3(num), in0=v3(num), in1=ev(li), op=ALU.add)
    rden = P.sbuf("lbrden", [128, 8], F32)
    P.I("dve", "reciprocal", out=rden[:, :], in_=den[:, :])
    lb = P.sbuf("lb", [128, 8], F32)
    P.I("dve", "tensor_tensor", out=lb[:, :], in0=num[:, :], in1=rden[:, :], op=ALU.mult)
    oml = P.sbuf("oml", [128, 8], F32)
    P.I("dve", "tensor_scalar", out=oml[:, :], in0=lb[:, :], scalar1=-1.0, scalar2=1.0, op0=ALU.mult, op1=ALU.add)

    K = alloc_pre(P)
    pst = [P.psum(f"pst{i}", [128, 512], F32) for i in range(2)]
    psf = [P.psum(f"psf{i}", [128, 512], F32) for i in range(3)]
    tmt = [P.sbuf(f"tmt{i}", [128, 1536], BF16) for i in range(2)]
    sg = [P.sbuf(f"sg{i}", [128, 512], F32) for i in range(2)]
    fob = [P.sbuf(f"fob{i}", [128, 512], BF16) for i in range(2)]
    fof = [P.sbuf(f"fof{i}", [128, 512], F32) for i in range(2)]
    TMB = [(1024, 512, 0, True), (3584, 512, 512, True), (3072, 512, 1024, False)]
    groups = [(g * 512, 4, False) for g in range(n_own // 512)] + [(n_own, n_ctx // 128, True)]
    kt = 0
    kf = 0
    kb_ = 0
    kff = 0
    for gi, (tok0, ntile, is_ctx) in enumerate(groups):
        n = ntile * 128
        if is_ctx:
            xfn = lambda i: inp["ctx"][i * 128:(i + 1) * 128, :]
            ab = AB[2:4]
        else:
            xfn = lambda i, tok0=tok0: inp["x"][tok0 + i * 128: tok0 + (i + 1) * 128, :]
            ab = AB[0:2]
        hT = emit_pre_group(P, C, K, xfn, ntile, ab, gi)
        for i in range(ntile):
            tt = tmt[(gi * 4 + i) % 2]
            for (wc0, wn, dc, silu) in TMB:
                ps = pst[kt % 2]
                kt += 1
                for j in range(8):
                    P.mm(ps[:, 0:wn], lhsT=hT[:, j, i * 128:(i + 1) * 128], rhs=wb[:, j, wc0:wc0 + wn], start=(j == 0), stop=(j == 7))
                if silu:
                    P.act(tt[:, dc:dc + wn], ps[:, 0:wn], AF.Silu)
                else:
                    P.I("dve", "tensor_copy", out=tt[:, dc:dc + wn], in_=ps[:, 0:wn])
            P.dma("sp", tm_d[tok0 + i * 128: tok0 + (i + 1) * 128, :], tt[:, :])

        def fproj(wc0):
            nonlocal kf
            ps = psf[kf % 3]
            kf += 1
            for j in range(8):
                P.mm(ps[:, 0:n], lhsT=wb[:, j, wc0:wc0 + 128], rhs=hT[:, j, 0:n], start=(j == 0), stop=(j == 7))
            return ps
        for c in range(4):
            pa = fproj(c * 128)
            pb = fproj(512 + c * 128)
            s_ = sg[kb_ % 2]
            P.act(s_[:, 0:n], pb[:, 0:n], AF.Sigmoid)
            fo = fob[kb_ % 2]
            kb_ += 1
            P.I("dve", "tensor_tensor", out=fo[:, 0:n], in0=pa[:, 0:n], in1=s_[:, 0:n], op=ALU.mult)
            P.dma("sp", fmb_d[c, :, tok0:tok0 + n], fo[:, 0:n])
        for hh in range(4):
            pq = fproj(1536 + hh * 128)
            fo = fof[kff % 2]
            kff += 1
            P.act(fo[:, 0:n], pq[:, 0:n], AF.Silu)
            P.dma("sp", fmf_d[hh, :, tok0:tok0 + n], fo[:, 0:n])
            for dr in range(2):
                pf = fproj(2048 + dr * 512 + hh * 128)
                s_ = sg[kb_ % 2]
                kb_ += 1
                P.act(s_[:, 0:n], pf[:, 0:n], AF.Sigmoid)
                fo = fof[kff % 2]
                kff += 1
                ci = dr * 4 + hh
                P.I("dve", "tensor_scalar", out=fo[:, 0:n], in0=s_[:, 0:n], scalar1=oml[:, ci:ci + 1], scalar2=lb[:, ci:ci + 1], op0=ALU.mult, op1=ALU.add)
                P.dma("sp", fmf_d[4 + dr * 4 + hh, :, tok0:tok0 + n], fo[:, 0:n])
    P.finish()
    return nc, P


def build_H(n_lat=4 * NOWN, n_ctx=NCTX):
    N = n_lat + n_ctx
    SEG = min(2048, n_lat)
    nc = bass.Bass("TRN2", target_bir_lowering=False)
    P = Prog(nc)
    inp = dict(qT=dram_in(nc, "qT", [128, N]), fTf=dram_in(nc, "fTf", [128, N]), fTb=dram_in(nc, "fTb", [128, N]),
               v=dram_in(nc, "v", [N, 128], BF16), ogs=dram_in(nc, "ogs", [N, 128], BF16), gn=dram_in(nc, "gn", [1, 128]),
               ident=dram_in(nc, "ident", [128, 128]), masks=dram_in(nc, "masks", [64, 128]), cmask=dram_in(nc, "cmask", [128, SEG]))
    yd_d = dram_out(nc, "yd", [N, 128], BF16)
    of_d = P.dram("ofd", [N, 128], F32, nslot=N // 64, sdim=0)
    C = mk_consts(P, inp)
    msk = P.sbuf("msk", [64, 128], F32)
    P.dma("sp", msk[:, :], inp["masks"])
    cm = P.sbuf("cm", [128, SEG], F32)
    P.dma("sp", cm[:, :], inp["cmask"])
    gnr = P.sbuf("gnr", [1, 128], F32)
    P.dma("sp", gnr[:, :], inp["gn"])
    psg = P.psum("psg", [128, 128], F32)
    gbc = P.sbuf("gbc", [128, 128], F32)
    P.mm(psg[:, :], lhsT=C["ones"][:, :], rhs=gnr[0:1, :])
    P.I("dve", "tensor_copy", out=gbc[:, :], in_=psg[:, :])
    qs = P.sbuf("qs", [128, SEG], F32)
    fs = P.sbuf("fs", [128, SEG], F32)
    lfs = P.sbuf("lfs", [128, SEG], F32)
    ks = P.sbuf("ks", [128, SEG], F32)
    Pc = P.sbuf("Pc", [128, SEG], F32)
    Pe = P.sbuf("Pe", [128, SEG], F32)
    vs = P.sbuf("vs", [64, SEG // 64, 128], BF16)
    gs = P.sbuf("gs", [64, SEG // 64, 128], BF16)
    S = [P.sbuf(f"S{i}", [128, 128], F32) for i in range(2)]
    cols = [P.sbuf(f"cols{i}", [128, 8], F32) for i in range(2)]
    eq = [P.sbuf(f"eq{i}", [128, 64], F32) for i in range(2)]
    ek = [P.sbuf(f"ek{i}", [128, 64], F32) for i in range(2)]
    ekl = [P.sbuf(f"ekl{i}", [128, 64], F32) for i in range(2)]
    qt = [P.sbuf(f"qt{i}", [128, 64], BF16) for i in range(2)]
    kt = [P.sbuf(f"kt{i}", [128, 64], BF16) for i in range(2)]
    kh = [P.sbuf(f"kh{i}", [128, 64], BF16) for i in range(2)]
    khs = [P.sbuf(f"khs{i}", [64, 128], BF16) for i in range(2)]
    S0m = [P.sbuf(f"S0m{i}", [128, 128], BF16) for i in range(2)]
    attm = [P.sbuf(f"attm{i}", [64, 64], BF16) for i in range(2)]
    osb = [P.sbuf(f"osb{i}", [64, 128], F32) for i in range(2)]
    ofs = [P.sbuf(f"ofs{i}", [64, 128], F32) for i in range(2)]
    osq = P.sbuf("hosq", [64, 128], BF16)
    oss = [P.sbuf(f"hoss{i}", [64, 4], F32) for i in range(2)]
    on = [P.sbuf(f"on{i}", [64, 128], F32) for i in range(2)]
    yb = [P.sbuf(f"yb{i}", [64, 128], BF16) for i in range(2)]
    p_kh = P.psum("p_kh", [64, 128], BF16)
    p_att = [P.psum(f"p_att{i}", [64, 64], F32) for i in range(2)]
    p_o = [P.psum(f"p_o{i}", [64, 128], F32) for i in range(2)]
    p_ds = [P.psum(f"p_ds{i}", [128, 128], F32) for i in range(2)]
    segs = [(0, n_ctx)] + [(n_ctx + i * SEG, SEG) for i in range(n_lat // SEG)]
    kc = 0
    for dirn in range(2):
        bwd = dirn == 1
        P.I("dve", "memset", ap=S[0][:, :], constant=0.0)
        cur = 0
        order = segs if not bwd else [segs[0]] + segs[1:][::-1]
        E = Pe if bwd else Pc
        for (a0, sl) in order:
            P.dma("sp", qs[:, 0:sl], inp["qT"][:, a0:a0 + sl])
            P.dma("sp", fs[:, 0:sl], inp["fTb" if bwd else "fTf"][:, a0:a0 + sl])
            P.dma("sp", vs.v(vs.h[:, 0:sl // 64, :]), inp["v"][a0:a0 + sl, :].rearrange("(n p) c -> p n c", p=64))
            if bwd:
                P.dma("sp", gs.v(gs.h[:, 0:sl // 64, :]), inp["ogs"][a0:a0 + sl, :].rearrange("(n p) c -> p n c", p=64))
            P.act(lfs[:, 0:sl], fs[:, 0:sl], AF.Ln)
            P.I("pool", "tensor_scalar", out=ks[:, 0:sl], in0=fs[:, 0:sl], scalar1=-1.0, scalar2=1.0, op0=ALU.mult, op1=ALU.add)
            P.I("dve", "tensor_tensor_scan", out=Pc[:, 0:sl], data0=cm[:, 0:sl], data1=lfs[:, 0:sl], initial=0.0, op0=ALU.mult, op1=ALU.add)
            if bwd:
                P.I("dve", "tensor_tensor", out=Pe[:, 0:sl], in0=Pc[:, 0:sl], in1=lfs[:, 0:sl], op=ALU.subtract)
            chunks = list(range(sl // 64))
            if bwd:
                chunks = chunks[::-1]
            for n_ in chunks:
                a = n_ * 64
                k2 = kc % 2
                kc += 1
                cl = cols[k2]
                tot = Pc[:, a + 63:a + 64]
                P.I("dve", "tensor_copy", out=cl[:, 0:1], in_=E[:, a + 31:a + 32])
                P.I("dve", "tensor_scalar", out=cl[:, 1:2], in0=E[:, a + 31:a + 32], scalar1=-1.0, scalar2=None, op0=ALU.mult)
                if bwd:
                    P.I("dve", "tensor_tensor", out=cl[:, 3:4], in0=tot, in1=cl[:, 0:1], op=ALU.subtract)
                else:
                    P.I("dve", "tensor_copy", out=cl[:, 3:4], in_=cl[:, 0:1])
                P.I("dve", "tensor_copy", out=cl[:, 4:5], in_=tot)
                P.act(cl[:, 5:7], cl[:, 3:5], AF.Exp)
                Ec = E[:, a:a + 64]
                if bwd:
                    P.act(eq[k2][:, :], Ec, AF.Exp, scale=-1.0, bias=cl[:, 0:1])
                    P.act(ek[k2][:, :], Ec, AF.Exp, scale=1.0, bias=cl[:, 1:2])
                    P.act(ekl[k2][:, :], Ec, AF.Exp)
                else:
                    P.act(eq[k2][:, :], Ec, AF.Exp, scale=1.0, bias=cl[:, 1:2])
                    P.act(ek[k2][:, :], Ec, AF.Exp, scale=-1.0, bias=cl[:, 0:1])
                    P.act(ekl[k2][:, :], Ec, AF.Exp, scale=-1.0, bias=cl[:, 4:5])
                P.I("dve", "tensor_tensor", out=qt[k2][:, :], in0=qs[:, a:a + 64], in1=eq[k2][:, :], op=ALU.mult)
                P.I("pool", "tensor_tensor", out=kt[k2][:, :], in0=ks[:, a:a + 64], in1=ek[k2][:, :], op=ALU.mult)
                P.I("pool", "tensor_tensor", out=kh[k2][:, :], in0=ks[:, a:a + 64], in1=ekl[k2][:, :], op=ALU.mult)
                P.tr(p_kh[:, :], kh[k2][:, :], C["idb"][:, :])
                P.act(khs[k2][:, :], p_kh[:, :], AF.Copy)
                P.I("dve", "tensor_scalar", out=S0m[k2][:, :], in0=S[cur][:, :], scalar1=cl[:, 5:6], scalar2=None, op0=ALU.mult)
                P.mm(p_att[k2][:, :], lhsT=kt[k2][:, :], rhs=qt[k2][:, :])
                mo = 64 if bwd else 0
                P.I("dve", "tensor_tensor", out=attm[k2][:, :], in0=p_att[k2][:, :], in1=msk[:, mo:mo + 64], op=ALU.mult)
                P.mm(p_o[k2][:, :], lhsT=qt[k2][:, :], rhs=S0m[k2][:, :], start=True, stop=False)
                P.mm(p_o[k2][:, :], lhsT=attm[k2][:, :], rhs=vs[:, n_, :], start=False, stop=True)
                P.mm(p_ds[k2][:, :], lhsT=khs[k2][:, :], rhs=vs[:, n_, :])
                P.I("dve", "scalar_tensor_tensor", out=S[1 - cur][:, :], in0=S[cur][:, :], scalar=cl[:, 6:7], in1=p_ds[k2][:, :], op0=ALU.mult, op1=ALU.add)
                cur = 1 - cur
                r0 = a0 + a
                if not bwd:
                    P.act(osb[k2][:, :], p_o[k2][:, :], AF.Copy)
                    P.dma("sp", of_d[r0:r0 + 64, :], osb[k2][:, :])
                else:
                    P.dma("sp", ofs[k2][:, :], of_d[r0:r0 + 64, :])
                    P.I("dve", "tensor_tensor", out=osb[k2][:, :], in0=p_o[k2][:, :], in1=ofs[k2][:, :], op=ALU.add)
                    P.act(osq[:, :], osb[k2][:, :], AF.Square)
                    ss = oss[k2]
                    P.I("dve", "tensor_reduce", out=ss[:, 0:1], in_=osq[:, :], axis=AX.X, op=ALU.add)
                    P.I("dve", "tensor_scalar", out=ss[:, 1:2], in0=ss[:, 0:1], scalar1=1.0 / 128, scalar2=EPS, op0=ALU.mult, op1=ALU.add)
                    P.act(ss[:, 3:4], ss[:, 1:2], AF.Sqrt)
                    P.I("dve", "reciprocal", out=ss[:, 2:3], in_=ss[:, 3:4])
                    P.I("dve", "scalar_tensor_tensor", out=on[k2][:, :], in0=osb[k2][:, :], scalar=ss[:, 2:3], in1=gbc[0:64, :], op0=ALU.mult, op1=ALU.mult)
                    P.I("pool", "tensor_tensor", out=yb[k2][:, :], in0=on[k2][:, :], in1=gs[:, n_, :], op=ALU.mult)
                    P.dma("sp", yd_d[r0:r0 + 64, :], yb[k2][:, :])
    P.finish()
    return nc, P


CSTOP = 0


def build_C_odd(n_own=NOWN, n_ctx=NCTX, with_ctx=True):
    nt = n_own + n_ctx
    nc = bass.Bass("TRN2", target_bir_lowering=False)
    P = Prog(nc)
    inp = dict(
        x=dram_in(nc, "x", [n_own, D]), ctx=dram_in(nc, "ctx", [n_ctx, D]), ccols=dram_in(nc, "ccols", [128, 16]),
        ada_w=dram_in(nc, "ada_w", [D, 1024]), ada_b=dram_in(nc, "ada_b", [1, 1024]), norm_post=dram_in(nc, "norm_post", [1, D]),
        w_out=dram_in(nc, "w_out", [D, D]), ident=dram_in(nc, "ident", [128, 128]),
        gluh=dram_in(nc, "gluh", [4, 128, n_own + 30], BF16), gluch=dram_in(nc, "gluch", [4, 128, n_ctx + 30], BF16),
        cgs=dram_in(nc, "cgs", [nt, 512], BF16), yd=dram_in(nc, "yd", [nt, 512], BF16),
        convw=dram_in(nc, "convw", [128, 124]), convb=dram_in(nc, "convb", [128, 4]),
        lng=dram_in(nc, "lng", [1, 512]), lnb=dram_in(nc, "lnb", [1, 512]))
    xo_d = dram_out(nc, "xo", [n_own, D], F32)
    co_d = dram_out(nc, "co", [n_ctx, D], F32)
    C = mk_consts(P, inp)
    stage = [P.sbuf(f"stage{i}", [128, 4, 512], F32) for i in range(2)]
    PSY = P.psum("PSY", [128, 1024], F32)
    PST = P.psum("PST", [128, D], BF16)
    PSM = P.psum("PSM", [128, 512], F32)
    PSZ = [P.psum(f"PSZ{i}", [128, 512], BF16) for i in range(2)]
    KP = post_setup(P, C, inp, stage, PSM)
    cw = P.sbuf("cw", [128, 124], F32)
    P.dma("sp", cw[:, :], inp["convw"])
    cb = P.sbuf("cb", [128, 4], F32)
    P.dma("sp", cb[:, :], inp["convb"])
    lrow = P.sbuf("lrow", [1, 1024], F32)
    P.dma("sp", lrow[:, 0:512], inp["lng"])
    P.dma("sp", lrow[:, 512:1024], inp["lnb"])
    lgb = P.sbuf("lgb", [128, 1024], F32)
    for hh in range(2):
        P.mm(PSM[:, :], lhsT=C["ones"][:, :], rhs=lrow[0:1, hh * 512:(hh + 1) * 512])
        P.I("dve", "tensor_copy", out=lgb[:, hh * 512:(hh + 1) * 512], in_=PSM[:, :])
    glu = [P.sbuf(f"glu{i}", [128, 4, 542], BF16) for i in range(2)]
    zT = [P.sbuf(f"zT{i}", [128, 4, 512], F32) for i in range(2)]
    zs = [P.sbuf(f"zs{i}", [128, 512], F32) for i in range(2)]
    zb = [P.sbuf(f"zb{i}", [128, 4, 512], BF16) for i in range(2)]
    zsq = P.sbuf("zsq", [128, 512], F32)
    st_ = [P.sbuf(f"lst{i}", [128, 8], F32) for i in range(2)]
    zn = [P.sbuf(f"zn{i}", [128, 512], F32) for i in range(2)]
    zl = [P.sbuf(f"zl{i}", [128, 512], F32) for i in range(2)]
    zsl = [P.sbuf(f"zsl{i}", [128, 512], F32) for i in range(2)]
    cg = [P.sbuf(f"cg{i}", [128, 512], BF16) for i in range(2)]
    mixs = [P.sbuf(f"mixs{i}", [128, D], BF16) for i in range(2)]
    blocks = [("x", b0, 512) for b0 in range(0, n_own, 512)]
    if with_ctx:
        blocks += [("c", 0, n_ctx)]
    kt = 0
    for bi, (kind, b0, nb) in enumerate(blocks):
        g_ = glu[bi % 2]
        src = inp["gluh"] if kind == "x" else inp["gluch"]
        P.dma("sp", g_.v(g_.h[:, :, 0:nb + 30]), src[:, :, b0:b0 + nb + 30].rearrange("c p t -> p c t"))
        z_ = zT[bi % 2]
        for c in range(4):
            acc = z_[:, c, 0:nb]
            P.I("dve", "tensor_scalar", out=acc, in0=g_[:, c, 0:nb], scalar1=cw[:, c * 31:c * 31 + 1], scalar2=cb[:, c:c + 1], op0=ALU.mult, op1=ALU.add)
            for w in range(1, 31):
                P.I("dve", "scalar_tensor_tensor", out=acc, in0=g_[:, c, w:w + nb], scalar=cw[:, c * 31 + w:c * 31 + w + 1], in1=acc, op0=ALU.mult, op1=ALU.add)
            P.act(zb[bi % 2][:, c, 0:nb], acc, AF.Copy)
        if CSTOP == 1:
            continue
        for i in range(nb // 128):
            k2 = kt % 2
            kt += 1
            tok0 = (b0 if kind == "x" else n_own) + i * 128
            pz = PSZ[k2]
            for c in range(4):
                P.tr(pz[:, c * 128:(c + 1) * 128], zb[bi % 2][:, c, i * 128:(i + 1) * 128], C["idb"][:, :])
            P.act(zs[k2][:, :], pz[:, :], AF.Copy)
            if CSTOP == 2:
                continue
            s = st_[k2]
            P.I("dve", "tensor_reduce", out=s[:, 0:1], in_=zs[k2][:, :], axis=AX.X, op=ALU.add)
            P.act(zsq[:, :], zs[k2][:, :], AF.Square)
            P.I("dve", "tensor_reduce", out=s[:, 1:2], in_=zsq[:, :], axis=AX.X, op=ALU.add)
            P.I("dve", "tensor_scalar", out=s[:, 2:3], in0=s[:, 0:1], scalar1=1.0 / 512, scalar2=None, op0=ALU.mult)
            P.I("dve", "tensor_tensor", out=s[:, 3:4], in0=s[:, 2:3], in1=s[:, 2:3], op=ALU.mult)
            P.I("dve", "scalar_tensor_tensor", out=s[:, 4:5], in0=s[:, 1:2], scalar=1.0 / 512, in1=s[:, 3:4], op0=ALU.mult, op1=ALU.subtract)
            P.I("dve", "tensor_scalar", out=s[:, 5:6], in0=s[:, 4:5], scalar1=EPS, scalar2=None, op0=ALU.add)
            P.act(s[:, 6:7], s[:, 5:6], AF.Sqrt)
            P.I("dve", "reciprocal", out=s[:, 7:8], in_=s[:, 6:7])
            if CSTOP == 3:
                continue
            P.I("dve", "tensor_scalar", out=zn[k2][:, :], in0=zs[k2][:, :], scalar1=s[:, 2:3], scalar2=s[:, 7:8], op0=ALU.subtract, op1=ALU.mult)
            P.I("pool", "tensor_tensor", out=zl[k2][:, :], in0=zn[k2][:, :], in1=lgb[:, 0:512], op=ALU.mult)
            P.I("pool", "tensor_tensor", out=zl[k2][:, :], in0=zl[k2][:, :], in1=lgb[:, 512:1024], op=ALU.add)
            P.act(zsl[k2][:, :], zl[k2][:, :], AF.Silu)
            P.dma("sp", cg[k2][:, :], inp["cgs"][tok0:tok0 + 128, :])
            mt = mixs[k2]
            P.dma("sp", mt[:, 512:1024], inp["yd"][tok0:tok0 + 128, :])
            P.I("pool", "tensor_tensor", out=mt[:, 0:512], in0=zsl[k2][:, :], in1=cg[k2][:, :], op=ALU.mult)
            if CSTOP == 4:
                continue
            if kind == "x":
                post_tile(P, C, KP, None, inp["x"][tok0:tok0 + 128, :], xo_d[tok0:tok0 + 128, :], 0, PST, PSY, mix_in_sbuf=mt)
            else:
                r0 = i * 128
                post_tile(P, C, KP, None, inp["ctx"][r0:r0 + 128, :], co_d[r0:r0 + 128, :], 1, PST, PSY, mix_in_sbuf=mt)
    if not with_ctx:
        for i in range(n_ctx // 128):
            xt = KP.xt[i % 2]
            P.dma("sp", xt[:, :], inp["ctx"][i * 128:(i + 1) * 128, :])
            P.dma("sp", co_d[i * 128:(i + 1) * 128, :], xt[:, :])
    P.finish()
    return nc, P


GRID_W = 64
import math
def rope_tables(tok0, n):
    t = np.arange(tok0, tok0 + n)
    row = (t // GRID_W).astype(np.float32); col = (t % GRID_W).astype(np.float32)
    inv = (10000.0 ** (-np.arange(16, dtype=np.float32) / 16)).astype(np.float32)
    cosT = np.zeros((128, n), np.float32); sinT = np.zeros((128, n), np.float32)
    for r in range(128):
        d = r % 64
        pos = row if d < 32 else col
        ang = (pos * inv[d % 16]).astype(np.float32)
        cosT[r] = np.cos(ang); sinT[r] = np.sin(ang)
    return cosT, sinT
def rope_rmat():
    R = np.zeros((128, 128), np.float32)
    for i in range(128):
        d = i % 32
        if d < 16:
            R[i + 16, i] = -1.0
        else:
            R[i - 16, i] = 1.0
    return R
def ccols(c_b, c_ctx):
    return np.concatenate([c_b.reshape(8, 128).T, c_ctx.reshape(8, 128).T], axis=1).astype(np.float32).copy()

POOL_WINDOWS = (2, 4, 8, 16)
def pool_bands(L, tile0, is_first, is_last):
    out = np.zeros((4, 144, 128), np.float32)
    for g, w in enumerate(POOL_WINDOWS):
        for c in range(128):
            t = tile0 + c
            lo = max(t - w // 2, 0); hi = min(t + w // 2 - 1, L - 1) + 1
            cnt = hi - lo
            for s in range(lo, hi):
                r = s - (tile0 - 8)
                out[g, r, c] += 1.0 / cnt
            out[g, c + 8, c] -= 1.0
    return out
def band_pack(L_lat, own0, n_own, L_ctx):
    types = [pool_bands(L_lat, own0, True, False), pool_bands(L_lat, own0 + 128 if n_own > 256 else own0 + 128, False, False),
             pool_bands(L_lat, own0 + n_own - 128, False, True), pool_bands(L_ctx, 0, True, False), pool_bands(L_ctx, L_ctx - 128, False, True)]
    A = np.zeros((128, 2560), np.float32); B = np.zeros((16, 2560), np.float32)
    for t, bm in enumerate(types):
        for g in range(4):
            A[:, (t * 4 + g) * 128:(t * 4 + g + 1) * 128] = bm[g, :128]
            B[:, (t * 4 + g) * 128:(t * 4 + g + 1) * 128] = bm[g, 128:]
    return A, B
def halo(u_seq, a, n):
    L = u_seq.shape[0]
    out = np.zeros((n + 16, u_seq.shape[1]), u_seq.dtype)
    lo = max(a - 8, 0); hi = min(a + n + 8, L)
    out[lo - (a - 8): hi - (a - 8)] = u_seq[lo:hi]
    return out


def halo15_T(seqT, a, n):
    L = seqT.shape[2]
    out = np.zeros((4, 128, n + 30), seqT.dtype)
    lo = max(a - 15, 0)
    hi = min(a + n + 15, L)
    out[:, :, lo - (a - 15):hi - (a - 15)] = seqT[:, :, lo:hi]
    return out


_PROGS = {}
_DEBUG_HOOK = None


def _prog(key, fn):
    if key not in _PROGS:
        _PROGS[key] = fn()[0]
    return _PROGS[key]


def _run(nc, in_maps):
    res = run_bass_kernel_spmd(nc, in_maps, core_ids=list(range(8)))
    return res.results


def kernel(x, c, ctx, c_ctx, ada_w, ada_b, norm_pre, norm_post, w_in_even, w_out_even,
           pool_w, pool_scale, diff_lambda, diff_subln, w_in_odd, w_out_odd,
           conv_w, conv_b, conv_ln_g, conv_ln_b, hgrn_norm, hgrn_lb):
    f32 = lambda a: np.ascontiguousarray(np.asarray(a, dtype=np.float32))
    x, c, ctx, c_ctx, ada_w, ada_b = f32(x), f32(c), f32(ctx), f32(c_ctx), f32(ada_w), f32(ada_b)
    norm_pre, norm_post = f32(norm_pre), f32(norm_post)
    NB, L = x.shape[0], x.shape[1]
    S4 = 4
    ident = np.eye(128, dtype=np.float32)
    rmat = rope_rmat()
    xs = [np.ascontiguousarray(x[r // 4, (r % 4) * NOWN:(r % 4 + 1) * NOWN]) for r in range(8)]
    cs = [np.ascontiguousarray(ctx[b]) for b in range(NB)]
    cc = [ccols(c[b], c_ctx) for b in range(NB)]
    ropes = [rope_tables((r % 4) * NOWN, NOWN) for r in range(4)]
    bands = [band_pack(L, s * NOWN, NOWN, NCTX) for s in range(4)]
    masks = np.zeros((64, 128), np.float32)
    s_, t_ = np.meshgrid(np.arange(64), np.arange(64), indexing="ij")
    masks[:, :64] = (t_ >= s_)
    masks[:, 64:] = (s_ >= t_)
    cmask = np.ones((128, 2048), np.float32)
    cmask[:, ::64] = 0
    lbraw = np.ascontiguousarray(f32(hgrn_lb).reshape(2, 4, 4, 128).transpose(3, 0, 1, 2).reshape(128, 32))
    for l in range(4):
        j = l // 2
        aw_pre = np.ascontiguousarray(ada_w[l][:, :2048])
        ab_pre = np.ascontiguousarray(ada_b[l][None, :2048])
        aw_post = np.ascontiguousarray(ada_w[l][:, 2048:])
        ab_post = np.ascontiguousarray(ada_b[l][None, 2048:])
        npre = np.ascontiguousarray(norm_pre[l][None])
        npost = np.ascontiguousarray(norm_post[l][None])
        if l % 2 == 0:
            w_in = f32(w_in_even[j])
            ncA = _prog("A_even", build_A_even)
            ims = [dict(x=xs[r], ctx=cs[r // 4], ccols=cc[r // 4], ada_w=aw_pre, ada_b=ab_pre, norm_pre=npre, w_in=w_in,
                        ident=ident, rmat=rmat, cosT=ropes[r % 4][0], sinT=ropes[r % 4][1]) for r in range(8)]
            ra = _run(ncA, ims)
            ncB = _prog("B_even", build_B_even)
            lam_init = 0.8 - 0.6 * math.exp(-0.3 * l)
            pw = np.ascontiguousarray(f32(pool_w[j]).transpose(1, 0, 2).reshape(64, 256))
            ims = []
            for r in range(8):
                b, s = r // 4, r % 4
                grp = [ra[b * 4 + q] for q in range(4)]
                tm = ra[r]["tm"]
                kT = np.concatenate([g["fm"][6:12, :, :NOWN] for g in grp] + [ra[r]["fm"][6:12, :, NOWN:]], axis=2)
                vall = np.concatenate([g["tm"][:NOWN, 512:1280] for g in grp] + [tm[NOWN:, 512:1280]], axis=0)
                useq = np.concatenate([g["tm"][:NOWN, 0:256] for g in grp], axis=0)
                ims.append(dict(
                    x=xs[r], ctx=cs[b], ccols=cc[b], ada_w=aw_post, ada_b=ab_post, norm_post=npost, w_out=f32(w_out_even[j]), ident=ident,
                    qT=np.ascontiguousarray(ra[r]["fm"][0:6]), kT=np.ascontiguousarray(kT), vall=np.ascontiguousarray(vall),
                    gates=np.ascontiguousarray(np.concatenate([tm[:, 256:512], tm[:, 1280:2048]], axis=1)),
                    uh=halo(useq, s * NOWN, NOWN), uch=halo(np.ascontiguousarray(tm[NOWN:, 0:256]), 0, NCTX),
                    bandA=bands[s][0], bandB=bands[s][1], pool_w=pw, pool_scale=f32(pool_scale[j])[None].copy(),
                    lamp=f32(diff_lambda[j]).reshape(1, 256).copy(), subln=f32(diff_subln[j])[None].copy(),
                    lamc=np.array([[lam_init, 1.0 - lam_init]], np.float32)))
            rb = _run(ncB, ims)
            del ra
        else:
            ncA = _prog(("A_odd", l), lambda: build_A_odd(l))
            ims = [dict(x=xs[r], ctx=cs[r // 4], ccols=cc[r // 4], ada_w=aw_pre, ada_b=ab_pre, norm_pre=npre, w_in=f32(w_in_odd[j]),
                        ident=ident, lbraw=lbraw) for r in range(8)]
            ra = _run(ncA, ims)
            ncH = _prog("H", build_H)
            ims = []
            for r in range(8):
                b, hh = r // 4, r % 4
                grp = [ra[b * 4 + q] for q in range(4)]
                cat_f = lambda blk: np.ascontiguousarray(np.concatenate([grp[0]["fmf"][blk][:, NOWN:]] + [g["fmf"][blk][:, :NOWN] for g in grp], axis=1))
                cat_t = lambda c0: np.ascontiguousarray(np.concatenate([grp[0]["tm"][NOWN:, c0:c0 + 128]] + [g["tm"][:NOWN, c0:c0 + 128] for g in grp], axis=0))
                ims.append(dict(qT=cat_f(hh), fTf=cat_f(4 + hh), fTb=cat_f(8 + hh), v=cat_t(1024 + hh * 128), ogs=cat_t(512 + hh * 128),
                                gn=f32(hgrn_norm[j])[None, hh * 128:(hh + 1) * 128].copy(), ident=ident, masks=masks, cmask=cmask))
            rh = _run(ncH, ims)
            with_ctx = l < 3
            ncC = _prog(("C_odd", with_ctx), lambda: build_C_odd(NOWN, NCTX, with_ctx))
            cw = f32(conv_w[j])
            convw = np.ascontiguousarray(cw.T.reshape(4, 128, 31).transpose(1, 0, 2).reshape(128, 124))
            convb = np.ascontiguousarray(f32(conv_b[j]).reshape(4, 128).T)
            ims = []
            for r in range(8):
                b, s = r // 4, r % 4
                grp = [ra[b * 4 + q] for q in range(4)]
                gseq = np.concatenate([g["fmb"][:, :, :NOWN] for g in grp], axis=2)
                gctx = np.ascontiguousarray(ra[r]["fmb"][:, :, NOWN:])
                yd = np.concatenate([np.concatenate([rh[b * 4 + hh]["yd"][NCTX + s * NOWN: NCTX + (s + 1) * NOWN] for hh in range(4)], axis=1),
                                     np.concatenate([rh[b * 4 + hh]["yd"][0:NCTX] for hh in range(4)], axis=1)], axis=0)
                ims.append(dict(
                    x=xs[r], ctx=cs[b], ccols=cc[b], ada_w=aw_post, ada_b=ab_post, norm_post=npost, w_out=f32(w_out_odd[j]), ident=ident,
                    gluh=halo15_T(gseq, s * NOWN, NOWN), gluch=halo15_T(gctx, 0, NCTX),
                    cgs=np.ascontiguousarray(ra[r]["tm"][:, 0:512]), yd=np.ascontiguousarray(yd), convw=convw, convb=convb,
                    lng=f32(conv_ln_g[j])[None].copy(), lnb=f32(conv_ln_b[j])[None].copy()))
            rb = _run(ncC, ims)
            del ra, rh
        xs = [np.ascontiguousarray(rb[r]["xo"]) for r in range(8)]
        cs = [np.ascontiguousarray(rb[b * 4]["co"]) for b in range(NB)]
        if _DEBUG_HOOK is not None:
            _DEBUG_HOOK(l, xs, cs)
    out = np.stack([np.concatenate(xs[b * 4:(b + 1) * 4], axis=0) for b in range(NB)], axis=0)
    return out.astype(np.float32)
```

```python
import contextlib
import numpy as np
import concourse.bass as bass
import concourse.mybir as mybir
from concourse.bass_utils import run_bass_kernel_spmd

F32 = mybir.dt.float32
BF16 = mybir.dt.bfloat16
ALU = mybir.AluOpType
AF = mybir.ActivationFunctionType
AX = mybir.AxisListType


class Buf:
    __slots__ = ("name", "last_w", "readers", "excl")

    def __init__(self, name):
        self.name = name
        self.last_w = None
        self.readers = []
        self.excl = False


class V:
    __slots__ = ("ap", "bufs")

    def __init__(self, ap, bufs):
        self.ap = ap
        self.bufs = bufs


class T:
    def __init__(self, h, name, nslot=1, sdim=1):
        self.h = h
        self.name = name
        self.nslot = nslot
        self.sdim = sdim
        self.bufs = [Buf(f"{name}.{i}") for i in range(nslot)]
        self.shape = list(h.shape)

    def __getitem__(self, idx):
        ap = self.h[idx]
        if self.nslot == 1:
            return V(ap, self.bufs)
        if not isinstance(idx, tuple):
            idx = (idx,)
        bufs = self.bufs
        if len(idx) > self.sdim:
            s = idx[self.sdim]
            n = self.shape[self.sdim]
            per = n // self.nslot
            if isinstance(s, int):
                bufs = [self.bufs[s // per]]
            elif isinstance(s, slice):
                a = 0 if s.start is None else s.start
                b = n if s.stop is None else s.stop
                bufs = self.bufs[a // per:(b - 1) // per + 1]
        return V(ap, bufs)

    def v(self, ap):
        return V(ap, self.bufs)


class Op:
    __slots__ = ("eng", "fn", "waits", "sem", "val", "is_dma", "inc")


ENGS = ("pe", "act", "dve", "pool", "sp")


class Prog:
    N_DMA_SEM = 8

    def __init__(self, nc):
        self.nc = nc
        self.es = contextlib.ExitStack()
        self.ops = {e: [] for e in ENGS}
        self.cnt = {e: 0 for e in ENGS}
        self.ndma = {e: 0 for e in ENGS}
        self.sem = {}
        self.dsem = {}
        self.waited = {e: {} for e in ENGS}
        self.semvals = {}
        for e in ENGS:
            self.sem[e] = self.es.enter_context(nc.semaphore(f"s_{e}"))
        for e in ("sp", "pool", "act"):
            self.dsem[e] = [self.es.enter_context(nc.semaphore(f"d_{e}{i}")) for i in range(self.N_DMA_SEM)]
        self.n_ops = 0

    def sbuf(self, name, shape, dt, nslot=1, sdim=1):
        h = self.es.enter_context(self.nc.sbuf_tensor("sb_" + name, list(shape), dt))
        return T(h, name, nslot, sdim)

    def psum(self, name, shape, dt, nslot=1, sdim=1):
        h = self.es.enter_context(self.nc.psum_tensor("ps_" + name, list(shape), dt))
        t = T(h, name, nslot, sdim)
        for b in t.bufs:
            b.excl = True
        return t

    def dram(self, name, shape, dt, kind="Internal", nslot=1, sdim=0):
        h = self.nc.dram_tensor(name, list(shape), dt, kind=kind)
        return T(h, name, nslot, sdim)

    def _deps(self, eng, reads, writes):
        deps = []
        for b in reads:
            if b.last_w is not None:
                deps.append((b.last_w, "raw"))
            if b.excl:
                for r in b.readers:
                    if r.eng != eng:
                        deps.append((r, "rar"))
        for b in writes:
            if b.last_w is not None:
                deps.append((b.last_w, "waw"))
            for r in b.readers:
                deps.append((r, "war"))
        need = {}
        for d, kind in deps:
            if not d.is_dma and d.eng == eng:
                if eng in ("pe", "sp"):
                    continue
                if kind in ("war", "rar"):
                    continue
            k = id(d.sem)
            if k not in need or need[k][1] < d.val:
                need[k] = (d.sem, d.val)
        out = []
        w = self.waited[eng]
        for k, (s, v) in need.items():
            if w.get(k, 0) >= v:
                continue
            w[k] = v
            out.append((s, v))
        return out

    def _record(self, eng, fn, reads, writes, is_dma=False):
        op = Op()
        op.eng = eng
        op.fn = fn
        op.is_dma = is_dma
        op.waits = self._deps(eng, reads, writes)
        if is_dma:
            i = self.ndma[eng]
            self.ndma[eng] += 1
            R = self.N_DMA_SEM
            op.sem = self.dsem[eng][i % R]
            op.val = 16 * (i // R + 1)
            op.inc = 16
            if i >= R:
                k = id(op.sem)
                pv = 16 * (i // R)
                if self.waited[eng].get(k, 0) < pv:
                    self.waited[eng][k] = pv
                    op.waits.append((op.sem, pv))
        else:
            self.cnt[eng] += 1
            op.sem = self.sem[eng]
            op.val = self.cnt[eng]
            op.inc = 1
        self.semvals[id(op.sem)] = (op.sem, op.val)
        for b in reads:
            b.readers.append(op)
        for b in writes:
            b.last_w = op
            b.readers = []
        self.ops[eng].append(op)
        self.n_ops += 1
        return op

    def I(self, eng, meth, *, extra_reads=(), extra_writes=(), dma_like=False, **kw):
        reads, writes = [], []
        kws = {}
        for k, a in kw.items():
            if isinstance(a, V):
                if k.startswith("out") or k in ("accum_out", "ap"):
                    writes += a.bufs
                else:
                    reads += a.bufs
                kws[k] = a.ap
            else:
                kws[k] = a
        for a in extra_reads:
            reads += a.bufs
        for a in extra_writes:
            writes += a.bufs
        is_dma = meth == "dma_start" or dma_like

        def fn(e, meth=meth, kws=kws):
            return getattr(e, meth)(**kws)
        return self._record(eng, fn, reads, writes, is_dma)

    def dma(self, eng, out, in_, **kw):
        o = out if isinstance(out, V) else V(out, [])
        i = in_ if isinstance(in_, V) else V(in_, [])
        return self.I(eng, "dma_start", out=o, in_=i, **kw)

    def mm(self, out, lhsT, rhs, start=True, stop=True, acc_read=False, **kw):
        return self.I("pe", "matmul", out=out, lhsT=lhsT, rhs=rhs, start=start, stop=stop, **kw)

    def tr(self, out, in_, ident):
        return self.I("pe", "transpose", out=out, in_=in_, identity=ident)

    def act(self, out, in_, func, eng="act", **kw):
        return self.I(eng, "activation", out=out, in_=in_, func=func, **kw)

    def finish(self):
        nc = self.nc
        fin = []
        for k, (s, v) in self.semvals.items():
            if self.waited["sp"].get(k, 0) < v:
                fin.append((s, v))
        with nc.Block() as block:
            def emit(eng_obj, name):
                for op in self.ops[name]:
                    for (s, v) in op.waits:
                        eng_obj.wait_ge(s, v)
                    ins = op.fn(eng_obj)
                    ins.then_inc(op.sem, op.inc)
                if name == "sp":
                    for (s, v) in fin:
                        eng_obj.wait_ge(s, v)

            if self.ops["sp"] or True:
                @block.sync
                def _(e):
                    emit(e, "sp")
            if self.ops["pe"]:
                @block.tensor
                def _(e):
                    emit(e, "pe")
            if self.ops["act"]:
                @block.scalar
                def _(e):
                    emit(e, "act")
            if self.ops["dve"]:
                @block.vector
                def _(e):
                    emit(e, "dve")
            if self.ops["pool"]:
                @block.gpsimd
                def _(e):
                    emit(e, "pool")
        self.es.close()


D = 1024
NOWN = 4096
NCTX = 256
NT = NOWN + NCTX
EPS = 1e-6
import ml_dtypes
NPBF = ml_dtypes.bfloat16


def mk_consts(P, inp):
    C = {}
    id32 = P.sbuf("id32", [128, 128], F32)
    P.dma("sp", id32[:, :], inp["ident"])
    idb = P.sbuf("idb", [128, 128], BF16)
    P.I("dve", "tensor_copy", out=idb[:, :], in_=id32[:, :])
    ones = P.sbuf("ones", [1, 128], F32)
    P.I("dve", "memset", ap=ones[:, :], constant=1.0)
    C["id32"], C["idb"], C["ones"] = id32, idb, ones
    return C


def load_w_bf16(P, w_dram, ncols, name, stage, engs=("dve", "pool")):
    wb = P.sbuf(name, [128, 8, ncols], BF16)
    wv = w_dram.rearrange("(j p) c -> p j c", p=128)
    nb = (ncols + 511) // 512
    for b in range(nb):
        c0, c1 = b * 512, min(ncols, (b + 1) * 512)
        for jh in range(2):
            st = stage[(2 * b + jh) % len(stage)]
            P.dma("sp", st[:, :, 0:c1 - c0], wv[:, jh * 4:jh * 4 + 4, c0:c1])
            P.I(engs[(2 * b + jh) % len(engs)], "tensor_copy", out=wb[:, jh * 4:jh * 4 + 4, c0:c1], in_=st[:, :, 0:c1 - c0])
    return wb


def ada_bcast(P, C, inp, nblk, modes, dests, stage, psA, gkey):
    ccol = P.sbuf("ccol", [128, 16], F32)
    P.dma("sp", ccol[:, :], inp["ccols"])
    csig = P.sbuf("csig", [128, 16], F32)
    P.act(csig[:, :], ccol[:, :], AF.Sigmoid)
    cact = P.sbuf("cact", [128, 16], BF16)
    P.I("dve", "tensor_tensor", out=cact[:, :], in0=ccol[:, :], in1=csig[:, :], op=ALU.mult)
    wv = inp["ada_w"].rearrange("(j p) c -> p j c", p=128)
    wbb = P.sbuf("adawb", [128, 8, 512], BF16)
    brow = [P.sbuf(f"brow{i}", [1, 512], F32) for i in range(2)]
    grow = [P.sbuf(f"grow{i}", [1, 512], F32) for i in range(2)]
    row = [P.sbuf(f"arow{i}", [1, 512], F32) for i in range(2)]
    row2 = [P.sbuf(f"arow2{i}", [1, 512], F32) for i in range(2)]
    k = 0
    for b in range(nblk):
        for jh in range(2):
            st = stage[(2 * b + jh) % len(stage)]
            P.dma("sp", st[:, :, :], wv[:, jh * 4:jh * 4 + 4, b * 512:(b + 1) * 512])
            P.I("dve", "tensor_copy", out=wbb[:, jh * 4:jh * 4 + 4, :], in_=st[:, :, :])
        P.dma("sp", brow[b % 2][:, :], inp["ada_b"][:, b * 512:(b + 1) * 512])
        mode = modes[b]
        if mode != "plain":
            gc = (b * 512) % D
            P.dma("sp", grow[b % 2][:, :], inp[gkey][:, gc:gc + 512])
        for v in range(2):
            for j in range(8):
                P.mm(psA[0:1, :], lhsT=cact[:, v * 8 + j:v * 8 + j + 1], rhs=wbb[:, j, :], start=(j == 0), stop=(j == 7))
            r = row[k % 2]
            P.I("dve", "tensor_tensor", out=r[:, :], in0=psA[0:1, :], in1=brow[b % 2][:, :], op=ALU.add)
            if mode == "onep_g":
                r2 = row2[k % 2]
                P.I("dve", "scalar_tensor_tensor", out=r2[:, :], in0=r[:, :], scalar=1.0, in1=grow[b % 2][:, :], op0=ALU.add, op1=ALU.mult)
                r = r2
            elif mode == "g":
                r2 = row2[k % 2]
                P.I("dve", "tensor_tensor", out=r2[:, :], in0=r[:, :], in1=grow[b % 2][:, :], op=ALU.mult)
                r = r2
            k += 1
            dt_, dc = dests[b][v]
            P.mm(psA[:, :], lhsT=C["ones"][:, :], rhs=r[0:1, :])
            P.I("dve", "tensor_copy", out=dt_[:, dc:dc + 512], in_=psA[:, :])


def pre_norm_rows(P, C, inp, _unused, stage, psA):
    A = [P.sbuf(f"Abc{v}", [128, 1024], F32) for v in range(2)]
    B = [P.sbuf(f"Bbc{v}", [128, 1024], F32) for v in range(2)]
    dests = [[(B[0], 0), (B[1], 0)], [(B[0], 512), (B[1], 512)], [(A[0], 0), (A[1], 0)], [(A[0], 512), (A[1], 512)]]
    ada_bcast(P, C, inp, 4, ["plain", "plain", "onep_g", "onep_g"], dests, stage, psA, "norm_pre")
    return [A[0], B[0], A[1], B[1]]


class PreCtx:
    pass


def emit_pre_group(P, C, K, x_view_fn, ntile, AB, gi):
    hT = K.hT[gi % 2]
    for i in range(ntile):
        k = K.cnt
        K.cnt += 1
        xt = K.xt[k % 2]
        P.dma("sp", xt[:, :], x_view_fn(i))
        sq = K.sq
        P.act(sq[:, :], xt[:, :], AF.Square)
        ssq = K.ssq[k % 2]
        P.I("dve", "tensor_reduce", out=ssq[:, 0:1], in_=sq[:, :], axis=AX.X, op=ALU.add)
        P.I("dve", "tensor_scalar", out=ssq[:, 1:2], in0=ssq[:, 0:1], scalar1=1.0 / D, scalar2=EPS, op0=ALU.mult, op1=ALU.add)
        P.act(ssq[:, 3:4], ssq[:, 1:2], AF.Sqrt)
        P.I("dve", "reciprocal", out=ssq[:, 2:3], in_=ssq[:, 3:4])
        t32 = K.t32
        P.I("dve", "scalar_tensor_tensor", out=t32[:, :], in0=xt[:, :], scalar=ssq[:, 2:3], in1=AB[0][:, :], op0=ALU.mult, op1=ALU.mult)
        hb = K.hb[k % 2]
        P.I("pool", "tensor_tensor", out=hb[:, :], in0=t32[:, :], in1=AB[1][:, :], op=ALU.add)
        ptr = K.ptr[k % 2]
        for j in range(8):
            P.tr(ptr[:, j * 128:(j + 1) * 128], hb[:, j * 128:(j + 1) * 128], C["idb"][:, :])
        P.act(hT.v(hT.h[:, :, i * 128:(i + 1) * 128]), ptr.v(ptr.h[:, :].rearrange("p (j t) -> p j t", j=8)), AF.Copy)
    return hT


def alloc_pre(P):
    K = PreCtx()
    K.cnt = 0
    K.xt = [P.sbuf(f"xt{i}", [128, D], F32) for i in range(2)]
    K.sq = P.sbuf("sq", [128, D], BF16)
    K.ssq = [P.sbuf(f"ssq{i}", [128, 4], F32) for i in range(2)]
    K.t32 = P.sbuf("t32", [128, D], F32)
    K.hb = [P.sbuf(f"hb{i}", [128, D], BF16) for i in range(2)]
    K.hT = [P.sbuf(f"hT{i}", [128, 8, 512], BF16) for i in range(2)]
    K.ptr = [P.psum(f"ptr{i}", [128, D], BF16) for i in range(2)]
    return K


def dram_in(nc, name, shape, dt=F32):
    return nc.dram_tensor(name, list(shape), dt, kind="ExternalInput").ap()


def dram_out(nc, name, shape, dt):
    return nc.dram_tensor(name, list(shape), dt, kind="ExternalOutput").ap()


STOP = 99


def build_A_even(n_own=NOWN, n_ctx=NCTX):
    nt = n_own + n_ctx
    nc = bass.Bass("TRN2", target_bir_lowering=False)
    P = Prog(nc)
    inp = dict(
        x=dram_in(nc, "x", [n_own, D]), ctx=dram_in(nc, "ctx", [n_ctx, D]), ccols=dram_in(nc, "ccols", [128, 16]),
        ada_w=dram_in(nc, "ada_w", [D, 2048]), ada_b=dram_in(nc, "ada_b", [1, 2048]), norm_pre=dram_in(nc, "norm_pre", [1, D]),
        w_in=dram_in(nc, "w_in", [D, 3584]), ident=dram_in(nc, "ident", [128, 128]), rmat=dram_in(nc, "rmat", [128, 128]),
        cosT=dram_in(nc, "cosT", [128, n_own]), sinT=dram_in(nc, "sinT", [128, n_own]))
    tm_d = dram_out(nc, "tm", [nt, 2048], BF16)
    fm_d = dram_out(nc, "fm", [12, 128, nt], BF16)
    C = mk_consts(P, inp)
    stage = [P.sbuf(f"stage{i}", [128, 4, 512], F32) for i in range(2)]
    psA = P.psum("psA", [128, 512], F32)
    if STOP == 0:
        P.finish()
        return nc, P
    AB = pre_norm_rows(P, C, inp, None, stage, psA)
    if STOP == 1:
        P.finish()
        return nc, P
    wb = load_w_bf16(P, inp["w_in"], 3584, "wb", stage)
    rm32 = P.sbuf("rm32", [128, 128], F32)
    P.dma("sp", rm32[:, :], inp["rmat"])
    rmb = P.sbuf("rmb", [128, 128], BF16)
    P.I("dve", "tensor_copy", out=rmb[:, :], in_=rm32[:, :])
    if STOP == 2:
        P.finish()
        return nc, P
    K = alloc_pre(P)
    pst = [P.psum(f"pst{i}", [128, 512], F32) for i in range(2)]
    psf = [P.psum(f"psf{i}", [128, 512], F32) for i in range(2)]
    psr = P.psum("psr", [128, 512], F32)
    tmt = [P.sbuf(f"tmt{i}", [128, 2048], BF16) for i in range(2)]
    cs = [P.sbuf(f"cs{i}", [128, 512], F32) for i in range(2)]
    sn = [P.sbuf(f"sn{i}", [128, 512], F32) for i in range(2)]
    qraw = [P.sbuf(f"qraw{i}", [128, 512], BF16) for i in range(2)]
    t1 = [P.sbuf(f"t1{i}", [128, 512], F32) for i in range(2)]
    t2 = [P.sbuf(f"t2{i}", [128, 512], F32) for i in range(2)]
    fmo = [P.sbuf(f"fmo{i}", [128, 512], BF16) for i in range(2)]
    TMB = [(0, 512, [(0, 0, 256, False), (256, 256, 256, True)]),
           (2048, 512, [(0, 512, 512, False)]),
           (2560, 512, [(0, 1024, 256, False), (256, 1280, 256, True)]),
           (3072, 512, [(0, 1536, 512, True)])]
    groups = [(g * 512, 4, False) for g in range(n_own // 512)] + [(n_own, n_ctx // 128, True)]
    kt = 0
    kf = 0
    for gi, (tok0, ntile, is_ctx) in enumerate(groups):
        n = ntile * 128
        if is_ctx:
            xfn = lambda i: inp["ctx"][i * 128:(i + 1) * 128, :]
            ab = AB[2:4]
        else:
            xfn = lambda i, tok0=tok0: inp["x"][tok0 + i * 128: tok0 + (i + 1) * 128, :]
            ab = AB[0:2]
            P.dma("sp", cs[gi % 2][:, 0:n], inp["cosT"][:, tok0:tok0 + n])
            P.dma("sp", sn[gi % 2][:, 0:n], inp["sinT"][:, tok0:tok0 + n])
        hT = emit_pre_group(P, C, K, xfn, ntile, ab, gi)
        if STOP == 3:
            break
        for i in range(ntile):
            tt = tmt[(gi * 4 + i) % 2]
            for (wc0, wn, eps_) in TMB:
                ps = pst[kt % 2]
                kt += 1
                for j in range(8):
                    P.mm(ps[:, 0:wn], lhsT=hT[:, j, i * 128:(i + 1) * 128], rhs=wb[:, j, wc0:wc0 + wn], start=(j == 0), stop=(j == 7))
                for (pc, dc, nn, silu) in eps_:
                    if silu:
                        P.act(tt[:, dc:dc + nn], ps[:, pc:pc + nn], AF.Silu)
                    else:
                        P.I("dve", "tensor_copy", out=tt[:, dc:dc + nn], in_=ps[:, pc:pc + nn])
            P.dma("sp", tm_d[tok0 + i * 128: tok0 + (i + 1) * 128, :], tt[:, :])
        if STOP == 4:
            break
        for blk in range(12):
            wc0 = 512 + blk * 128
            ps = psf[kf % 2]
            for j in range(8):
                P.mm(ps[:, 0:n], lhsT=wb[:, j, wc0:wc0 + 128], rhs=hT[:, j, 0:n], start=(j == 0), stop=(j == 7))
            fo = fmo[kf % 2]
            if is_ctx or STOP == 7:
                P.act(fo[:, 0:n], ps[:, 0:n], AF.Copy)
            else:
                qr = qraw[kf % 2]
                P.act(qr[:, 0:n], ps[:, 0:n], AF.Copy)
                if STOP not in (8, 9):
                    P.mm(psr[:, 0:n], lhsT=rmb[:, :], rhs=qr[:, 0:n])
                P.I("dve", "tensor_tensor", out=t1[kf % 2][:, 0:n], in0=(qr if STOP == 9 else ps)[:, 0:n], in1=cs[gi % 2][:, 0:n], op=ALU.mult, extra_reads=([qr[:, 0:n]] if STOP == 10 else []))
                P.I("dve", "tensor_tensor", out=t2[kf % 2][:, 0:n], in0=(psr if STOP not in (8, 9) else (ps if STOP == 8 else qr))[:, 0:n], in1=sn[gi % 2][:, 0:n], op=ALU.mult)
                P.I("dve" if STOP == 5 else "pool", "tensor_tensor", out=fo[:, 0:n], in0=t1[kf % 2][:, 0:n], in1=t2[kf % 2][:, 0:n], op=ALU.add)
            if STOP != 6:
                P.dma("sp", fm_d[blk, :, tok0:tok0 + n], fo[:, 0:n])
            kf += 1
    P.finish()
    return nc, P


def post_setup(P, C, inp, stage, psM):
    G = [P.sbuf(f"Gbc{v}", [128, 1024], F32) for v in range(2)]
    dests = [[(G[0], 0), (G[1], 0)], [(G[0], 512), (G[1], 512)]]
    ada_bcast(P, C, inp, 2, ["g", "g"], dests, stage, psM, "norm_post")
    wo = load_w_bf16(P, inp["w_out"], 1024, "wo", stage)
    K = PreCtx()
    K.G = G
    K.wo = wo
    K.mixt = [P.sbuf(f"mixt{i}", [128, D], BF16) for i in range(2)]
    K.mixT = [P.sbuf(f"mixT{i}", [128, 8, 128], BF16) for i in range(2)]
    K.xt = [P.sbuf(f"pxt{i}", [128, D], F32) for i in range(2)]
    K.sq = P.sbuf("psq", [128, D], BF16)
    K.ssq = [P.sbuf(f"pssq{i}", [128, 4], F32) for i in range(2)]
    K.t32 = P.sbuf("pt32", [128, D], F32)
    K.xo = [P.sbuf(f"pxo{i}", [128, D], F32) for i in range(2)]
    K.cnt = 0
    return K


def post_tile(P, C, K, mix_src, x_src, x_dst, v, psT, psY, mix_in_sbuf=None):
    k = K.cnt
    K.cnt += 1
    if mix_in_sbuf is None:
        mt = K.mixt[k % 2]
        P.dma("sp", mt[:, :], mix_src)
    else:
        mt = mix_in_sbuf
    for j in range(8):
        P.tr(psT[:, j * 128:(j + 1) * 128], mt[:, j * 128:(j + 1) * 128], C["idb"][:, :])
    mT = K.mixT[k % 2]
    P.act(mT.v(mT.h[:, :, :]), psT.v(psT.h[:, :].rearrange("p (j t) -> p j t", j=8)), AF.Copy)
    for cb in range(2):
        for j in range(8):
            P.mm(psY[:, cb * 512:(cb + 1) * 512], lhsT=mT[:, j, :], rhs=K.wo[:, j, cb * 512:(cb + 1) * 512], start=(j == 0), stop=(j == 7))
    xt = K.xt[k % 2]
    P.dma("sp", xt[:, :], x_src)
    P.act(K.sq[:, :], psY[:, 0:1024], AF.Square)
    ssq = K.ssq[k % 2]
    P.I("dve", "tensor_reduce", out=ssq[:, 0:1], in_=K.sq[:, :], axis=AX.X, op=ALU.add)
    P.I("dve", "tensor_scalar", out=ssq[:, 1:2], in0=ssq[:, 0:1], scalar1=1.0 / D, scalar2=EPS, op0=ALU.mult, op1=ALU.add)
    P.act(ssq[:, 3:4], ssq[:, 1:2], AF.Sqrt)
    P.I("dve", "reciprocal", out=ssq[:, 2:3], in_=ssq[:, 3:4])
    P.I("dve", "scalar_tensor_tensor", out=K.t32[:, :], in0=psY[:, 0:1024], scalar=ssq[:, 2:3], in1=K.G[v][:, :], op0=ALU.mult, op1=ALU.mult)
    xo = K.xo[k % 2]
    P.I("pool", "tensor_tensor", out=xo[:, :], in0=K.t32[:, :], in1=xt[:, :], op=ALU.add)
    P.dma("sp", x_dst, xo[:, :])


def build_B_even(n_own=NOWN, n_lat=4 * NOWN, n_ctx=NCTX):
    nt = n_own + n_ctx
    nk = n_lat + n_ctx
    nkb = nk // 128
    nc = bass.Bass("TRN2", target_bir_lowering=False)
    P = Prog(nc)
    inp = dict(
        x=dram_in(nc, "x", [n_own, D]), ctx=dram_in(nc, "ctx", [n_ctx, D]), ccols=dram_in(nc, "ccols", [128, 16]),
        ada_w=dram_in(nc, "ada_w", [D, 1024]), ada_b=dram_in(nc, "ada_b", [1, 1024]), norm_post=dram_in(nc, "norm_post", [1, D]),
        w_out=dram_in(nc, "w_out", [D, D]), ident=dram_in(nc, "ident", [128, 128]),
        qT=dram_in(nc, "qT", [6, 128, nt], BF16), kT=dram_in(nc, "kT", [6, 128, nk], BF16), vall=dram_in(nc, "vall", [nk, 768], BF16),
        gates=dram_in(nc, "gates", [nt, 1024], BF16), uh=dram_in(nc, "uh", [n_own + 16, 256], BF16), uch=dram_in(nc, "uch", [n_ctx + 16, 256], BF16),
        bandA=dram_in(nc, "bandA", [128, 5 * 4 * 128]), bandB=dram_in(nc, "bandB", [16, 5 * 4 * 128]),
        pool_w=dram_in(nc, "pool_w", [64, 4 * 64]), pool_scale=dram_in(nc, "pool_scale", [1, 256]),
        lamp=dram_in(nc, "lamp", [1, 256]), subln=dram_in(nc, "subln", [1, 128]), lamc=dram_in(nc, "lamc", [1, 2]))
    xo_d = dram_out(nc, "xo", [n_own, D], F32)
    co_d = dram_out(nc, "co", [n_ctx, D], F32)
    mix_d = P.dram("mixd", [nt, D], BF16, nslot=nt // 128, sdim=0)
    C = mk_consts(P, inp)
    stage = [P.sbuf(f"stage{i}", [128, 4, 512], F32) for i in range(2)]
    PSA = P.psum("PSA", [128, 3 * 512], F32, nslot=3)
    PSO = P.psum("PSO", [128, 3 * 512], F32, nslot=3)
    PST = P.psum("PST", [128, D], BF16)
    PSM = P.psum("PSM", [128, 512], F32)
    KP = post_setup(P, C, inp, stage, PSM)

    lamp = P.sbuf("lamp", [1, 256], F32)
    P.dma("sp", lamp[:, :], inp["lamp"])
    lamc = P.sbuf("lamc", [1, 2], F32)
    P.dma("sp", lamc[:, :], inp["lamc"])
    lw = P.sbuf("lw", [1, 16], F32)
    lpp = P.sbuf("lpp", [1, 128], F32)
    P.I("dve", "tensor_tensor", out=lpp[:, 0:64], in0=lamp[:, 0:64], in1=lamp[:, 64:128], op=ALU.mult)
    P.I("dve", "tensor_tensor", out=lpp[:, 64:128], in0=lamp[:, 128:192], in1=lamp[:, 192:256], op=ALU.mult)
    P.I("dve", "tensor_reduce", out=lw[:, 0:1], in_=lpp[:, 0:64], axis=AX.X, op=ALU.add)
    P.I("dve", "tensor_reduce", out=lw[:, 1:2], in_=lpp[:, 64:128], axis=AX.X, op=ALU.add)
    P.act(lw[:, 2:4], lw[:, 0:2], AF.Exp)
    P.I("dve", "tensor_tensor", out=lw[:, 4:5], in0=lw[:, 2:3], in1=lw[:, 3:4], op=ALU.subtract)
    P.I("dve", "tensor_tensor", out=lw[:, 5:6], in0=lw[:, 4:5], in1=lamc[:, 0:1], op=ALU.add)
    P.I("dve", "tensor_scalar", out=lw[:, 6:7], in0=lw[:, 5:6], scalar1=-1.0, scalar2=None, op0=ALU.mult)
    neglam = P.sbuf("neglam", [128, 1], F32)
    P.mm(PSM[:, 0:1], lhsT=C["ones"][:, :], rhs=lw[0:1, 6:7])
    P.I("dve", "tensor_copy", out=neglam[:, :], in_=PSM[:, 0:1])
    sub = P.sbuf("sub", [1, 128], F32)
    P.dma("sp", sub[:, :], inp["subln"])
    sub2 = P.sbuf("sub2", [1, 128], F32)
    P.I("dve", "tensor_scalar", out=sub2[:, :], in0=sub[:, :], scalar1=lamc[:, 1:2], scalar2=None, op0=ALU.mult)
    subg = P.sbuf("subg", [128, 128], F32)
    P.mm(PSM[:, 0:128], lhsT=C["ones"][:, :], rhs=sub2[0:1, :])
    P.I("dve", "tensor_copy", out=subg[:, :], in_=PSM[:, 0:128])
    psc = P.sbuf("psc", [1, 256], F32)
    P.dma("sp", psc[:, :], inp["pool_scale"])
    pscb = P.sbuf("pscb", [128, 256], F32)
    P.mm(PSM[:, 0:256], lhsT=C["ones"][:, :], rhs=psc[0:1, :])
    P.I("dve", "tensor_copy", out=pscb[:, :], in_=PSM[:, 0:256])

    bA = P.sbuf("bA", [128, 2560], BF16)
    bB = P.sbuf("bB", [16, 2560], BF16)
    sflat = [st.h[:, :, :].rearrange("p a b -> p (a b)") for st in stage]
    P.dma("sp", stage[0].v(sflat[0][:, 0:2048]), inp["bandA"][:, 0:2048])
    P.I("dve", "tensor_copy", out=bA[:, 0:2048], in_=stage[0].v(sflat[0][:, 0:2048]))
    P.dma("sp", stage[1].v(sflat[1][:, 0:512]), inp["bandA"][:, 2048:2560])
    P.I("dve", "tensor_copy", out=bA[:, 2048:2560], in_=stage[1].v(sflat[1][:, 0:512]))
    P.dma("sp", stage[0].v(sflat[0][0:16, 0:2048]), inp["bandB"][:, 0:2048])
    P.I("dve", "tensor_copy", out=bB[:, 0:2048], in_=stage[0].v(sflat[0][0:16, 0:2048]))
    P.dma("sp", stage[1].v(sflat[1][0:16, 0:512]), inp["bandB"][:, 2048:2560])
    P.I("dve", "tensor_copy", out=bB[:, 2048:2560], in_=stage[1].v(sflat[1][0:16, 0:512]))
    pw32 = P.sbuf("pw32", [64, 256], F32)
    P.dma("sp", pw32[:, :], inp["pool_w"])
    pw = P.sbuf("pw", [64, 256], BF16)
    P.I("dve", "tensor_copy", out=pw[:, :], in_=pw32[:, :])
    uA = [P.sbuf(f"uA{i}", [128, 256], BF16) for i in range(2)]
    uB = [P.sbuf(f"uB{i}", [16, 256], BF16) for i in range(2)]
    dT = [P.sbuf(f"dT{i}", [64, 512], BF16) for i in range(2)]
    gat = [P.sbuf(f"gat{i}", [128, 256], BF16) for i in range(2)]
    ytmp = [P.sbuf(f"ytmp{i}", [128, 256], F32) for i in range(2)]
    ya = [P.sbuf(f"ya{i}", [128, 256], BF16) for i in range(2)]
    ntile_own = n_own // 128
    ntile_ctx = n_ctx // 128
    tiles = [("x", i) for i in range(ntile_own)] + [("c", i) for i in range(ntile_ctx)]
    for k, (kind, i) in enumerate(tiles):
        if kind == "x":
            src = inp["uh"]
            typ = 0 if i == 0 else (2 if i == ntile_own - 1 else 1)
            row0 = i * 128
            tok0 = i * 128
        else:
            src = inp["uch"]
            typ = 3 if i == 0 else 4
            row0 = i * 128
            tok0 = n_own + i * 128
        P.dma("sp", uA[k % 2][:, :], src[row0:row0 + 128, :])
        P.dma("sp", uB[k % 2][:, :], src[row0 + 128:row0 + 144, :])
        P.dma("sp", gat[k % 2][:, :], inp["gates"][tok0:tok0 + 128, 0:256])
        for g in range(4):
            bo = (typ * 4 + g) * 128
            P.mm(PSM[0:64, g * 128:(g + 1) * 128], lhsT=uA[k % 2][:, g * 64:(g + 1) * 64], rhs=bA[:, bo:bo + 128], start=True, stop=False)
            P.mm(PSM[0:64, g * 128:(g + 1) * 128], lhsT=uB[k % 2][:, g * 64:(g + 1) * 64], rhs=bB[:, bo:bo + 128], start=False, stop=True)
        P.act(dT[k % 2][:, :], PSM[0:64, :], AF.Copy)
        yps = PSA[:, 1024:1024 + 256]
        for g in range(4):
            P.mm(PSA[:, 1024 + g * 64:1024 + (g + 1) * 64], lhsT=dT[k % 2][:, g * 128:(g + 1) * 128], rhs=pw[:, g * 64:(g + 1) * 64])
        P.I("dve", "tensor_tensor", out=ytmp[k % 2][:, :], in0=yps, in1=pscb[:, :], op=ALU.mult)
        P.I("pool", "tensor_tensor", out=ya[k % 2][:, :], in0=ytmp[k % 2][:, :], in1=gat[k % 2][:, :], op=ALU.mult)
        P.dma("sp", mix_d[tok0:tok0 + 128, 0:256], ya[k % 2][:, :])

    kTs = P.sbuf("kTs", [128, nk], BF16)
    vau = P.sbuf("vau", [128, nkb, 128], BF16)
    qz = [[P.sbuf(f"qz{i}_{m}", [128, 512], BF16) for m in range(2)] for i in range(2)]
    for i in range(2):
        for m in range(2):
            P.I("pool", "memset", ap=qz[i][m][:, :], constant=0.0)
    pT = P.sbuf("pT", [128, 3 * 512], BF16, nslot=3)
    dacc = [KP.xt[m] for m in range(2)]
    onec = P.sbuf("onec", [128, 1], F32)
    P.I("dve", "memset", ap=onec[:, :], constant=1.0)
    onecb = P.sbuf("onecb", [128, 1], BF16)
    P.I("dve", "memset", ap=onecb[:, :], constant=1.0)
    ones33 = P.sbuf("ones33", [33, 128], F32)
    P.I("dve", "memset", ap=ones33[:, :], constant=1.0)
    rrow = P.sbuf("rrow", [33, 512], F32, nslot=1)
    rbc = [KP.xo[m] for m in range(2)]
    tt1 = KP.t32
    tt2 = T(KP.t32.h, "t32b")
    tt2.bufs = KP.t32.bufs
    oTb = P.sbuf("oTb", [128, 512], BF16)
    gb = [P.sbuf(f"gb{i}", [128, 128], BF16) for i in range(2)]
    osq = P.sbuf("osq", [128, 128], BF16)
    oss = [P.sbuf(f"oss{i}", [128, 4], F32) for i in range(2)]
    o3 = [P.sbuf(f"o3{i}", [128, 128], F32) for i in range(2)]
    ob = [P.sbuf(f"ob{i}", [128, 128], BF16) for i in range(2)]
    vsrc = inp["vall"].rearrange("(kb p) c -> p kb c", p=128)
    qblocks = [(qb * 512, 512, list(range(nkb))) for qb in range(n_own // 512)] + \
              [(n_own + c0, min(512, n_ctx - c0), list(range(n_lat // 128, nkb))) for c0 in range(0, n_ctx, 512)]
    kq = 0
    kf = 0
    for h in range(6):
        P.dma("sp", kTs[:, :], inp["kT"][h, :, :])
        for c0 in range(0, nkb, 32):
            c1 = min(nkb, c0 + 32)
            P.dma("sp", vau.v(vau.h[:, c0:c1, :]), vsrc[:, c0:c1, h * 128:(h + 1) * 128])
        for (q0, qn, kbs) in qblocks:
            nqs = qn // 128
            qt = qz[kq % 2]
            kq += 1
            for m in range(2):
                P.dma("sp", qt[m][m * 64:(m + 1) * 64, 0:qn], inp["qT"][h, m * 64:(m + 1) * 64, q0:q0 + qn])
            its = [(kb, m) for kb in kbs for m in range(2)]

            def QK(it):
                kb, m = its[it]
                s = it % 3
                P.mm(PSA[:, s * 512:s * 512 + qn], lhsT=kTs[:, kb * 128:(kb + 1) * 128], rhs=qt[m][:, 0:qn])

            pe_den = [False, False]
            QK(0)
            if len(its) > 1:
                QK(1)
            for it, (kb, m) in enumerate(its):
                s = it % 3
                if it + 2 < len(its):
                    QK(it + 2)
                P.act(pT[:, s * 512:s * 512 + qn], PSA[:, s * 512:s * 512 + qn], AF.Exp, scale=0.125)
                P.mm(PSO[:, m * 512:m * 512 + qn], lhsT=vau[:, kb, :], rhs=pT[:, s * 512:s * 512 + qn], start=(kb == kbs[0]), stop=(kb == kbs[-1]))
                if (kb - kbs[0]) % 3 == 2:
                    P.mm(PSM[32 * m:32 * m + 1, 0:qn], lhsT=onecb[:, :], rhs=pT[:, s * 512:s * 512 + qn], start=(not pe_den[m]), stop=False)
                    pe_den[m] = True
                elif kb == kbs[0]:
                    P.I("dve", "tensor_copy", out=dacc[m][:, 0:qn], in_=pT[:, s * 512:s * 512 + qn])
                else:
                    P.I("dve", "tensor_tensor", out=dacc[m][:, 0:qn], in0=dacc[m][:, 0:qn], in1=pT[:, s * 512:s * 512 + qn], op=ALU.add)
            for m in range(2):
                pr = 32 * m
                P.mm(PSM[pr:pr + 1, 0:qn], lhsT=onec[:, :], rhs=dacc[m][:, 0:qn], start=(not pe_den[m]), stop=True)
                P.I("dve", "reciprocal", out=rrow[pr:pr + 1, 0:qn], in_=PSM[pr:pr + 1, 0:qn])
                if m == 1:
                    P.I("dve", "tensor_scalar", out=rrow[pr:pr + 1, 0:qn], in0=rrow[pr:pr + 1, 0:qn], scalar1=neglam[pr:pr + 1, 0:1], scalar2=None, op0=ALU.mult)
                P.mm(PSO[:, 1024:1024 + qn], lhsT=ones33[pr:pr + 1, :], rhs=rrow[pr:pr + 1, 0:qn])
                P.act(rbc[m][:, 0:qn], PSO[:, 1024:1024 + qn], AF.Copy)
            P.I("dve", "tensor_tensor", out=tt1[:, 0:qn], in0=PSO[:, 0:qn], in1=rbc[0][:, 0:qn], op=ALU.mult)
            P.I("dve", "tensor_tensor", out=tt2[:, 512:512 + qn], in0=PSO[:, 512:512 + qn], in1=rbc[1][:, 0:qn], op=ALU.mult)
            P.I("pool", "tensor_tensor", out=oTb[:, 0:qn], in0=tt1[:, 0:qn], in1=tt2[:, 512:512 + qn], op=ALU.add)
            for qs in range(nqs):
                P.tr(PST[:, qs * 128:(qs + 1) * 128], oTb[:, qs * 128:(qs + 1) * 128], C["idb"][:, :])
            for qs in range(nqs):
                tok0 = q0 + qs * 128
                o2v = PST[:, qs * 128:(qs + 1) * 128]
                P.dma("sp", gb[kf % 2][:, :], inp["gates"][tok0:tok0 + 128, 256 + h * 128:256 + (h + 1) * 128])
                P.act(osq[:, :], o2v, AF.Square)
                ss = oss[kf % 2]
                P.I("dve", "tensor_reduce", out=ss[:, 0:1], in_=osq[:, :], axis=AX.X, op=ALU.add)
                P.I("dve", "tensor_scalar", out=ss[:, 1:2], in0=ss[:, 0:1], scalar1=1.0 / 128, scalar2=EPS, op0=ALU.mult, op1=ALU.add)
                P.act(ss[:, 3:4], ss[:, 1:2], AF.Sqrt)
                P.I("dve", "reciprocal", out=ss[:, 2:3], in_=ss[:, 3:4])
                P.I("dve", "scalar_tensor_tensor", out=o3[kf % 2][:, :], in0=o2v, scalar=ss[:, 2:3], in1=subg[:, :], op0=ALU.mult, op1=ALU.mult)
                P.I("pool", "tensor_tensor", out=ob[kf % 2][:, :], in0=o3[kf % 2][:, :], in1=gb[kf % 2][:, :], op=ALU.mult)
                P.dma("sp", mix_d[tok0:tok0 + 128, 256 + h * 128:256 + (h + 1) * 128], ob[kf % 2][:, :])
                kf += 1

    for k, (kind, i) in enumerate(tiles):
        if kind == "x":
            post_tile(P, C, KP, mix_d[i * 128:(i + 1) * 128, :], inp["x"][i * 128:(i + 1) * 128, :], xo_d[i * 128:(i + 1) * 128, :], 0, PST, PSA)
        else:
            t0 = n_own + i * 128
            post_tile(P, C, KP, mix_d[t0:t0 + 128, :], inp["ctx"][i * 128:(i + 1) * 128, :], co_d[i * 128:(i + 1) * 128, :], 1, PST, PSA)
    P.finish()
    return nc, P


def build_A_odd(layer, n_own=NOWN, n_ctx=NCTX):
    nt = n_own + n_ctx
    nc = bass.Bass("TRN2", target_bir_lowering=False)
    P = Prog(nc)
    inp = dict(
        x=dram_in(nc, "x", [n_own, D]), ctx=dram_in(nc, "ctx", [n_ctx, D]), ccols=dram_in(nc, "ccols", [128, 16]),
        ada_w=dram_in(nc, "ada_w", [D, 2048]), ada_b=dram_in(nc, "ada_b", [1, 2048]), norm_pre=dram_in(nc, "norm_pre", [1, D]),
        w_in=dram_in(nc, "w_in", [D, 4096]), ident=dram_in(nc, "ident", [128, 128]), lbraw=dram_in(nc, "lbraw", [128, 32]))
    tm_d = dram_out(nc, "tm", [nt, 1536], BF16)
    fmb_d = dram_out(nc, "fmb", [4, 128, nt], BF16)
    fmf_d = dram_out(nc, "fmf", [12, 128, nt], F32)
    C = mk_consts(P, inp)
    stage = [P.sbuf(f"stage{i}", [128, 4, 512], F32) for i in range(2)]
    psA = P.psum("psA", [128, 512], F32)
    AB = pre_norm_rows(P, C, inp, None, stage, psA)
    wb = load_w_bf16(P, inp["w_in"], 4096, "wb", stage)
    lbr = P.sbuf("lbr", [128, 32], F32)
    P.dma("sp", lbr[:, :], inp["lbraw"])
    lbe = P.sbuf("lbe", [128, 32], F32)
    P.act(lbe[:, :], lbr[:, :], AF.Exp)
    ev = lambda li: lbe.v(lbe.h[:, :].rearrange("p (d l h) -> p d l h", d=2, l=4)[:, :, li, :])
    den = P.sbuf("lbden", [128, 8], F32)
    num = P.sbuf("lbnum", [128, 8], F32)
    v3 = lambda t: t.v(t.h[:, :].rearrange("p (d h) -> p d h", d=2))
    P.I("dve", "tensor_tensor", out=v3(den), in0=ev(0), in1=ev(1), op=ALU.add)
    P.I("dve", "tensor_tensor", out=v3(den), in0=v3(den), in1=ev(2), op=ALU.add)
    P.I("dve", "tensor_tensor", out=v3(den), in0=v3(den), in1=ev(3), op=ALU.add)
    P.I("dve", "tensor_copy", out=v3(num), in_=ev(1))
    for li in range(2, layer + 1):
        P.I("dve", "tensor_tensor", out=v3(num), in0=v3(num), in1=ev(li), op=ALU.add)
    rden = P.sbuf("lbrden", [128, 8], F32)
    P.I("dve", "reciprocal", out=rden[:, :], in_=den[:, :])
    lb = P.sbuf("lb", [128, 8], F32)
    P.I("dve", "tensor_tensor", out=lb[:, :], in0=num[:, :], in1=rden[:, :], op=ALU.mult)
    oml = P.sbuf("oml", [128, 8], F32)
    P.I("dve", "tensor_scalar", out=oml[:, :], in0=lb[:, :], scalar1=-1.0, scalar2=1.0, op0=ALU.mult, op1=ALU.add)

    K = alloc_pre(P)
    pst = [P.psum(f"pst{i}", [128, 512], F32) for i in range(2)]
    psf = [P.psum(f"psf{i}", [128, 512], F32) for i in range(3)]
    tmt = [P.sbuf(f"tmt{i}", [128, 1536], BF16) for i in range(2)]
    sg = [P.sbuf(f"sg{i}", [128, 512], F32) for i in range(2)]
    fob = [P.sbuf(f"fob{i}", [128, 512], BF16) for i in range(2)]
    fof = [P.sbuf(f"fof{i}", [128, 512], F32) for i in range(2)]
    TMB = [(1024, 512, 0, True), (3584, 512, 512, True), (3072, 512, 1024, False)]
    groups = [(g * 512, 4, False) for g in range(n_own // 512)] + [(n_own, n_ctx // 128, True)]
    kt = 0
    kf = 0
    kb_ = 0
    kff = 0
    for gi, (tok0, ntile, is_ctx) in enumerate(groups):
        n = ntile * 128
        if is_ctx:
            xfn = lambda i: inp["ctx"][i * 128:(i + 1) * 128, :]
            ab = AB[2:4]
        else:
            xfn = lambda i, tok0=tok0: inp["x"][tok0 + i * 128: tok0 + (i + 1) * 128, :]
            ab = AB[0:2]
        hT = emit_pre_group(P, C, K, xfn, ntile, ab, gi)
        for i in range(ntile):
            tt = tmt[(gi * 4 + i) % 2]
            for (wc0, wn, dc, silu) in TMB:
                ps = pst[kt % 2]
                kt += 1
                for j in range(8):
                    P.mm(ps[:, 0:wn], lhsT=hT[:, j, i * 128:(i + 1) * 128], rhs=wb[:, j, wc0:wc0 + wn], start=(j == 0), stop=(j == 7))
                if silu:
                    P.act(tt[:, dc:dc + wn], ps[:, 0:wn], AF.Silu)
                else:
                    P.I("dve", "tensor_copy", out=tt[:, dc:dc + wn], in_=ps[:, 0:wn])
            P.dma("sp", tm_d[tok0 + i * 128: tok0 + (i + 1) * 128, :], tt[:, :])

        def fproj(wc0):
            nonlocal kf
            ps = psf[kf % 3]
            kf += 1
            for j in range(8):
                P.mm(ps[:, 0:n], lhsT=wb[:, j, wc0:wc0 + 128], rhs=hT[:, j, 0:n], start=(j == 0), stop=(j == 7))
            return ps
        for c in range(4):
            pa = fproj(c * 128)
            pb = fproj(512 + c * 128)
            s_ = sg[kb_ % 2]
            P.act(s_[:, 0:n], pb[:, 0:n], AF.Sigmoid)
            fo = fob[kb_ % 2]
            kb_ += 1
            P.I("dve", "tensor_tensor", out=fo[:, 0:n], in0=pa[:, 0:n], in1=s_[:, 0:n], op=ALU.mult)
            P.dma("sp", fmb_d[c, :, tok0:tok0 + n], fo[:, 0:n])
        for hh in range(4):
            pq = fproj(1536 + hh * 128)
            fo = fof[kff % 2]
            kff += 1
            P.act(fo[:, 0:n], pq[:, 0:n], AF.Silu)
            P.dma("sp", fmf_d[hh, :, tok0:tok0 + n], fo[:, 0:n])
            for dr in range(2):
                pf = fproj(2048 + dr * 512 + hh * 128)
                s_ = sg[kb_ % 2]
                kb_ += 1
                P.act(s_[:, 0:n], pf[:, 0:n], AF.Sigmoid)
                fo = fof[kff % 2]
                kff += 1
                ci = dr * 4 + hh
                P.I("dve", "tensor_scalar", out=fo[:, 0:n], in0=s_[:, 0:n], scalar1=oml[:, ci:ci + 1], scalar2=lb[:, ci:ci + 1], op0=ALU.mult, op1=ALU.add)
                P.dma("sp", fmf_d[4 + dr * 4 + hh, :, tok0:tok0 + n], fo[:, 0:n])
    P.finish()
    return nc, P


def build_H(n_lat=4 * NOWN, n_ctx=NCTX):
    N = n_lat + n_ctx
    SEG = min(2048, n_lat)
    nc = bass.Bass("TRN2", target_bir_lowering=False)
    P = Prog(nc)
    inp = dict(qT=dram_in(nc, "qT", [128, N]), fTf=dram_in(nc, "fTf", [128, N]), fTb=dram_in(nc, "fTb", [128, N]),
               v=dram_in(nc, "v", [N, 128], BF16), ogs=dram_in(nc, "ogs", [N, 128], BF16), gn=dram_in(nc, "gn", [1, 128]),
               ident=dram_in(nc, "ident", [128, 128]), masks=dram_in(nc, "masks", [64, 128]), cmask=dram_in(nc, "cmask", [128, SEG]))
    yd_d = dram_out(nc, "yd", [N, 128], BF16)
    o_d = [P.dram(f"od{d_}", [N, 128], F32, nslot=N // 64, sdim=0) for d_ in range(2)]
    C = mk_consts(P, inp)
    msk = P.sbuf("msk", [64, 128], F32)
    P.dma("sp", msk[:, :], inp["masks"])
    cm = P.sbuf("cm", [128, SEG], F32)
    P.dma("sp", cm[:, :], inp["cmask"])
    gnr = P.sbuf("gnr", [1, 128], F32)
    P.dma("sp", gnr[:, :], inp["gn"])
    p_ds_pre = [P.psum(f"p_ds{i}", [128, 128], F32) for i in range(2)]
    psg = p_ds_pre[0]
    gbc = P.sbuf("gbc", [128, 128], F32)
    P.mm(psg[:, :], lhsT=C["ones"][:, :], rhs=gnr[0:1, :])
    P.I("dve", "tensor_copy", out=gbc[:, :], in_=psg[:, :])
    segs = [(0, n_ctx)] + [(n_ctx + i * SEG, SEG) for i in range(n_lat // SEG)]

    def sweep(dirn):
        bwd = dirn == 1
        d_ = f"d{dirn}"
        qs = P.sbuf("qs" + d_, [128, SEG], F32)
        fs = P.sbuf("fs" + d_, [128, SEG], F32)
        lfs = P.sbuf("lfs" + d_, [128, SEG], F32)
        ks = P.sbuf("ks" + d_, [128, SEG], F32)
        Pc = P.sbuf("Pc" + d_, [128, SEG], F32)
        Pe = P.sbuf("Pe" + d_, [128, SEG], F32) if bwd else None
        vs = P.sbuf("vs" + d_, [64, SEG // 64, 128], BF16)
        S = [P.sbuf(f"S{i}" + d_, [128, 128], F32) for i in range(2)]
        nE = P.sbuf("nE" + d_, [128, SEG], F32)
        X = P.sbuf("X" + d_, [128, SEG // 64, 2], F32)
        EX = P.sbuf("EX" + d_, [128, SEG // 64, 2], F32)
        eq = [P.sbuf(f"eq{i}" + d_, [128, 64], F32) for i in range(2)]
        ek = [P.sbuf(f"ek{i}" + d_, [128, 64], F32) for i in range(2)]
        ekl = [P.sbuf(f"ekl{i}" + d_, [128, 64], F32) for i in range(2)]
        qt = [P.sbuf(f"qt{i}" + d_, [128, 64], BF16) for i in range(2)]
        kt = [P.sbuf(f"kt{i}" + d_, [128, 64], BF16) for i in range(2)]
        kh = [P.sbuf(f"kh{i}" + d_, [128, 64], BF16) for i in range(2)]
        khs = [P.sbuf(f"khs{i}" + d_, [64, 128], BF16) for i in range(2)]
        S0m = [P.sbuf(f"S0m{i}" + d_, [128, 128], BF16) for i in range(2)]
        attm = [P.sbuf(f"attm{i}" + d_, [64, 64], BF16) for i in range(2)]
        osb = [P.sbuf(f"osb{i}" + d_, [64, 128], F32) for i in range(2)]
        p_kh = P.psum("p_kh" + d_, [64, 128], BF16)
        p_att = P.psum("p_att" + d_, [64, 64], F32)
        p_o = P.psum("p_o" + d_, [64, 128], F32)
        p_ds = p_ds_pre[dirn]
        P.I("dve", "memset", ap=S[0][:, :], constant=0.0)
        cur = 0
        kc = 0
        order = segs if not bwd else [segs[0]] + segs[1:][::-1]
        E = Pe if bwd else Pc
        for (a0, sl) in order:
            P.dma("sp", qs[:, 0:sl], inp["qT"][:, a0:a0 + sl])
            P.dma("sp", fs[:, 0:sl], inp["fTb" if bwd else "fTf"][:, a0:a0 + sl])
            P.dma("sp", vs.v(vs.h[:, 0:sl // 64, :]), inp["v"][a0:a0 + sl, :].rearrange("(n p) c -> p n c", p=64))
            P.act(lfs[:, 0:sl], fs[:, 0:sl], AF.Ln)
            P.I("pool", "tensor_scalar", out=ks[:, 0:sl], in0=fs[:, 0:sl], scalar1=-1.0, scalar2=1.0, op0=ALU.mult, op1=ALU.add)
            P.I("dve", "tensor_tensor_scan", out=Pc[:, 0:sl], data0=cm[:, 0:sl], data1=lfs[:, 0:sl], initial=0.0, op0=ALU.mult, op1=ALU.add)
            if bwd:
                P.I("dve", "tensor_tensor", out=Pe[:, 0:sl], in0=Pc[:, 0:sl], in1=lfs[:, 0:sl], op=ALU.subtract)
            nch = sl // 64
            P.I("pool", "tensor_scalar", out=nE[:, 0:sl], in0=E[:, 0:sl], scalar1=-1.0, scalar2=None, op0=ALU.mult)
            col = lambda t_, c_: t_.v(t_.h[:, 0:sl].rearrange("p (n c) -> p n c", c=64)[:, :, c_])
            if bwd:
                P.I("dve", "tensor_tensor", out=X.v(X.h[:, 0:nch, 0]), in0=col(Pc, 63), in1=col(Pe, 31), op=ALU.subtract)
            else:
                P.I("dve", "tensor_copy", out=X.v(X.h[:, 0:nch, 0]), in_=col(Pc, 31))
            P.I("dve", "tensor_copy", out=X.v(X.h[:, 0:nch, 1]), in_=col(Pc, 63))
            P.act(EX.v(EX.h[:, 0:nch, :]), X.v(X.h[:, 0:nch, :]), AF.Exp)
            chunks = list(range(sl // 64))
            if bwd:
                chunks = chunks[::-1]
            for n_ in chunks:
                a = n_ * 64
                k2 = kc % 2
                kc += 1
                Ec = E[:, a:a + 64]
                pmid = E[:, a + 31:a + 32]
                nmid = nE[:, a + 31:a + 32]
                tot = Pc[:, a + 63:a + 64]
                if bwd:
                    P.act(eq[k2][:, :], Ec, AF.Exp, scale=-1.0, bias=pmid)
                    P.act(ek[k2][:, :], Ec, AF.Exp, scale=1.0, bias=nmid)
                    P.act(ekl[k2][:, :], Ec, AF.Exp)
                else:
                    P.act(eq[k2][:, :], Ec, AF.Exp, scale=1.0, bias=nmid)
                    P.act(ek[k2][:, :], Ec, AF.Exp, scale=-1.0, bias=pmid)
                    P.act(ekl[k2][:, :], Ec, AF.Exp, scale=-1.0, bias=tot)
                P.I("dve", "tensor_tensor", out=qt[k2][:, :], in0=qs[:, a:a + 64], in1=eq[k2][:, :], op=ALU.mult)
                P.I("pool", "tensor_tensor", out=kt[k2][:, :], in0=ks[:, a:a + 64], in1=ek[k2][:, :], op=ALU.mult)
                P.I("pool", "tensor_tensor", out=kh[k2][:, :], in0=ks[:, a:a + 64], in1=ekl[k2][:, :], op=ALU.mult)
                P.tr(p_kh[:, :], kh[k2][:, :], C["idb"][:, :])
                P.act(khs[k2][:, :], p_kh[:, :], AF.Copy)
                P.I("dve", "tensor_scalar", out=S0m[k2][:, :], in0=S[cur][:, :], scalar1=EX[:, n_, 0:1], scalar2=None, op0=ALU.mult)
                P.mm(p_att[:, :], lhsT=kt[k2][:, :], rhs=qt[k2][:, :])
                mo = 64 if bwd else 0
                P.I("dve", "tensor_tensor", out=attm[k2][:, :], in0=p_att[:, :], in1=msk[:, mo:mo + 64], op=ALU.mult)
                P.mm(p_o[:, :], lhsT=qt[k2][:, :], rhs=S0m[k2][:, :], start=True, stop=False)
                P.mm(p_o[:, :], lhsT=attm[k2][:, :], rhs=vs[:, n_, :], start=False, stop=True)
                P.mm(p_ds[:, :], lhsT=khs[k2][:, :], rhs=vs[:, n_, :])
                P.I("dve", "scalar_tensor_tensor", out=S[1 - cur][:, :], in0=S[cur][:, :], scalar=EX[:, n_, 1:2], in1=p_ds[:, :], op0=ALU.mult, op1=ALU.add)
                cur = 1 - cur
                r0 = a0 + a
                P.act(osb[k2][:, :], p_o[:, :], AF.Copy)
                P.dma("sp", o_d[dirn][r0:r0 + 64, :], osb[k2][:, :])
                yield

    gens = [sweep(0), sweep(1)]
    alive = [True, True]
    while any(alive):
        for d_ in range(2):
            if alive[d_]:
                try:
                    next(gens[d_])
                except StopIteration:
                    alive[d_] = False
    o1 = [P.sbuf(f"co1{i}", [128, 128], F32) for i in range(2)]
    o2 = [P.sbuf(f"co2{i}", [128, 128], F32) for i in range(2)]
    og = [P.sbuf(f"cog{i}", [128, 128], BF16) for i in range(2)]
    osm = [P.sbuf(f"cosm{i}", [128, 128], F32) for i in range(2)]
    osq = P.sbuf("hosq", [128, 128], BF16)
    oss = [P.sbuf(f"hoss{i}", [128, 4], F32) for i in range(2)]
    on = [P.sbuf(f"on{i}", [128, 128], F32) for i in range(2)]
    yb = [P.sbuf(f"yb{i}", [128, 128], BF16) for i in range(2)]
    for t in range(N // 128):
        k2 = t % 2
        r0 = t * 128
        P.dma("sp", o1[k2][:, :], o_d[0][r0:r0 + 128, :])
        P.dma("sp", o2[k2][:, :], o_d[1][r0:r0 + 128, :])
        P.dma("sp", og[k2][:, :], inp["ogs"][r0:r0 + 128, :])
        P.I("dve", "tensor_tensor", out=osm[k2][:, :], in0=o1[k2][:, :], in1=o2[k2][:, :], op=ALU.add)
        P.act(osq[:, :], osm[k2][:, :], AF.Square)
        ss = oss[k2]
        P.I("dve", "tensor_reduce", out=ss[:, 0:1], in_=osq[:, :], axis=AX.X, op=ALU.add)
        P.I("dve", "tensor_scalar", out=ss[:, 1:2], in0=ss[:, 0:1], scalar1=1.0 / 128, scalar2=EPS, op0=ALU.mult, op1=ALU.add)
        P.act(ss[:, 3:4], ss[:, 1:2], AF.Sqrt)
        P.I("dve", "reciprocal", out=ss[:, 2:3], in_=ss[:, 3:4])
        P.I("dve", "scalar_tensor_tensor", out=on[k2][:, :], in0=osm[k2][:, :], scalar=ss[:, 2:3], in1=gbc[:, :], op0=ALU.mult, op1=ALU.mult)
        P.I("pool", "tensor_tensor", out=yb[k2][:, :], in0=on[k2][:, :], in1=og[k2][:, :], op=ALU.mult)
        P.dma("sp", yd_d[r0:r0 + 128, :], yb[k2][:, :])
    P.finish()
    return nc, P


CSTOP = 0


def build_C_odd(n_own=NOWN, n_ctx=NCTX, with_ctx=True):
    nt = n_own + n_ctx
    nc = bass.Bass("TRN2", target_bir_lowering=False)
    P = Prog(nc)
    inp = dict(
        x=dram_in(nc, "x", [n_own, D]), ctx=dram_in(nc, "ctx", [n_ctx, D]), ccols=dram_in(nc, "ccols", [128, 16]),
        ada_w=dram_in(nc, "ada_w", [D, 1024]), ada_b=dram_in(nc, "ada_b", [1, 1024]), norm_post=dram_in(nc, "norm_post", [1, D]),
        w_out=dram_in(nc, "w_out", [D, D]), ident=dram_in(nc, "ident", [128, 128]),
        gluh=dram_in(nc, "gluh", [4, 128, n_own + 30], BF16), gluch=dram_in(nc, "gluch", [4, 128, n_ctx + 30], BF16),
        cgs=dram_in(nc, "cgs", [nt, 512], BF16), yd=dram_in(nc, "yd", [nt, 512], BF16),
        convw=dram_in(nc, "convw", [128, 124]), convb=dram_in(nc, "convb", [128, 4]),
        lng=dram_in(nc, "lng", [1, 512]), lnb=dram_in(nc, "lnb", [1, 512]))
    xo_d = dram_out(nc, "xo", [n_own, D], F32)
    co_d = dram_out(nc, "co", [n_ctx, D], F32)
    C = mk_consts(P, inp)
    stage = [P.sbuf(f"stage{i}", [128, 4, 512], F32) for i in range(2)]
    PSY = P.psum("PSY", [128, 1024], F32)
    PST = P.psum("PST", [128, D], BF16)
    PSM = P.psum("PSM", [128, 512], F32)
    PSZ = [P.psum(f"PSZ{i}", [128, 512], BF16) for i in range(2)]
    KP = post_setup(P, C, inp, stage, PSM)
    cw = P.sbuf("cw", [128, 124], F32)
    P.dma("sp", cw[:, :], inp["convw"])
    cb = P.sbuf("cb", [128, 4], F32)
    P.dma("sp", cb[:, :], inp["convb"])
    lrow = P.sbuf("lrow", [1, 1024], F32)
    P.dma("sp", lrow[:, 0:512], inp["lng"])
    P.dma("sp", lrow[:, 512:1024], inp["lnb"])
    lgb = P.sbuf("lgb", [128, 1024], F32)
    for hh in range(2):
        P.mm(PSM[:, :], lhsT=C["ones"][:, :], rhs=lrow[0:1, hh * 512:(hh + 1) * 512])
        P.I("dve", "tensor_copy", out=lgb[:, hh * 512:(hh + 1) * 512], in_=PSM[:, :])
    glu = [P.sbuf(f"glu{i}", [128, 4, 542], BF16) for i in range(2)]
    zT = [P.sbuf(f"zT{i}", [128, 4, 512], F32) for i in range(2)]
    zs = [P.sbuf(f"zs{i}", [128, 512], F32) for i in range(2)]
    zb = [P.sbuf(f"zb{i}", [128, 4, 512], BF16) for i in range(2)]
    zsq = P.sbuf("zsq", [128, 512], F32)
    st_ = [P.sbuf(f"lst{i}", [128, 8], F32) for i in range(2)]
    zn = [P.sbuf(f"zn{i}", [128, 512], F32) for i in range(2)]
    zl = [P.sbuf(f"zl{i}", [128, 512], F32) for i in range(2)]
    zsl = [P.sbuf(f"zsl{i}", [128, 512], F32) for i in range(2)]
    cg = [P.sbuf(f"cg{i}", [128, 512], BF16) for i in range(2)]
    mixs = [P.sbuf(f"mixs{i}", [128, D], BF16) for i in range(2)]
    blocks = [("x", b0, 512) for b0 in range(0, n_own, 512)]
    if with_ctx:
        blocks += [("c", 0, n_ctx)]
    kt = 0
    for bi, (kind, b0, nb) in enumerate(blocks):
        g_ = glu[bi % 2]
        src = inp["gluh"] if kind == "x" else inp["gluch"]
        P.dma("sp", g_.v(g_.h[:, :, 0:nb + 30]), src[:, :, b0:b0 + nb + 30].rearrange("c p t -> p c t"))
        z_ = zT[bi % 2]
        for c in range(4):
            acc = z_[:, c, 0:nb]
            P.I("dve", "tensor_scalar", out=acc, in0=g_[:, c, 0:nb], scalar1=cw[:, c * 31:c * 31 + 1], scalar2=cb[:, c:c + 1], op0=ALU.mult, op1=ALU.add)
            for w in range(1, 31):
                P.I("dve", "scalar_tensor_tensor", out=acc, in0=g_[:, c, w:w + nb], scalar=cw[:, c * 31 + w:c * 31 + w + 1], in1=acc, op0=ALU.mult, op1=ALU.add)
            P.act(zb[bi % 2][:, c, 0:nb], acc, AF.Copy)
        if CSTOP == 1:
            continue
        for i in range(nb // 128):
            k2 = kt % 2
            kt += 1
            tok0 = (b0 if kind == "x" else n_own) + i * 128
            pz = PSZ[k2]
            for c in range(4):
                P.tr(pz[:, c * 128:(c + 1) * 128], zb[bi % 2][:, c, i * 128:(i + 1) * 128], C["idb"][:, :])
            P.act(zs[k2][:, :], pz[:, :], AF.Copy)
            if CSTOP == 2:
                continue
            s = st_[k2]
            P.I("dve", "tensor_reduce", out=s[:, 0:1], in_=zs[k2][:, :], axis=AX.X, op=ALU.add)
            P.act(zsq[:, :], zs[k2][:, :], AF.Square)
            P.I("dve", "tensor_reduce", out=s[:, 1:2], in_=zsq[:, :], axis=AX.X, op=ALU.add)
            P.I("dve", "tensor_scalar", out=s[:, 2:3], in0=s[:, 0:1], scalar1=1.0 / 512, scalar2=None, op0=ALU.mult)
            P.I("dve", "tensor_tensor", out=s[:, 3:4], in0=s[:, 2:3], in1=s[:, 2:3], op=ALU.mult)
            P.I("dve", "scalar_tensor_tensor", out=s[:, 4:5], in0=s[:, 1:2], scalar=1.0 / 512, in1=s[:, 3:4], op0=ALU.mult, op1=ALU.subtract)
            P.I("dve", "tensor_scalar", out=s[:, 5:6], in0=s[:, 4:5], scalar1=EPS, scalar2=None, op0=ALU.add)
            P.act(s[:, 6:7], s[:, 5:6], AF.Sqrt)
            P.I("dve", "reciprocal", out=s[:, 7:8], in_=s[:, 6:7])
            if CSTOP == 3:
                continue
            P.I("dve", "tensor_scalar", out=zn[k2][:, :], in0=zs[k2][:, :], scalar1=s[:, 2:3], scalar2=s[:, 7:8], op0=ALU.subtract, op1=ALU.mult)
            P.I("pool", "tensor_tensor", out=zl[k2][:, :], in0=zn[k2][:, :], in1=lgb[:, 0:512], op=ALU.mult)
            P.I("pool", "tensor_tensor", out=zl[k2][:, :], in0=zl[k2][:, :], in1=lgb[:, 512:1024], op=ALU.add)
            P.act(zsl[k2][:, :], zl[k2][:, :], AF.Silu)
            P.dma("sp", cg[k2][:, :], inp["cgs"][tok0:tok0 + 128, :])
            mt = mixs[k2]
            P.dma("sp", mt[:, 512:1024], inp["yd"][tok0:tok0 + 128, :])
            P.I("pool", "tensor_tensor", out=mt[:, 0:512], in0=zsl[k2][:, :], in1=cg[k2][:, :], op=ALU.mult)
            if CSTOP == 4:
                continue
            if kind == "x":
                post_tile(P, C, KP, None, inp["x"][tok0:tok0 + 128, :], xo_d[tok0:tok0 + 128, :], 0, PST, PSY, mix_in_sbuf=mt)
            else:
                r0 = i * 128
                post_tile(P, C, KP, None, inp["ctx"][r0:r0 + 128, :], co_d[r0:r0 + 128, :], 1, PST, PSY, mix_in_sbuf=mt)
    if not with_ctx:
        for i in range(n_ctx // 128):
            xt = KP.xt[i % 2]
            P.dma("sp", xt[:, :], inp["ctx"][i * 128:(i + 1) * 128, :])
            P.dma("sp", co_d[i * 128:(i + 1) * 128, :], xt[:, :])
    P.finish()
    return nc, P


GRID_W = 64
import math
def rope_tables(tok0, n):
    t = np.arange(tok0, tok0 + n)
    row = (t // GRID_W).astype(np.float32); col = (t % GRID_W).astype(np.float32)
    inv = (10000.0 ** (-np.arange(16, dtype=np.float32) / 16)).astype(np.float32)
    cosT = np.zeros((128, n), np.float32); sinT = np.zeros((128, n), np.float32)
    for r in range(128):
        d = r % 64
        pos = row if d < 32 else col
        ang = (pos * inv[d % 16]).astype(np.float32)
        cosT[r] = np.cos(ang); sinT[r] = np.sin(ang)
    return cosT, sinT
def rope_rmat():
    R = np.zeros((128, 128), np.float32)
    for i in range(128):
        d = i % 32
        if d < 16:
            R[i + 16, i] = -1.0
        else:
            R[i - 16, i] = 1.0
    return R
def ccols(c_b, c_ctx):
    return np.concatenate([c_b.reshape(8, 128).T, c_ctx.reshape(8, 128).T], axis=1).astype(np.float32).copy()

POOL_WINDOWS = (2, 4, 8, 16)
def pool_bands(L, tile0, is_first, is_last):
    out = np.zeros((4, 144, 128), np.float32)
    for g, w in enumerate(POOL_WINDOWS):
        for c in range(128):
            t = tile0 + c
            lo = max(t - w // 2, 0); hi = min(t + w // 2 - 1, L - 1) + 1
            cnt = hi - lo
            for s in range(lo, hi):
                r = s - (tile0 - 8)
                out[g, r, c] += 1.0 / cnt
            out[g, c + 8, c] -= 1.0
    return out
def band_pack(L_lat, own0, n_own, L_ctx):
    types = [pool_bands(L_lat, own0, True, False), pool_bands(L_lat, own0 + 128 if n_own > 256 else own0 + 128, False, False),
             pool_bands(L_lat, own0 + n_own - 128, False, True), pool_bands(L_ctx, 0, True, False), pool_bands(L_ctx, L_ctx - 128, False, True)]
    A = np.zeros((128, 2560), np.float32); B = np.zeros((16, 2560), np.float32)
    for t, bm in enumerate(types):
        for g in range(4):
            A[:, (t * 4 + g) * 128:(t * 4 + g + 1) * 128] = bm[g, :128]
            B[:, (t * 4 + g) * 128:(t * 4 + g + 1) * 128] = bm[g, 128:]
    return A, B
def halo(u_seq, a, n):
    L = u_seq.shape[0]
    out = np.zeros((n + 16, u_seq.shape[1]), u_seq.dtype)
    lo = max(a - 8, 0); hi = min(a + n + 8, L)
    out[lo - (a - 8): hi - (a - 8)] = u_seq[lo:hi]
    return out


def halo15_T(seqT, a, n):
    L = seqT.shape[2]
    out = np.zeros((4, 128, n + 30), seqT.dtype)
    lo = max(a - 15, 0)
    hi = min(a + n + 15, L)
    out[:, :, lo - (a - 15):hi - (a - 15)] = seqT[:, :, lo:hi]
    return out


_PROGS = {}
_DEBUG_HOOK = None


def _prog(key, fn):
    if key not in _PROGS:
        _PROGS[key] = fn()[0]
    return _PROGS[key]


def _run(nc, in_maps):
    res = run_bass_kernel_spmd(nc, in_maps, core_ids=list(range(8)))
    return res.results


def kernel(x, c, ctx, c_ctx, ada_w, ada_b, norm_pre, norm_post, w_in_even, w_out_even,
           pool_w, pool_scale, diff_lambda, diff_subln, w_in_odd, w_out_odd,
           conv_w, conv_b, conv_ln_g, conv_ln_b, hgrn_norm, hgrn_lb):
    f32 = lambda a: np.ascontiguousarray(np.asarray(a, dtype=np.float32))
    x, c, ctx, c_ctx, ada_w, ada_b = f32(x), f32(c), f32(ctx), f32(c_ctx), f32(ada_w), f32(ada_b)
    norm_pre, norm_post = f32(norm_pre), f32(norm_post)
    NB, L = x.shape[0], x.shape[1]
    S4 = 4
    ident = np.eye(128, dtype=np.float32)
    rmat = rope_rmat()
    xs = [np.ascontiguousarray(x[r // 4, (r % 4) * NOWN:(r % 4 + 1) * NOWN]) for r in range(8)]
    cs = [np.ascontiguousarray(ctx[b]) for b in range(NB)]
    cc = [ccols(c[b], c_ctx) for b in range(NB)]
    ropes = [rope_tables((r % 4) * NOWN, NOWN) for r in range(4)]
    bands = [band_pack(L, s * NOWN, NOWN, NCTX) for s in range(4)]
    masks = np.zeros((64, 128), np.float32)
    s_, t_ = np.meshgrid(np.arange(64), np.arange(64), indexing="ij")
    masks[:, :64] = (t_ >= s_)
    masks[:, 64:] = (s_ >= t_)
    cmask = np.ones((128, 2048), np.float32)
    cmask[:, ::64] = 0
    lbraw = np.ascontiguousarray(f32(hgrn_lb).reshape(2, 4, 4, 128).transpose(3, 0, 1, 2).reshape(128, 32))
    for l in range(4):
        j = l // 2
        aw_pre = np.ascontiguousarray(ada_w[l][:, :2048])
        ab_pre = np.ascontiguousarray(ada_b[l][None, :2048])
        aw_post = np.ascontiguousarray(ada_w[l][:, 2048:])
        ab_post = np.ascontiguousarray(ada_b[l][None, 2048:])
        npre = np.ascontiguousarray(norm_pre[l][None])
        npost = np.ascontiguousarray(norm_post[l][None])
        if l % 2 == 0:
            w_in = f32(w_in_even[j])
            ncA = _prog("A_even", build_A_even)
            ims = [dict(x=xs[r], ctx=cs[r // 4], ccols=cc[r // 4], ada_w=aw_pre, ada_b=ab_pre, norm_pre=npre, w_in=w_in,
                        ident=ident, rmat=rmat, cosT=ropes[r % 4][0], sinT=ropes[r % 4][1]) for r in range(8)]
            ra = _run(ncA, ims)
            ncB = _prog("B_even", build_B_even)
            lam_init = 0.8 - 0.6 * math.exp(-0.3 * l)
            pw = np.ascontiguousarray(f32(pool_w[j]).transpose(1, 0, 2).reshape(64, 256))
            ims = []
            for r in range(8):
                b, s = r // 4, r % 4
                grp = [ra[b * 4 + q] for q in range(4)]
                tm = ra[r]["tm"]
                kT = np.concatenate([g["fm"][6:12, :, :NOWN] for g in grp] + [ra[r]["fm"][6:12, :, NOWN:]], axis=2)
                vall = np.concatenate([g["tm"][:NOWN, 512:1280] for g in grp] + [tm[NOWN:, 512:1280]], axis=0)
                useq = np.concatenate([g["tm"][:NOWN, 0:256] for g in grp], axis=0)
                ims.append(dict(
                    x=xs[r], ctx=cs[b], ccols=cc[b], ada_w=aw_post, ada_b=ab_post, norm_post=npost, w_out=f32(w_out_even[j]), ident=ident,
                    qT=np.ascontiguousarray(ra[r]["fm"][0:6]), kT=np.ascontiguousarray(kT), vall=np.ascontiguousarray(vall),
                    gates=np.ascontiguousarray(np.concatenate([tm[:, 256:512], tm[:, 1280:2048]], axis=1)),
                    uh=halo(useq, s * NOWN, NOWN), uch=halo(np.ascontiguousarray(tm[NOWN:, 0:256]), 0, NCTX),
                    bandA=bands[s][0], bandB=bands[s][1], pool_w=pw, pool_scale=f32(pool_scale[j])[None].copy(),
                    lamp=f32(diff_lambda[j]).reshape(1, 256).copy(), subln=f32(diff_subln[j])[None].copy(),
                    lamc=np.array([[lam_init, 1.0 - lam_init]], np.float32)))
            rb = _run(ncB, ims)
            del ra
        else:
            ncA = _prog(("A_odd", l), lambda: build_A_odd(l))
            ims = [dict(x=xs[r], ctx=cs[r // 4], ccols=cc[r // 4], ada_w=aw_pre, ada_b=ab_pre, norm_pre=npre, w_in=f32(w_in_odd[j]),
                        ident=ident, lbraw=lbraw) for r in range(8)]
            ra = _run(ncA, ims)
            ncH = _prog("H", build_H)
            ims = []
            for r in range(8):
                b, hh = r // 4, r % 4
                grp = [ra[b * 4 + q] for q in range(4)]
                cat_f = lambda blk: np.ascontiguousarray(np.concatenate([grp[0]["fmf"][blk][:, NOWN:]] + [g["fmf"][blk][:, :NOWN] for g in grp], axis=1))
                cat_t = lambda c0: np.ascontiguousarray(np.concatenate([grp[0]["tm"][NOWN:, c0:c0 + 128]] + [g["tm"][:NOWN, c0:c0 + 128] for g in grp], axis=0))
                ims.append(dict(qT=cat_f(hh), fTf=cat_f(4 + hh), fTb=cat_f(8 + hh), v=cat_t(1024 + hh * 128), ogs=cat_t(512 + hh * 128),
                                gn=f32(hgrn_norm[j])[None, hh * 128:(hh + 1) * 128].copy(), ident=ident, masks=masks, cmask=cmask))
            rh = _run(ncH, ims)
            with_ctx = l < 3
            ncC = _prog(("C_odd", with_ctx), lambda: build_C_odd(NOWN, NCTX, with_ctx))
            cw = f32(conv_w[j])
            convw = np.ascontiguousarray(cw.T.reshape(4, 128, 31).transpose(1, 0, 2).reshape(128, 124))
            convb = np.ascontiguousarray(f32(conv_b[j]).reshape(4, 128).T)
            ims = []
            for r in range(8):
                b, s = r // 4, r % 4
                grp = [ra[b * 4 + q] for q in range(4)]
                gseq = np.concatenate([g["fmb"][:, :, :NOWN] for g in grp], axis=2)
                gctx = np.ascontiguousarray(ra[r]["fmb"][:, :, NOWN:])
                yd = np.concatenate([np.concatenate([rh[b * 4 + hh]["yd"][NCTX + s * NOWN: NCTX + (s + 1) * NOWN] for hh in range(4)], axis=1),
                                     np.concatenate([rh[b * 4 + hh]["yd"][0:NCTX] for hh in range(4)], axis=1)], axis=0)
                ims.append(dict(
                    x=xs[r], ctx=cs[b], ccols=cc[b], ada_w=aw_post, ada_b=ab_post, norm_post=npost, w_out=f32(w_out_odd[j]), ident=ident,
                    gluh=halo15_T(gseq, s * NOWN, NOWN), gluch=halo15_T(gctx, 0, NCTX),
                    cgs=np.ascontiguousarray(ra[r]["tm"][:, 0:512]), yd=np.ascontiguousarray(yd), convw=convw, convb=convb,
                    lng=f32(conv_ln_g[j])[None].copy(), lnb=f32(conv_ln_b[j])[None].copy()))
            rb = _run(ncC, ims)
            del ra, rh
        xs = [np.ascontiguousarray(rb[r]["xo"]) for r in range(8)]
        cs = [np.ascontiguousarray(rb[b * 4]["co"]) for b in range(NB)]
        if _DEBUG_HOOK is not None:
            _DEBUG_HOOK(l, xs, cs)
    out = np.stack([np.concatenate(xs[b * 4:(b + 1) * 4], axis=0) for b in range(NB)], axis=0)
    return out.astype(np.float32)
```

```python
import contextlib
import numpy as np
import concourse.bass as bass
import concourse.mybir as mybir
from concourse.bass_utils import run_bass_kernel_spmd

F32 = mybir.dt.float32
BF16 = mybir.dt.bfloat16
ALU = mybir.AluOpType
AF = mybir.ActivationFunctionType
AX = mybir.AxisListType


class Buf:
    __slots__ = ("name", "last_w", "readers", "excl")

    def __init__(self, name):
        self.name = name
        self.last_w = None
        self.readers = []
        self.excl = False


class V:
    __slots__ = ("ap", "bufs")

    def __init__(self, ap, bufs):
        self.ap = ap
        self.bufs = bufs


class T:
    def __init__(self, h, name, nslot=1, sdim=1):
        self.h = h
        self.name = name
        self.nslot = nslot
        self.sdim = sdim
        self.bufs = [Buf(f"{name}.{i}") for i in range(nslot)]
        self.shape = list(h.shape)

    def __getitem__(self, idx):
        ap = self.h[idx]
        if self.nslot == 1:
            return V(ap, self.bufs)
        if not isinstance(idx, tuple):
            idx = (idx,)
        bufs = self.bufs
        if len(idx) > self.sdim:
            s = idx[self.sdim]
            n = self.shape[self.sdim]
            per = n // self.nslot
            if isinstance(s, int):
                bufs = [self.bufs[s // per]]
            elif isinstance(s, slice):
                a = 0 if s.start is None else s.start
                b = n if s.stop is None else s.stop
                bufs = self.bufs[a // per:(b - 1) // per + 1]
        return V(ap, bufs)

    def v(self, ap):
        return V(ap, self.bufs)


class Op:
    __slots__ = ("eng", "fn", "waits", "sem", "val", "is_dma", "inc")


ENGS = ("pe", "act", "dve", "pool", "sp")


class Prog:
    N_DMA_SEM = 8

    def __init__(self, nc):
        self.nc = nc
        self.es = contextlib.ExitStack()
        self.ops = {e: [] for e in ENGS}
        self.cnt = {e: 0 for e in ENGS}
        self.ndma = {e: 0 for e in ENGS}
        self.sem = {}
        self.dsem = {}
        self.waited = {e: {} for e in ENGS}
        self.semvals = {}
        for e in ENGS:
            self.sem[e] = self.es.enter_context(nc.semaphore(f"s_{e}"))
        for e in ("sp", "pool", "act"):
            self.dsem[e] = [self.es.enter_context(nc.semaphore(f"d_{e}{i}")) for i in range(self.N_DMA_SEM)]
        self.n_ops = 0

    def sbuf(self, name, shape, dt, nslot=1, sdim=1):
        h = self.es.enter_context(self.nc.sbuf_tensor("sb_" + name, list(shape), dt))
        return T(h, name, nslot, sdim)

    def psum(self, name, shape, dt, nslot=1, sdim=1):
        h = self.es.enter_context(self.nc.psum_tensor("ps_" + name, list(shape), dt))
        t = T(h, name, nslot, sdim)
        for b in t.bufs:
            b.excl = True
        return t

    def dram(self, name, shape, dt, kind="Internal", nslot=1, sdim=0):
        h = self.nc.dram_tensor(name, list(shape), dt, kind=kind)
        return T(h, name, nslot, sdim)

    def _deps(self, eng, reads, writes):
        deps = []
        for b in reads:
            if b.last_w is not None:
                deps.append((b.last_w, "raw"))
            if b.excl:
                for r in b.readers:
                    if r.eng != eng:
                        deps.append((r, "rar"))
        for b in writes:
            if b.last_w is not None:
                deps.append((b.last_w, "waw"))
            for r in b.readers:
                deps.append((r, "war"))
        need = {}
        for d, kind in deps:
            if not d.is_dma and d.eng == eng:
                if eng in ("pe", "sp"):
                    continue
                if kind in ("war", "rar"):
                    continue
            k = id(d.sem)
            if k not in need or need[k][1] < d.val:
                need[k] = (d.sem, d.val)
        out = []
        w = self.waited[eng]
        for k, (s, v) in need.items():
            if w.get(k, 0) >= v:
                continue
            w[k] = v
            out.append((s, v))
        return out

    def _record(self, eng, fn, reads, writes, is_dma=False):
        op = Op()
        op.eng = eng
        op.fn = fn
        op.is_dma = is_dma
        op.waits = self._deps(eng, reads, writes)
        if is_dma:
            i = self.ndma[eng]
            self.ndma[eng] += 1
            R = self.N_DMA_SEM
            op.sem = self.dsem[eng][i % R]
            op.val = 16 * (i // R + 1)
            op.inc = 16
            if i >= R:
                k = id(op.sem)
                pv = 16 * (i // R)
                if self.waited[eng].get(k, 0) < pv:
                    self.waited[eng][k] = pv
                    op.waits.append((op.sem, pv))
        else:
            self.cnt[eng] += 1
            op.sem = self.sem[eng]
            op.val = self.cnt[eng]
            op.inc = 1
        self.semvals[id(op.sem)] = (op.sem, op.val)
        for b in reads:
            b.readers.append(op)
        for b in writes:
            b.last_w = op
            b.readers = []
        self.ops[eng].append(op)
        self.n_ops += 1
        return op

    def I(self, eng, meth, *, extra_reads=(), extra_writes=(), dma_like=False, **kw):
        reads, writes = [], []
        kws = {}
        for k, a in kw.items():
            if isinstance(a, V):
                if k.startswith("out") or k in ("accum_out", "ap"):
                    writes += a.bufs
                else:
                    reads += a.bufs
                kws[k] = a.ap
            else:
                kws[k] = a
        for a in extra_reads:
            reads += a.bufs
        for a in extra_writes:
            writes += a.bufs
        is_dma = meth == "dma_start" or dma_like

        def fn(e, meth=meth, kws=kws):
            return getattr(e, meth)(**kws)
        return self._record(eng, fn, reads, writes, is_dma)

    def dma(self, eng, out, in_, **kw):
        o = out if isinstance(out, V) else V(out, [])
        i = in_ if isinstance(in_, V) else V(in_, [])
        return self.I(eng, "dma_start", out=o, in_=i, **kw)

    def mm(self, out, lhsT, rhs, start=True, stop=True, acc_read=False, **kw):
        return self.I("pe", "matmul", out=out, lhsT=lhsT, rhs=rhs, start=start, stop=stop, **kw)

    def tr(self, out, in_, ident):
        return self.I("pe", "transpose", out=out, in_=in_, identity=ident)

    def act(self, out, in_, func, eng="act", **kw):
        return self.I(eng, "activation", out=out, in_=in_, func=func, **kw)

    def finish(self):
        nc = self.nc
        fin = []
        for k, (s, v) in self.semvals.items():
            if self.waited["sp"].get(k, 0) < v:
                fin.append((s, v))
        with nc.Block() as block:
            def emit(eng_obj, name):
                for op in self.ops[name]:
                    for (s, v) in op.waits:
                        eng_obj.wait_ge(s, v)
                    ins = op.fn(eng_obj)
                    ins.then_inc(op.sem, op.inc)
                if name == "sp":
                    for (s, v) in fin:
                        eng_obj.wait_ge(s, v)

            if self.ops["sp"] or True:
                @block.sync
                def _(e):
                    emit(e, "sp")
            if self.ops["pe"]:
                @block.tensor
                def _(e):
                    emit(e, "pe")
            if self.ops["act"]:
                @block.scalar
                def _(e):
                    emit(e, "act")
            if self.ops["dve"]:
                @block.vector
                def _(e):
                    emit(e, "dve")
            if self.ops["pool"]:
                @block.gpsimd
                def _(e):
                    emit(e, "pool")
        self.es.close()


D = 1024
NOWN = 4096
NCTX = 256
NT = NOWN + NCTX
EPS = 1e-6
import ml_dtypes
NPBF = ml_dtypes.bfloat16


def mk_consts(P, inp):
    C = {}
    id32 = P.sbuf("id32", [128, 128], F32)
    P.dma("sp", id32[:, :], inp["ident"])
    idb = P.sbuf("idb", [128, 128], BF16)
    P.I("dve", "tensor_copy", out=idb[:, :], in_=id32[:, :])
    ones = P.sbuf("ones", [1, 128], F32)
    P.I("dve", "memset", ap=ones[:, :], constant=1.0)
    C["id32"], C["idb"], C["ones"] = id32, idb, ones
    return C


def load_w_bf16(P, w_dram, ncols, name, stage, engs=("dve", "pool")):
    wb = P.sbuf(name, [128, 8, ncols], BF16)
    wv = w_dram.rearrange("(j p) c -> p j c", p=128)
    nb = (ncols + 511) // 512
    for b in range(nb):
        c0, c1 = b * 512, min(ncols, (b + 1) * 512)
        for jh in range(2):
            st = stage[(2 * b + jh) % len(stage)]
            P.dma(("sp", "pool")[jh], st[:, :, 0:c1 - c0], wv[:, jh * 4:jh * 4 + 4, c0:c1])
            P.I(engs[(2 * b + jh) % len(engs)], "tensor_copy", out=wb[:, jh * 4:jh * 4 + 4, c0:c1], in_=st[:, :, 0:c1 - c0])
    return wb


def ada_bcast(P, C, inp, nblk, modes, dests, stage, psA, gkey):
    ccol = P.sbuf("ccol", [128, 16], F32)
    P.dma("sp", ccol[:, :], inp["ccols"])
    csig = P.sbuf("csig", [128, 16], F32)
    P.act(csig[:, :], ccol[:, :], AF.Sigmoid)
    cact = P.sbuf("cact", [128, 16], BF16)
    P.I("dve", "tensor_tensor", out=cact[:, :], in0=ccol[:, :], in1=csig[:, :], op=ALU.mult)
    wv = inp["ada_w"].rearrange("(j p) c -> p j c", p=128)
    wbb = P.sbuf("adawb", [128, 8, 512], BF16)
    brow = [P.sbuf(f"brow{i}", [1, 512], F32) for i in range(2)]
    grow = [P.sbuf(f"grow{i}", [1, 512], F32) for i in range(2)]
    row = [P.sbuf(f"arow{i}", [1, 512], F32) for i in range(2)]
    row2 = [P.sbuf(f"arow2{i}", [1, 512], F32) for i in range(2)]
    k = 0
    for b in range(nblk):
        for jh in range(2):
            st = stage[(2 * b + jh) % len(stage)]
            P.dma(("sp", "pool")[jh], st[:, :, :], wv[:, jh * 4:jh * 4 + 4, b * 512:(b + 1) * 512])
            P.I("dve", "tensor_copy", out=wbb[:, jh * 4:jh * 4 + 4, :], in_=st[:, :, :])
        P.dma("sp", brow[b % 2][:, :], inp["ada_b"][:, b * 512:(b + 1) * 512])
        mode = modes[b]
        if mode != "plain":
            gc = (b * 512) % D
            P.dma("sp", grow[b % 2][:, :], inp[gkey][:, gc:gc + 512])
        for v in range(2):
            for j in range(8):
                P.mm(psA[0:1, :], lhsT=cact[:, v * 8 + j:v * 8 + j + 1], rhs=wbb[:, j, :], start=(j == 0), stop=(j == 7))
            r = row[k % 2]
            P.I("dve", "tensor_tensor", out=r[:, :], in0=psA[0:1, :], in1=brow[b % 2][:, :], op=ALU.add)
            if mode == "onep_g":
                r2 = row2[k % 2]
                P.I("dve", "scalar_tensor_tensor", out=r2[:, :], in0=r[:, :], scalar=1.0, in1=grow[b % 2][:, :], op0=ALU.add, op1=ALU.mult)
                r = r2
            elif mode == "g":
                r2 = row2[k % 2]
                P.I("dve", "tensor_tensor", out=r2[:, :], in0=r[:, :], in1=grow[b % 2][:, :], op=ALU.mult)
                r = r2
            k += 1
            dt_, dc = dests[b][v]
            P.mm(psA[:, :], lhsT=C["ones"][:, :], rhs=r[0:1, :])
            P.I("dve", "tensor_copy", out=dt_[:, dc:dc + 512], in_=psA[:, :])


def pre_norm_rows(P, C, inp, _unused, stage, psA):
    A = [P.sbuf(f"Abc{v}", [128, 1024], F32) for v in range(2)]
    B = [P.sbuf(f"Bbc{v}", [128, 1024], F32) for v in range(2)]
    dests = [[(B[0], 0), (B[1], 0)], [(B[0], 512), (B[1], 512)], [(A[0], 0), (A[1], 0)], [(A[0], 512), (A[1], 512)]]
    ada_bcast(P, C, inp, 4, ["plain", "plain", "onep_g", "onep_g"], dests, stage, psA, "norm_pre")
    return [A[0], B[0], A[1], B[1]]


class PreCtx:
    pass


def emit_pre_group(P, C, K, x_view_fn, ntile, AB, gi):
    hT = K.hT[gi % 2]
    for i in range(ntile):
        k = K.cnt
        K.cnt += 1
        xt = K.xt[k % 2]
        P.dma("sp", xt[:, :], x_view_fn(i))
        sq = K.sq
        P.act(sq[:, :], xt[:, :], AF.Square)
        ssq = K.ssq[k % 2]
        P.I("dve", "tensor_reduce", out=ssq[:, 0:1], in_=sq[:, :], axis=AX.X, op=ALU.add)
        P.I("dve", "tensor_scalar", out=ssq[:, 1:2], in0=ssq[:, 0:1], scalar1=1.0 / D, scalar2=EPS, op0=ALU.mult, op1=ALU.add)
        P.act(ssq[:, 3:4], ssq[:, 1:2], AF.Sqrt)
        P.I("dve", "reciprocal", out=ssq[:, 2:3], in_=ssq[:, 3:4])
        t32 = K.t32
        P.I("dve", "scalar_tensor_tensor", out=t32[:, :], in0=xt[:, :], scalar=ssq[:, 2:3], in1=AB[0][:, :], op0=ALU.mult, op1=ALU.mult)
        hb = K.hb[k % 2]
        P.I("pool", "tensor_tensor", out=hb[:, :], in0=t32[:, :], in1=AB[1][:, :], op=ALU.add)
        ptr = K.ptr[k % 2]
        for j in range(8):
            P.tr(ptr[:, j * 128:(j + 1) * 128], hb[:, j * 128:(j + 1) * 128], C["idb"][:, :])
        P.act(hT.v(hT.h[:, :, i * 128:(i + 1) * 128]), ptr.v(ptr.h[:, :].rearrange("p (j t) -> p j t", j=8)), AF.Copy)
    return hT


def alloc_pre(P):
    K = PreCtx()
    K.cnt = 0
    K.xt = [P.sbuf(f"xt{i}", [128, D], F32) for i in range(2)]
    K.sq = P.sbuf("sq", [128, D], BF16)
    K.ssq = [P.sbuf(f"ssq{i}", [128, 4], F32) for i in range(2)]
    K.t32 = P.sbuf("t32", [128, D], F32)
    K.hb = [P.sbuf(f"hb{i}", [128, D], BF16) for i in range(2)]
    K.hT = [P.sbuf(f"hT{i}", [128, 8, 512], BF16) for i in range(2)]
    K.ptr = [P.psum(f"ptr{i}", [128, D], BF16) for i in range(2)]
    return K


def dram_in(nc, name, shape, dt=F32):
    return nc.dram_tensor(name, list(shape), dt, kind="ExternalInput").ap()


def dram_out(nc, name, shape, dt):
    return nc.dram_tensor(name, list(shape), dt, kind="ExternalOutput").ap()


STOP = 99


def build_A_even(n_own=NOWN, n_ctx=NCTX):
    nt = n_own + n_ctx
    nc = bass.Bass("TRN2", target_bir_lowering=False)
    P = Prog(nc)
    inp = dict(
        x=dram_in(nc, "x", [n_own, D]), ctx=dram_in(nc, "ctx", [n_ctx, D]), ccols=dram_in(nc, "ccols", [128, 16]),
        ada_w=dram_in(nc, "ada_w", [D, 2048]), ada_b=dram_in(nc, "ada_b", [1, 2048]), norm_pre=dram_in(nc, "norm_pre", [1, D]),
        w_in=dram_in(nc, "w_in", [D, 3584]), ident=dram_in(nc, "ident", [128, 128]), rmat=dram_in(nc, "rmat", [128, 128]),
        cosT=dram_in(nc, "cosT", [128, n_own]), sinT=dram_in(nc, "sinT", [128, n_own]))
    tm_d = dram_out(nc, "tm", [nt, 2048], BF16)
    fm_d = dram_out(nc, "fm", [12, 128, nt], BF16)
    C = mk_consts(P, inp)
    stage = [P.sbuf(f"stage{i}", [128, 4, 512], F32) for i in range(2)]
    psA = P.psum("psA", [128, 512], F32)
    if STOP == 0:
        P.finish()
        return nc, P
    AB = pre_norm_rows(P, C, inp, None, stage, psA)
    if STOP == 1:
        P.finish()
        return nc, P
    wb = load_w_bf16(P, inp["w_in"], 3584, "wb", stage)
    rm32 = P.sbuf("rm32", [128, 128], F32)
    P.dma("sp", rm32[:, :], inp["rmat"])
    rmb = P.sbuf("rmb", [128, 128], BF16)
    P.I("dve", "tensor_copy", out=rmb[:, :], in_=rm32[:, :])
    if STOP == 2:
        P.finish()
        return nc, P
    K = alloc_pre(P)
    pst = [P.psum(f"pst{i}", [128, 512], F32) for i in range(2)]
    psf = [P.psum(f"psf{i}", [128, 512], F32) for i in range(2)]
    psr = P.psum("psr", [128, 512], F32)
    tmt = [P.sbuf(f"tmt{i}", [128, 2048], BF16) for i in range(2)]
    cs = [P.sbuf(f"cs{i}", [128, 512], F32) for i in range(2)]
    sn = [P.sbuf(f"sn{i}", [128, 512], F32) for i in range(2)]
    qraw = [P.sbuf(f"qraw{i}", [128, 512], BF16) for i in range(2)]
    t1 = [P.sbuf(f"t1{i}", [128, 512], F32) for i in range(2)]
    t2 = [P.sbuf(f"t2{i}", [128, 512], F32) for i in range(2)]
    fmo = [P.sbuf(f"fmo{i}", [128, 512], BF16) for i in range(2)]
    TMB = [(0, 512, [(0, 0, 256, False), (256, 256, 256, True)]),
           (2048, 512, [(0, 512, 512, False)]),
           (2560, 512, [(0, 1024, 256, False), (256, 1280, 256, True)]),
           (3072, 512, [(0, 1536, 512, True)])]
    groups = [(g * 512, 4, False) for g in range(n_own // 512)] + [(n_own, n_ctx // 128, True)]
    kt = 0
    kf = 0
    for gi, (tok0, ntile, is_ctx) in enumerate(groups):
        n = ntile * 128
        if is_ctx:
            xfn = lambda i: inp["ctx"][i * 128:(i + 1) * 128, :]
            ab = AB[2:4]
        else:
            xfn = lambda i, tok0=tok0: inp["x"][tok0 + i * 128: tok0 + (i + 1) * 128, :]
            ab = AB[0:2]
            P.dma("sp", cs[gi % 2][:, 0:n], inp["cosT"][:, tok0:tok0 + n])
            P.dma("sp", sn[gi % 2][:, 0:n], inp["sinT"][:, tok0:tok0 + n])
        hT = emit_pre_group(P, C, K, xfn, ntile, ab, gi)
        if STOP == 3:
            break
        for i in range(ntile):
            tt = tmt[(gi * 4 + i) % 2]
            for (wc0, wn, eps_) in TMB:
                ps = pst[kt % 2]
                kt += 1
                for j in range(8):
                    P.mm(ps[:, 0:wn], lhsT=hT[:, j, i * 128:(i + 1) * 128], rhs=wb[:, j, wc0:wc0 + wn], start=(j == 0), stop=(j == 7))
                for (pc, dc, nn, silu) in eps_:
                    if silu:
                        P.act(tt[:, dc:dc + nn], ps[:, pc:pc + nn], AF.Silu)
                    else:
                        P.I("dve", "tensor_copy", out=tt[:, dc:dc + nn], in_=ps[:, pc:pc + nn])
            P.dma("sp", tm_d[tok0 + i * 128: tok0 + (i + 1) * 128, :], tt[:, :])
        if STOP == 4:
            break
        for blk in range(12):
            wc0 = 512 + blk * 128
            ps = psf[kf % 2]
            for j in range(8):
                P.mm(ps[:, 0:n], lhsT=wb[:, j, wc0:wc0 + 128], rhs=hT[:, j, 0:n], start=(j == 0), stop=(j == 7))
            fo = fmo[kf % 2]
            if is_ctx or STOP == 7:
                P.act(fo[:, 0:n], ps[:, 0:n], AF.Copy)
            else:
                qr = qraw[kf % 2]
                P.act(qr[:, 0:n], ps[:, 0:n], AF.Copy)
                if STOP not in (8, 9):
                    P.mm(psr[:, 0:n], lhsT=rmb[:, :], rhs=qr[:, 0:n])
                P.I("dve", "tensor_tensor", out=t1[kf % 2][:, 0:n], in0=(qr if STOP == 9 else ps)[:, 0:n], in1=cs[gi % 2][:, 0:n], op=ALU.mult, extra_reads=([qr[:, 0:n]] if STOP == 10 else []))
                P.I("dve", "tensor_tensor", out=t2[kf % 2][:, 0:n], in0=(psr if STOP not in (8, 9) else (ps if STOP == 8 else qr))[:, 0:n], in1=sn[gi % 2][:, 0:n], op=ALU.mult)
                P.I("dve" if STOP == 5 else "pool", "tensor_tensor", out=fo[:, 0:n], in0=t1[kf % 2][:, 0:n], in1=t2[kf % 2][:, 0:n], op=ALU.add)
            if STOP != 6:
                P.dma("sp", fm_d[blk, :, tok0:tok0 + n], fo[:, 0:n])
            kf += 1
    P.finish()
    return nc, P


def post_setup(P, C, inp, stage, psM):
    G = [P.sbuf(f"Gbc{v}", [128, 1024], F32) for v in range(2)]
    dests = [[(G[0], 0), (G[1], 0)], [(G[0], 512), (G[1], 512)]]
    ada_bcast(P, C, inp, 2, ["g", "g"], dests, stage, psM, "norm_post")
    wo = load_w_bf16(P, inp["w_out"], 1024, "wo", stage)
    K = PreCtx()
    K.G = G
    K.wo = wo
    K.mixt = [P.sbuf(f"mixt{i}", [128, D], BF16) for i in range(2)]
    K.mixT = [P.sbuf(f"mixT{i}", [128, 8, 128], BF16) for i in range(2)]
    K.xt = [P.sbuf(f"pxt{i}", [128, D], F32) for i in range(2)]
    K.sq = P.sbuf("psq", [128, D], BF16)
    K.ssq = [P.sbuf(f"pssq{i}", [128, 4], F32) for i in range(2)]
    K.t32 = P.sbuf("pt32", [128, D], F32)
    K.xo = [P.sbuf(f"pxo{i}", [128, D], F32) for i in range(2)]
    K.cnt = 0
    return K


def post_tile(P, C, K, mix_src, x_src, x_dst, v, psT, psY, mix_in_sbuf=None):
    k = K.cnt
    K.cnt += 1
    if mix_in_sbuf is None:
        mt = K.mixt[k % 2]
        P.dma("sp", mt[:, :], mix_src)
    else:
        mt = mix_in_sbuf
    for j in range(8):
        P.tr(psT[:, j * 128:(j + 1) * 128], mt[:, j * 128:(j + 1) * 128], C["idb"][:, :])
    mT = K.mixT[k % 2]
    P.act(mT.v(mT.h[:, :, :]), psT.v(psT.h[:, :].rearrange("p (j t) -> p j t", j=8)), AF.Copy)
    for cb in range(2):
        for j in range(8):
            P.mm(psY[:, cb * 512:(cb + 1) * 512], lhsT=mT[:, j, :], rhs=K.wo[:, j, cb * 512:(cb + 1) * 512], start=(j == 0), stop=(j == 7))
    xt = K.xt[k % 2]
    P.dma("sp", xt[:, :], x_src)
    P.act(K.sq[:, :], psY[:, 0:1024], AF.Square)
    ssq = K.ssq[k % 2]
    P.I("dve", "tensor_reduce", out=ssq[:, 0:1], in_=K.sq[:, :], axis=AX.X, op=ALU.add)
    P.I("dve", "tensor_scalar", out=ssq[:, 1:2], in0=ssq[:, 0:1], scalar1=1.0 / D, scalar2=EPS, op0=ALU.mult, op1=ALU.add)
    P.act(ssq[:, 3:4], ssq[:, 1:2], AF.Sqrt)
    P.I("dve", "reciprocal", out=ssq[:, 2:3], in_=ssq[:, 3:4])
    P.I("dve", "scalar_tensor_tensor", out=K.t32[:, :], in0=psY[:, 0:1024], scalar=ssq[:, 2:3], in1=K.G[v][:, :], op0=ALU.mult, op1=ALU.mult)
    xo = K.xo[k % 2]
    P.I("pool", "tensor_tensor", out=xo[:, :], in0=K.t32[:, :], in1=xt[:, :], op=ALU.add)
    P.dma("sp", x_dst, xo[:, :])


def build_B_even(n_own=NOWN, n_lat=4 * NOWN, n_ctx=NCTX):
    nt = n_own + n_ctx
    nk = n_lat + n_ctx
    nkb = nk // 128
    nc = bass.Bass("TRN2", target_bir_lowering=False)
    P = Prog(nc)
    inp = dict(
        x=dram_in(nc, "x", [n_own, D]), ctx=dram_in(nc, "ctx", [n_ctx, D]), ccols=dram_in(nc, "ccols", [128, 16]),
        ada_w=dram_in(nc, "ada_w", [D, 1024]), ada_b=dram_in(nc, "ada_b", [1, 1024]), norm_post=dram_in(nc, "norm_post", [1, D]),
        w_out=dram_in(nc, "w_out", [D, D]), ident=dram_in(nc, "ident", [128, 128]),
        qT=dram_in(nc, "qT", [6, 128, nt], BF16), kT=dram_in(nc, "kT", [6, 128, nk], BF16), vall=dram_in(nc, "vall", [nk, 768], BF16),
        gates=dram_in(nc, "gates", [nt, 1024], BF16), uh=dram_in(nc, "uh", [n_own + 16, 256], BF16), uch=dram_in(nc, "uch", [n_ctx + 16, 256], BF16),
        bandA=dram_in(nc, "bandA", [128, 5 * 4 * 128]), bandB=dram_in(nc, "bandB", [16, 5 * 4 * 128]),
        pool_w=dram_in(nc, "pool_w", [64, 4 * 64]), pool_scale=dram_in(nc, "pool_scale", [1, 256]),
        lamp=dram_in(nc, "lamp", [1, 256]), subln=dram_in(nc, "subln", [1, 128]), lamc=dram_in(nc, "lamc", [1, 2]))
    xo_d = dram_out(nc, "xo", [n_own, D], F32)
    co_d = dram_out(nc, "co", [n_ctx, D], F32)
    mix_d = P.dram("mixd", [nt, D], BF16, nslot=nt // 128, sdim=0)
    C = mk_consts(P, inp)
    stage = [P.sbuf(f"stage{i}", [128, 4, 512], F32) for i in range(2)]
    PSA = P.psum("PSA", [128, 3 * 512], F32, nslot=3)
    PSO = P.psum("PSO", [128, 3 * 512], F32, nslot=3)
    PST = P.psum("PST", [128, D], BF16)
    PSM = P.psum("PSM", [128, 512], F32)
    KP = post_setup(P, C, inp, stage, PSM)

    lamp = P.sbuf("lamp", [1, 256], F32)
    P.dma("sp", lamp[:, :], inp["lamp"])
    lamc = P.sbuf("lamc", [1, 2], F32)
    P.dma("sp", lamc[:, :], inp["lamc"])
    lw = P.sbuf("lw", [1, 16], F32)
    lpp = P.sbuf("lpp", [1, 128], F32)
    P.I("dve", "tensor_tensor", out=lpp[:, 0:64], in0=lamp[:, 0:64], in1=lamp[:, 64:128], op=ALU.mult)
    P.I("dve", "tensor_tensor", out=lpp[:, 64:128], in0=lamp[:, 128:192], in1=lamp[:, 192:256], op=ALU.mult)
    P.I("dve", "tensor_reduce", out=lw[:, 0:1], in_=lpp[:, 0:64], axis=AX.X, op=ALU.add)
    P.I("dve", "tensor_reduce", out=lw[:, 1:2], in_=lpp[:, 64:128], axis=AX.X, op=ALU.add)
    P.act(lw[:, 2:4], lw[:, 0:2], AF.Exp)
    P.I("dve", "tensor_tensor", out=lw[:, 4:5], in0=lw[:, 2:3], in1=lw[:, 3:4], op=ALU.subtract)
    P.I("dve", "tensor_tensor", out=lw[:, 5:6], in0=lw[:, 4:5], in1=lamc[:, 0:1], op=ALU.add)
    P.I("dve", "tensor_scalar", out=lw[:, 6:7], in0=lw[:, 5:6], scalar1=-1.0, scalar2=None, op0=ALU.mult)
    neglam = P.sbuf("neglam", [128, 1], F32)
    P.mm(PSM[:, 0:1], lhsT=C["ones"][:, :], rhs=lw[0:1, 6:7])
    P.I("dve", "tensor_copy", out=neglam[:, :], in_=PSM[:, 0:1])
    sub = P.sbuf("sub", [1, 128], F32)
    P.dma("sp", sub[:, :], inp["subln"])
    sub2 = P.sbuf("sub2", [1, 128], F32)
    P.I("dve", "tensor_scalar", out=sub2[:, :], in0=sub[:, :], scalar1=lamc[:, 1:2], scalar2=None, op0=ALU.mult)
    subg = P.sbuf("subg", [128, 128], F32)
    P.mm(PSM[:, 0:128], lhsT=C["ones"][:, :], rhs=sub2[0:1, :])
    P.I("dve", "tensor_copy", out=subg[:, :], in_=PSM[:, 0:128])
    psc = P.sbuf("psc", [1, 256], F32)
    P.dma("sp", psc[:, :], inp["pool_scale"])
    pscb = P.sbuf("pscb", [128, 256], F32)
    P.mm(PSM[:, 0:256], lhsT=C["ones"][:, :], rhs=psc[0:1, :])
    P.I("dve", "tensor_copy", out=pscb[:, :], in_=PSM[:, 0:256])

    bA = P.sbuf("bA", [128, 2560], BF16)
    bB = P.sbuf("bB", [16, 2560], BF16)
    sflat = [st.h[:, :, :].rearrange("p a b -> p (a b)") for st in stage]
    P.dma("sp", stage[0].v(sflat[0][:, 0:2048]), inp["bandA"][:, 0:2048])
    P.I("dve", "tensor_copy", out=bA[:, 0:2048], in_=stage[0].v(sflat[0][:, 0:2048]))
    P.dma("sp", stage[1].v(sflat[1][:, 0:512]), inp["bandA"][:, 2048:2560])
    P.I("dve", "tensor_copy", out=bA[:, 2048:2560], in_=stage[1].v(sflat[1][:, 0:512]))
    P.dma("sp", stage[0].v(sflat[0][0:16, 0:2048]), inp["bandB"][:, 0:2048])
    P.I("dve", "tensor_copy", out=bB[:, 0:2048], in_=stage[0].v(sflat[0][0:16, 0:2048]))
    P.dma("sp", stage[1].v(sflat[1][0:16, 0:512]), inp["bandB"][:, 2048:2560])
    P.I("dve", "tensor_copy", out=bB[:, 2048:2560], in_=stage[1].v(sflat[1][0:16, 0:512]))
    pw32 = P.sbuf("pw32", [64, 256], F32)
    P.dma("sp", pw32[:, :], inp["pool_w"])
    pw = P.sbuf("pw", [64, 256], BF16)
    P.I("dve", "tensor_copy", out=pw[:, :], in_=pw32[:, :])
    uA = [P.sbuf(f"uA{i}", [128, 256], BF16) for i in range(2)]
    uB = [P.sbuf(f"uB{i}", [16, 256], BF16) for i in range(2)]
    dT = [P.sbuf(f"dT{i}", [64, 512], BF16) for i in range(2)]
    gat = [P.sbuf(f"gat{i}", [128, 256], BF16) for i in range(2)]
    ytmp = [P.sbuf(f"ytmp{i}", [128, 256], F32) for i in range(2)]
    ya = [P.sbuf(f"ya{i}", [128, 256], BF16) for i in range(2)]
    ntile_own = n_own // 128
    ntile_ctx = n_ctx // 128
    tiles = [("x", i) for i in range(ntile_own)] + [("c", i) for i in range(ntile_ctx)]
    for k, (kind, i) in enumerate(tiles):
        if kind == "x":
            src = inp["uh"]
            typ = 0 if i == 0 else (2 if i == ntile_own - 1 else 1)
            row0 = i * 128
            tok0 = i * 128
        else:
            src = inp["uch"]
            typ = 3 if i == 0 else 4
            row0 = i * 128
            tok0 = n_own + i * 128
        P.dma("sp", uA[k % 2][:, :], src[row0:row0 + 128, :])
        P.dma("sp", uB[k % 2][:, :], src[row0 + 128:row0 + 144, :])
        P.dma("sp", gat[k % 2][:, :], inp["gates"][tok0:tok0 + 128, 0:256])
        for g in range(4):
            bo = (typ * 4 + g) * 128
            P.mm(PSM[0:64, g * 128:(g + 1) * 128], lhsT=uA[k % 2][:, g * 64:(g + 1) * 64], rhs=bA[:, bo:bo + 128], start=True, stop=False)
            P.mm(PSM[0:64, g * 128:(g + 1) * 128], lhsT=uB[k % 2][:, g * 64:(g + 1) * 64], rhs=bB[:, bo:bo + 128], start=False, stop=True)
        P.act(dT[k % 2][:, :], PSM[0:64, :], AF.Copy)
        yps = PSA[:, 1024:1024 + 256]
        for g in range(4):
            P.mm(PSA[:, 1024 + g * 64:1024 + (g + 1) * 64], lhsT=dT[k % 2][:, g * 128:(g + 1) * 128], rhs=pw[:, g * 64:(g + 1) * 64])
        P.I("dve", "tensor_tensor", out=ytmp[k % 2][:, :], in0=yps, in1=pscb[:, :], op=ALU.mult)
        P.I("pool", "tensor_tensor", out=ya[k % 2][:, :], in0=ytmp[k % 2][:, :], in1=gat[k % 2][:, :], op=ALU.mult)
        P.dma("sp", mix_d[tok0:tok0 + 128, 0:256], ya[k % 2][:, :])

    kTs = P.sbuf("kTs", [128, nk], BF16)
    vau = P.sbuf("vau", [128, nkb, 128], BF16)
    qz = [[P.sbuf(f"qz{i}_{m}", [128, 512], BF16) for m in range(2)] for i in range(2)]
    for i in range(2):
        for m in range(2):
            P.I("pool", "memset", ap=qz[i][m][:, :], constant=0.0)
    pT = P.sbuf("pT", [128, 3 * 512], BF16, nslot=3)
    dacc = [KP.xt[m] for m in range(2)]
    onec = P.sbuf("onec", [128, 1], F32)
    P.I("dve", "memset", ap=onec[:, :], constant=1.0)
    onecb = P.sbuf("onecb", [128, 1], BF16)
    P.I("dve", "memset", ap=onecb[:, :], constant=1.0)
    ones33 = P.sbuf("ones33", [33, 128], F32)
    P.I("dve", "memset", ap=ones33[:, :], constant=1.0)
    rrow = P.sbuf("rrow", [33, 512], F32, nslot=1)
    rbc = [KP.xo[m] for m in range(2)]
    tt1 = KP.t32
    tt2 = T(KP.t32.h, "t32b")
    tt2.bufs = KP.t32.bufs
    oTb = P.sbuf("oTb", [128, 512], BF16)
    gb = [P.sbuf(f"gb{i}", [128, 128], BF16) for i in range(2)]
    osq = P.sbuf("osq", [128, 128], BF16)
    oss = [P.sbuf(f"oss{i}", [128, 4], F32) for i in range(2)]
    o3 = [P.sbuf(f"o3{i}", [128, 128], F32) for i in range(2)]
    ob = [P.sbuf(f"ob{i}", [128, 128], BF16) for i in range(2)]
    vsrc = inp["vall"].rearrange("(kb p) c -> p kb c", p=128)
    qblocks = [(qb * 512, 512, list(range(nkb))) for qb in range(n_own // 512)] + \
              [(n_own + c0, min(512, n_ctx - c0), list(range(n_lat // 128, nkb))) for c0 in range(0, n_ctx, 512)]
    kq = 0
    kf = 0
    for h in range(6):
        nq_ = 4
        for qi in range(nq_):
            k0, k1 = (nk * qi // nq_) // 128 * 128, (nk * (qi + 1) // nq_) // 128 * 128 if qi < nq_ - 1 else nk
            P.dma(("sp", "pool")[qi % 2], kTs[:, k0:k1], inp["kT"][h, :, k0:k1])
        for ci, c0 in enumerate(range(0, nkb, 32)):
            c1 = min(nkb, c0 + 32)
            P.dma(("sp", "pool")[ci % 2], vau.v(vau.h[:, c0:c1, :]), vsrc[:, c0:c1, h * 128:(h + 1) * 128])
        for (q0, qn, kbs) in qblocks:
            nqs = qn // 128
            qt = qz[kq % 2]
            kq += 1
            for m in range(2):
                P.dma("sp", qt[m][m * 64:(m + 1) * 64, 0:qn], inp["qT"][h, m * 64:(m + 1) * 64, q0:q0 + qn])
            its = [(kb, m) for kb in kbs for m in range(2)]

            def QK(it):
                kb, m = its[it]
                s = it % 3
                P.mm(PSA[:, s * 512:s * 512 + qn], lhsT=kTs[:, kb * 128:(kb + 1) * 128], rhs=qt[m][:, 0:qn])

            pe_den = [False, False]
            QK(0)
            if len(its) > 1:
                QK(1)
            for it, (kb, m) in enumerate(its):
                s = it % 3
                if it + 2 < len(its):
                    QK(it + 2)
                P.act(pT[:, s * 512:s * 512 + qn], PSA[:, s * 512:s * 512 + qn], AF.Exp, scale=0.125)
                P.mm(PSO[:, m * 512:m * 512 + qn], lhsT=vau[:, kb, :], rhs=pT[:, s * 512:s * 512 + qn], start=(kb == kbs[0]), stop=(kb == kbs[-1]))
                if (kb - kbs[0]) % 3 == 2:
                    P.mm(PSM[32 * m:32 * m + 1, 0:qn], lhsT=onecb[:, :], rhs=pT[:, s * 512:s * 512 + qn], start=(not pe_den[m]), stop=False)
                    pe_den[m] = True
                elif kb == kbs[0]:
                    P.I("dve", "tensor_copy", out=dacc[m][:, 0:qn], in_=pT[:, s * 512:s * 512 + qn])
                else:
                    P.I("dve", "tensor_tensor", out=dacc[m][:, 0:qn], in0=dacc[m][:, 0:qn], in1=pT[:, s * 512:s * 512 + qn], op=ALU.add)
            for m in range(2):
                pr = 32 * m
                P.mm(PSM[pr:pr + 1, 0:qn], lhsT=onec[:, :], rhs=dacc[m][:, 0:qn], start=(not pe_den[m]), stop=True)
                P.I("dve", "reciprocal", out=rrow[pr:pr + 1, 0:qn], in_=PSM[pr:pr + 1, 0:qn])
                if m == 1:
                    P.I("dve", "tensor_scalar", out=rrow[pr:pr + 1, 0:qn], in0=rrow[pr:pr + 1, 0:qn], scalar1=neglam[pr:pr + 1, 0:1], scalar2=None, op0=ALU.mult)
                P.mm(PSO[:, 1024:1024 + qn], lhsT=ones33[pr:pr + 1, :], rhs=rrow[pr:pr + 1, 0:qn])
                P.act(rbc[m][:, 0:qn], PSO[:, 1024:1024 + qn], AF.Copy)
            P.I("dve", "tensor_tensor", out=tt1[:, 0:qn], in0=PSO[:, 0:qn], in1=rbc[0][:, 0:qn], op=ALU.mult)
            P.I("dve", "tensor_tensor", out=tt2[:, 512:512 + qn], in0=PSO[:, 512:512 + qn], in1=rbc[1][:, 0:qn], op=ALU.mult)
            P.I("pool", "tensor_tensor", out=oTb[:, 0:qn], in0=tt1[:, 0:qn], in1=tt2[:, 512:512 + qn], op=ALU.add)
            for qs in range(nqs):
                P.tr(PST[:, qs * 128:(qs + 1) * 128], oTb[:, qs * 128:(qs + 1) * 128], C["idb"][:, :])
            for qs in range(nqs):
                tok0 = q0 + qs * 128
                o2v = PST[:, qs * 128:(qs + 1) * 128]
                P.dma("sp", gb[kf % 2][:, :], inp["gates"][tok0:tok0 + 128, 256 + h * 128:256 + (h + 1) * 128])
                P.act(osq[:, :], o2v, AF.Square)
                ss = oss[kf % 2]
                P.I("dve", "tensor_reduce", out=ss[:, 0:1], in_=osq[:, :], axis=AX.X, op=ALU.add)
                P.I("dve", "tensor_scalar", out=ss[:, 1:2], in0=ss[:, 0:1], scalar1=1.0 / 128, scalar2=EPS, op0=ALU.mult, op1=ALU.add)
                P.act(ss[:, 3:4], ss[:, 1:2], AF.Sqrt)
                P.I("dve", "reciprocal", out=ss[:, 2:3], in_=ss[:, 3:4])
                P.I("dve", "scalar_tensor_tensor", out=o3[kf % 2][:, :], in0=o2v, scalar=ss[:, 2:3], in1=subg[:, :], op0=ALU.mult, op1=ALU.mult)
                P.I("pool", "tensor_tensor", out=ob[kf % 2][:, :], in0=o3[kf % 2][:, :], in1=gb[kf % 2][:, :], op=ALU.mult)
                P.dma("sp", mix_d[tok0:tok0 + 128, 256 + h * 128:256 + (h + 1) * 128], ob[kf % 2][:, :])
                kf += 1

    for k, (kind, i) in enumerate(tiles):
        if kind == "x":
            post_tile(P, C, KP, mix_d[i * 128:(i + 1) * 128, :], inp["x"][i * 128:(i + 1) * 128, :], xo_d[i * 128:(i + 1) * 128, :], 0, PST, PSA)
        else:
            t0 = n_own + i * 128
            post_tile(P, C, KP, mix_d[t0:t0 + 128, :], inp["ctx"][i * 128:(i + 1) * 128, :], co_d[i * 128:(i + 1) * 128, :], 1, PST, PSA)
    P.finish()
    return nc, P


def build_A_odd(layer, n_own=NOWN, n_ctx=NCTX):
    nt = n_own + n_ctx
    nc = bass.Bass("TRN2", target_bir_lowering=False)
    P = Prog(nc)
    inp = dict(
        x=dram_in(nc, "x", [n_own, D]), ctx=dram_in(nc, "ctx", [n_ctx, D]), ccols=dram_in(nc, "ccols", [128, 16]),
        ada_w=dram_in(nc, "ada_w", [D, 2048]), ada_b=dram_in(nc, "ada_b", [1, 2048]), norm_pre=dram_in(nc, "norm_pre", [1, D]),
        w_in=dram_in(nc, "w_in", [D, 4096]), ident=dram_in(nc, "ident", [128, 128]), lbraw=dram_in(nc, "lbraw", [128, 32]))
    tm_d = dram_out(nc, "tm", [nt, 1536], BF16)
    fmb_d = dram_out(nc, "fmb", [4, 128, nt], BF16)
    fmf_d = dram_out(nc, "fmf", [12, 128, nt], F32)
    C = mk_consts(P, inp)
    stage = [P.sbuf(f"stage{i}", [128, 4, 512], F32) for i in range(2)]
    psA = P.psum("psA", [128, 512], F32)
    AB = pre_norm_rows(P, C, inp, None, stage, psA)
    wb = load_w_bf16(P, inp["w_in"], 4096, "wb", stage)
    lbr = P.sbuf("lbr", [128, 32], F32)
    P.dma("sp", lbr[:, :], inp["lbraw"])
    lbe = P.sbuf("lbe", [128, 32], F32)
    P.act(lbe[:, :], lbr[:, :], AF.Exp)
    ev = lambda li: lbe.v(lbe.h[:, :].rearrange("p (d l h) -> p d l h", d=2, l=4)[:, :, li, :])
    den = P.sbuf("lbden", [128, 8], F32)
    num = P.sbuf("lbnum", [128, 8], F32)
    v3 = lambda t: t.v(t.h[:, :].rearrange("p (d h) -> p d h", d=2))
    P.I("dve", "tensor_tensor", out=v3(den), in0=ev(0), in1=ev(1), op=ALU.add)
    P.I("dve", "tensor_tensor", out=v3(den), in0=v3(den), in1=ev(2), op=ALU.add)
    P.I("dve", "tensor_tensor", out=v3(den), in0=v3(den), in1=ev(3), op=ALU.add)
    P.I("dve", "tensor_copy", out=v3(num), in_=ev(1))
    for li in range(2, layer + 1):
        P.I("dve", "tensor_tensor", out=v3(num), in0=v3(num), in1=ev(li), op=ALU.add)
    rden = P.sbuf("lbrden", [128, 8], F32)
    P.I("dve", "reciprocal", out=rden[:, :], in_=den[:, :])
    lb = P.sbuf("lb", [128, 8], F32)
    P.I("dve", "tensor_tensor", out=lb[:, :], in0=num[:, :], in1=rden[:, :], op=ALU.mult)
    oml = P.sbuf("oml", [128, 8], F32)
    P.I("dve", "tensor_scalar", out=oml[:, :], in0=lb[:, :], scalar1=-1.0, scalar2=1.0, op0=ALU.mult, op1=ALU.add)

    K = alloc_pre(P)
    pst = [P.psum(f"pst{i}", [128, 512], F32) for i in range(2)]
    psf = [P.psum(f"psf{i}", [128, 512], F32) for i in range(3)]
    tmt = [P.sbuf(f"tmt{i}", [128, 1536], BF16) for i in range(2)]
    sg = [P.sbuf(f"sg{i}", [128, 512], F32) for i in range(2)]
    fob = [P.sbuf(f"fob{i}", [128, 512], BF16) for i in range(2)]
    fof = [P.sbuf(f"fof{i}", [128, 512], F32) for i in range(2)]
    TMB = [(1024, 512, 0, True), (3584, 512, 512, True), (3072, 512, 1024, False)]
    groups = [(g * 512, 4, False) for g in range(n_own // 512)] + [(n_own, n_ctx // 128, True)]
    kt = 0
    kf = 0
    kb_ = 0
    kff = 0
    for gi, (tok0, ntile, is_ctx) in enumerate(groups):
        n = ntile * 128
        if is_ctx:
            xfn = lambda i: inp["ctx"][i * 128:(i + 1) * 128, :]
            ab = AB[2:4]
        else:
            xfn = lambda i, tok0=tok0: inp["x"][tok0 + i * 128: tok0 + (i + 1) * 128, :]
            ab = AB[0:2]
        hT = emit_pre_group(P, C, K, xfn, ntile, ab, gi)
        for i in range(ntile):
            tt = tmt[(gi * 4 + i) % 2]
            for (wc0, wn, dc, silu) in TMB:
                ps = pst[kt % 2]
                kt += 1
                for j in range(8):
                    P.mm(ps[:, 0:wn], lhsT=hT[:, j, i * 128:(i + 1) * 128], rhs=wb[:, j, wc0:wc0 + wn], start=(j == 0), stop=(j == 7))
                if silu:
                    P.act(tt[:, dc:dc + wn], ps[:, 0:wn], AF.Silu)
                else:
                    P.I("dve", "tensor_copy", out=tt[:, dc:dc + wn], in_=ps[:, 0:wn])
            P.dma("sp", tm_d[tok0 + i * 128: tok0 + (i + 1) * 128, :], tt[:, :])

        def fproj(wc0):
            nonlocal kf
            ps = psf[kf % 3]
            kf += 1
            for j in range(8):
                P.mm(ps[:, 0:n], lhsT=wb[:, j, wc0:wc0 + 128], rhs=hT[:, j, 0:n], start=(j == 0), stop=(j == 7))
            return ps
        for c in range(4):
            pa = fproj(c * 128)
            pb = fproj(512 + c * 128)
            s_ = sg[kb_ % 2]
            P.act(s_[:, 0:n], pb[:, 0:n], AF.Sigmoid)
            fo = fob[kb_ % 2]
            kb_ += 1
            P.I("dve", "tensor_tensor", out=fo[:, 0:n], in0=pa[:, 0:n], in1=s_[:, 0:n], op=ALU.mult)
            P.dma("sp", fmb_d[c, :, tok0:tok0 + n], fo[:, 0:n])
        for hh in range(4):
            pq = fproj(1536 + hh * 128)
            fo = fof[kff % 2]
            kff += 1
            P.act(fo[:, 0:n], pq[:, 0:n], AF.Silu)
            P.dma("sp", fmf_d[hh, :, tok0:tok0 + n], fo[:, 0:n])
            for dr in range(2):
                pf = fproj(2048 + dr * 512 + hh * 128)
                s_ = sg[kb_ % 2]
                kb_ += 1
                P.act(s_[:, 0:n], pf[:, 0:n], AF.Sigmoid)
                fo = fof[kff % 2]
                kff += 1
                ci = dr * 4 + hh
                P.I("dve", "tensor_scalar", out=fo[:, 0:n], in0=s_[:, 0:n], scalar1=oml[:, ci:ci + 1], scalar2=lb[:, ci:ci + 1], op0=ALU.mult, op1=ALU.add)
                P.dma("sp", fmf_d[4 + dr * 4 + hh, :, tok0:tok0 + n], fo[:, 0:n])
    P.finish()
    return nc, P


def build_H(n_lat=4 * NOWN, n_ctx=NCTX):
    N = n_lat + n_ctx
    SEG = min(2048, n_lat)
    nc = bass.Bass("TRN2", target_bir_lowering=False)
    P = Prog(nc)
    inp = dict(qT=dram_in(nc, "qT", [128, N]), fTf=dram_in(nc, "fTf", [128, N]), fTb=dram_in(nc, "fTb", [128, N]),
               v=dram_in(nc, "v", [N, 128], BF16), ogs=dram_in(nc, "ogs", [N, 128], BF16), gn=dram_in(nc, "gn", [1, 128]),
               ident=dram_in(nc, "ident", [128, 128]), masks=dram_in(nc, "masks", [64, 128]), cmask=dram_in(nc, "cmask", [128, SEG]))
    yd_d = dram_out(nc, "yd", [N, 128], BF16)
    o_d = [P.dram(f"od{d_}", [N, 128], F32, nslot=N // 64, sdim=0) for d_ in range(2)]
    C = mk_consts(P, inp)
    msk = P.sbuf("msk", [64, 128], F32)
    P.dma("sp", msk[:, :], inp["masks"])
    cm = P.sbuf("cm", [128, SEG], F32)
    P.dma("sp", cm[:, :], inp["cmask"])
    gnr = P.sbuf("gnr", [1, 128], F32)
    P.dma("sp", gnr[:, :], inp["gn"])
    p_ds_pre = [P.psum(f"p_ds{i}", [128, 128], F32) for i in range(2)]
    psg = p_ds_pre[0]
    gbc = P.sbuf("gbc", [128, 128], F32)
    P.mm(psg[:, :], lhsT=C["ones"][:, :], rhs=gnr[0:1, :])
    P.I("dve", "tensor_copy", out=gbc[:, :], in_=psg[:, :])
    segs = [(0, n_ctx)] + [(n_ctx + i * SEG, SEG) for i in range(n_lat // SEG)]

    def sweep(dirn):
        bwd = dirn == 1
        d_ = f"d{dirn}"
        qs = P.sbuf("qs" + d_, [128, SEG], F32)
        fs = P.sbuf("fs" + d_, [128, SEG], F32)
        lfs = P.sbuf("lfs" + d_, [128, SEG], F32)
        ks = P.sbuf("ks" + d_, [128, SEG], F32)
        Pc = P.sbuf("Pc" + d_, [128, SEG], F32)
        Pe = P.sbuf("Pe" + d_, [128, SEG], F32) if bwd else None
        vs = P.sbuf("vs" + d_, [64, SEG // 64, 128], BF16)
        S = [P.sbuf(f"S{i}" + d_, [128, 128], F32) for i in range(2)]
        nE = P.sbuf("nE" + d_, [128, SEG], F32)
        X = P.sbuf("X" + d_, [128, SEG // 64, 2], F32)
        EX = P.sbuf("EX" + d_, [128, SEG // 64, 2], F32)
        eq = [P.sbuf(f"eq{i}" + d_, [128, 64], F32) for i in range(2)]
        ek = [P.sbuf(f"ek{i}" + d_, [128, 64], F32) for i in range(2)]
        ekl = [P.sbuf(f"ekl{i}" + d_, [128, 64], F32) for i in range(2)]
        qt = [P.sbuf(f"qt{i}" + d_, [128, 64], BF16) for i in range(2)]
        kt = [P.sbuf(f"kt{i}" + d_, [128, 64], BF16) for i in range(2)]
        kh = [P.sbuf(f"kh{i}" + d_, [128, 64], BF16) for i in range(2)]
        khs = [P.sbuf(f"khs{i}" + d_, [64, 128], BF16) for i in range(2)]
        S0m = [P.sbuf(f"S0m{i}" + d_, [128, 128], BF16) for i in range(2)]
        attm = [P.sbuf(f"attm{i}" + d_, [64, 64], BF16) for i in range(2)]
        osb = [P.sbuf(f"osb{i}" + d_, [64, 128], F32) for i in range(2)]
        p_kh = P.psum("p_kh" + d_, [64, 128], BF16)
        p_att = P.psum("p_att" + d_, [64, 64], F32)
        p_o = P.psum("p_o" + d_, [64, 128], F32)
        p_ds = p_ds_pre[dirn]
        P.I("dve", "memset", ap=S[0][:, :], constant=0.0)
        cur = 0
        kc = 0
        order = segs if not bwd else [segs[0]] + segs[1:][::-1]
        E = Pe if bwd else Pc
        for (a0, sl) in order:
            P.dma("sp", qs[:, 0:sl], inp["qT"][:, a0:a0 + sl])
            P.dma("sp", fs[:, 0:sl], inp["fTb" if bwd else "fTf"][:, a0:a0 + sl])
            P.dma("sp", vs.v(vs.h[:, 0:sl // 64, :]), inp["v"][a0:a0 + sl, :].rearrange("(n p) c -> p n c", p=64))
            P.act(lfs[:, 0:sl], fs[:, 0:sl], AF.Ln)
            P.I("pool", "tensor_scalar", out=ks[:, 0:sl], in0=fs[:, 0:sl], scalar1=-1.0, scalar2=1.0, op0=ALU.mult, op1=ALU.add)
            P.I("dve", "tensor_tensor_scan", out=Pc[:, 0:sl], data0=cm[:, 0:sl], data1=lfs[:, 0:sl], initial=0.0, op0=ALU.mult, op1=ALU.add)
            if bwd:
                P.I("dve", "tensor_tensor", out=Pe[:, 0:sl], in0=Pc[:, 0:sl], in1=lfs[:, 0:sl], op=ALU.subtract)
            nch = sl // 64
            P.I("pool", "tensor_scalar", out=nE[:, 0:sl], in0=E[:, 0:sl], scalar1=-1.0, scalar2=None, op0=ALU.mult)
            col = lambda t_, c_: t_.v(t_.h[:, 0:sl].rearrange("p (n c) -> p n c", c=64)[:, :, c_])
            if bwd:
                P.I("dve", "tensor_tensor", out=X.v(X.h[:, 0:nch, 0]), in0=col(Pc, 63), in1=col(Pe, 31), op=ALU.subtract)
            else:
                P.I("dve", "tensor_copy", out=X.v(X.h[:, 0:nch, 0]), in_=col(Pc, 31))
            P.I("dve", "tensor_copy", out=X.v(X.h[:, 0:nch, 1]), in_=col(Pc, 63))
            P.act(EX.v(EX.h[:, 0:nch, :]), X.v(X.h[:, 0:nch, :]), AF.Exp)
            chunks = list(range(sl // 64))
            if bwd:
                chunks = chunks[::-1]
            for n_ in chunks:
                a = n_ * 64
                k2 = kc % 2
                kc += 1
                Ec = E[:, a:a + 64]
                pmid = E[:, a + 31:a + 32]
                nmid = nE[:, a + 31:a + 32]
                tot = Pc[:, a + 63:a + 64]
                if bwd:
                    P.act(eq[k2][:, :], Ec, AF.Exp, scale=-1.0, bias=pmid)
                    P.act(ek[k2][:, :], Ec, AF.Exp, scale=1.0, bias=nmid)
                    P.act(ekl[k2][:, :], Ec, AF.Exp)
                else:
                    P.act(eq[k2][:, :], Ec, AF.Exp, scale=1.0, bias=nmid)
                    P.act(ek[k2][:, :], Ec, AF.Exp, scale=-1.0, bias=pmid)
                    P.act(ekl[k2][:, :], Ec, AF.Exp, scale=-1.0, bias=tot)
                P.I("dve", "tensor_tensor", out=qt[k2][:, :], in0=qs[:, a:a + 64], in1=eq[k2][:, :], op=ALU.mult)
                P.I("pool", "tensor_tensor", out=kt[k2][:, :], in0=ks[:, a:a + 64], in1=ek[k2][:, :], op=ALU.mult)
                P.I("pool", "tensor_tensor", out=kh[k2][:, :], in0=ks[:, a:a + 64], in1=ekl[k2][:, :], op=ALU.mult)
                P.tr(p_kh[:, :], kh[k2][:, :], C["idb"][:, :])
                P.act(khs[k2][:, :], p_kh[:, :], AF.Copy)
                P.I("dve", "tensor_scalar", out=S0m[k2][:, :], in0=S[cur][:, :], scalar1=EX[:, n_, 0:1], scalar2=None, op0=ALU.mult)
                P.mm(p_att[:, :], lhsT=kt[k2][:, :], rhs=qt[k2][:, :])
                mo = 64 if bwd else 0
                P.I("dve", "tensor_tensor", out=attm[k2][:, :], in0=p_att[:, :], in1=msk[:, mo:mo + 64], op=ALU.mult)
                P.mm(p_o[:, :], lhsT=qt[k2][:, :], rhs=S0m[k2][:, :], start=True, stop=False)
                P.mm(p_o[:, :], lhsT=attm[k2][:, :], rhs=vs[:, n_, :], start=False, stop=True)
                P.mm(p_ds[:, :], lhsT=khs[k2][:, :], rhs=vs[:, n_, :])
                P.I("dve", "scalar_tensor_tensor", out=S[1 - cur][:, :], in0=S[cur][:, :], scalar=EX[:, n_, 1:2], in1=p_ds[:, :], op0=ALU.mult, op1=ALU.add)
                cur = 1 - cur
                r0 = a0 + a
                P.act(osb[k2][:, :], p_o[:, :], AF.Copy)
                P.dma("sp", o_d[dirn][r0:r0 + 64, :], osb[k2][:, :])
                yield

    gens = [sweep(0), sweep(1)]
    alive = [True, True]
    while any(alive):
        for d_ in range(2):
            if alive[d_]:
                try:
                    next(gens[d_])
                except StopIteration:
                    alive[d_] = False
    o1 = [P.sbuf(f"co1{i}", [128, 128], F32) for i in range(2)]
    o2 = [P.sbuf(f"co2{i}", [128, 128], F32) for i in range(2)]
    og = [P.sbuf(f"cog{i}", [128, 128], BF16) for i in range(2)]
    osm = [P.sbuf(f"cosm{i}", [128, 128], F32) for i in range(2)]
    osq = P.sbuf("hosq", [128, 128], BF16)
    oss = [P.sbuf(f"hoss{i}", [128, 4], F32) for i in range(2)]
    on = [P.sbuf(f"on{i}", [128, 128], F32) for i in range(2)]
    yb = [P.sbuf(f"yb{i}", [128, 128], BF16) for i in range(2)]
    for t in range(N // 128):
        k2 = t % 2
        r0 = t * 128
        P.dma("sp", o1[k2][:, :], o_d[0][r0:r0 + 128, :])
        P.dma("sp", o2[k2][:, :], o_d[1][r0:r0 + 128, :])
        P.dma("sp", og[k2][:, :], inp["ogs"][r0:r0 + 128, :])
        P.I("dve", "tensor_tensor", out=osm[k2][:, :], in0=o1[k2][:, :], in1=o2[k2][:, :], op=ALU.add)
        P.act(osq[:, :], osm[k2][:, :], AF.Square)
        ss = oss[k2]
        P.I("dve", "tensor_reduce", out=ss[:, 0:1], in_=osq[:, :], axis=AX.X, op=ALU.add)
        P.I("dve", "tensor_scalar", out=ss[:, 1:2], in0=ss[:, 0:1], scalar1=1.0 / 128, scalar2=EPS, op0=ALU.mult, op1=ALU.add)
        P.act(ss[:, 3:4], ss[:, 1:2], AF.Sqrt)
        P.I("dve", "reciprocal", out=ss[:, 2:3], in_=ss[:, 3:4])
        P.I("dve", "scalar_tensor_tensor", out=on[k2][:, :], in0=osm[k2][:, :], scalar=ss[:, 2:3], in1=gbc[:, :], op0=ALU.mult, op1=ALU.mult)
        P.I("pool", "tensor_tensor", out=yb[k2][:, :], in0=on[k2][:, :], in1=og[k2][:, :], op=ALU.mult)
        P.dma("sp", yd_d[r0:r0 + 128, :], yb[k2][:, :])
    P.finish()
    return nc, P


CSTOP = 0


def build_C_odd(n_own=NOWN, n_ctx=NCTX, with_ctx=True):
    nt = n_own + n_ctx
    nc = bass.Bass("TRN2", target_bir_lowering=False)
    P = Prog(nc)
    inp = dict(
        x=dram_in(nc, "x", [n_own, D]), ctx=dram_in(nc, "ctx", [n_ctx, D]), ccols=dram_in(nc, "ccols", [128, 16]),
        ada_w=dram_in(nc, "ada_w", [D, 1024]), ada_b=dram_in(nc, "ada_b", [1, 1024]), norm_post=dram_in(nc, "norm_post", [1, D]),
        w_out=dram_in(nc, "w_out", [D, D]), ident=dram_in(nc, "ident", [128, 128]),
        gluh=dram_in(nc, "gluh", [4, 128, n_own + 30], BF16), gluch=dram_in(nc, "gluch", [4, 128, n_ctx + 30], BF16),
        cgs=dram_in(nc, "cgs", [nt, 512], BF16), yd=dram_in(nc, "yd", [nt, 512], BF16),
        convw=dram_in(nc, "convw", [128, 124]), convb=dram_in(nc, "convb", [128, 4]),
        lng=dram_in(nc, "lng", [1, 512]), lnb=dram_in(nc, "lnb", [1, 512]))
    xo_d = dram_out(nc, "xo", [n_own, D], F32)
    co_d = dram_out(nc, "co", [n_ctx, D], F32)
    C = mk_consts(P, inp)
    stage = [P.sbuf(f"stage{i}", [128, 4, 512], F32) for i in range(2)]
    PSY = P.psum("PSY", [128, 1024], F32)
    PST = P.psum("PST", [128, D], BF16)
    PSM = P.psum("PSM", [128, 512], F32)
    PSZ = [P.psum(f"PSZ{i}", [128, 512], BF16) for i in range(2)]
    KP = post_setup(P, C, inp, stage, PSM)
    cw = P.sbuf("cw", [128, 124], F32)
    P.dma("sp", cw[:, :], inp["convw"])
    cb = P.sbuf("cb", [128, 4], F32)
    P.dma("sp", cb[:, :], inp["convb"])
    lrow = P.sbuf("lrow", [1, 1024], F32)
    P.dma("sp", lrow[:, 0:512], inp["lng"])
    P.dma("sp", lrow[:, 512:1024], inp["lnb"])
    lgb = P.sbuf("lgb", [128, 1024], F32)
    for hh in range(2):
        P.mm(PSM[:, :], lhsT=C["ones"][:, :], rhs=lrow[0:1, hh * 512:(hh + 1) * 512])
        P.I("dve", "tensor_copy", out=lgb[:, hh * 512:(hh + 1) * 512], in_=PSM[:, :])
    glu = [P.sbuf(f"glu{i}", [128, 4, 542], BF16) for i in range(2)]
    dg = P.sbuf("dg", [128, 124, 128], BF16)
    for k_ in range(124):
        P.I("dve" if k_ % 2 == 0 else "pool", "tensor_scalar", out=dg[:, k_, :], in0=C["id32"][:, :], scalar1=cw[:, k_:k_ + 1], scalar2=None, op0=ALU.mult)
    PSC = [P.psum(f"PSC{i}", [128, 512], F32) for i in range(2)]
    zs = [P.sbuf(f"zs{i}", [128, 512], F32) for i in range(2)]
    zb = [P.sbuf(f"zb{i}", [128, 4, 512], BF16) for i in range(2)]
    zsq = P.sbuf("zsq", [128, 512], F32)
    st_ = [P.sbuf(f"lst{i}", [128, 8], F32) for i in range(2)]
    zn = [P.sbuf(f"zn{i}", [128, 512], F32) for i in range(2)]
    zl = [P.sbuf(f"zl{i}", [128, 512], F32) for i in range(2)]
    zsl = [P.sbuf(f"zsl{i}", [128, 512], F32) for i in range(2)]
    cg = [P.sbuf(f"cg{i}", [128, 512], BF16) for i in range(2)]
    mixs = [P.sbuf(f"mixs{i}", [128, D], BF16) for i in range(2)]
    blocks = [("x", b0, 512) for b0 in range(0, n_own, 512)]
    if with_ctx:
        blocks += [("c", 0, n_ctx)]
    kt = 0
    for bi, (kind, b0, nb) in enumerate(blocks):
        g_ = glu[bi % 2]
        src = inp["gluh"] if kind == "x" else inp["gluch"]
        P.dma("sp", g_.v(g_.h[:, :, 0:nb + 30]), src[:, :, b0:b0 + nb + 30].rearrange("c p t -> p c t"))
        for c in range(4):
            pc = PSC[(bi * 4 + c) % 2]
            for w in range(31):
                P.mm(pc[:, 0:nb], lhsT=dg[:, c * 31 + w, :], rhs=g_[:, c, w:w + nb], start=(w == 0), stop=(w == 30))
            P.act(zb[bi % 2][:, c, 0:nb], pc[:, 0:nb], AF.Identity, bias=cb[:, c:c + 1])
        if CSTOP == 1:
            continue
        for i in range(nb // 128):
            k2 = kt % 2
            kt += 1
            tok0 = (b0 if kind == "x" else n_own) + i * 128
            pz = PSZ[k2]
            for c in range(4):
                P.tr(pz[:, c * 128:(c + 1) * 128], zb[bi % 2][:, c, i * 128:(i + 1) * 128], C["idb"][:, :])
            P.act(zs[k2][:, :], pz[:, :], AF.Copy)
            if CSTOP == 2:
                continue
            s = st_[k2]
            P.I("dve", "tensor_reduce", out=s[:, 0:1], in_=zs[k2][:, :], axis=AX.X, op=ALU.add)
            P.act(zsq[:, :], zs[k2][:, :], AF.Square)
            P.I("dve", "tensor_reduce", out=s[:, 1:2], in_=zsq[:, :], axis=AX.X, op=ALU.add)
            P.I("dve", "tensor_scalar", out=s[:, 2:3], in0=s[:, 0:1], scalar1=1.0 / 512, scalar2=None, op0=ALU.mult)
            P.I("dve", "tensor_tensor", out=s[:, 3:4], in0=s[:, 2:3], in1=s[:, 2:3], op=ALU.mult)
            P.I("dve", "scalar_tensor_tensor", out=s[:, 4:5], in0=s[:, 1:2], scalar=1.0 / 512, in1=s[:, 3:4], op0=ALU.mult, op1=ALU.subtract)
            P.I("dve", "tensor_scalar", out=s[:, 5:6], in0=s[:, 4:5], scalar1=EPS, scalar2=None, op0=ALU.add)
            P.act(s[:, 6:7], s[:, 5:6], AF.Sqrt)
            P.I("dve", "reciprocal", out=s[:, 7:8], in_=s[:, 6:7])
            if CSTOP == 3:
                continue
            P.I("dve", "tensor_scalar", out=zn[k2][:, :], in0=zs[k2][:, :], scalar1=s[:, 2:3], scalar2=s[:, 7:8], op0=ALU.subtract, op1=ALU.mult)
            P.I("pool", "tensor_tensor", out=zl[k2][:, :], in0=zn[k2][:, :], in1=lgb[:, 0:512], op=ALU.mult)
            P.I("pool", "tensor_tensor", out=zl[k2][:, :], in0=zl[k2][:, :], in1=lgb[:, 512:1024], op=ALU.add)
            P.act(zsl[k2][:, :], zl[k2][:, :], AF.Silu)
            P.dma("sp", cg[k2][:, :], inp["cgs"][tok0:tok0 + 128, :])
            mt = mixs[k2]
            P.dma("sp", mt[:, 512:1024], inp["yd"][tok0:tok0 + 128, :])
            P.I("pool", "tensor_tensor", out=mt[:, 0:512], in0=zsl[k2][:, :], in1=cg[k2][:, :], op=ALU.mult)
            if CSTOP == 4:
                continue
            if kind == "x":
                post_tile(P, C, KP, None, inp["x"][tok0:tok0 + 128, :], xo_d[tok0:tok0 + 128, :], 0, PST, PSY, mix_in_sbuf=mt)
            else:
                r0 = i * 128
                post_tile(P, C, KP, None, inp["ctx"][r0:r0 + 128, :], co_d[r0:r0 + 128, :], 1, PST, PSY, mix_in_sbuf=mt)
    if not with_ctx:
        for i in range(n_ctx // 128):
            xt = KP.xt[i % 2]
            P.dma("sp", xt[:, :], inp["ctx"][i * 128:(i + 1) * 128, :])
            P.dma("sp", co_d[i * 128:(i + 1) * 128, :], xt[:, :])
    P.finish()
    return nc, P


GRID_W = 64
import math
def rope_tables(tok0, n):
    t = np.arange(tok0, tok0 + n)
    row = (t // GRID_W).astype(np.float32); col = (t % GRID_W).astype(np.float32)
    inv = (10000.0 ** (-np.arange(16, dtype=np.float32) / 16)).astype(np.float32)
    cosT = np.zeros((128, n), np.float32); sinT = np.zeros((128, n), np.float32)
    for r in range(128):
        d = r % 64
        pos = row if d < 32 else col
        ang = (pos * inv[d % 16]).astype(np.float32)
        cosT[r] = np.cos(ang); sinT[r] = np.sin(ang)
    return cosT, sinT
def rope_rmat():
    R = np.zeros((128, 128), np.float32)
    for i in range(128):
        d = i % 32
        if d < 16:
            R[i + 16, i] = -1.0
        else:
            R[i - 16, i] = 1.0
    return R
def ccols(c_b, c_ctx):
    return np.concatenate([c_b.reshape(8, 128).T, c_ctx.reshape(8, 128).T], axis=1).astype(np.float32).copy()

POOL_WINDOWS = (2, 4, 8, 16)
def pool_bands(L, tile0, is_first, is_last):
    out = np.zeros((4, 144, 128), np.float32)
    for g, w in enumerate(POOL_WINDOWS):
        for c in range(128):
            t = tile0 + c
            lo = max(t - w // 2, 0); hi = min(t + w // 2 - 1, L - 1) + 1
            cnt = hi - lo
            for s in range(lo, hi):
                r = s - (tile0 - 8)
                out[g, r, c] += 1.0 / cnt
            out[g, c + 8, c] -= 1.0
    return out
def band_pack(L_lat, own0, n_own, L_ctx):
    types = [pool_bands(L_lat, own0, True, False), pool_bands(L_lat, own0 + 128 if n_own > 256 else own0 + 128, False, False),
             pool_bands(L_lat, own0 + n_own - 128, False, True), pool_bands(L_ctx, 0, True, False), pool_bands(L_ctx, L_ctx - 128, False, True)]
    A = np.zeros((128, 2560), np.float32); B = np.zeros((16, 2560), np.float32)
    for t, bm in enumerate(types):
        for g in range(4):
            A[:, (t * 4 + g) * 128:(t * 4 + g + 1) * 128] = bm[g, :128]
            B[:, (t * 4 + g) * 128:(t * 4 + g + 1) * 128] = bm[g, 128:]
    return A, B
def halo(u_seq, a, n):
    L = u_seq.shape[0]
    out = np.zeros((n + 16, u_seq.shape[1]), u_seq.dtype)
    lo = max(a - 8, 0); hi = min(a + n + 8, L)
    out[lo - (a - 8): hi - (a - 8)] = u_seq[lo:hi]
    return out


def halo15_T(seqT, a, n):
    L = seqT.shape[2]
    out = np.zeros((4, 128, n + 30), seqT.dtype)
    lo = max(a - 15, 0)
    hi = min(a + n + 15, L)
    out[:, :, lo - (a - 15):hi - (a - 15)] = seqT[:, :, lo:hi]
    return out


_PROGS = {}
_DEBUG_HOOK = None


def _prog(key, fn):
    if key not in _PROGS:
        _PROGS[key] = fn()[0]
    return _PROGS[key]


def _run(nc, in_maps):
    res = run_bass_kernel_spmd(nc, in_maps, core_ids=list(range(8)))
    return res.results


def kernel(x, c, ctx, c_ctx, ada_w, ada_b, norm_pre, norm_post, w_in_even, w_out_even,
           pool_w, pool_scale, diff_lambda, diff_subln, w_in_odd, w_out_odd,
           conv_w, conv_b, conv_ln_g, conv_ln_b, hgrn_norm, hgrn_lb):
    f32 = lambda a: np.ascontiguousarray(np.asarray(a, dtype=np.float32))
    x, c, ctx, c_ctx, ada_w, ada_b = f32(x), f32(c), f32(ctx), f32(c_ctx), f32(ada_w), f32(ada_b)
    norm_pre, norm_post = f32(norm_pre), f32(norm_post)
    NB, L = x.shape[0], x.shape[1]
    S4 = 4
    ident = np.eye(128, dtype=np.float32)
    rmat = rope_rmat()
    xs = [np.ascontiguousarray(x[r // 4, (r % 4) * NOWN:(r % 4 + 1) * NOWN]) for r in range(8)]
    cs = [np.ascontiguousarray(ctx[b]) for b in range(NB)]
    cc = [ccols(c[b], c_ctx) for b in range(NB)]
    ropes = [rope_tables((r % 4) * NOWN, NOWN) for r in range(4)]
    bands = [band_pack(L, s * NOWN, NOWN, NCTX) for s in range(4)]
    masks = np.zeros((64, 128), np.float32)
    s_, t_ = np.meshgrid(np.arange(64), np.arange(64), indexing="ij")
    masks[:, :64] = (t_ >= s_)
    masks[:, 64:] = (s_ >= t_)
    cmask = np.ones((128, 2048), np.float32)
    cmask[:, ::64] = 0
    lbraw = np.ascontiguousarray(f32(hgrn_lb).reshape(2, 4, 4, 128).transpose(3, 0, 1, 2).reshape(128, 32))
    for l in range(4):
        j = l // 2
        aw_pre = np.ascontiguousarray(ada_w[l][:, :2048])
        ab_pre = np.ascontiguousarray(ada_b[l][None, :2048])
        aw_post = np.ascontiguousarray(ada_w[l][:, 2048:])
        ab_post = np.ascontiguousarray(ada_b[l][None, 2048:])
        npre = np.ascontiguousarray(norm_pre[l][None])
        npost = np.ascontiguousarray(norm_post[l][None])
        if l % 2 == 0:
            w_in = f32(w_in_even[j])
            ncA = _prog("A_even", build_A_even)
            ims = [dict(x=xs[r], ctx=cs[r // 4], ccols=cc[r // 4], ada_w=aw_pre, ada_b=ab_pre, norm_pre=npre, w_in=w_in,
                        ident=ident, rmat=rmat, cosT=ropes[r % 4][0], sinT=ropes[r % 4][1]) for r in range(8)]
            ra = _run(ncA, ims)
            ncB = _prog("B_even", build_B_even)
            lam_init = 0.8 - 0.6 * math.exp(-0.3 * l)
            pw = np.ascontiguousarray(f32(pool_w[j]).transpose(1, 0, 2).reshape(64, 256))
            ims = []
            for r in range(8):
                b, s = r // 4, r % 4
                grp = [ra[b * 4 + q] for q in range(4)]
                tm = ra[r]["tm"]
                kT = np.concatenate([g["fm"][6:12, :, :NOWN] for g in grp] + [ra[r]["fm"][6:12, :, NOWN:]], axis=2)
                vall = np.concatenate([g["tm"][:NOWN, 512:1280] for g in grp] + [tm[NOWN:, 512:1280]], axis=0)
                useq = np.concatenate([g["tm"][:NOWN, 0:256] for g in grp], axis=0)
                ims.append(dict(
                    x=xs[r], ctx=cs[b], ccols=cc[b], ada_w=aw_post, ada_b=ab_post, norm_post=npost, w_out=f32(w_out_even[j]), ident=ident,
                    qT=np.ascontiguousarray(ra[r]["fm"][0:6]), kT=np.ascontiguousarray(kT), vall=np.ascontiguousarray(vall),
                    gates=np.ascontiguousarray(np.concatenate([tm[:, 256:512], tm[:, 1280:2048]], axis=1)),
                    uh=halo(useq, s * NOWN, NOWN), uch=halo(np.ascontiguousarray(tm[NOWN:, 0:256]), 0, NCTX),
                    bandA=bands[s][0], bandB=bands[s][1], pool_w=pw, pool_scale=f32(pool_scale[j])[None].copy(),
                    lamp=f32(diff_lambda[j]).reshape(1, 256).copy(), subln=f32(diff_subln[j])[None].copy(),
                    lamc=np.array([[lam_init, 1.0 - lam_init]], np.float32)))
            rb = _run(ncB, ims)
            del ra
        else:
            ncA = _prog(("A_odd", l), lambda: build_A_odd(l))
            ims = [dict(x=xs[r], ctx=cs[r // 4], ccols=cc[r // 4], ada_w=aw_pre, ada_b=ab_pre, norm_pre=npre, w_in=f32(w_in_odd[j]),
                        ident=ident, lbraw=lbraw) for r in range(8)]
            ra = _run(ncA, ims)
            ncH = _prog("H", build_H)
            ims = []
            for r in range(8):
                b, hh = r // 4, r % 4
                grp = [ra[b * 4 + q] for q in range(4)]
                cat_f = lambda blk: np.ascontiguousarray(np.concatenate([grp[0]["fmf"][blk][:, NOWN:]] + [g["fmf"][blk][:, :NOWN] for g in grp], axis=1))
                cat_t = lambda c0: np.ascontiguousarray(np.concatenate([grp[0]["tm"][NOWN:, c0:c0 + 128]] + [g["tm"][:NOWN, c0:c0 + 128] for g in grp], axis=0))
                ims.append(dict(qT=cat_f(hh), fTf=cat_f(4 + hh), fTb=cat_f(8 + hh), v=cat_t(1024 + hh * 128), ogs=cat_t(512 + hh * 128),
                                gn=f32(hgrn_norm[j])[None, hh * 128:(hh + 1) * 128].copy(), ident=ident, masks=masks, cmask=cmask))
            rh = _run(ncH, ims)
            with_ctx = l < 3
            ncC = _prog(("C_odd", with_ctx), lambda: build_C_odd(NOWN, NCTX, with_ctx))
            cw = f32(conv_w[j])
            convw = np.ascontiguousarray(cw.T.reshape(4, 128, 31).transpose(1, 0, 2).reshape(128, 124))
            convb = np.ascontiguousarray(f32(conv_b[j]).reshape(4, 128).T)
            ims = []
            for r in range(8):
                b, s = r // 4, r % 4
                grp = [ra[b * 4 + q] for q in range(4)]
                gseq = np.concatenate([g["fmb"][:, :, :NOWN] for g in grp], axis=2)
                gctx = np.ascontiguousarray(ra[r]["fmb"][:, :, NOWN:])
                yd = np.concatenate([np.concatenate([rh[b * 4 + hh]["yd"][NCTX + s * NOWN: NCTX + (s + 1) * NOWN] for hh in range(4)], axis=1),
                                     np.concatenate([rh[b * 4 + hh]["yd"][0:NCTX] for hh in range(4)], axis=1)], axis=0)
                ims.append(dict(
                    x=xs[r], ctx=cs[b], ccols=cc[b], ada_w=aw_post, ada_b=ab_post, norm_post=npost, w_out=f32(w_out_odd[j]), ident=ident,
                    gluh=halo15_T(gseq, s * NOWN, NOWN), gluch=halo15_T(gctx, 0, NCTX),
                    cgs=np.ascontiguousarray(ra[r]["tm"][:, 0:512]), yd=np.ascontiguousarray(yd), convw=convw, convb=convb,
                    lng=f32(conv_ln_g[j])[None].copy(), lnb=f32(conv_ln_b[j])[None].copy()))
            rb = _run(ncC, ims)
            del ra, rh
        xs = [np.ascontiguousarray(rb[r]["xo"]) for r in range(8)]
        cs = [np.ascontiguousarray(rb[b * 4]["co"]) for b in range(NB)]
        if _DEBUG_HOOK is not None:
            _DEBUG_HOOK(l, xs, cs)
    out = np.stack([np.concatenate(xs[b * 4:(b + 1) * 4], axis=0) for b in range(NB)], axis=0)
    return out.astype(np.float32)
```

```python
import contextlib
import numpy as np
import concourse.bass as bass
import concourse.mybir as mybir
from concourse.bass_utils import run_bass_kernel_spmd

F32 = mybir.dt.float32
BF16 = mybir.dt.bfloat16
ALU = mybir.AluOpType
AF = mybir.ActivationFunctionType
AX = mybir.AxisListType


class Buf:
    __slots__ = ("name", "last_w", "readers", "excl")

    def __init__(self, name):
        self.name = name
        self.last_w = None
        self.readers = []
        self.excl = False


class V:
    __slots__ = ("ap", "bufs")

    def __init__(self, ap, bufs):
        self.ap = ap
        self.bufs = bufs


class T:
    def __init__(self, h, name, nslot=1, sdim=1):
        self.h = h
        self.name = name
        self.nslot = nslot
        self.sdim = sdim
        self.bufs = [Buf(f"{name}.{i}") for i in range(nslot)]
        self.shape = list(h.shape)

    def __getitem__(self, idx):
        ap = self.h[idx]
        if self.nslot == 1:
            return V(ap, self.bufs)
        if not isinstance(idx, tuple):
            idx = (idx,)
        bufs = self.bufs
        if len(idx) > self.sdim:
            s = idx[self.sdim]
            n = self.shape[self.sdim]
            per = n // self.nslot
            if isinstance(s, int):
                bufs = [self.bufs[s // per]]
            elif isinstance(s, slice):
                a = 0 if s.start is None else s.start
                b = n if s.stop is None else s.stop
                bufs = self.bufs[a // per:(b - 1) // per + 1]
        return V(ap, bufs)

    def v(self, ap):
        return V(ap, self.bufs)


class Op:
    __slots__ = ("eng", "fn", "waits", "sem", "val", "is_dma", "inc")


ENGS = ("pe", "act", "dve", "pool", "sp")


class Prog:
    N_DMA_SEM = 8

    def __init__(self, nc):
        self.nc = nc
        self.es = contextlib.ExitStack()
        self.ops = {e: [] for e in ENGS}
        self.cnt = {e: 0 for e in ENGS}
        self.ndma = {e: 0 for e in ENGS}
        self.sem = {}
        self.dsem = {}
        self.waited = {e: {} for e in ENGS}
        self.semvals = {}
        for e in ENGS:
            self.sem[e] = self.es.enter_context(nc.semaphore(f"s_{e}"))
        for e in ("sp", "pool", "act"):
            self.dsem[e] = [self.es.enter_context(nc.semaphore(f"d_{e}{i}")) for i in range(self.N_DMA_SEM)]
        self.n_ops = 0

    def sbuf(self, name, shape, dt, nslot=1, sdim=1):
        h = self.es.enter_context(self.nc.sbuf_tensor("sb_" + name, list(shape), dt))
        return T(h, name, nslot, sdim)

    def psum(self, name, shape, dt, nslot=1, sdim=1):
        h = self.es.enter_context(self.nc.psum_tensor("ps_" + name, list(shape), dt))
        t = T(h, name, nslot, sdim)
        for b in t.bufs:
            b.excl = True
        return t

    def dram(self, name, shape, dt, kind="Internal", nslot=1, sdim=0):
        h = self.nc.dram_tensor(name, list(shape), dt, kind=kind)
        return T(h, name, nslot, sdim)

    def _deps(self, eng, reads, writes, is_dma=False):
        deps = []
        for b in reads:
            if b.last_w is not None:
                deps.append((b.last_w, "raw"))
            if b.excl:
                for r in b.readers:
                    if r.eng != eng:
                        deps.append((r, "rar"))
        for b in writes:
            if b.last_w is not None:
                deps.append((b.last_w, "waw"))
            for r in b.readers:
                deps.append((r, "war"))
        need = {}
        for d, kind in deps:
            if not d.is_dma and d.eng == eng and not is_dma:
                if eng in ("pe", "sp"):
                    continue
                if kind in ("war", "rar"):
                    continue
            k = id(d.sem)
            if k not in need or need[k][1] < d.val:
                need[k] = (d.sem, d.val)
        out = []
        w = self.waited[eng]
        for k, (s, v) in need.items():
            if w.get(k, 0) >= v:
                continue
            w[k] = v
            out.append((s, v))
        return out

    def _record(self, eng, fn, reads, writes, is_dma=False):
        op = Op()
        op.eng = eng
        op.fn = fn
        op.is_dma = is_dma
        op.waits = self._deps(eng, reads, writes, is_dma)
        if is_dma:
            i = self.ndma[eng]
            self.ndma[eng] += 1
            R = self.N_DMA_SEM
            op.sem = self.dsem[eng][i % R]
            op.val = 16 * (i // R + 1)
            op.inc = 16
            if i >= R:
                k = id(op.sem)
                pv = 16 * (i // R)
                if self.waited[eng].get(k, 0) < pv:
                    self.waited[eng][k] = pv
                    op.waits.append((op.sem, pv))
        else:
            self.cnt[eng] += 1
            op.sem = self.sem[eng]
            op.val = self.cnt[eng]
            op.inc = 1
        self.semvals[id(op.sem)] = (op.sem, op.val)
        for b in reads:
            b.readers.append(op)
        for b in writes:
            b.last_w = op
            b.readers = []
        self.ops[eng].append(op)
        self.n_ops += 1
        return op

    def I(self, eng, meth, *, extra_reads=(), extra_writes=(), dma_like=False, **kw):
        reads, writes = [], []
        kws = {}
        for k, a in kw.items():
            if isinstance(a, V):
                if k.startswith("out") or k in ("accum_out", "ap"):
                    writes += a.bufs
                else:
                    reads += a.bufs
                kws[k] = a.ap
            else:
                kws[k] = a
        for a in extra_reads:
            reads += a.bufs
        for a in extra_writes:
            writes += a.bufs
        is_dma = meth == "dma_start" or dma_like

        def fn(e, meth=meth, kws=kws):
            return getattr(e, meth)(**kws)
        return self._record(eng, fn, reads, writes, is_dma)

    def dma(self, eng, out, in_, **kw):
        o = out if isinstance(out, V) else V(out, [])
        i = in_ if isinstance(in_, V) else V(in_, [])
        return self.I(eng, "dma_start", out=o, in_=i, **kw)

    def mm(self, out, lhsT, rhs, start=True, stop=True, acc_read=False, **kw):
        return self.I("pe", "matmul", out=out, lhsT=lhsT, rhs=rhs, start=start, stop=stop, **kw)

    def tr(self, out, in_, ident):
        return self.I("pe", "transpose", out=out, in_=in_, identity=ident)

    def act(self, out, in_, func, eng="act", **kw):
        return self.I(eng, "activation", out=out, in_=in_, func=func, **kw)

    def finish(self):
        nc = self.nc
        fin = []
        for k, (s, v) in self.semvals.items():
            if self.waited["sp"].get(k, 0) < v:
                fin.append((s, v))
        with nc.Block() as block:
            def emit(eng_obj, name):
                for op in self.ops[name]:
                    for (s, v) in op.waits:
                        eng_obj.wait_ge(s, v)
                    ins = op.fn(eng_obj)
                    ins.then_inc(op.sem, op.inc)
                if name == "sp":
                    for (s, v) in fin:
                        eng_obj.wait_ge(s, v)

            if self.ops["sp"] or True:
                @block.sync
                def _(e):
                    emit(e, "sp")
            if self.ops["pe"]:
                @block.tensor
                def _(e):
                    emit(e, "pe")
            if self.ops["act"]:
                @block.scalar
                def _(e):
                    emit(e, "act")
            if self.ops["dve"]:
                @block.vector
                def _(e):
                    emit(e, "dve")
            if self.ops["pool"]:
                @block.gpsimd
                def _(e):
                    emit(e, "pool")
        self.es.close()


D = 1024
NOWN = 4096
NCTX = 256
NT = NOWN + NCTX
EPS = 1e-6
import ml_dtypes
NPBF = ml_dtypes.bfloat16


def mk_consts(P, inp):
    C = {}
    id32 = P.sbuf("id32", [128, 128], F32)
    P.dma("sp", id32[:, :], inp["ident"])
    idb = P.sbuf("idb", [128, 128], BF16)
    P.I("dve", "tensor_copy", out=idb[:, :], in_=id32[:, :])
    ones = P.sbuf("ones", [1, 128], F32)
    P.I("dve", "memset", ap=ones[:, :], constant=1.0)
    C["id32"], C["idb"], C["ones"] = id32, idb, ones
    return C


def load_w_bf16(P, w_dram, ncols, name, stage, engs=("dve", "pool")):
    wb = P.sbuf(name, [128, 8, ncols], BF16)
    wv = w_dram.rearrange("(j p) c -> p j c", p=128)
    nb = (ncols + 511) // 512
    for b in range(nb):
        c0, c1 = b * 512, min(ncols, (b + 1) * 512)
        for jh in range(2):
            st = stage[(2 * b + jh) % len(stage)]
            P.dma(("sp", "pool")[jh], st[:, :, 0:c1 - c0], wv[:, jh * 4:jh * 4 + 4, c0:c1])
            P.I(engs[(2 * b + jh) % len(engs)], "tensor_copy", out=wb[:, jh * 4:jh * 4 + 4, c0:c1], in_=st[:, :, 0:c1 - c0])
    return wb


def ada_bcast(P, C, inp, nblk, modes, dests, stage, psA, gkey):
    ccol = P.sbuf("ccol", [128, 16], F32)
    P.dma("sp", ccol[:, :], inp["ccols"])
    csig = P.sbuf("csig", [128, 16], F32)
    P.act(csig[:, :], ccol[:, :], AF.Sigmoid)
    cact = P.sbuf("cact", [128, 16], BF16)
    P.I("dve", "tensor_tensor", out=cact[:, :], in0=ccol[:, :], in1=csig[:, :], op=ALU.mult)
    wv = inp["ada_w"].rearrange("(j p) c -> p j c", p=128)
    wbb = P.sbuf("adawb", [128, 8, 512], BF16)
    brow = [P.sbuf(f"brow{i}", [1, 512], F32) for i in range(2)]
    grow = [P.sbuf(f"grow{i}", [1, 512], F32) for i in range(2)]
    row = [P.sbuf(f"arow{i}", [1, 512], F32) for i in range(2)]
    row2 = [P.sbuf(f"arow2{i}", [1, 512], F32) for i in range(2)]
    k = 0
    for b in range(nblk):
        for jh in range(2):
            st = stage[(2 * b + jh) % len(stage)]
            P.dma(("sp", "pool")[jh], st[:, :, :], wv[:, jh * 4:jh * 4 + 4, b * 512:(b + 1) * 512])
            P.I("dve", "tensor_copy", out=wbb[:, jh * 4:jh * 4 + 4, :], in_=st[:, :, :])
        P.dma("sp", brow[b % 2][:, :], inp["ada_b"][:, b * 512:(b + 1) * 512])
        mode = modes[b]
        if mode != "plain":
            gc = (b * 512) % D
            P.dma("sp", grow[b % 2][:, :], inp[gkey][:, gc:gc + 512])
        for v in range(2):
            for j in range(8):
                P.mm(psA[0:1, :], lhsT=cact[:, v * 8 + j:v * 8 + j + 1], rhs=wbb[:, j, :], start=(j == 0), stop=(j == 7))
            r = row[k % 2]
            P.I("dve", "tensor_tensor", out=r[:, :], in0=psA[0:1, :], in1=brow[b % 2][:, :], op=ALU.add)
            if mode == "onep_g":
                r2 = row2[k % 2]
                P.I("dve", "scalar_tensor_tensor", out=r2[:, :], in0=r[:, :], scalar=1.0, in1=grow[b % 2][:, :], op0=ALU.add, op1=ALU.mult)
                r = r2
            elif mode == "g":
                r2 = row2[k % 2]
                P.I("dve", "tensor_tensor", out=r2[:, :], in0=r[:, :], in1=grow[b % 2][:, :], op=ALU.mult)
                r = r2
            k += 1
            dt_, dc = dests[b][v]
            P.mm(psA[:, :], lhsT=C["ones"][:, :], rhs=r[0:1, :])
            P.I("dve", "tensor_copy", out=dt_[:, dc:dc + 512], in_=psA[:, :])


def pre_norm_rows(P, C, inp, _unused, stage, psA):
    A = [P.sbuf(f"Abc{v}", [128, 1024], F32) for v in range(2)]
    B = [P.sbuf(f"Bbc{v}", [128, 1024], F32) for v in range(2)]
    dests = [[(B[0], 0), (B[1], 0)], [(B[0], 512), (B[1], 512)], [(A[0], 0), (A[1], 0)], [(A[0], 512), (A[1], 512)]]
    ada_bcast(P, C, inp, 4, ["plain", "plain", "onep_g", "onep_g"], dests, stage, psA, "norm_pre")
    return [A[0], B[0], A[1], B[1]]


class PreCtx:
    pass


def emit_pre_group(P, C, K, x_view_fn, ntile, AB, gi):
    hT = K.hT[gi % 2]
    for i in range(ntile):
        k = K.cnt
        K.cnt += 1
        xt = K.xt[k % 2]
        P.dma("sp", xt[:, :], x_view_fn(i))
        sq = K.sq
        P.act(sq[:, :], xt[:, :], AF.Square)
        ssq = K.ssq[k % 2]
        P.I("dve", "tensor_reduce", out=ssq[:, 0:1], in_=sq[:, :], axis=AX.X, op=ALU.add)
        P.I("dve", "tensor_scalar", out=ssq[:, 1:2], in0=ssq[:, 0:1], scalar1=1.0 / D, scalar2=EPS, op0=ALU.mult, op1=ALU.add)
        P.act(ssq[:, 3:4], ssq[:, 1:2], AF.Sqrt)
        P.I("dve", "reciprocal", out=ssq[:, 2:3], in_=ssq[:, 3:4])
        t32 = K.t32
        P.I("dve", "scalar_tensor_tensor", out=t32[:, :], in0=xt[:, :], scalar=ssq[:, 2:3], in1=AB[0][:, :], op0=ALU.mult, op1=ALU.mult)
        hb = K.hb[k % 2]
        P.I("pool", "tensor_tensor", out=hb[:, :], in0=t32[:, :], in1=AB[1][:, :], op=ALU.add)
        ptr = K.ptr[k % 2]
        for j in range(8):
            P.tr(ptr[:, j * 128:(j + 1) * 128], hb[:, j * 128:(j + 1) * 128], C["idb"][:, :])
        P.act(hT.v(hT.h[:, :, i * 128:(i + 1) * 128]), ptr.v(ptr.h[:, :].rearrange("p (j t) -> p j t", j=8)), AF.Copy)
    return hT


def alloc_pre(P):
    K = PreCtx()
    K.cnt = 0
    K.xt = [P.sbuf(f"xt{i}", [128, D], F32) for i in range(2)]
    K.sq = P.sbuf("sq", [128, D], BF16)
    K.ssq = [P.sbuf(f"ssq{i}", [128, 4], F32) for i in range(2)]
    K.t32 = P.sbuf("t32", [128, D], F32)
    K.hb = [P.sbuf(f"hb{i}", [128, D], BF16) for i in range(2)]
    K.hT = [P.sbuf(f"hT{i}", [128, 8, 512], BF16) for i in range(2)]
    K.ptr = [P.psum(f"ptr{i}", [128, D], BF16) for i in range(2)]
    return K


def dram_in(nc, name, shape, dt=F32):
    return nc.dram_tensor(name, list(shape), dt, kind="ExternalInput").ap()


def dram_out(nc, name, shape, dt):
    return nc.dram_tensor(name, list(shape), dt, kind="ExternalOutput").ap()


STOP = 99


def build_A_even(n_own=NOWN, n_ctx=NCTX):
    nt = n_own + n_ctx
    nc = bass.Bass("TRN2", target_bir_lowering=False)
    P = Prog(nc)
    inp = dict(
        x=dram_in(nc, "x", [n_own, D]), ctx=dram_in(nc, "ctx", [n_ctx, D]), ccols=dram_in(nc, "ccols", [128, 16]),
        ada_w=dram_in(nc, "ada_w", [D, 2048]), ada_b=dram_in(nc, "ada_b", [1, 2048]), norm_pre=dram_in(nc, "norm_pre", [1, D]),
        w_in=dram_in(nc, "w_in", [D, 3584]), ident=dram_in(nc, "ident", [128, 128]), rmat=dram_in(nc, "rmat", [128, 128]),
        cosT=dram_in(nc, "cosT", [128, n_own]), sinT=dram_in(nc, "sinT", [128, n_own]))
    tm_d = dram_out(nc, "tm", [nt, 2048], BF16)
    fm_d = dram_out(nc, "fm", [12, 128, nt], BF16)
    C = mk_consts(P, inp)
    stage = [P.sbuf(f"stage{i}", [128, 4, 512], F32) for i in range(2)]
    psA = P.psum("psA", [128, 512], F32)
    if STOP == 0:
        P.finish()
        return nc, P
    AB = pre_norm_rows(P, C, inp, None, stage, psA)
    if STOP == 1:
        P.finish()
        return nc, P
    wb = load_w_bf16(P, inp["w_in"], 3584, "wb", stage)
    rm32 = P.sbuf("rm32", [128, 128], F32)
    P.dma("sp", rm32[:, :], inp["rmat"])
    rmb = P.sbuf("rmb", [128, 128], BF16)
    P.I("dve", "tensor_copy", out=rmb[:, :], in_=rm32[:, :])
    if STOP == 2:
        P.finish()
        return nc, P
    K = alloc_pre(P)
    pst = [P.psum(f"pst{i}", [128, 512], F32) for i in range(2)]
    psf = [P.psum(f"psf{i}", [128, 512], F32) for i in range(2)]
    psr = P.psum("psr", [128, 512], F32)
    tmt = [P.sbuf(f"tmt{i}", [128, 2048], BF16) for i in range(2)]
    cs = [P.sbuf(f"cs{i}", [128, 512], F32) for i in range(2)]
    sn = [P.sbuf(f"sn{i}", [128, 512], F32) for i in range(2)]
    qraw = [P.sbuf(f"qraw{i}", [128, 512], BF16) for i in range(2)]
    t1 = [P.sbuf(f"t1{i}", [128, 512], F32) for i in range(2)]
    t2 = [P.sbuf(f"t2{i}", [128, 512], F32) for i in range(2)]
    fmo = [P.sbuf(f"fmo{i}", [128, 512], BF16) for i in range(2)]
    TMB = [(0, 512, [(0, 0, 256, False), (256, 256, 256, True)]),
           (2048, 512, [(0, 512, 512, False)]),
           (2560, 512, [(0, 1024, 256, False), (256, 1280, 256, True)]),
           (3072, 512, [(0, 1536, 512, True)])]
    groups = [(g * 512, 4, False) for g in range(n_own // 512)] + [(n_own, n_ctx // 128, True)]
    kt = 0
    kf = 0
    for gi, (tok0, ntile, is_ctx) in enumerate(groups):
        n = ntile * 128
        if is_ctx:
            xfn = lambda i: inp["ctx"][i * 128:(i + 1) * 128, :]
            ab = AB[2:4]
        else:
            xfn = lambda i, tok0=tok0: inp["x"][tok0 + i * 128: tok0 + (i + 1) * 128, :]
            ab = AB[0:2]
            P.dma("sp", cs[gi % 2][:, 0:n], inp["cosT"][:, tok0:tok0 + n])
            P.dma("sp", sn[gi % 2][:, 0:n], inp["sinT"][:, tok0:tok0 + n])
        hT = emit_pre_group(P, C, K, xfn, ntile, ab, gi)
        if STOP == 3:
            break
        for i in range(ntile):
            tt = tmt[(gi * 4 + i) % 2]
            for (wc0, wn, eps_) in TMB:
                ps = pst[kt % 2]
                kt += 1
                for j in range(8):
                    P.mm(ps[:, 0:wn], lhsT=hT[:, j, i * 128:(i + 1) * 128], rhs=wb[:, j, wc0:wc0 + wn], start=(j == 0), stop=(j == 7))
                for (pc, dc, nn, silu) in eps_:
                    if silu:
                        P.act(tt[:, dc:dc + nn], ps[:, pc:pc + nn], AF.Silu)
                    else:
                        P.I("dve", "tensor_copy", out=tt[:, dc:dc + nn], in_=ps[:, pc:pc + nn])
            P.dma("sp", tm_d[tok0 + i * 128: tok0 + (i + 1) * 128, :], tt[:, :])
        if STOP == 4:
            break
        for blk in range(12):
            wc0 = 512 + blk * 128
            ps = psf[kf % 2]
            for j in range(8):
                P.mm(ps[:, 0:n], lhsT=wb[:, j, wc0:wc0 + 128], rhs=hT[:, j, 0:n], start=(j == 0), stop=(j == 7))
            fo = fmo[kf % 2]
            if is_ctx or STOP == 7:
                P.act(fo[:, 0:n], ps[:, 0:n], AF.Copy)
            else:
                qr = qraw[kf % 2]
                P.act(qr[:, 0:n], ps[:, 0:n], AF.Copy)
                if STOP not in (8, 9):
                    P.mm(psr[:, 0:n], lhsT=rmb[:, :], rhs=qr[:, 0:n])
                P.I("dve", "tensor_tensor", out=t1[kf % 2][:, 0:n], in0=(qr if STOP == 9 else ps)[:, 0:n], in1=cs[gi % 2][:, 0:n], op=ALU.mult, extra_reads=([qr[:, 0:n]] if STOP == 10 else []))
                P.I("dve", "tensor_tensor", out=t2[kf % 2][:, 0:n], in0=(psr if STOP not in (8, 9) else (ps if STOP == 8 else qr))[:, 0:n], in1=sn[gi % 2][:, 0:n], op=ALU.mult)
                P.I("dve" if STOP == 5 else "pool", "tensor_tensor", out=fo[:, 0:n], in0=t1[kf % 2][:, 0:n], in1=t2[kf % 2][:, 0:n], op=ALU.add)
            if STOP != 6:
                P.dma("sp", fm_d[blk, :, tok0:tok0 + n], fo[:, 0:n])
            kf += 1
    P.finish()
    return nc, P


def post_setup(P, C, inp, stage, psM):
    G = [P.sbuf(f"Gbc{v}", [128, 1024], F32) for v in range(2)]
    dests = [[(G[0], 0), (G[1], 0)], [(G[0], 512), (G[1], 512)]]
    ada_bcast(P, C, inp, 2, ["g", "g"], dests, stage, psM, "norm_post")
    wo = load_w_bf16(P, inp["w_out"], 1024, "wo", stage)
    K = PreCtx()
    K.G = G
    K.wo = wo
    K.mixt = [P.sbuf(f"mixt{i}", [128, D], BF16) for i in range(2)]
    K.mixT = [P.sbuf(f"mixT{i}", [128, 8, 128], BF16) for i in range(2)]
    K.xt = [P.sbuf(f"pxt{i}", [128, D], F32) for i in range(2)]
    K.sq = P.sbuf("psq", [128, D], BF16)
    K.ssq = [P.sbuf(f"pssq{i}", [128, 4], F32) for i in range(2)]
    K.t32 = P.sbuf("pt32", [128, D], F32)
    K.xo = [P.sbuf(f"pxo{i}", [128, D], F32) for i in range(2)]
    K.cnt = 0
    return K


def post_tile(P, C, K, mix_src, x_src, x_dst, v, psT, psY, mix_in_sbuf=None):
    k = K.cnt
    K.cnt += 1
    if mix_in_sbuf is None:
        mt = K.mixt[k % 2]
        P.dma("sp", mt[:, :], mix_src)
    else:
        mt = mix_in_sbuf
    for j in range(8):
        P.tr(psT[:, j * 128:(j + 1) * 128], mt[:, j * 128:(j + 1) * 128], C["idb"][:, :])
    mT = K.mixT[k % 2]
    P.act(mT.v(mT.h[:, :, :]), psT.v(psT.h[:, :].rearrange("p (j t) -> p j t", j=8)), AF.Copy)
    for cb in range(2):
        for j in range(8):
            P.mm(psY[:, cb * 512:(cb + 1) * 512], lhsT=mT[:, j, :], rhs=K.wo[:, j, cb * 512:(cb + 1) * 512], start=(j == 0), stop=(j == 7))
    xt = K.xt[k % 2]
    P.dma("sp", xt[:, :], x_src)
    P.act(K.sq[:, :], psY[:, 0:1024], AF.Square)
    ssq = K.ssq[k % 2]
    P.I("dve", "tensor_reduce", out=ssq[:, 0:1], in_=K.sq[:, :], axis=AX.X, op=ALU.add)
    P.I("dve", "tensor_scalar", out=ssq[:, 1:2], in0=ssq[:, 0:1], scalar1=1.0 / D, scalar2=EPS, op0=ALU.mult, op1=ALU.add)
    P.act(ssq[:, 3:4], ssq[:, 1:2], AF.Sqrt)
    P.I("dve", "reciprocal", out=ssq[:, 2:3], in_=ssq[:, 3:4])
    P.I("dve", "scalar_tensor_tensor", out=K.t32[:, :], in0=psY[:, 0:1024], scalar=ssq[:, 2:3], in1=K.G[v][:, :], op0=ALU.mult, op1=ALU.mult)
    xo = K.xo[k % 2]
    P.I("pool", "tensor_tensor", out=xo[:, :], in0=K.t32[:, :], in1=xt[:, :], op=ALU.add)
    P.dma("sp", x_dst, xo[:, :])


def build_B_even(n_own=NOWN, n_lat=4 * NOWN, n_ctx=NCTX):
    nt = n_own + n_ctx
    nk = n_lat + n_ctx
    nkb = nk // 128
    nc = bass.Bass("TRN2", target_bir_lowering=False)
    P = Prog(nc)
    inp = dict(
        x=dram_in(nc, "x", [n_own, D]), ctx=dram_in(nc, "ctx", [n_ctx, D]), ccols=dram_in(nc, "ccols", [128, 16]),
        ada_w=dram_in(nc, "ada_w", [D, 1024]), ada_b=dram_in(nc, "ada_b", [1, 1024]), norm_post=dram_in(nc, "norm_post", [1, D]),
        w_out=dram_in(nc, "w_out", [D, D]), ident=dram_in(nc, "ident", [128, 128]),
        qT=dram_in(nc, "qT", [6, 128, nt], BF16), kT=dram_in(nc, "kT", [6, 128, nk], BF16), vall=dram_in(nc, "vall", [nk, 768], BF16),
        gates=dram_in(nc, "gates", [nt, 1024], BF16), uh=dram_in(nc, "uh", [n_own + 16, 256], BF16), uch=dram_in(nc, "uch", [n_ctx + 16, 256], BF16),
        bandA=dram_in(nc, "bandA", [128, 5 * 4 * 128]), bandB=dram_in(nc, "bandB", [16, 5 * 4 * 128]),
        pool_w=dram_in(nc, "pool_w", [64, 4 * 64]), pool_scale=dram_in(nc, "pool_scale", [1, 256]),
        lamp=dram_in(nc, "lamp", [1, 256]), subln=dram_in(nc, "subln", [1, 128]), lamc=dram_in(nc, "lamc", [1, 2]))
    xo_d = dram_out(nc, "xo", [n_own, D], F32)
    co_d = dram_out(nc, "co", [n_ctx, D], F32)
    mix_d = P.dram("mixd", [nt, D], BF16, nslot=nt // 128, sdim=0)
    C = mk_consts(P, inp)
    stage = [P.sbuf(f"stage{i}", [128, 4, 512], F32) for i in range(2)]
    PSA = P.psum("PSA", [128, 3 * 512], F32, nslot=3)
    PSO = P.psum("PSO", [128, 3 * 512], F32, nslot=3)
    PST = P.psum("PST", [128, D], BF16)
    PSM = P.psum("PSM", [128, 512], F32)
    KP = post_setup(P, C, inp, stage, PSM)

    lamp = P.sbuf("lamp", [1, 256], F32)
    P.dma("sp", lamp[:, :], inp["lamp"])
    lamc = P.sbuf("lamc", [1, 2], F32)
    P.dma("sp", lamc[:, :], inp["lamc"])
    lw = P.sbuf("lw", [1, 16], F32)
    lpp = P.sbuf("lpp", [1, 128], F32)
    P.I("dve", "tensor_tensor", out=lpp[:, 0:64], in0=lamp[:, 0:64], in1=lamp[:, 64:128], op=ALU.mult)
    P.I("dve", "tensor_tensor", out=lpp[:, 64:128], in0=lamp[:, 128:192], in1=lamp[:, 192:256], op=ALU.mult)
    P.I("dve", "tensor_reduce", out=lw[:, 0:1], in_=lpp[:, 0:64], axis=AX.X, op=ALU.add)
    P.I("dve", "tensor_reduce", out=lw[:, 1:2], in_=lpp[:, 64:128], axis=AX.X, op=ALU.add)
    P.act(lw[:, 2:4], lw[:, 0:2], AF.Exp)
    P.I("dve", "tensor_tensor", out=lw[:, 4:5], in0=lw[:, 2:3], in1=lw[:, 3:4], op=ALU.subtract)
    P.I("dve", "tensor_tensor", out=lw[:, 5:6], in0=lw[:, 4:5], in1=lamc[:, 0:1], op=ALU.add)
    P.I("dve", "tensor_scalar", out=lw[:, 6:7], in0=lw[:, 5:6], scalar1=-1.0, scalar2=None, op0=ALU.mult)
    neglam = P.sbuf("neglam", [128, 1], F32)
    P.mm(PSM[:, 0:1], lhsT=C["ones"][:, :], rhs=lw[0:1, 6:7])
    P.I("dve", "tensor_copy", out=neglam[:, :], in_=PSM[:, 0:1])
    sub = P.sbuf("sub", [1, 128], F32)
    P.dma("sp", sub[:, :], inp["subln"])
    sub2 = P.sbuf("sub2", [1, 128], F32)
    P.I("dve", "tensor_scalar", out=sub2[:, :], in0=sub[:, :], scalar1=lamc[:, 1:2], scalar2=None, op0=ALU.mult)
    subg = P.sbuf("subg", [128, 128], F32)
    P.mm(PSM[:, 0:128], lhsT=C["ones"][:, :], rhs=sub2[0:1, :])
    P.I("dve", "tensor_copy", out=subg[:, :], in_=PSM[:, 0:128])
    psc = P.sbuf("psc", [1, 256], F32)
    P.dma("sp", psc[:, :], inp["pool_scale"])
    pscb = P.sbuf("pscb", [128, 256], F32)
    P.mm(PSM[:, 0:256], lhsT=C["ones"][:, :], rhs=psc[0:1, :])
    P.I("dve", "tensor_copy", out=pscb[:, :], in_=PSM[:, 0:256])

    bA = P.sbuf("bA", [128, 2560], BF16)
    bB = P.sbuf("bB", [16, 2560], BF16)
    sflat = [st.h[:, :, :].rearrange("p a b -> p (a b)") for st in stage]
    P.dma("sp", stage[0].v(sflat[0][:, 0:2048]), inp["bandA"][:, 0:2048])
    P.I("dve", "tensor_copy", out=bA[:, 0:2048], in_=stage[0].v(sflat[0][:, 0:2048]))
    P.dma("sp", stage[1].v(sflat[1][:, 0:512]), inp["bandA"][:, 2048:2560])
    P.I("dve", "tensor_copy", out=bA[:, 2048:2560], in_=stage[1].v(sflat[1][:, 0:512]))
    P.dma("sp", stage[0].v(sflat[0][0:16, 0:2048]), inp["bandB"][:, 0:2048])
    P.I("dve", "tensor_copy", out=bB[:, 0:2048], in_=stage[0].v(sflat[0][0:16, 0:2048]))
    P.dma("sp", stage[1].v(sflat[1][0:16, 0:512]), inp["bandB"][:, 2048:2560])
    P.I("dve", "tensor_copy", out=bB[:, 2048:2560], in_=stage[1].v(sflat[1][0:16, 0:512]))
    pw32 = P.sbuf("pw32", [64, 256], F32)
    P.dma("sp", pw32[:, :], inp["pool_w"])
    pw = P.sbuf("pw", [64, 256], BF16)
    P.I("dve", "tensor_copy", out=pw[:, :], in_=pw32[:, :])
    uA = [P.sbuf(f"uA{i}", [128, 256], BF16) for i in range(2)]
    uB = [P.sbuf(f"uB{i}", [16, 256], BF16) for i in range(2)]
    dT = [P.sbuf(f"dT{i}", [64, 512], BF16) for i in range(2)]
    gat = [P.sbuf(f"gat{i}", [128, 256], BF16) for i in range(2)]
    ytmp = [P.sbuf(f"ytmp{i}", [128, 256], F32) for i in range(2)]
    ya = [P.sbuf(f"ya{i}", [128, 256], BF16) for i in range(2)]
    ntile_own = n_own // 128
    ntile_ctx = n_ctx // 128
    tiles = [("x", i) for i in range(ntile_own)] + [("c", i) for i in range(ntile_ctx)]
    for k, (kind, i) in enumerate(tiles):
        if kind == "x":
            src = inp["uh"]
            typ = 0 if i == 0 else (2 if i == ntile_own - 1 else 1)
            row0 = i * 128
            tok0 = i * 128
        else:
            src = inp["uch"]
            typ = 3 if i == 0 else 4
            row0 = i * 128
            tok0 = n_own + i * 128
        P.dma("sp", uA[k % 2][:, :], src[row0:row0 + 128, :])
        P.dma("sp", uB[k % 2][:, :], src[row0 + 128:row0 + 144, :])
        P.dma("sp", gat[k % 2][:, :], inp["gates"][tok0:tok0 + 128, 0:256])
        for g in range(4):
            bo = (typ * 4 + g) * 128
            P.mm(PSM[0:64, g * 128:(g + 1) * 128], lhsT=uA[k % 2][:, g * 64:(g + 1) * 64], rhs=bA[:, bo:bo + 128], start=True, stop=False)
            P.mm(PSM[0:64, g * 128:(g + 1) * 128], lhsT=uB[k % 2][:, g * 64:(g + 1) * 64], rhs=bB[:, bo:bo + 128], start=False, stop=True)
        P.act(dT[k % 2][:, :], PSM[0:64, :], AF.Copy)
        yps = PSA[:, 1024:1024 + 256]
        for g in range(4):
            P.mm(PSA[:, 1024 + g * 64:1024 + (g + 1) * 64], lhsT=dT[k % 2][:, g * 128:(g + 1) * 128], rhs=pw[:, g * 64:(g + 1) * 64])
        P.I("dve", "tensor_tensor", out=ytmp[k % 2][:, :], in0=yps, in1=pscb[:, :], op=ALU.mult)
        P.I("pool", "tensor_tensor", out=ya[k % 2][:, :], in0=ytmp[k % 2][:, :], in1=gat[k % 2][:, :], op=ALU.mult)
        P.dma("sp", mix_d[tok0:tok0 + 128, 0:256], ya[k % 2][:, :])

    kTs = P.sbuf("kTs", [128, nk], BF16)
    vau = P.sbuf("vau", [128, nkb, 128], BF16)
    qz = [[P.sbuf(f"qz{i}_{m}", [128, 512], BF16) for m in range(2)] for i in range(2)]
    for i in range(2):
        for m in range(2):
            P.I("pool", "memset", ap=qz[i][m][:, :], constant=0.0)
    pT = P.sbuf("pT", [128, 3 * 512], BF16, nslot=3)
    dacc = [KP.xt[m] for m in range(2)]
    onec = P.sbuf("onec", [128, 1], F32)
    P.I("dve", "memset", ap=onec[:, :], constant=1.0)
    onecb = P.sbuf("onecb", [128, 1], BF16)
    P.I("dve", "memset", ap=onecb[:, :], constant=1.0)
    ones33 = P.sbuf("ones33", [33, 128], F32)
    P.I("dve", "memset", ap=ones33[:, :], constant=1.0)
    rrow = P.sbuf("rrow", [33, 512], F32, nslot=1)
    rbc = [KP.xo[m] for m in range(2)]
    tt1 = KP.t32
    tt2 = T(KP.t32.h, "t32b")
    tt2.bufs = KP.t32.bufs
    oTb = P.sbuf("oTb", [128, 512], BF16)
    gb = [P.sbuf(f"gb{i}", [128, 128], BF16) for i in range(2)]
    osq = P.sbuf("osq", [128, 128], BF16)
    oss = [P.sbuf(f"oss{i}", [128, 4], F32) for i in range(2)]
    o3 = [P.sbuf(f"o3{i}", [128, 128], F32) for i in range(2)]
    ob = [P.sbuf(f"ob{i}", [128, 128], BF16) for i in range(2)]
    vsrc = inp["vall"].rearrange("(kb p) c -> p kb c", p=128)
    qblocks = [(qb * 512, 512, list(range(nkb))) for qb in range(n_own // 512)] + \
              [(n_own + c0, min(512, n_ctx - c0), list(range(n_lat // 128, nkb))) for c0 in range(0, n_ctx, 512)]
    kq = 0
    kf = 0
    for h in range(6):
        nq_ = 4
        for qi in range(nq_):
            k0, k1 = (nk * qi // nq_) // 128 * 128, (nk * (qi + 1) // nq_) // 128 * 128 if qi < nq_ - 1 else nk
            P.dma(("sp", "pool")[qi % 2], kTs[:, k0:k1], inp["kT"][h, :, k0:k1])
        for ci, c0 in enumerate(range(0, nkb, 32)):
            c1 = min(nkb, c0 + 32)
            P.dma(("sp", "pool")[ci % 2], vau.v(vau.h[:, c0:c1, :]), vsrc[:, c0:c1, h * 128:(h + 1) * 128])
        for (q0, qn, kbs) in qblocks:
            nqs = qn // 128
            qt = qz[kq % 2]
            kq += 1
            for m in range(2):
                P.dma("sp", qt[m][m * 64:(m + 1) * 64, 0:qn], inp["qT"][h, m * 64:(m + 1) * 64, q0:q0 + qn])
            its = [(kb, m) for kb in kbs for m in range(2)]

            def QK(it):
                kb, m = its[it]
                s = it % 3
                P.mm(PSA[:, s * 512:s * 512 + qn], lhsT=kTs[:, kb * 128:(kb + 1) * 128], rhs=qt[m][:, 0:qn])

            pe_den = [False, False]
            QK(0)
            if len(its) > 1:
                QK(1)
            for it, (kb, m) in enumerate(its):
                s = it % 3
                if it + 2 < len(its):
                    QK(it + 2)
                P.act(pT[:, s * 512:s * 512 + qn], PSA[:, s * 512:s * 512 + qn], AF.Exp, scale=0.125)
                P.mm(PSO[:, m * 512:m * 512 + qn], lhsT=vau[:, kb, :], rhs=pT[:, s * 512:s * 512 + qn], start=(kb == kbs[0]), stop=(kb == kbs[-1]))
                if (kb - kbs[0]) % 3 == 2:
                    P.mm(PSM[32 * m:32 * m + 1, 0:qn], lhsT=onecb[:, :], rhs=pT[:, s * 512:s * 512 + qn], start=(not pe_den[m]), stop=False)
                    pe_den[m] = True
                elif kb == kbs[0]:
                    P.I("dve", "tensor_copy", out=dacc[m][:, 0:qn], in_=pT[:, s * 512:s * 512 + qn])
                else:
                    P.I("dve", "tensor_tensor", out=dacc[m][:, 0:qn], in0=dacc[m][:, 0:qn], in1=pT[:, s * 512:s * 512 + qn], op=ALU.add)
            for m in range(2):
                pr = 32 * m
                P.mm(PSM[pr:pr + 1, 0:qn], lhsT=onec[:, :], rhs=dacc[m][:, 0:qn], start=(not pe_den[m]), stop=True)
                P.I("dve", "reciprocal", out=rrow[pr:pr + 1, 0:qn], in_=PSM[pr:pr + 1, 0:qn])
                if m == 1:
                    P.I("dve", "tensor_scalar", out=rrow[pr:pr + 1, 0:qn], in0=rrow[pr:pr + 1, 0:qn], scalar1=neglam[pr:pr + 1, 0:1], scalar2=None, op0=ALU.mult)
                P.mm(PSO[:, 1024:1024 + qn], lhsT=ones33[pr:pr + 1, :], rhs=rrow[pr:pr + 1, 0:qn])
                P.act(rbc[m][:, 0:qn], PSO[:, 1024:1024 + qn], AF.Copy)
            P.I("dve", "tensor_tensor", out=tt1[:, 0:qn], in0=PSO[:, 0:qn], in1=rbc[0][:, 0:qn], op=ALU.mult)
            P.I("dve", "tensor_tensor", out=tt2[:, 512:512 + qn], in0=PSO[:, 512:512 + qn], in1=rbc[1][:, 0:qn], op=ALU.mult)
            P.I("pool", "tensor_tensor", out=oTb[:, 0:qn], in0=tt1[:, 0:qn], in1=tt2[:, 512:512 + qn], op=ALU.add)
            for qs in range(nqs):
                P.tr(PST[:, qs * 128:(qs + 1) * 128], oTb[:, qs * 128:(qs + 1) * 128], C["idb"][:, :])
            for qs in range(nqs):
                tok0 = q0 + qs * 128
                o2v = PST[:, qs * 128:(qs + 1) * 128]
                P.dma("sp", gb[kf % 2][:, :], inp["gates"][tok0:tok0 + 128, 256 + h * 128:256 + (h + 1) * 128])
                P.act(osq[:, :], o2v, AF.Square)
                ss = oss[kf % 2]
                P.I("dve", "tensor_reduce", out=ss[:, 0:1], in_=osq[:, :], axis=AX.X, op=ALU.add)
                P.I("dve", "tensor_scalar", out=ss[:, 1:2], in0=ss[:, 0:1], scalar1=1.0 / 128, scalar2=EPS, op0=ALU.mult, op1=ALU.add)
                P.act(ss[:, 3:4], ss[:, 1:2], AF.Sqrt)
                P.I("dve", "reciprocal", out=ss[:, 2:3], in_=ss[:, 3:4])
                P.I("dve", "scalar_tensor_tensor", out=o3[kf % 2][:, :], in0=o2v, scalar=ss[:, 2:3], in1=subg[:, :], op0=ALU.mult, op1=ALU.mult)
                P.I("pool", "tensor_tensor", out=ob[kf % 2][:, :], in0=o3[kf % 2][:, :], in1=gb[kf % 2][:, :], op=ALU.mult)
                P.dma("sp", mix_d[tok0:tok0 + 128, 256 + h * 128:256 + (h + 1) * 128], ob[kf % 2][:, :])
                kf += 1

    for k, (kind, i) in enumerate(tiles):
        if kind == "x":
            post_tile(P, C, KP, mix_d[i * 128:(i + 1) * 128, :], inp["x"][i * 128:(i + 1) * 128, :], xo_d[i * 128:(i + 1) * 128, :], 0, PST, PSA)
        else:
            t0 = n_own + i * 128
            post_tile(P, C, KP, mix_d[t0:t0 + 128, :], inp["ctx"][i * 128:(i + 1) * 128, :], co_d[i * 128:(i + 1) * 128, :], 1, PST, PSA)
    P.finish()
    return nc, P


def build_A_odd(layer, n_own=NOWN, n_ctx=NCTX):
    nt = n_own + n_ctx
    nc = bass.Bass("TRN2", target_bir_lowering=False)
    P = Prog(nc)
    inp = dict(
        x=dram_in(nc, "x", [n_own, D]), ctx=dram_in(nc, "ctx", [n_ctx, D]), ccols=dram_in(nc, "ccols", [128, 16]),
        ada_w=dram_in(nc, "ada_w", [D, 2048]), ada_b=dram_in(nc, "ada_b", [1, 2048]), norm_pre=dram_in(nc, "norm_pre", [1, D]),
        w_in=dram_in(nc, "w_in", [D, 4096]), ident=dram_in(nc, "ident", [128, 128]), lbraw=dram_in(nc, "lbraw", [128, 32]))
    tm_d = dram_out(nc, "tm", [nt, 1536], BF16)
    fmb_d = dram_out(nc, "fmb", [4, 128, nt], BF16)
    fmf_d = dram_out(nc, "fmf", [12, 128, nt], F32)
    C = mk_consts(P, inp)
    stage = [P.sbuf(f"stage{i}", [128, 4, 512], F32) for i in range(2)]
    psA = P.psum("psA", [128, 512], F32)
    AB = pre_norm_rows(P, C, inp, None, stage, psA)
    wb = load_w_bf16(P, inp["w_in"], 4096, "wb", stage)
    lbr = P.sbuf("lbr", [128, 32], F32)
    P.dma("sp", lbr[:, :], inp["lbraw"])
    lbe = P.sbuf("lbe", [128, 32], F32)
    P.act(lbe[:, :], lbr[:, :], AF.Exp)
    ev = lambda li: lbe.v(lbe.h[:, :].rearrange("p (d l h) -> p d l h", d=2, l=4)[:, :, li, :])
    den = P.sbuf("lbden", [128, 8], F32)
    num = P.sbuf("lbnum", [128, 8], F32)
    v3 = lambda t: t.v(t.h[:, :].rearrange("p (d h) -> p d h", d=2))
    P.I("dve", "tensor_tensor", out=v3(den), in0=ev(0), in1=ev(1), op=ALU.add)
    P.I("dve", "tensor_tensor", out=v3(den), in0=v3(den), in1=ev(2), op=ALU.add)
    P.I("dve", "tensor_tensor", out=v3(den), in0=v3(den), in1=ev(3), op=ALU.add)
    P.I("dve", "tensor_copy", out=v3(num), in_=ev(1))
    for li in range(2, layer + 1):
        P.I("dve", "tensor_tensor", out=v3(num), in0=v3(num), in1=ev(li), op=ALU.add)
    rden = P.sbuf("lbrden", [128, 8], F32)
    P.I("dve", "reciprocal", out=rden[:, :], in_=den[:, :])
    lb = P.sbuf("lb", [128, 8], F32)
    P.I("dve", "tensor_tensor", out=lb[:, :], in0=num[:, :], in1=rden[:, :], op=ALU.mult)
    oml = P.sbuf("oml", [128, 8], F32)
    P.I("dve", "tensor_scalar", out=oml[:, :], in0=lb[:, :], scalar1=-1.0, scalar2=1.0, op0=ALU.mult, op1=ALU.add)

    K = alloc_pre(P)
    pst = [P.psum(f"pst{i}", [128, 512], F32) for i in range(2)]
    psf = [P.psum(f"psf{i}", [128, 512], F32) for i in range(3)]
    tmt = [P.sbuf(f"tmt{i}", [128, 1536], BF16) for i in range(2)]
    sg = [P.sbuf(f"sg{i}", [128, 512], F32) for i in range(2)]
    fob = [P.sbuf(f"fob{i}", [128, 512], BF16) for i in range(2)]
    fof = [P.sbuf(f"fof{i}", [128, 512], F32) for i in range(2)]
    TMB = [(1024, 512, 0, True), (3584, 512, 512, True), (3072, 512, 1024, False)]
    groups = [(g * 512, 4, False) for g in range(n_own // 512)] + [(n_own, n_ctx // 128, True)]
    kt = 0
    kf = 0
    kb_ = 0
    kff = 0
    for gi, (tok0, ntile, is_ctx) in enumerate(groups):
        n = ntile * 128
        if is_ctx:
            xfn = lambda i: inp["ctx"][i * 128:(i + 1) * 128, :]
            ab = AB[2:4]
        else:
            xfn = lambda i, tok0=tok0: inp["x"][tok0 + i * 128: tok0 + (i + 1) * 128, :]
            ab = AB[0:2]
        hT = emit_pre_group(P, C, K, xfn, ntile, ab, gi)
        for i in range(ntile):
            tt = tmt[(gi * 4 + i) % 2]
            for (wc0, wn, dc, silu) in TMB:
                ps = pst[kt % 2]
                kt += 1
                for j in range(8):
                    P.mm(ps[:, 0:wn], lhsT=hT[:, j, i * 128:(i + 1) * 128], rhs=wb[:, j, wc0:wc0 + wn], start=(j == 0), stop=(j == 7))
                if silu:
                    P.act(tt[:, dc:dc + wn], ps[:, 0:wn], AF.Silu)
                else:
                    P.I("dve", "tensor_copy", out=tt[:, dc:dc + wn], in_=ps[:, 0:wn])
            P.dma("sp", tm_d[tok0 + i * 128: tok0 + (i + 1) * 128, :], tt[:, :])

        def fproj(wc0):
            nonlocal kf
            ps = psf[kf % 3]
            kf += 1
            for j in range(8):
                P.mm(ps[:, 0:n], lhsT=wb[:, j, wc0:wc0 + 128], rhs=hT[:, j, 0:n], start=(j == 0), stop=(j == 7))
            return ps
        for c in range(4):
            pa = fproj(c * 128)
            pb = fproj(512 + c * 128)
            s_ = sg[kb_ % 2]
            P.act(s_[:, 0:n], pb[:, 0:n], AF.Sigmoid)
            fo = fob[kb_ % 2]
            kb_ += 1
            P.I("dve", "tensor_tensor", out=fo[:, 0:n], in0=pa[:, 0:n], in1=s_[:, 0:n], op=ALU.mult)
            P.dma("sp", fmb_d[c, :, tok0:tok0 + n], fo[:, 0:n])
        for hh in range(4):
            pq = fproj(1536 + hh * 128)
            fo = fof[kff % 2]
            kff += 1
            P.act(fo[:, 0:n], pq[:, 0:n], AF.Silu)
            P.dma("sp", fmf_d[hh, :, tok0:tok0 + n], fo[:, 0:n])
            for dr in range(2):
                pf = fproj(2048 + dr * 512 + hh * 128)
                s_ = sg[kb_ % 2]
                kb_ += 1
                P.act(s_[:, 0:n], pf[:, 0:n], AF.Sigmoid)
                fo = fof[kff % 2]
                kff += 1
                ci = dr * 4 + hh
                P.I("dve", "tensor_scalar", out=fo[:, 0:n], in0=s_[:, 0:n], scalar1=oml[:, ci:ci + 1], scalar2=lb[:, ci:ci + 1], op0=ALU.mult, op1=ALU.add)
                P.dma("sp", fmf_d[4 + dr * 4 + hh, :, tok0:tok0 + n], fo[:, 0:n])
    P.finish()
    return nc, P


def build_H(n_lat=4 * NOWN, n_ctx=NCTX):
    N = n_lat + n_ctx
    SEG = min(2048, n_lat)
    nc = bass.Bass("TRN2", target_bir_lowering=False)
    P = Prog(nc)
    inp = dict(qT=dram_in(nc, "qT", [128, N]), fTf=dram_in(nc, "fTf", [128, N]), fTb=dram_in(nc, "fTb", [128, N]),
               v=dram_in(nc, "v", [N, 128], BF16), ogs=dram_in(nc, "ogs", [N, 128], BF16), gn=dram_in(nc, "gn", [1, 128]),
               ident=dram_in(nc, "ident", [128, 128]), masks=dram_in(nc, "masks", [64, 128]), cmask=dram_in(nc, "cmask", [128, SEG]))
    yd_d = dram_out(nc, "yd", [N, 128], BF16)
    o_d = [P.dram(f"od{d_}", [N, 128], F32, nslot=N // 64, sdim=0) for d_ in range(2)]
    C = mk_consts(P, inp)
    msk = P.sbuf("msk", [64, 128], F32)
    P.dma("sp", msk[:, :], inp["masks"])
    cm = P.sbuf("cm", [128, SEG], F32)
    P.dma("sp", cm[:, :], inp["cmask"])
    gnr = P.sbuf("gnr", [1, 128], F32)
    P.dma("sp", gnr[:, :], inp["gn"])
    p_ds_pre = [P.psum(f"p_ds{i}", [128, 128], F32) for i in range(2)]
    psg = p_ds_pre[0]
    gbc = P.sbuf("gbc", [128, 128], F32)
    P.mm(psg[:, :], lhsT=C["ones"][:, :], rhs=gnr[0:1, :])
    P.I("dve", "tensor_copy", out=gbc[:, :], in_=psg[:, :])
    segs = [(0, n_ctx)] + [(n_ctx + i * SEG, SEG) for i in range(n_lat // SEG)]

    def sweep(dirn):
        bwd = dirn == 1
        d_ = f"d{dirn}"
        qs = P.sbuf("qs" + d_, [128, SEG], F32)
        fs = P.sbuf("fs" + d_, [128, SEG], F32)
        lfs = P.sbuf("lfs" + d_, [128, SEG], F32)
        ks = P.sbuf("ks" + d_, [128, SEG], F32)
        Pc = P.sbuf("Pc" + d_, [128, SEG], F32)
        Pe = P.sbuf("Pe" + d_, [128, SEG], F32) if bwd else None
        vs = P.sbuf("vs" + d_, [64, SEG // 64, 128], BF16)
        S = [P.sbuf(f"S{i}" + d_, [128, 128], F32) for i in range(2)]
        nE = P.sbuf("nE" + d_, [128, SEG], F32)
        X = P.sbuf("X" + d_, [128, SEG // 64, 2], F32)
        EX = P.sbuf("EX" + d_, [128, SEG // 64, 2], F32)
        eq = [P.sbuf(f"eq{i}" + d_, [128, 64], F32) for i in range(2)]
        ek = [P.sbuf(f"ek{i}" + d_, [128, 64], F32) for i in range(2)]
        ekl = [P.sbuf(f"ekl{i}" + d_, [128, 64], F32) for i in range(2)]
        qt = [P.sbuf(f"qt{i}" + d_, [128, 64], BF16) for i in range(2)]
        kt = [P.sbuf(f"kt{i}" + d_, [128, 64], BF16) for i in range(2)]
        kh = [P.sbuf(f"kh{i}" + d_, [128, 64], BF16) for i in range(2)]
        khs = [P.sbuf(f"khs{i}" + d_, [64, 128], BF16) for i in range(2)]
        S0m = [P.sbuf(f"S0m{i}" + d_, [128, 128], BF16) for i in range(2)]
        attm = [P.sbuf(f"attm{i}" + d_, [64, 64], BF16) for i in range(2)]
        osb = [P.sbuf(f"osb{i}" + d_, [64, 128], F32) for i in range(2)]
        p_kh = P.psum("p_kh" + d_, [64, 128], BF16)
        p_att = P.psum("p_att" + d_, [64, 64], F32)
        p_o = P.psum("p_o" + d_, [64, 128], F32)
        p_ds = p_ds_pre[dirn]
        P.I("dve", "memset", ap=S[0][:, :], constant=0.0)
        cur = 0
        kc = 0
        order = segs if not bwd else [segs[0]] + segs[1:][::-1]
        E = Pe if bwd else Pc
        for (a0, sl) in order:
            P.dma("sp", qs[:, 0:sl], inp["qT"][:, a0:a0 + sl])
            P.dma("sp", fs[:, 0:sl], inp["fTb" if bwd else "fTf"][:, a0:a0 + sl])
            P.dma("sp", vs.v(vs.h[:, 0:sl // 64, :]), inp["v"][a0:a0 + sl, :].rearrange("(n p) c -> p n c", p=64))
            P.act(lfs[:, 0:sl], fs[:, 0:sl], AF.Ln)
            P.I("pool", "tensor_scalar", out=ks[:, 0:sl], in0=fs[:, 0:sl], scalar1=-1.0, scalar2=1.0, op0=ALU.mult, op1=ALU.add)
            P.I("dve", "tensor_tensor_scan", out=Pc[:, 0:sl], data0=cm[:, 0:sl], data1=lfs[:, 0:sl], initial=0.0, op0=ALU.mult, op1=ALU.add)
            if bwd:
                P.I("dve", "tensor_tensor", out=Pe[:, 0:sl], in0=Pc[:, 0:sl], in1=lfs[:, 0:sl], op=ALU.subtract)
            nch = sl // 64
            P.I("pool", "tensor_scalar", out=nE[:, 0:sl], in0=E[:, 0:sl], scalar1=-1.0, scalar2=None, op0=ALU.mult)
            col = lambda t_, c_: t_.v(t_.h[:, 0:sl].rearrange("p (n c) -> p n c", c=64)[:, :, c_])
            if bwd:
                P.I("dve", "tensor_tensor", out=X.v(X.h[:, 0:nch, 0]), in0=col(Pc, 63), in1=col(Pe, 31), op=ALU.subtract)
            else:
                P.I("dve", "tensor_copy", out=X.v(X.h[:, 0:nch, 0]), in_=col(Pc, 31))
            P.I("dve", "tensor_copy", out=X.v(X.h[:, 0:nch, 1]), in_=col(Pc, 63))
            P.act(EX.v(EX.h[:, 0:nch, :]), X.v(X.h[:, 0:nch, :]), AF.Exp)
            chunks = list(range(sl // 64))
            if bwd:
                chunks = chunks[::-1]
            for n_ in chunks:
                a = n_ * 64
                k2 = kc % 2
                kc += 1
                Ec = E[:, a:a + 64]
                pmid = E[:, a + 31:a + 32]
                nmid = nE[:, a + 31:a + 32]
                tot = Pc[:, a + 63:a + 64]
                if bwd:
                    P.act(eq[k2][:, :], Ec, AF.Exp, scale=-1.0, bias=pmid)
                    P.act(ek[k2][:, :], Ec, AF.Exp, scale=1.0, bias=nmid)
                    P.act(ekl[k2][:, :], Ec, AF.Exp)
                else:
                    P.act(eq[k2][:, :], Ec, AF.Exp, scale=1.0, bias=nmid)
                    P.act(ek[k2][:, :], Ec, AF.Exp, scale=-1.0, bias=pmid)
                    P.act(ekl[k2][:, :], Ec, AF.Exp, scale=-1.0, bias=tot)
                P.I("dve", "tensor_tensor", out=qt[k2][:, :], in0=qs[:, a:a + 64], in1=eq[k2][:, :], op=ALU.mult)
                P.I("pool", "tensor_tensor", out=kt[k2][:, :], in0=ks[:, a:a + 64], in1=ek[k2][:, :], op=ALU.mult)
                P.I("pool", "tensor_tensor", out=kh[k2][:, :], in0=ks[:, a:a + 64], in1=ekl[k2][:, :], op=ALU.mult)
                P.tr(p_kh[:, :], kh[k2][:, :], C["idb"][:, :])
                P.act(khs[k2][:, :], p_kh[:, :], AF.Copy)
                P.I("dve", "tensor_scalar", out=S0m[k2][:, :], in0=S[cur][:, :], scalar1=EX[:, n_, 0:1], scalar2=None, op0=ALU.mult)
                P.mm(p_att[:, :], lhsT=kt[k2][:, :], rhs=qt[k2][:, :])
                mo = 64 if bwd else 0
                P.I("dve", "tensor_tensor", out=attm[k2][:, :], in0=p_att[:, :], in1=msk[:, mo:mo + 64], op=ALU.mult)
                P.mm(p_o[:, :], lhsT=qt[k2][:, :], rhs=S0m[k2][:, :], start=True, stop=False)
                P.mm(p_o[:, :], lhsT=attm[k2][:, :], rhs=vs[:, n_, :], start=False, stop=True)
                P.mm(p_ds[:, :], lhsT=khs[k2][:, :], rhs=vs[:, n_, :])
                P.I("dve", "scalar_tensor_tensor", out=S[1 - cur][:, :], in0=S[cur][:, :], scalar=EX[:, n_, 1:2], in1=p_ds[:, :], op0=ALU.mult, op1=ALU.add)
                cur = 1 - cur
                r0 = a0 + a
                P.act(osb[k2][:, :], p_o[:, :], AF.Copy)
                P.dma("sp", o_d[dirn][r0:r0 + 64, :], osb[k2][:, :])
                yield

    gens = [sweep(0), sweep(1)]
    alive = [True, True]
    while any(alive):
        for d_ in range(2):
            if alive[d_]:
                try:
                    next(gens[d_])
                except StopIteration:
                    alive[d_] = False
    o1 = [P.sbuf(f"co1{i}", [128, 128], F32) for i in range(2)]
    o2 = [P.sbuf(f"co2{i}", [128, 128], F32) for i in range(2)]
    og = [P.sbuf(f"cog{i}", [128, 128], BF16) for i in range(2)]
    osm = [P.sbuf(f"cosm{i}", [128, 128], F32) for i in range(2)]
    osq = P.sbuf("hosq", [128, 128], BF16)
    oss = [P.sbuf(f"hoss{i}", [128, 4], F32) for i in range(2)]
    on = [P.sbuf(f"on{i}", [128, 128], F32) for i in range(2)]
    yb = [P.sbuf(f"yb{i}", [128, 128], BF16) for i in range(2)]
    for t in range(N // 128):
        k2 = t % 2
        r0 = t * 128
        P.dma("sp", o1[k2][:, :], o_d[0][r0:r0 + 128, :])
        P.dma("sp", o2[k2][:, :], o_d[1][r0:r0 + 128, :])
        P.dma("sp", og[k2][:, :], inp["ogs"][r0:r0 + 128, :])
        P.I("dve", "tensor_tensor", out=osm[k2][:, :], in0=o1[k2][:, :], in1=o2[k2][:, :], op=ALU.add)
        P.act(osq[:, :], osm[k2][:, :], AF.Square)
        ss = oss[k2]
        P.I("dve", "tensor_reduce", out=ss[:, 0:1], in_=osq[:, :], axis=AX.X, op=ALU.add)
        P.I("dve", "tensor_scalar", out=ss[:, 1:2], in0=ss[:, 0:1], scalar1=1.0 / 128, scalar2=EPS, op0=ALU.mult, op1=ALU.add)
        P.act(ss[:, 3:4], ss[:, 1:2], AF.Sqrt)
        P.I("dve", "reciprocal", out=ss[:, 2:3], in_=ss[:, 3:4])
        P.I("dve", "scalar_tensor_tensor", out=on[k2][:, :], in0=osm[k2][:, :], scalar=ss[:, 2:3], in1=gbc[:, :], op0=ALU.mult, op1=ALU.mult)
        P.I("pool", "tensor_tensor", out=yb[k2][:, :], in0=on[k2][:, :], in1=og[k2][:, :], op=ALU.mult)
        P.dma("sp", yd_d[r0:r0 + 128, :], yb[k2][:, :])
    P.finish()
    return nc, P


CSTOP = 0


def build_C_odd(n_own=NOWN, n_ctx=NCTX, with_ctx=True):
    nt = n_own + n_ctx
    nc = bass.Bass("TRN2", target_bir_lowering=False)
    P = Prog(nc)
    inp = dict(
        x=dram_in(nc, "x", [n_own, D]), ctx=dram_in(nc, "ctx", [n_ctx, D]), ccols=dram_in(nc, "ccols", [128, 16]),
        ada_w=dram_in(nc, "ada_w", [D, 1024]), ada_b=dram_in(nc, "ada_b", [1, 1024]), norm_post=dram_in(nc, "norm_post", [1, D]),
        w_out=dram_in(nc, "w_out", [D, D]), ident=dram_in(nc, "ident", [128, 128]),
        gluh=dram_in(nc, "gluh", [4, 128, n_own + 30], BF16), gluch=dram_in(nc, "gluch", [4, 128, n_ctx + 30], BF16),
        cgs=dram_in(nc, "cgs", [nt, 512], BF16), yd=dram_in(nc, "yd", [nt, 512], BF16),
        convw=dram_in(nc, "convw", [128, 124]), convb=dram_in(nc, "convb", [128, 4]),
        lng=dram_in(nc, "lng", [1, 512]), lnb=dram_in(nc, "lnb", [1, 512]))
    xo_d = dram_out(nc, "xo", [n_own, D], F32)
    co_d = dram_out(nc, "co", [n_ctx, D], F32)
    C = mk_consts(P, inp)
    stage = [P.sbuf(f"stage{i}", [128, 4, 512], F32) for i in range(2)]
    PSY = P.psum("PSY", [128, 1024], F32)
    PST = P.psum("PST", [128, D], BF16)
    PSM = P.psum("PSM", [128, 512], F32)
    PSZ = [P.psum(f"PSZ{i}", [128, 512], BF16) for i in range(2)]
    KP = post_setup(P, C, inp, stage, PSM)
    cw = P.sbuf("cw", [128, 124], F32)
    P.dma("sp", cw[:, :], inp["convw"])
    cb = P.sbuf("cb", [128, 4], F32)
    P.dma("sp", cb[:, :], inp["convb"])
    lrow = P.sbuf("lrow", [1, 1024], F32)
    P.dma("sp", lrow[:, 0:512], inp["lng"])
    P.dma("sp", lrow[:, 512:1024], inp["lnb"])
    lgb = P.sbuf("lgb", [128, 1024], F32)
    for hh in range(2):
        P.mm(PSM[:, :], lhsT=C["ones"][:, :], rhs=lrow[0:1, hh * 512:(hh + 1) * 512])
        P.I("dve", "tensor_copy", out=lgb[:, hh * 512:(hh + 1) * 512], in_=PSM[:, :])
    glu = [P.sbuf(f"glu{i}", [128, 4, 542], BF16) for i in range(2)]
    dg = P.sbuf("dg", [128, 124, 128], BF16)
    for k_ in range(124):
        P.I("dve" if k_ % 2 == 0 else "pool", "tensor_scalar", out=dg[:, k_, :], in0=C["id32"][:, :], scalar1=cw[:, k_:k_ + 1], scalar2=None, op0=ALU.mult)
    PSC = [P.psum(f"PSC{i}", [128, 512], F32) for i in range(2)]
    zs = [P.sbuf(f"zs{i}", [128, 512], F32) for i in range(2)]
    zb = [P.sbuf(f"zb{i}", [128, 4, 512], BF16) for i in range(2)]
    zsq = P.sbuf("zsq", [128, 512], F32)
    st_ = [P.sbuf(f"lst{i}", [128, 8], F32) for i in range(2)]
    zn = [P.sbuf(f"zn{i}", [128, 512], F32) for i in range(2)]
    zl = [P.sbuf(f"zl{i}", [128, 512], F32) for i in range(2)]
    zsl = [P.sbuf(f"zsl{i}", [128, 512], F32) for i in range(2)]
    cg = [P.sbuf(f"cg{i}", [128, 512], BF16) for i in range(2)]
    mixs = [P.sbuf(f"mixs{i}", [128, D], BF16) for i in range(2)]
    blocks = [("x", b0, 512) for b0 in range(0, n_own, 512)]
    if with_ctx:
        blocks += [("c", 0, n_ctx)]
    kt = 0
    for bi, (kind, b0, nb) in enumerate(blocks):
        g_ = glu[bi % 2]
        src = inp["gluh"] if kind == "x" else inp["gluch"]
        P.dma("sp", g_.v(g_.h[:, :, 0:nb + 30]), src[:, :, b0:b0 + nb + 30].rearrange("c p t -> p c t"))
        for c in range(4):
            pc = PSC[(bi * 4 + c) % 2]
            for w in range(31):
                P.mm(pc[:, 0:nb], lhsT=dg[:, c * 31 + w, :], rhs=g_[:, c, w:w + nb], start=(w == 0), stop=(w == 30))
            P.act(zb[bi % 2][:, c, 0:nb], pc[:, 0:nb], AF.Identity, bias=cb[:, c:c + 1])
        if CSTOP == 1:
            continue
        for i in range(nb // 128):
            k2 = kt % 2
            kt += 1
            tok0 = (b0 if kind == "x" else n_own) + i * 128
            pz = PSZ[k2]
            for c in range(4):
                P.tr(pz[:, c * 128:(c + 1) * 128], zb[bi % 2][:, c, i * 128:(i + 1) * 128], C["idb"][:, :])
            P.act(zs[k2][:, :], pz[:, :], AF.Copy)
            if CSTOP == 2:
                continue
            s = st_[k2]
            P.I("dve", "tensor_reduce", out=s[:, 0:1], in_=zs[k2][:, :], axis=AX.X, op=ALU.add)
            P.act(zsq[:, :], zs[k2][:, :], AF.Square)
            P.I("dve", "tensor_reduce", out=s[:, 1:2], in_=zsq[:, :], axis=AX.X, op=ALU.add)
            P.I("dve", "tensor_scalar", out=s[:, 2:3], in0=s[:, 0:1], scalar1=1.0 / 512, scalar2=None, op0=ALU.mult)
            P.I("dve", "tensor_tensor", out=s[:, 3:4], in0=s[:, 2:3], in1=s[:, 2:3], op=ALU.mult)
            P.I("dve", "scalar_tensor_tensor", out=s[:, 4:5], in0=s[:, 1:2], scalar=1.0 / 512, in1=s[:, 3:4], op0=ALU.mult, op1=ALU.subtract)
            P.I("dve", "tensor_scalar", out=s[:, 5:6], in0=s[:, 4:5], scalar1=EPS, scalar2=None, op0=ALU.add)
            P.act(s[:, 6:7], s[:, 5:6], AF.Sqrt)
            P.I("dve", "reciprocal", out=s[:, 7:8], in_=s[:, 6:7])
            if CSTOP == 3:
                continue
            P.I("dve", "tensor_scalar", out=zn[k2][:, :], in0=zs[k2][:, :], scalar1=s[:, 2:3], scalar2=s[:, 7:8], op0=ALU.subtract, op1=ALU.mult)
            P.I("pool", "tensor_tensor", out=zl[k2][:, :], in0=zn[k2][:, :], in1=lgb[:, 0:512], op=ALU.mult)
            P.I("pool", "tensor_tensor", out=zl[k2][:, :], in0=zl[k2][:, :], in1=lgb[:, 512:1024], op=ALU.add)
            P.act(zsl[k2][:, :], zl[k2][:, :], AF.Silu)
            P.dma("sp", cg[k2][:, :], inp["cgs"][tok0:tok0 + 128, :])
            mt = mixs[k2]
            P.dma("sp", mt[:, 512:1024], inp["yd"][tok0:tok0 + 128, :])
            P.I("pool", "tensor_tensor", out=mt[:, 0:512], in0=zsl[k2][:, :], in1=cg[k2][:, :], op=ALU.mult)
            if CSTOP == 4:
                continue
            if kind == "x":
                post_tile(P, C, KP, None, inp["x"][tok0:tok0 + 128, :], xo_d[tok0:tok0 + 128, :], 0, PST, PSY, mix_in_sbuf=mt)
            else:
                r0 = i * 128
                post_tile(P, C, KP, None, inp["ctx"][r0:r0 + 128, :], co_d[r0:r0 + 128, :], 1, PST, PSY, mix_in_sbuf=mt)
    if not with_ctx:
        for i in range(n_ctx // 128):
            xt = KP.xt[i % 2]
            P.dma("sp", xt[:, :], inp["ctx"][i * 128:(i + 1) * 128, :])
            P.dma("sp", co_d[i * 128:(i + 1) * 128, :], xt[:, :])
    P.finish()
    return nc, P


GRID_W = 64
import math
def rope_tables(tok0, n):
    t = np.arange(tok0, tok0 + n)
    row = (t // GRID_W).astype(np.float32); col = (t % GRID_W).astype(np.float32)
    inv = (10000.0 ** (-np.arange(16, dtype=np.float32) / 16)).astype(np.float32)
    cosT = np.zeros((128, n), np.float32); sinT = np.zeros((128, n), np.float32)
    for r in range(128):
        d = r % 64
        pos = row if d < 32 else col
        ang = (pos * inv[d % 16]).astype(np.float32)
        cosT[r] = np.cos(ang); sinT[r] = np.sin(ang)
    return cosT, sinT
def rope_rmat():
    R = np.zeros((128, 128), np.float32)
    for i in range(128):
        d = i % 32
        if d < 16:
            R[i + 16, i] = -1.0
        else:
            R[i - 16, i] = 1.0
    return R
def ccols(c_b, c_ctx):
    return np.concatenate([c_b.reshape(8, 128).T, c_ctx.reshape(8, 128).T], axis=1).astype(np.float32).copy()

POOL_WINDOWS = (2, 4, 8, 16)
def pool_bands(L, tile0, is_first, is_last):
    out = np.zeros((4, 144, 128), np.float32)
    for g, w in enumerate(POOL_WINDOWS):
        for c in range(128):
            t = tile0 + c
            lo = max(t - w // 2, 0); hi = min(t + w // 2 - 1, L - 1) + 1
            cnt = hi - lo
            for s in range(lo, hi):
                r = s - (tile0 - 8)
                out[g, r, c] += 1.0 / cnt
            out[g, c + 8, c] -= 1.0
    return out
def band_pack(L_lat, own0, n_own, L_ctx):
    types = [pool_bands(L_lat, own0, True, False), pool_bands(L_lat, own0 + 128 if n_own > 256 else own0 + 128, False, False),
             pool_bands(L_lat, own0 + n_own - 128, False, True), pool_bands(L_ctx, 0, True, False), pool_bands(L_ctx, L_ctx - 128, False, True)]
    A = np.zeros((128, 2560), np.float32); B = np.zeros((16, 2560), np.float32)
    for t, bm in enumerate(types):
        for g in range(4):
            A[:, (t * 4 + g) * 128:(t * 4 + g + 1) * 128] = bm[g, :128]
            B[:, (t * 4 + g) * 128:(t * 4 + g + 1) * 128] = bm[g, 128:]
    return A, B
def halo(u_seq, a, n):
    L = u_seq.shape[0]
    out = np.zeros((n + 16, u_seq.shape[1]), u_seq.dtype)
    lo = max(a - 8, 0); hi = min(a + n + 8, L)
    out[lo - (a - 8): hi - (a - 8)] = u_seq[lo:hi]
    return out


def halo15_T(seqT, a, n):
    L = seqT.shape[2]
    out = np.zeros((4, 128, n + 30), seqT.dtype)
    lo = max(a - 15, 0)
    hi = min(a + n + 15, L)
    out[:, :, lo - (a - 15):hi - (a - 15)] = seqT[:, :, lo:hi]
    return out


_PROGS = {}
_DEBUG_HOOK = None


def _prog(key, fn):
    if key not in _PROGS:
        _PROGS[key] = fn()[0]
    return _PROGS[key]


def _run(nc, in_maps):
    res = run_bass_kernel_spmd(nc, in_maps, core_ids=list(range(8)))
    return res.results


def kernel(x, c, ctx, c_ctx, ada_w, ada_b, norm_pre, norm_post, w_in_even, w_out_even,
           pool_w, pool_scale, diff_lambda, diff_subln, w_in_odd, w_out_odd,
           conv_w, conv_b, conv_ln_g, conv_ln_b, hgrn_norm, hgrn_lb):
    f32 = lambda a: np.ascontiguousarray(np.asarray(a, dtype=np.float32))
    x, c, ctx, c_ctx, ada_w, ada_b = f32(x), f32(c), f32(ctx), f32(c_ctx), f32(ada_w), f32(ada_b)
    norm_pre, norm_post = f32(norm_pre), f32(norm_post)
    NB, L = x.shape[0], x.shape[1]
    S4 = 4
    ident = np.eye(128, dtype=np.float32)
    rmat = rope_rmat()
    xs = [np.ascontiguousarray(x[r // 4, (r % 4) * NOWN:(r % 4 + 1) * NOWN]) for r in range(8)]
    cs = [np.ascontiguousarray(ctx[b]) for b in range(NB)]
    cc = [ccols(c[b], c_ctx) for b in range(NB)]
    ropes = [rope_tables((r % 4) * NOWN, NOWN) for r in range(4)]
    bands = [band_pack(L, s * NOWN, NOWN, NCTX) for s in range(4)]
    masks = np.zeros((64, 128), np.float32)
    s_, t_ = np.meshgrid(np.arange(64), np.arange(64), indexing="ij")
    masks[:, :64] = (t_ >= s_)
    masks[:, 64:] = (s_ >= t_)
    cmask = np.ones((128, 2048), np.float32)
    cmask[:, ::64] = 0
    lbraw = np.ascontiguousarray(f32(hgrn_lb).reshape(2, 4, 4, 128).transpose(3, 0, 1, 2).reshape(128, 32))
    for l in range(4):
        j = l // 2
        aw_pre = np.ascontiguousarray(ada_w[l][:, :2048])
        ab_pre = np.ascontiguousarray(ada_b[l][None, :2048])
        aw_post = np.ascontiguousarray(ada_w[l][:, 2048:])
        ab_post = np.ascontiguousarray(ada_b[l][None, 2048:])
        npre = np.ascontiguousarray(norm_pre[l][None])
        npost = np.ascontiguousarray(norm_post[l][None])
        if l % 2 == 0:
            w_in = f32(w_in_even[j])
            ncA = _prog("A_even", build_A_even)
            ims = [dict(x=xs[r], ctx=cs[r // 4], ccols=cc[r // 4], ada_w=aw_pre, ada_b=ab_pre, norm_pre=npre, w_in=w_in,
                        ident=ident, rmat=rmat, cosT=ropes[r % 4][0], sinT=ropes[r % 4][1]) for r in range(8)]
            ra = _run(ncA, ims)
            ncB = _prog("B_even", build_B_even)
            lam_init = 0.8 - 0.6 * math.exp(-0.3 * l)
            pw = np.ascontiguousarray(f32(pool_w[j]).transpose(1, 0, 2).reshape(64, 256))
            ims = []
            for r in range(8):
                b, s = r // 4, r % 4
                grp = [ra[b * 4 + q] for q in range(4)]
                tm = ra[r]["tm"]
                kT = np.concatenate([g["fm"][6:12, :, :NOWN] for g in grp] + [ra[r]["fm"][6:12, :, NOWN:]], axis=2)
                vall = np.concatenate([g["tm"][:NOWN, 512:1280] for g in grp] + [tm[NOWN:, 512:1280]], axis=0)
                useq = np.concatenate([g["tm"][:NOWN, 0:256] for g in grp], axis=0)
                ims.append(dict(
                    x=xs[r], ctx=cs[b], ccols=cc[b], ada_w=aw_post, ada_b=ab_post, norm_post=npost, w_out=f32(w_out_even[j]), ident=ident,
                    qT=np.ascontiguousarray(ra[r]["fm"][0:6]), kT=np.ascontiguousarray(kT), vall=np.ascontiguousarray(vall),
                    gates=np.ascontiguousarray(np.concatenate([tm[:, 256:512], tm[:, 1280:2048]], axis=1)),
                    uh=halo(useq, s * NOWN, NOWN), uch=halo(np.ascontiguousarray(tm[NOWN:, 0:256]), 0, NCTX),
                    bandA=bands[s][0], bandB=bands[s][1], pool_w=pw, pool_scale=f32(pool_scale[j])[None].copy(),
                    lamp=f32(diff_lambda[j]).reshape(1, 256).copy(), subln=f32(diff_subln[j])[None].copy(),
                    lamc=np.array([[lam_init, 1.0 - lam_init]], np.float32)))
            rb = _run(ncB, ims)
            del ra
        else:
            ncA = _prog(("A_odd", l), lambda: build_A_odd(l))
            ims = [dict(x=xs[r], ctx=cs[r // 4], ccols=cc[r // 4], ada_w=aw_pre, ada_b=ab_pre, norm_pre=npre, w_in=f32(w_in_odd[j]),
                        ident=ident, lbraw=lbraw) for r in range(8)]
            ra = _run(ncA, ims)
            ncH = _prog("H", build_H)
            ims = []
            for r in range(8):
                b, hh = r // 4, r % 4
                grp = [ra[b * 4 + q] for q in range(4)]
                cat_f = lambda blk: np.ascontiguousarray(np.concatenate([grp[0]["fmf"][blk][:, NOWN:]] + [g["fmf"][blk][:, :NOWN] for g in grp], axis=1))
                cat_t = lambda c0: np.ascontiguousarray(np.concatenate([grp[0]["tm"][NOWN:, c0:c0 + 128]] + [g["tm"][:NOWN, c0:c0 + 128] for g in grp], axis=0))
                ims.append(dict(qT=cat_f(hh), fTf=cat_f(4 + hh), fTb=cat_f(8 + hh), v=cat_t(1024 + hh * 128), ogs=cat_t(512 + hh * 128),
                                gn=f32(hgrn_norm[j])[None, hh * 128:(hh + 1) * 128].copy(), ident=ident, masks=masks, cmask=cmask))
            rh = _run(ncH, ims)
            with_ctx = l < 3
            ncC = _prog(("C_odd", with_ctx), lambda: build_C_odd(NOWN, NCTX, with_ctx))
            cw = f32(conv_w[j])
            convw = np.ascontiguousarray(cw.T.reshape(4, 128, 31).transpose(1, 0, 2).reshape(128, 124))
            convb = np.ascontiguousarray(f32(conv_b[j]).reshape(4, 128).T)
            ims = []
            for r in range(8):
                b, s = r // 4, r % 4
                grp = [ra[b * 4 + q] for q in range(4)]
                gseq = np.concatenate([g["fmb"][:, :, :NOWN] for g in grp], axis=2)
                gctx = np.ascontiguousarray(ra[r]["fmb"][:, :, NOWN:])
                yd = np.concatenate([np.concatenate([rh[b * 4 + hh]["yd"][NCTX + s * NOWN: NCTX + (s + 1) * NOWN] for hh in range(4)], axis=1),
                                     np.concatenate([rh[b * 4 + hh]["yd"][0:NCTX] for hh in range(4)], axis=1)], axis=0)
                ims.append(dict(
                    x=xs[r], ctx=cs[b], ccols=cc[b], ada_w=aw_post, ada_b=ab_post, norm_post=npost, w_out=f32(w_out_odd[j]), ident=ident,
                    gluh=halo15_T(gseq, s * NOWN, NOWN), gluch=halo15_T(gctx, 0, NCTX),
                    cgs=np.ascontiguousarray(ra[r]["tm"][:, 0:512]), yd=np.ascontiguousarray(yd), convw=convw, convb=convb,
                    lng=f32(conv_ln_g[j])[None].copy(), lnb=f32(conv_ln_b[j])[None].copy()))
            rb = _run(ncC, ims)
            del ra, rh
        xs = [np.ascontiguousarray(rb[r]["xo"]) for r in range(8)]
        cs = [np.ascontiguousarray(rb[b * 4]["co"]) for b in range(NB)]
        if _DEBUG_HOOK is not None:
            _DEBUG_HOOK(l, xs, cs)
    out = np.stack([np.concatenate(xs[b * 4:(b + 1) * 4], axis=0) for b in range(NB)], axis=0)
    return out.astype(np.float32)
```
